# Optimizing a Trainium2 kernel written in Bass

```python
import math
import jax
import jax.numpy as jnp
from jax import lax
import numpy as np

D_MODEL = 1024
BATCH = 16
SEQ = 4096
DEPTH = 4

CHUNK = 64
N_MIXERS = 4
MEM_LEN = 256
ALPHA = (2 * DEPTH) ** 0.25
BETA = (8 * DEPTH) ** -0.25
LN_EPS = 1e-5
NEG_INF = -1e30
Q_BLOCK = 128

S5_GROUP = 16
S5_GROUPS = D_MODEL // S5_GROUP
S5_STATE = 64

DA_HEAD = 64
DA_HEADS = D_MODEL // (2 * DA_HEAD)

M2_INNER = 2 * D_MODEL
M2_HEADDIM = 64
M2_HEADS = M2_INNER // M2_HEADDIM
M2_GROUPS = 4
M2_STATE = 128
M2_CONV = 4
M2_XBC = M2_INNER + 2 * M2_GROUPS * M2_STATE

ML_INNER = 2 * D_MODEL
ML_HEADS = 4
ML_HEADDIM = ML_INNER // ML_HEADS
ML_CONV = 4

XA_HEADS = 4
XA_HEADDIM = D_MODEL // XA_HEADS

PK_HEADS = 8
PK_NKEYS = 128
PK_EXPERTS = PK_NKEYS * PK_NKEYS
PK_QDIM = 256
PK_TOPK = 16
PK_BLOCK = 128

kernel_name = 'hybrid_chunk_causal_encoder'


def layer_norm(x, g, b):
    xf = x.astype(jnp.float32)
    mu = jnp.mean(xf, axis=-1, keepdims=True)
    var = jnp.mean(jnp.square(xf - mu), axis=-1, keepdims=True)
    y = (xf - mu) * lax.rsqrt(var + LN_EPS) * g.astype(jnp.float32) + b.astype(jnp.float32)
    return y.astype(x.dtype)


def rms_norm(x):
    xf = x.astype(jnp.float32)
    return xf * lax.rsqrt(jnp.mean(jnp.square(xf), axis=-1, keepdims=True) + LN_EPS)


def head_layer_norm(x):
    xf = x.astype(jnp.float32)
    mu = jnp.mean(xf, axis=-1, keepdims=True)
    var = jnp.mean(jnp.square(xf - mu), axis=-1, keepdims=True)
    return (xf - mu) * lax.rsqrt(var + LN_EPS)


def causal_dwconv(x, w, b):
    k, c = w.shape
    y = lax.conv_general_dilated(x, w.astype(x.dtype)[:, None, :], window_strides=(1,),
                                 padding=[(k - 1, 0)], dimension_numbers=('NWC', 'WIO', 'NWC'),
                                 feature_group_count=c)
    return y + b.astype(x.dtype)


def to_chunks(a):
    b, l = a.shape[:2]
    return jnp.moveaxis(a.reshape(b, l // CHUNK, CHUNK, *a.shape[2:]), 1, 0)


def from_chunks(a):
    nc, b, q = a.shape[:3]
    return jnp.moveaxis(a, 0, 1).reshape(b, nc * q, *a.shape[3:])


def alibi_slopes(n_heads):
    return 2.0 ** (-8.0 * jnp.arange(1, n_heads + 1, dtype=jnp.float32) / n_heads)


def s5_mixer(x, lam_re, lam_im, log_dt, b_re, b_im, c_re, c_im, d_skip, w_glu, b_glu):
    bsz, seq, _ = x.shape
    f32 = jnp.float32
    lam = lax.complex(lam_re.astype(f32), lam_im.astype(f32))
    dt = jnp.exp(log_dt.astype(f32))[:, None]
    lam_bar = jnp.exp(lam * dt)
    b_bar = ((lam_bar - 1.0) / lam)[..., None] * lax.complex(b_re.astype(f32), b_im.astype(f32))
    c_mat = lax.complex(c_re.astype(f32), c_im.astype(f32))
    xf = x.astype(f32)
    xc = to_chunks(xf.reshape(bsz, seq, S5_GROUPS, S5_GROUP))
    a_seq = jnp.broadcast_to(lam_bar, (bsz, CHUNK, S5_GROUPS, S5_STATE))

    def combine(left, right):
        a_l, b_l = left
        a_r, b_r = right
        return a_r * a_l, a_r * b_l + b_r

    def step(h, xq):
        bu = jnp.einsum('gph,bqgh->bqgp', b_bar, xq)
        a_cum, h_loc = lax.associative_scan(combine, (a_seq, bu), axis=1)
        hs = h_loc + a_cum * h[:, None]
        y = jnp.einsum('ghp,bqgp->bqgh', c_mat, hs).real
        return hs[:, -1], y

    h0 = jnp.zeros((bsz, S5_GROUPS, S5_STATE), jnp.complex64)
    _, y = lax.scan(step, h0, xc)
    y = from_chunks(y).reshape(bsz, seq, D_MODEL) + d_skip.astype(f32) * xf
    y = jax.nn.gelu(y)
    val, gate = jnp.split(y @ w_glu.astype(f32) + b_glu.astype(f32), 2, axis=-1)
    return (val * jax.nn.sigmoid(gate)).astype(x.dtype)


def diff_attention(x, w_qkv, lam, subln_g, w_o, layer_idx):
    bsz, seq, _ = x.shape
    f32 = jnp.float32
    n_h, d = DA_HEADS, DA_HEAD
    q, k, v = jnp.split(x @ w_qkv, 3, axis=-1)
    q = q.reshape(bsz, seq, n_h, 2, d)
    k = k.reshape(bsz, seq, n_h, 2, d)
    v = v.reshape(bsz, seq, n_h, 2 * d)
    lam_init = 0.8 - 0.6 * math.exp(-0.3 * layer_idx)
    lf = lam.astype(f32)
    lam_full = jnp.exp(jnp.sum(lf[0] * lf[1])) - jnp.exp(jnp.sum(lf[2] * lf[3])) + lam_init
    slopes = alibi_slopes(n_h)
    k_pos = jnp.arange(seq)
    n_blk = seq // Q_BLOCK
    q_blocks = jnp.moveaxis(q.reshape(bsz, n_blk, Q_BLOCK, n_h, 2, d), 1, 0)
    scale = d ** -0.5

    def attend_block(args):
        qb, blk = args
        t = blk * Q_BLOCK + jnp.arange(Q_BLOCK)
        s = jnp.einsum('bqhjd,bkhjd->bhjqk', qb, k).astype(f32) * scale
        dist = jnp.abs(t[:, None] - k_pos[None, :]).astype(f32)
        bias = -slopes[:, None, None] * dist
        allowed = (k_pos // CHUNK)[None, :] <= (t // CHUNK)[:, None]
        s = jnp.where(allowed, s + bias[None, :, None], NEG_INF)
        p = jax.nn.softmax(s, axis=-1)
        a = p[:, :, 0] - lam_full * p[:, :, 1]
        return jnp.einsum('bhqk,bkhe->bqhe', a.astype(v.dtype), v)

    o = lax.map(attend_block, (q_blocks, jnp.arange(n_blk)))
    o = jnp.moveaxis(o, 0, 1).reshape(bsz, seq, n_h, 2 * d)
    o = rms_norm(o) * subln_g.astype(f32) * (1.0 - lam_init)
    return o.reshape(bsz, seq, D_MODEL).astype(x.dtype) @ w_o


def mamba2_mixer(x, w_in, conv_w, conv_b, dt_bias, a_log, d_skip, norm_g, w_out):
    bsz, seq, _ = x.shape
    f32 = jnp.float32
    n_g, n_r, n_p, n_n = M2_GROUPS, M2_HEADS // M2_GROUPS, M2_HEADDIM, M2_STATE
    z, xbc, dt = jnp.split(x @ w_in, [M2_INNER, M2_INNER + M2_XBC], axis=-1)
    xbc = jax.nn.silu(causal_dwconv(xbc, conv_w, conv_b))
    xs, bm, cm = jnp.split(xbc, [M2_INNER, M2_INNER + n_g * n_n], axis=-1)
    dt = jax.nn.softplus(dt.astype(f32) + dt_bias.astype(f32))
    a_gr = -jnp.exp(a_log.astype(f32)).reshape(n_g, n_r)
    xs = xs.astype(f32).reshape(bsz, seq, n_g, n_r, n_p)
    tri = jnp.tril(jnp.ones((CHUNK, CHUNK), dtype=bool))

    def step(h, inp):
        xq, dtq, bq, cq = inp
        a_cs = jnp.cumsum(dtq * a_gr, axis=1)
        seg = a_cs[:, :, None] - a_cs[:, None, :]
        lmat = jnp.exp(jnp.where(tri[None, :, :, None, None], seg, NEG_INF))
        xdt = xq * dtq[..., None]
        cb = jnp.einsum('bign,bjgn->bijg', cq, bq)
        y_diag = jnp.einsum('bijg,bijgr,bjgrp->bigrp', cb, lmat, xdt)
        y_off = jnp.einsum('bign,bgrpn->bigrp', cq, h) * jnp.exp(a_cs)[..., None]
        decay = jnp.exp(a_cs[:, -1:] - a_cs)
        h_new = h * jnp.exp(a_cs[:, -1])[..., None, None] + jnp.einsum('bjgn,bjgr,bjgrp->bgrpn', bq, decay, xdt)
        return h_new, y_diag + y_off

    h0 = jnp.zeros((bsz, n_g, n_r, n_p, n_n), f32)
    inputs = (to_chunks(xs), to_chunks(dt.reshape(bsz, seq, n_g, n_r)),
              to_chunks(bm.astype(f32).reshape(bsz, seq, n_g, n_n)),
              to_chunks(cm.astype(f32).reshape(bsz, seq, n_g, n_n)))
    _, y = lax.scan(step, h0, inputs)
    y = from_chunks(y) + d_skip.astype(f32).reshape(n_g, n_r)[:, :, None] * xs
    y = y.reshape(bsz, seq, M2_INNER) * jax.nn.silu(z.astype(f32))
    y = rms_norm(y.reshape(bsz, seq, n_g, M2_INNER // n_g)).reshape(bsz, seq, M2_INNER) * norm_g.astype(f32)
    return y.astype(x.dtype) @ w_out


def mlstm_mixer(x, w_in, conv_w, conv_b, w_q, w_k, w_v, w_gates, b_gates, norm_g, skip, w_down):
    bsz, seq, _ = x.shape
    f32 = jnp.float32
    n_h, dh = ML_HEADS, ML_HEADDIM
    xm, og = jnp.split(x @ w_in, 2, axis=-1)
    xc = jax.nn.silu(causal_dwconv(xm, conv_w, conv_b))
    xch = xc.reshape(bsz, seq, n_h, dh)
    q = jnp.einsum('blhd,hde->blhe', xch, w_q)
    k = jnp.einsum('blhd,hde->blhe', xch, w_k) * dh ** -0.5
    v = jnp.einsum('blhd,hde->blhe', xm.reshape(bsz, seq, n_h, dh), w_v)
    gates = (q.reshape(bsz, seq, ML_INNER) @ w_gates[0] + k.reshape(bsz, seq, ML_INNER) @ w_gates[1]
             + v.reshape(bsz, seq, ML_INNER) @ w_gates[2] + b_gates).astype(f32)
    log_i = gates[..., :n_h]
    log_f = jax.nn.log_sigmoid(gates[..., n_h:])
    tri = jnp.tril(jnp.ones((CHUNK, CHUNK), dtype=bool))

    def heads_first(a):
        return jnp.moveaxis(to_chunks(a.astype(f32)), 3, 2)

    def step(carry, inp):
        c_st, n_st, m_st = carry
        qq, kk, vv, ii, ff = inp
        bcs = jnp.cumsum(ff, axis=-1)
        dmat = jnp.where(tri, bcs[..., :, None] - bcs[..., None, :] + ii[..., None, :], NEG_INF)
        inter = bcs + m_st[..., None]
        m_row = jnp.maximum(jnp.max(dmat, axis=-1), inter)
        s = jnp.einsum('bhid,bhjd->bhij', qq, kk) * jnp.exp(dmat - m_row[..., None])
        w_inter = jnp.exp(inter - m_row)
        num = jnp.einsum('bhij,bhjd->bhid', s, vv) + w_inter[..., None] * jnp.einsum('bhid,bhde->bhie', qq, c_st)
        den = jnp.sum(s, axis=-1) + w_inter * jnp.einsum('bhid,bhd->bhi', qq, n_st)
        h = num / jnp.maximum(jnp.abs(den), jnp.exp(-m_row))[..., None]
        b_last = bcs[..., -1]
        g = b_last[..., None] - bcs + ii
        m_new = jnp.maximum(b_last + m_st, jnp.max(g, axis=-1))
        wk = jnp.exp(g - m_new[..., None])
        carry_decay = jnp.exp(b_last + m_st - m_new)
        c_new = carry_decay[..., None, None] * c_st + jnp.einsum('bhj,bhjd,bhje->bhde', wk, kk, vv)
        n_new = carry_decay[..., None] * n_st + jnp.einsum('bhj,bhjd->bhd', wk, kk)
        return (c_new, n_new, m_new), h

    carry0 = (jnp.zeros((bsz, n_h, dh, dh), f32), jnp.zeros((bsz, n_h, dh), f32), jnp.zeros((bsz, n_h), f32))
    inputs = (heads_first(q), heads_first(k), heads_first(v), heads_first(log_i), heads_first(log_f))
    _, h = lax.scan(step, carry0, inputs)
    h = from_chunks(jnp.moveaxis(h, 2, 3))
    hn = head_layer_norm(h) * norm_g.astype(f32).reshape(n_h, dh)
    hn = hn + skip.astype(f32).reshape(n_h, dh) * xch.astype(f32)
    out = hn.reshape(bsz, seq, ML_INNER) * jax.nn.sigmoid(og.astype(f32))
    return out.astype(x.dtype) @ w_down


def mem_cross_attention(x, mem, w_q, w_kv, w_o):
    bsz, seq, _ = x.shape
    q = (x @ w_q).reshape(bsz, seq, XA_HEADS, XA_HEADDIM)
    k, v = jnp.split(mem @ w_kv, 2, axis=-1)
    k = k.reshape(bsz, -1, XA_HEADS, XA_HEADDIM)
    v = v.reshape(bsz, -1, XA_HEADS, XA_HEADDIM)
    s = jnp.einsum('blhd,bmhd->bhlm', q, k).astype(jnp.float32) * XA_HEADDIM ** -0.5
    p = jax.nn.softmax(s, axis=-1)
    o = jnp.einsum('bhlm,bmhd->blhd', p.astype(v.dtype), v)
    return o.reshape(bsz, seq, D_MODEL) @ w_o


def peer_ffn(x, w_query, sub_keys, u, v):
    bsz, seq, dm = x.shape
    f32 = jnp.float32
    xb = x.reshape(bsz * seq // PK_BLOCK, PK_BLOCK, dm)
    half = PK_QDIM // 2

    def block(xt):
        tb = xt.shape[0]
        q = (xt @ w_query).reshape(tb, PK_HEADS, 2, half).astype(f32)
        s = jnp.einsum('thjd,jkd->thjk', q, sub_keys.astype(f32))
        sv, si = lax.top_k(s, PK_TOPK)
        cand = sv[:, :, 0, :, None] + sv[:, :, 1, None, :]
        cand_idx = si[:, :, 0, :, None] * PK_NKEYS + si[:, :, 1, None, :]
        cv, ci = lax.top_k(cand.reshape(tb, PK_HEADS, PK_TOPK * PK_TOPK), PK_TOPK)
        eidx = jnp.take_along_axis(cand_idx.reshape(tb, PK_HEADS, PK_TOPK * PK_TOPK), ci, axis=-1)
        g = jax.nn.softmax(cv, axis=-1)
        ue = u[eidx]
        ve = v[eidx]
        act = jax.nn.gelu(jnp.einsum('thkd,td->thk', ue, xt).astype(f32)) * g
        return jnp.einsum('thk,thkd->td', act.astype(ve.dtype), ve)

    y = lax.map(block, xb)
    return y.reshape(bsz, seq, dm).astype(x.dtype)


def setup_inputs(seed: int = 0) -> dict:
    key = jax.random.key(seed)
    ks = iter(jax.random.split(key, 64))
    f32 = jnp.float32

    def nrm(shape, scale):
        return jax.random.normal(next(ks), shape, f32) * scale

    def unif(shape, lo, hi):
        return jax.random.uniform(next(ks), shape, f32, lo, hi)

    n_a, n_b, n_c, n_d = [(DEPTH - t + N_MIXERS - 1) // N_MIXERS for t in range(N_MIXERS)]
    dm = D_MODEL
    inp = {}
    inp['x'] = nrm((BATCH, SEQ, dm), 1.0)
    inp['mem'] = nrm((BATCH, MEM_LEN, dm), 1.0)
    n_idx = jnp.arange(S5_STATE, dtype=f32)
    inp['s5_lam_re'] = -0.5 + nrm((n_a, S5_GROUPS, S5_STATE), 0.01)
    inp['s5_lam_im'] = math.pi * n_idx + nrm((n_a, S5_GROUPS, S5_STATE), 0.01)
    inp['s5_log_dt'] = unif((n_a, S5_GROUPS), math.log(1e-3), math.log(1e-1))
    inp['s5_b_re'] = nrm((n_a, S5_GROUPS, S5_STATE, S5_GROUP), (2 * S5_GROUP) ** -0.5)
    inp['s5_b_im'] = nrm((n_a, S5_GROUPS, S5_STATE, S5_GROUP), (2 * S5_GROUP) ** -0.5)
    inp['s5_c_re'] = nrm((n_a, S5_GROUPS, S5_GROUP, S5_STATE), 0.5)
    inp['s5_c_im'] = nrm((n_a, S5_GROUPS, S5_GROUP, S5_STATE), 0.5)
    inp['s5_d'] = nrm((n_a, dm), 1.0)
    inp['s5_w_glu'] = jnp.concatenate([nrm((n_a, dm, dm), dm ** -0.5 * BETA), nrm((n_a, dm, dm), dm ** -0.5)], axis=-1)
    inp['s5_b_glu'] = nrm((n_a, 2 * dm), 0.01)
    inp['da_w_qkv'] = nrm((n_b, dm, 3 * dm), dm ** -0.5)
    inp['da_lambda'] = nrm((n_b, 4, DA_HEAD), 0.1)
    inp['da_subln_g'] = 1.0 + nrm((n_b, 2 * DA_HEAD), 0.01)
    inp['da_w_o'] = nrm((n_b, dm, dm), dm ** -0.5 * BETA)
    dt0 = jnp.exp(unif((n_c, M2_HEADS), math.log(1e-3), math.log(1e-1)))
    inp['m2_w_in'] = nrm((n_c, dm, M2_INNER + M2_XBC + M2_HEADS), dm ** -0.5)
    inp['m2_conv_w'] = nrm((n_c, M2_CONV, M2_XBC), M2_CONV ** -0.5)
    inp['m2_conv_b'] = nrm((n_c, M2_XBC), 0.01)
    inp['m2_dt_bias'] = dt0 + jnp.log(-jnp.expm1(-dt0))
    inp['m2_a_log'] = jnp.log(unif((n_c, M2_HEADS), 1.0, 16.0))
    inp['m2_d'] = 1.0 + nrm((n_c, M2_HEADS), 0.01)
    inp['m2_norm_g'] = 1.0 + nrm((n_c, M2_INNER), 0.01)
    inp['m2_w_out'] = nrm((n_c, M2_INNER, dm), M2_INNER ** -0.5 * BETA)
    inp['ml_w_in'] = nrm((n_d, dm, 2 * ML_INNER), dm ** -0.5)
    inp['ml_conv_w'] = nrm((n_d, ML_CONV, ML_INNER), ML_CONV ** -0.5)
    inp['ml_conv_b'] = nrm((n_d, ML_INNER), 0.01)
    inp['ml_w_q'] = nrm((n_d, ML_HEADS, ML_HEADDIM, ML_HEADDIM), ML_HEADDIM ** -0.5)
    inp['ml_w_k'] = nrm((n_d, ML_HEADS, ML_HEADDIM, ML_HEADDIM), ML_HEADDIM ** -0.5)
    inp['ml_w_v'] = nrm((n_d, ML_HEADS, ML_HEADDIM, ML_HEADDIM), ML_HEADDIM ** -0.5)
    inp['ml_w_gates'] = nrm((n_d, 3, ML_INNER, 2 * ML_HEADS), (3 * ML_INNER) ** -0.5)
    inp['ml_b_gates'] = jnp.concatenate([nrm((n_d, ML_HEADS), 0.1),
                                         jnp.linspace(3.0, 6.0, ML_HEADS, dtype=f32) + nrm((n_d, ML_HEADS), 0.01)], axis=-1)
    inp['ml_norm_g'] = 1.0 + nrm((n_d, ML_INNER), 0.01)
    inp['ml_skip'] = 1.0 + nrm((n_d, ML_INNER), 0.01)
    inp['ml_w_down'] = nrm((n_d, ML_INNER, dm), ML_INNER ** -0.5 * BETA)
    inp['xa_w_q'] = nrm((DEPTH, dm, dm), dm ** -0.5)
    inp['xa_w_kv'] = nrm((DEPTH, dm, 2 * dm), dm ** -0.5)
    inp['xa_w_o'] = nrm((DEPTH, dm, dm), dm ** -0.5 * BETA)
    inp['pk_w_query'] = nrm((DEPTH, dm, PK_HEADS * PK_QDIM), dm ** -0.5)
    inp['pk_sub_keys'] = nrm((DEPTH, 2, PK_NKEYS, PK_QDIM // 2), (PK_QDIM // 2) ** -0.5)
    inp['pk_u'] = nrm((DEPTH, PK_EXPERTS, dm), dm ** -0.5)
    inp['pk_v'] = nrm((DEPTH, PK_EXPERTS, dm), BETA * PK_HEADS ** -0.5)
    inp['ln_g'] = 1.0 + nrm((DEPTH, 3, dm), 0.01)
    inp['ln_b'] = nrm((DEPTH, 3, dm), 0.01)
    return inp


def reference(x, mem,
              s5_lam_re, s5_lam_im, s5_log_dt, s5_b_re, s5_b_im, s5_c_re, s5_c_im, s5_d, s5_w_glu, s5_b_glu,
              da_w_qkv, da_lambda, da_subln_g, da_w_o,
              m2_w_in, m2_conv_w, m2_conv_b, m2_dt_bias, m2_a_log, m2_d, m2_norm_g, m2_w_out,
              ml_w_in, ml_conv_w, ml_conv_b, ml_w_q, ml_w_k, ml_w_v, ml_w_gates, ml_b_gates, ml_norm_g, ml_skip, ml_w_down,
              xa_w_q, xa_w_kv, xa_w_o,
              pk_w_query, pk_sub_keys, pk_u, pk_v,
              ln_g, ln_b):
    h = x
    for i in range(DEPTH):
        kind, j = i % N_MIXERS, i // N_MIXERS
        if kind == 0:
            y = s5_mixer(h, s5_lam_re[j], s5_lam_im[j], s5_log_dt[j], s5_b_re[j], s5_b_im[j],
                         s5_c_re[j], s5_c_im[j], s5_d[j], s5_w_glu[j], s5_b_glu[j])
        elif kind == 1:
            y = diff_attention(h, da_w_qkv[j], da_lambda[j], da_subln_g[j], da_w_o[j], i)
        elif kind == 2:
            y = mamba2_mixer(h, m2_w_in[j], m2_conv_w[j], m2_conv_b[j], m2_dt_bias[j], m2_a_log[j],
                             m2_d[j], m2_norm_g[j], m2_w_out[j])
        else:
            y = mlstm_mixer(h, ml_w_in[j], ml_conv_w[j], ml_conv_b[j], ml_w_q[j], ml_w_k[j], ml_w_v[j],
                            ml_w_gates[j], ml_b_gates[j], ml_norm_g[j], ml_skip[j], ml_w_down[j])
        h = layer_norm(ALPHA * h + y, ln_g[i, 0], ln_b[i, 0])
        h = layer_norm(ALPHA * h + mem_cross_attention(h, mem, xa_w_q[i], xa_w_kv[i], xa_w_o[i]), ln_g[i, 1], ln_b[i, 1])
        h = layer_norm(ALPHA * h + peer_ffn(h, pk_w_query[i], pk_sub_keys[i], pk_u[i], pk_v[i]), ln_g[i, 2], ln_b[i, 2])
    return h
```

```python
import numpy as np
import ml_dtypes
from contextlib import ExitStack
import concourse.bass as bass
import concourse.mybir as mybir
from concourse.bass_utils import run_bass_kernel_spmd

F32 = mybir.dt.float32
BF16 = mybir.dt.bfloat16
I32 = mybir.dt.int32
U32 = mybir.dt.uint32
AF = mybir.ActivationFunctionType
ALU = mybir.AluOpType
AX = mybir.AxisListType

ENGS = ("pe", "dve", "act", "pool", "sp")
NDSEM = 8


class Res:
    __slots__ = ("name", "w", "r")

    def __init__(self, name):
        self.name = name
        self.w = None
        self.r = {}


class T:
    __slots__ = ("t", "res")

    def __init__(self, t, res):
        self.t = t
        self.res = res

    def __getitem__(self, key):
        return self.t[key]

    def ap(self):
        return self.t.ap()


class KB:
    def __init__(self):
        self.nc = bass.Bass("TRN2", target_bir_lowering=False)
        self.stack = ExitStack()
        self.stacks = [self.stack]
        self.q = {e: [] for e in ENGS}
        self.cnt = {e: 0 for e in ENGS}
        self.waited = {e: {} for e in ENGS}
        self.sems = {}
        for e in ENGS:
            self.sems[e] = self.stack.enter_context(self.nc.semaphore("s_" + e))
        self.dsem = {}
        self.dval = {}
        self.dnext = {}
        for qn in ("sp", "pool", "act"):
            self.dsem[qn] = [self.stack.enter_context(self.nc.semaphore("d_%s%d" % (qn, i))) for i in range(NDSEM)]
            self.dval[qn] = [0] * NDSEM
            self.dnext[qn] = 0
        self.n_ins = 0
        self.uid = 0
        nc = self.nc
        self.eng = {"pe": nc.tensor, "dve": nc.vector, "act": nc.scalar, "pool": nc.gpsimd, "sp": nc.sync}

    def sb(self, shape, dtype, name=None):
        self.uid += 1
        name = "%s_%d" % (name or "sb", self.uid)
        t = self.stacks[-1].enter_context(self.nc.sbuf_tensor(name, list(shape), dtype))
        return T(t, Res(name))

    def ps(self, shape, dtype, name=None):
        self.uid += 1
        name = "%s_%d" % (name or "ps", self.uid)
        t = self.stacks[-1].enter_context(self.nc.psum_tensor(name, list(shape), dtype))
        return T(t, Res(name))

    def dram(self, name, shape, dtype, kind="Internal"):
        t = self.nc.dram_tensor(name, list(shape), dtype, kind=kind)
        return T(t, Res(name))

    def _wait(self, e, tok):
        key, val = tok
        if self.waited[e].get(key, 0) >= val:
            return
        self.waited[e][key] = val
        sem = self._sem(key)
        self.eng[e].wait_ge(sem, val)

    def _sem(self, key):
        if isinstance(key, str):
            return self.sems[key]
        return self.dsem[key[0]][key[1]]

    def _deps(self, e, reads, writes, skip_self=False):
        toks = []
        for r in reads:
            if r.w is not None:
                toks.append(r.w)
        for w in writes:
            if w.w is not None:
                toks.append(w.w)
            toks.extend(w.r.items())
        for tok in toks:
            if skip_self and tok[0] == e:
                continue
            self._wait(e, tok)

    def _commit(self, tok, reads, writes):
        for r in reads:
            r.r[tok[0]] = tok[1]
        for w in writes:
            w.w = tok
            w.r = {}

    @staticmethod
    def _res(lst):
        out = []
        for x in lst:
            if x is None:
                continue
            out.append(x.res if isinstance(x, T) else x)
        return out

    def op(self, e, fn, reads=(), writes=()):
        reads = self._res(reads)
        writes = self._res(writes)
        self._deps(e, reads, writes, skip_self=(e == "pe"))
        self.cnt[e] += 1
        tok = (e, self.cnt[e])
        sem = self.sems[e]
        fn(self.eng[e]).then_inc(sem, 1)
        self._commit(tok, reads, writes)
        self.n_ins += 1

    def dma(self, qn, out, in_, reads=(), writes=(), **kw):
        reads = self._res(reads)
        writes = self._res(writes)
        i = self.dnext[qn]
        self.dnext[qn] = (i + 1) % NDSEM
        key = (qn, i)
        if self.dval[qn][i] > 0:
            self._wait(qn, (key, self.dval[qn][i]))
        self._deps(qn, reads, writes)
        self.dval[qn][i] += 16
        tok = (key, self.dval[qn][i])
        sem = self.dsem[qn][i]
        self.eng[qn].dma_start(out=out, in_=in_, **kw).then_inc(sem, 16)
        self._commit(tok, reads, writes)
        self.n_ins += 1

    def barrier(self):
        for e in ENGS:
            for qn in self.dsem:
                for i in range(NDSEM):
                    if self.dval[qn][i] > 0:
                        self._wait(e, ((qn, i), self.dval[qn][i]))
            for e2 in ENGS:
                if e2 != e and self.cnt[e2] > 0:
                    self._wait(e, (e2, self.cnt[e2]))

    def scope(self):
        kb = self

        class _S:
            def __enter__(s2):
                kb.stacks.append(ExitStack())

            def __exit__(s2, *a):
                kb.barrier()
                kb.stacks.pop().close()
                return False
        return _S()

    def finish(self, final_res=()):
        for qn in self.dsem:
            for i in range(NDSEM):
                if self.dval[qn][i] > 0:
                    self._wait("sp", ((qn, i), self.dval[qn][i]))
        for e in ENGS:
            if e != "sp" and self.cnt[e] > 0:
                self._wait("sp", (e, self.cnt[e]))
        self.stack.close()
        return self.nc

import os
DBG = os.environ.get('KDBG', '')

D = 1024
ALPHA = 8 ** 0.25
LN_EPS = 1e-5
NEG = -30000.0

INPUT_SPECS = [
    ("x", None), ("mem", None),
    ("s5_lam_re", (1, 64, 64)), ("s5_lam_im", (1, 64, 64)), ("s5_log_dt", (1, 64)),
    ("s5_b_re", (1, 64, 64, 16)), ("s5_b_im", (1, 64, 64, 16)), ("s5_c_re", (1, 64, 16, 64)), ("s5_c_im", (1, 64, 16, 64)),
    ("s5_d", (1, 1024)), ("s5_w_glu", (1, 1024, 2048)), ("s5_b_glu", (1, 2048)),
    ("da_w_qkv", (1, 1024, 3072)), ("da_lambda", (1, 4, 64)), ("da_subln_g", (1, 128)), ("da_w_o", (1, 1024, 1024)),
    ("m2_w_in", (1, 1024, 5152)), ("m2_conv_w", (1, 4, 3072)), ("m2_conv_b", (1, 3072)), ("m2_dt_bias", (1, 32)),
    ("m2_a_log", (1, 32)), ("m2_d", (1, 32)), ("m2_norm_g", (1, 2048)), ("m2_w_out", (1, 2048, 1024)),
    ("ml_w_in", (1, 1024, 4096)), ("ml_conv_w", (1, 4, 2048)), ("ml_conv_b", (1, 2048)),
    ("ml_w_q", (1, 4, 512, 512)), ("ml_w_k", (1, 4, 512, 512)), ("ml_w_v", (1, 4, 512, 512)),
    ("ml_w_gates", (1, 3, 2048, 8)), ("ml_b_gates", (1, 8)), ("ml_norm_g", (1, 2048)), ("ml_skip", (1, 2048)),
    ("ml_w_down", (1, 2048, 1024)),
    ("xa_w_q", (4, 1024, 1024)), ("xa_w_kv", (4, 1024, 2048)), ("xa_w_o", (4, 1024, 1024)),
    ("pk_w_query", (4, 1024, 2048)), ("pk_sub_keys", (4, 2, 128, 128)), ("pk_u", (4, 16384, 1024)), ("pk_v", (4, 16384, 1024)),
    ("ln_g", (4, 3, 1024)), ("ln_b", (4, 3, 1024)),
]


class Ctx:
    pass


def host_consts(T_=4096):
    c = {}
    hm = np.zeros((128, 2), np.float32)
    hm[:64, 0] = 0.125
    hm[64:, 1] = 0.125
    c["c_hmask"] = hm
    c["c_r0"] = np.tile(-(T_ - np.arange(T_, dtype=np.float32))[None, :], (128, 1)).astype(np.float32)
    qq = np.arange(128, dtype=np.float32)
    c["c_dbase"] = (-np.abs(qq[:, None] - qq[None, :]) + qq[:, None]).astype(np.float32)
    dm = np.zeros((128, 128), np.float32)
    dm[:64, 64:] = NEG
    c["c_dmask"] = dm
    sl = np.zeros((128, 4, 128), np.float32)
    for hh in range(4):
        sl[hh, hh, :] = 1.0
    c["c_sel"] = sl
    c["c_triU"] = np.triu(np.ones((128, 128), np.float32))
    c["c_Lst"] = np.tril(np.ones((128, 128), np.float32), -1)
    c["c_ident"] = np.eye(128, dtype=np.float32)
    c["c_iota16"] = np.tile(np.arange(16, dtype=np.float32)[None, :], (128, 1))
    J = np.zeros((128, 128), np.float32)
    for p in range(64):
        J[p, p + 64] = -1.0
        J[p + 64, p] = 1.0
    c["c_J"] = J
    sg = np.ones((128, 1), np.float32)
    sg[:64] = -1.0
    c["c_sgn"] = sg
    gm = np.zeros((128, 8), np.float32)
    cm = np.zeros((128, 8, 128), np.float32)
    for j in range(8):
        gm[j * 16:(j + 1) * 16, j] = 1.0
        cm[:, j, j * 16:(j + 1) * 16] = 1.0
    c["c_gmask"] = gm
    c["c_cmask"] = cm
    return c


def setup(T_, NSEQ, needed):
    c = Ctx()
    k = KB()
    c.k = k
    c.T = T_
    c.NPF = 3
    c.NSEQ = NSEQ
    c.NT = T_ * NSEQ
    c.NTILES = c.NT // 128
    c.inp = {}
    for name, shp in INPUT_SPECS:
        if name not in needed:
            continue
        if name == "x":
            shp = (c.NT, D)
        elif name == "mem":
            shp = (NSEQ * 256, D)
        c.inp[name] = k.dram(name, list(shp), F32, kind="ExternalInput")
    for name, arr in host_consts(T_).items():
        c.inp[name] = k.dram(name, list(arr.shape), F32, kind="ExternalInput")
    c.out = k.dram("out", [c.NT, D], F32, kind="ExternalOutput")
    c.hA = k.dram("hA", [c.NT, D], F32)
    c.hB = k.dram("hB", [c.NT, D], F32)
    c.identf = k.sb([128, 128], F32, "identf")
    c.identb = k.sb([128, 128], BF16, "identb")
    k.dma("sp", c.identf[:], c.inp["c_ident"][:, :], writes=[c.identf])
    k.op("dve", lambda e: e.tensor_copy(out=c.identb[:], in_=c.identf[:]), [c.identf], [c.identb])
    c.pf = [k.ps([128, 1024], F32, "pf%d" % i) for i in range(c.NPF)]
    c.pb = [k.ps([128, 1024], BF16, "pb%d" % i) for i in range(2)]
    c.pfi = 0
    c.pbi = 0
    c.pf_n = len(c.pf)
    return c


def psf(c):
    c.pfi = (c.pfi + 1) % c.pf_n
    return c.pf[c.pfi]


def psb(c):
    c.pbi = (c.pbi + 1) % len(c.pb)
    return c.pb[c.pbi]


def load_w(c, dst, src_ap, K, N, q="pool"):
    k = c.k
    KC = K // 128
    for n0 in range(0, N, 2048):
        n1 = min(N, n0 + 2048)
        for c0 in range(0, KC, 8):
            c1 = min(KC, c0 + 8)
            k.dma(q, dst[:, c0:c1, n0:n1],
                  src_ap[c0 * 128:c1 * 128, n0:n1].rearrange("(c p) n -> p c n", p=128), writes=[dst])


def load_bc(c, dst, src_ap, q="sp"):
    c.k.dma(q, dst[:], src_ap.broadcast_to([128, src_ap.shape[-1]]), writes=[dst])


def transpose_into(c, dst_fn, src_fn, C, src, dst, evac="act"):
    k = c.k
    p = psb(c)
    for ci in range(C):
        k.op("pe", lambda e, ci=ci: e.transpose(out=p[:, ci * 128:(ci + 1) * 128], in_=src_fn(ci), identity=c.identb[:]),
             [src, c.identb], [p])
    if evac == "act":
        k.op("act", lambda e: e.copy(out=dst_fn(), in_=p[:, 0:C * 128]), [p], [dst])
    else:
        k.op("dve", lambda e: e.tensor_copy(out=dst_fn(), in_=p[:, 0:C * 128]), [p], [dst])


def tail_ln(c, ht, y_fn, ysrc, g_bc, b_bc, outt, z, st, mv, rstd):
    k = c.k
    for hf in range(2):
        sl = slice(hf * 512, (hf + 1) * 512)
        k.op("dve", lambda e, hf=hf, sl=sl: e.scalar_tensor_tensor(out=z[:, sl], in0=ht[:, sl], scalar=ALPHA, in1=y_fn(hf),
                                                                    op0=ALU.mult, op1=ALU.add), [ht, ysrc], [z])
        k.op("dve", lambda e, hf=hf, sl=sl: e.bn_stats(out=st[:, hf, :], in_=z[:, sl]), [z], [st])
    k.op("dve", lambda e: e.bn_aggr(out=mv[:], in_=st[:].rearrange("p a b -> p (a b)")), [st], [mv])
    k.op("act", lambda e: e.activation(out=rstd[:], in_=mv[:, 1:2], func=AF.Sqrt, bias=LN_EPS, scale=1.0), [mv], [rstd])
    k.op("dve", lambda e: e.reciprocal(out=rstd[:], in_=rstd[:]), [rstd], [rstd])
    k.op("dve", lambda e: e.tensor_scalar(out=z[:], in0=z[:], scalar1=mv[:, 0:1], scalar2=rstd[:, 0:1],
                                          op0=ALU.subtract, op1=ALU.mult), [z, mv, rstd], [z])
    k.op("pool", lambda e: e.tensor_tensor(out=z[:], in0=z[:], in1=g_bc[:], op=ALU.mult), [z, g_bc], [z])
    k.op("pool", lambda e: e.tensor_tensor(out=outt[:], in0=z[:], in1=b_bc[:], op=ALU.add), [z, b_bc], [outt])


class Tail:
    def __init__(self, c, li, sub):
        k = c.k
        self.c = c
        self.g = k.sb([128, D], F32, "lng")
        self.b = k.sb([128, D], F32, "lnb")
        load_bc(c, self.g, c.inp["ln_g"][li, sub:sub + 1, :])
        load_bc(c, self.b, c.inp["ln_b"][li, sub:sub + 1, :])
        self.z = [k.sb([128, D], F32, "z") for _ in range(2)]
        self.o = [k.sb([128, D], F32, "ho") for _ in range(2)]
        self.st = [k.sb([128, 2, 6], F32, "st") for _ in range(2)]
        self.mv = [k.sb([128, 2], F32, "mv") for _ in range(2)]
        self.rs = [k.sb([128, 1], F32, "rs") for _ in range(2)]
        self.i = 0

    def run(self, ht, y_fn, ysrc, h_out, ti):
        c = self.c
        i = self.i
        self.i = (i + 1) % 2
        tail_ln(c, ht, y_fn, ysrc, self.g, self.b, self.o[i], self.z[i], self.st[i], self.mv[i], self.rs[i])
        c.k.dma("sp", h_out[ti * 128:(ti + 1) * 128, :], self.o[i][:], reads=[self.o[i]])


def phase_xa(c, li, h_in, h_out):
    k = c.k
    with k.scope():
        wq = k.sb([128, 8, 1024], BF16, "wq")
        wkv = k.sb([128, 8, 2048], BF16, "wkv")
        wo = k.sb([128, 8, 1024], BF16, "wo")
        load_w(c, wq, c.inp["xa_w_q"][li], 1024, 1024)
        load_w(c, wkv, c.inp["xa_w_kv"][li], 1024, 2048)
        load_w(c, wo, c.inp["xa_w_o"][li], 1024, 1024)
        tl = Tail(c, li, 1)
        KT = [k.sb([128, 8, 256], BF16, "KT") for _ in range(c.NSEQ)]
        V = [k.sb([128, 2, 1024], BF16, "V") for _ in range(c.NSEQ)]
        memf = k.sb([128, 1024], F32, "memf")
        memb = k.sb([128, 1024], BF16, "memb")
        memT = k.sb([128, 8, 256], BF16, "memT")
        for s in range(c.NSEQ):
            for mc in range(2):
                k.dma("sp", memf[:], c.inp["mem"][s * 256 + mc * 128: s * 256 + (mc + 1) * 128, :], writes=[memf])
                k.op("dve", lambda e: e.tensor_copy(out=memb[:], in_=memf[:]), [memf], [memb])
                transpose_into(c, lambda mc=mc: memT[:, :, mc * 128:(mc + 1) * 128],
                               lambda ci: memb[:, ci * 128:(ci + 1) * 128], 8, memb, memT)
            for fc in range(8):
                p = psf(c)
                for kc in range(8):
                    k.op("pe", lambda e, fc=fc, kc=kc, p=p: e.matmul(out=p[:, 0:256], lhsT=wkv[:, kc, fc * 128:(fc + 1) * 128],
                                                                     rhs=memT[:, kc, :], start=(kc == 0), stop=(kc == 7)),
                         [wkv, memT], [p])
                k.op("act", lambda e, fc=fc, p=p, s=s: e.copy(out=KT[s][:, fc, :], in_=p[:, 0:256]), [p], [KT[s]])
            for mc in range(2):
                p = psf(c)
                for nb in range(2):
                    for kc in range(8):
                        k.op("pe", lambda e, nb=nb, kc=kc, p=p, mc=mc: e.matmul(
                            out=p[:, nb * 512:(nb + 1) * 512], lhsT=memT[:, kc, mc * 128:(mc + 1) * 128],
                            rhs=wkv[:, kc, 1024 + nb * 512:1024 + (nb + 1) * 512], start=(kc == 0), stop=(kc == 7)),
                            [wkv, memT], [p])
                k.op("dve", lambda e, p=p, mc=mc, s=s: e.tensor_copy(out=V[s][:, mc, :], in_=p[:]), [p], [V[s]])
        hf = [k.sb([128, D], F32, "hf") for _ in range(2)]
        hb = [k.sb([128, D], BF16, "hb") for _ in range(2)]
        hT = [k.sb([128, 8, 128], BF16, "hT") for _ in range(2)]
        qT = [k.sb([128, 8, 128], BF16, "qT") for _ in range(2)]
        ssb = [k.sb([128, 4, 256], F32, "ssb") for _ in range(2)]
        P = [k.sb([128, 4, 256], BF16, "P") for _ in range(2)]
        PT = [k.sb([128, 8, 128], BF16, "PT") for _ in range(2)]
        ob = [k.sb([128, D], BF16, "ob") for _ in range(2)]
        oT = [k.sb([128, 8, 128], BF16, "oT") for _ in range(2)]
        mx = [k.sb([128, 4], F32, "mx") for _ in range(2)]
        sm = [k.sb([128, 4], F32, "sm") for _ in range(2)]

        def load(ti):
            k.dma("sp", hf[ti % 2][:], h_in[ti * 128:(ti + 1) * 128, :], writes=[hf[ti % 2]])

        load(0)
        tps = c.T // 128
        for ti in range(c.NTILES):
            if ti + 1 < c.NTILES:
                load(ti + 1)
            b = ti % 2
            s = ti // tps
            k.op("dve", lambda e, b=b: e.tensor_copy(out=hb[b][:], in_=hf[b][:]), [hf[b]], [hb[b]])
            transpose_into(c, lambda b=b: hT[b][:].rearrange("p a b -> p (a b)"),
                           lambda ci, b=b: hb[b][:, ci * 128:(ci + 1) * 128], 8, hb[b], hT[b])
            p = psf(c)
            for fc in range(8):
                for kc in range(8):
                    k.op("pe", lambda e, fc=fc, kc=kc, p=p, b=b: e.matmul(
                        out=p[:, fc * 128:(fc + 1) * 128], lhsT=wq[:, kc, fc * 128:(fc + 1) * 128], rhs=hT[b][:, kc, :],
                        start=(kc == 0), stop=(kc == 7)), [wq, hT[b]], [p])
            k.op("act", lambda e, p=p, b=b: e.activation(out=qT[b][:].rearrange("p a b -> p (a b)"), in_=p[:], func=AF.Copy,
                                                         scale=0.0625), [p], [qT[b]])
            p = psf(c)
            for hd in range(4):
                for cc in range(2):
                    k.op("pe", lambda e, hd=hd, cc=cc, p=p, b=b, s=s: e.matmul(
                        out=p[:, hd * 256:(hd + 1) * 256], lhsT=qT[b][:, 2 * hd + cc, :], rhs=KT[s][:, 2 * hd + cc, :],
                        start=(cc == 0), stop=(cc == 1)), [qT[b], KT[s]], [p])
            k.op("dve", lambda e, p=p, b=b: e.tensor_copy(out=ssb[b][:].rearrange("p a b -> p (a b)"), in_=p[:]), [p], [ssb[b]])
            k.op("dve", lambda e, b=b: e.tensor_reduce(out=mx[b][:], in_=ssb[b][:], axis=AX.X, op=ALU.max, negate=True),
                 [ssb[b]], [mx[b]])
            for hd in range(4):
                k.op("act", lambda e, hd=hd, b=b: e.activation(out=P[b][:, hd, :], in_=ssb[b][:, hd, :], func=AF.Exp,
                                                               bias=mx[b][:, hd:hd + 1], scale=1.0,
                                                               accum_out=sm[b][:, hd:hd + 1]), [ssb[b], mx[b]], [P[b], sm[b]])
            k.op("dve", lambda e, b=b: e.reciprocal(out=sm[b][:], in_=sm[b][:]), [sm[b]], [sm[b]])
            transpose_into(c, lambda b=b: PT[b][:].rearrange("p a b -> p (a b)"),
                           lambda ci, b=b: P[b][:, ci // 2, (ci % 2) * 128:(ci % 2 + 1) * 128], 8, P[b], PT[b], evac="dve")
            p = psf(c)
            for hd in range(4):
                for mc in range(2):
                    k.op("pe", lambda e, hd=hd, mc=mc, p=p, b=b, s=s: e.matmul(
                        out=p[:, hd * 256:(hd + 1) * 256], lhsT=PT[b][:, 2 * hd + mc, :], rhs=V[s][:, mc, hd * 256:(hd + 1) * 256],
                        start=(mc == 0), stop=(mc == 1)), [PT[b], V[s]], [p])
            for hd in range(4):
                k.op("act", lambda e, hd=hd, p=p, b=b: e.activation(out=ob[b][:, hd * 256:(hd + 1) * 256],
                                                                    in_=p[:, hd * 256:(hd + 1) * 256], func=AF.Copy,
                                                                    scale=sm[b][:, hd:hd + 1]), [p, sm[b]], [ob[b]])
            transpose_into(c, lambda b=b: oT[b][:].rearrange("p a b -> p (a b)"),
                           lambda ci, b=b: ob[b][:, ci * 128:(ci + 1) * 128], 8, ob[b], oT[b], evac="dve")
            p = psf(c)
            for nb in range(2):
                for kc in range(8):
                    k.op("pe", lambda e, nb=nb, kc=kc, p=p, b=b: e.matmul(
                        out=p[:, nb * 512:(nb + 1) * 512], lhsT=oT[b][:, kc, :], rhs=wo[:, kc, nb * 512:(nb + 1) * 512],
                        start=(kc == 0), stop=(kc == 7)), [oT[b], wo], [p])
            tl.run(hf[b], lambda hf_, p=p: p[:, hf_ * 512:(hf_ + 1) * 512], p, h_out, ti)


def phase_peer(c, li, h_in, h_out):
    k = c.k
    NS = 16
    GRP = 8
    UV = k.dram("pk_UV%d" % li, [16384, 2048], BF16)
    with k.scope():
        stf = [k.sb([128, 8192], F32, "stf") for _ in range(2)]
        stb = [k.sb([128, 8192], BF16, "stb") for _ in range(2)]
        it = 0
        for which, nm in enumerate(("pk_u", "pk_v")):
            src = c.inp[nm][li]
            for r0 in range(0, 16384, 1024):
                a, bb = stf[it % 2], stb[it % 2]
                k.dma("sp", a[:].rearrange("p (r d) -> p r d", r=8), src[r0:r0 + 1024, :].rearrange("(p r) d -> p r d", r=8), writes=[a])
                if it % 2 == 0:
                    k.op("act", lambda e, a=a, bb=bb: e.copy(out=bb[:], in_=a[:]), [a], [bb])
                else:
                    k.op("dve", lambda e, a=a, bb=bb: e.tensor_copy(out=bb[:], in_=a[:]), [a], [bb])
                k.dma("sp", UV[r0:r0 + 1024, which * 1024:(which + 1) * 1024].rearrange("(p r) d -> p r d", r=8),
                      bb[:].rearrange("p (r d) -> p r d", r=8), reads=[bb])
                it += 1
    with k.scope():
        wqy = k.sb([128, 8, 2048], BF16, "wqy")
        load_w(c, wqy, c.inp["pk_w_query"][li], 1024, 2048)
        tl = Tail(c, li, 2)
        iota16 = k.sb([128, 16], F32, "iota16")
        k.dma("sp", iota16[:], c.inp["c_iota16"][:, :], writes=[iota16])
        skf = k.sb([128, 128], F32, "skf")
        skb = k.sb([128, 128], BF16, "skb")
        skT = k.sb([128, 2, 128], BF16, "skT")
        for j in range(2):
            k.dma("sp", skf[:], c.inp["pk_sub_keys"][li, j], writes=[skf])
            k.op("dve", lambda e: e.tensor_copy(out=skb[:], in_=skf[:]), [skf], [skb])
            transpose_into(c, lambda j=j: skT[:, j, :], lambda ci: skb[:, :], 1, skb, skT)
        hf = [k.sb([128, D], F32, "hf") for _ in range(2)]
        hb = k.sb([128, D], BF16, "hb")
        hT = k.sb([128, 8, 128], BF16, "hT")
        qT = k.sb([128, 16, 128], BF16, "qT")
        ssb = k.sb([128, 16, 128], F32, "ssb")
        tmp = k.sb([128, 16, 128], F32, "tmp")
        sv = k.sb([128, 16, 16], F32, "sv")
        si = k.sb([128, 16, 16], U32, "si")
        sif = k.sb([128, 16, 16], F32, "sif")
        cand = k.sb([128, 8, 16, 16], F32, "cand")
        tmpc = k.sb([128, 8, 256], F32, "tmpc")
        cv = k.sb([128, 8, 16], F32, "cv")
        ci = k.sb([128, 8, 16], U32, "ci")
        cab = [k.sb([128, 8, 16], U32, "cab") for _ in range(2)]
        cabf = [k.sb([128, 8, 16], F32, "cabf") for _ in range(2)]
        oh = k.sb([128, 8, 16, 16], F32, "oh")
        k12 = [k.sb([128, 8, 16], F32, "k12") for _ in range(2)]
        eif = k.sb([128, 8, 16], F32, "eif")
        eiu = [k.sb([128, 8, 16], U32, "eiu") for _ in range(2)]
        gt = [k.sb([128, 8, 16], F32, "gt") for _ in range(2)]
        zs = k.sb([128, 8], F32, "zs")
        dots = [k.sb([128, 128], F32, "dots") for _ in range(2)]
        actv = [k.sb([128, 128], F32, "actv") for _ in range(2)]
        junk = k.sb([128, D], BF16, "junk")
        uv = [k.sb([128, 2048], BF16, "uv") for _ in range(NS)]
        dg = [k.sb([128, 128], BF16, "dg") for _ in range(NS)]
        pacc = c.pf[2]
        c.pf_n = 2

        def load(ti):
            k.dma("sp", hf[ti % 2][:], h_in[ti * 128:(ti + 1) * 128, :], writes=[hf[ti % 2]])

        load(0)
        for ti in range(c.NTILES):
            if ti + 1 < c.NTILES:
                load(ti + 1)
            b = ti % 2
            hfb = hf[b]
            gtb, dt_, av_ = gt[b], dots[b], actv[b]
            k.op("dve", lambda e: e.tensor_copy(out=hb[:], in_=hfb[:]), [hfb], [hb])
            transpose_into(c, lambda: hT[:].rearrange("p a b -> p (a b)"), lambda ci_: hb[:, ci_ * 128:(ci_ + 1) * 128], 8, hb, hT)
            for half in range(2):
                p = psf(c)
                for fc in range(8):
                    for kc in range(8):
                        k.op("pe", lambda e, fc=fc, kc=kc, p=p, half=half: e.matmul(
                            out=p[:, fc * 128:(fc + 1) * 128], lhsT=wqy[:, kc, (half * 8 + fc) * 128:(half * 8 + fc + 1) * 128],
                            rhs=hT[:, kc, :], start=(kc == 0), stop=(kc == 7)), [wqy, hT], [p])
                k.op("act", lambda e, p=p, half=half: e.copy(out=qT[:, half * 8:(half + 1) * 8, :].rearrange("p a b -> p (a b)"),
                                                            in_=p[:]), [p], [qT])
            for half in range(2):
                p = psf(c)
                for fc in range(8):
                    cidx = half * 8 + fc
                    k.op("pe", lambda e, fc=fc, cidx=cidx, p=p: e.matmul(
                        out=p[:, fc * 128:(fc + 1) * 128], lhsT=qT[:, cidx, :], rhs=skT[:, cidx % 2, :], start=True, stop=True),
                        [qT, skT], [p])
                k.op("act", lambda e, p=p, half=half: e.copy(out=ssb[:, half * 8:(half + 1) * 8, :].rearrange("p a b -> p (a b)"),
                                                            in_=p[:]), [p], [ssb])
            for cc in range(16):
                k.op("dve", lambda e, cc=cc: e.max(out=sv[:, cc, 0:8], in_=ssb[:, cc, :]), [ssb], [sv])
                k.op("dve", lambda e, cc=cc: e.match_replace(out=tmp[:, cc, :], in_to_replace=sv[:, cc, 0:8], in_values=ssb[:, cc, :],
                                                             imm_value=-1e30), [ssb, sv], [tmp])
                k.op("dve", lambda e, cc=cc: e.max(out=sv[:, cc, 8:16], in_=tmp[:, cc, :]), [tmp], [sv])
                k.op("dve", lambda e, cc=cc: e.max_index(out=si[:, cc, 0:8], in_max=sv[:, cc, 0:8], in_values=ssb[:, cc, :]),
                     [ssb, sv], [si])
                k.op("dve", lambda e, cc=cc: e.max_index(out=si[:, cc, 8:16], in_max=sv[:, cc, 8:16], in_values=ssb[:, cc, :]),
                     [ssb, sv], [si])
            k.op("dve", lambda e: e.tensor_copy(out=sif[:], in_=si[:]), [si], [sif])
            for h in range(8):
                cflat = lambda h=h: cand[:, h, :, :].rearrange("p a b -> p (a b)")
                k.op("dve", lambda e, h=h: e.tensor_tensor(out=cand[:, h, :, :], in0=sv[:, 2 * h, :].unsqueeze(2).broadcast_to([128, 16, 16]),
                                                           in1=sv[:, 2 * h + 1, :].unsqueeze(1).broadcast_to([128, 16, 16]), op=ALU.add),
                     [sv], [cand])
                k.op("dve", lambda e, h=h, cflat=cflat: e.max(out=cv[:, h, 0:8], in_=cflat()), [cand], [cv])
                k.op("dve", lambda e, h=h, cflat=cflat: e.match_replace(out=tmpc[:, h, :], in_to_replace=cv[:, h, 0:8], in_values=cflat(),
                                                                       imm_value=-1e30), [cand, cv], [tmpc])
                k.op("dve", lambda e, h=h: e.max(out=cv[:, h, 8:16], in_=tmpc[:, h, :]), [tmpc], [cv])
                k.op("dve", lambda e, h=h, cflat=cflat: e.max_index(out=ci[:, h, 0:8], in_max=cv[:, h, 0:8], in_values=cflat()),
                     [cand, cv], [ci])
                k.op("dve", lambda e, h=h, cflat=cflat: e.max_index(out=ci[:, h, 8:16], in_max=cv[:, h, 8:16], in_values=cflat()),
                     [cand, cv], [ci])
            k.op("dve", lambda e: e.tensor_single_scalar(out=cab[0][:], in_=ci[:], scalar=4, op=ALU.logical_shift_right), [ci], [cab[0]])
            k.op("dve", lambda e: e.tensor_single_scalar(out=cab[1][:], in_=ci[:], scalar=15, op=ALU.bitwise_and), [ci], [cab[1]])
            for j in range(2):
                k.op("dve", lambda e, j=j: e.tensor_copy(out=cabf[j][:], in_=cab[j][:]), [cab[j]], [cabf[j]])
                k.op("dve", lambda e, j=j: e.tensor_tensor(
                    out=oh[:], in0=cabf[j][:].unsqueeze(3).broadcast_to([128, 8, 16, 16]),
                    in1=iota16[:].unsqueeze(1).unsqueeze(1).broadcast_to([128, 8, 16, 16]), op=ALU.is_equal), [cabf[j], iota16], [oh])
                k.op("dve", lambda e, j=j: e.tensor_tensor(
                    out=oh[:], in0=oh[:], in1=sif[:, j::2, :].unsqueeze(2).broadcast_to([128, 8, 16, 16]), op=ALU.mult), [oh, sif], [oh])
                k.op("dve", lambda e, j=j: e.tensor_reduce(out=k12[j][:], in_=oh[:], axis=AX.X, op=ALU.add), [oh], [k12[j]])
            k.op("dve", lambda e: e.scalar_tensor_tensor(out=eif[:].rearrange("p a b -> p (a b)"), in0=k12[0][:].rearrange("p a b -> p (a b)"),
                                                         scalar=128.0, in1=k12[1][:].rearrange("p a b -> p (a b)"),
                                                         op0=ALU.mult, op1=ALU.add), [k12[0], k12[1]], [eif])
            eb = eiu[b]
            k.op("dve", lambda e: e.tensor_tensor(out=gtb[:], in0=cv[:], in1=cv[:, :, 0:1].broadcast_to([128, 8, 16]), op=ALU.subtract),
                 [cv], [gtb])
            k.op("act", lambda e: e.activation(out=gtb[:], in_=gtb[:], func=AF.Exp), [gtb], [gtb])
            k.op("dve", lambda e: e.tensor_reduce(out=zs[:], in_=gtb[:], axis=AX.X, op=ALU.add), [gtb], [zs])
            k.op("dve", lambda e: e.reciprocal(out=zs[:], in_=zs[:]), [zs], [zs])
            k.op("dve", lambda e: e.tensor_tensor(out=gtb[:], in0=gtb[:], in1=zs[:].unsqueeze(2).broadcast_to([128, 8, 16]), op=ALU.mult),
                 [gtb, zs], [gtb])
            k.op("dve", lambda e: e.tensor_copy(out=eb[:], in_=eif[:]), [eif], [eb])
            for g0 in range(0, 128, GRP):
                for hk in range(g0, g0 + GRP):
                    slot = uv[hk % NS]
                    gather(c, slot, UV.t[:, :], eb, hk // 16, hk % 16)
                    k.op("dve", lambda e, slot=slot, hk=hk: e.scalar_tensor_tensor(out=junk[:], in0=slot[:, 0:1024], scalar=1.0, in1=hfb[:], op0=ALU.mult,
                                                                                   op1=ALU.mult, accum_out=dt_[:, hk:hk + 1]),
                         [slot, hfb], [junk, dt_])
                k.op("act", lambda e, g0=g0: e.activation(out=av_[:, g0:g0 + GRP], in_=dt_[:, g0:g0 + GRP], func=AF.Gelu_apprx_tanh), [dt_], [av_])
                k.op("dve", lambda e, g0=g0: e.tensor_tensor(out=av_[:, g0:g0 + GRP], in0=av_[:, g0:g0 + GRP],
                                                             in1=gtb[:].rearrange("p a b -> p (a b)")[:, g0:g0 + GRP], op=ALU.mult), [av_, gtb], [av_])
                for hk in range(g0, g0 + GRP):
                    slot = uv[hk % NS]
                    dgs = dg[hk % NS]
                    k.op("act", lambda e, dgs=dgs, hk=hk: e.activation(out=dgs[:], in_=c.identb[:], func=AF.Copy, scale=av_[:, hk:hk + 1]), [c.identb, av_], [dgs])
                    for nb in range(2):
                        k.op("pe", lambda e, dgs=dgs, slot=slot, nb=nb, hk=hk: e.matmul(out=pacc[:, nb * 512:(nb + 1) * 512], lhsT=dgs[:],
                                                                                       rhs=slot[:, 1024 + nb * 512:1024 + (nb + 1) * 512],
                                                                                       start=(hk == 0), stop=(hk == 127)), [dgs, slot], [pacc])
            tl.run(hfb, lambda hf_: pacc[:, hf_ * 512:(hf_ + 1) * 512], pacc, h_out, ti)
        c.pf_n = 3


def gather(c, dst, table, eb, h, kk):
    k = c.k
    qn = "pool"
    reads = k._res([eb])
    writes = k._res([dst])
    i = k.dnext[qn]
    k.dnext[qn] = (i + 1) % NDSEM
    key = (qn, i)
    if k.dval[qn][i] > 0:
        k._wait(qn, (key, k.dval[qn][i]))
    k._deps(qn, reads, writes)
    k.dval[qn][i] += 16
    tok = (key, k.dval[qn][i])
    k.nc.gpsimd.indirect_dma_start(out=dst[:], out_offset=None, in_=table,
                                   in_offset=bass.IndirectOffsetOnAxis(ap=eb[:, h, kk:kk + 1], axis=0)).then_inc(k.dsem[qn][i], 16)
    k._commit(tok, reads, writes)
    k.n_ins += 1


def phase_s5(c, li, h_in, h_out):
    k = c.k
    T_ = c.T
    NB = T_ // 512
    nlev = int(np.log2(T_))
    GT = k.dram("s5_GT", [c.NSEQ, 8, 128, T_], BF16)
    TWO_PI = 2.0 * np.pi
    with k.scope():
        Jm = k.sb([128, 128], F32, "Jm")
        sgn = k.sb([128, 1], F32, "sgn")
        gmask = k.sb([128, 8], F32, "gmask")
        cmask = k.sb([128, 8, 128], F32, "cmask")
        k.dma("sp", Jm[:], c.inp["c_J"][:, :], writes=[Jm])
        k.dma("sp", sgn[:], c.inp["c_sgn"][:, :], writes=[sgn])
        k.dma("sp", gmask[:], c.inp["c_gmask"][:, :], writes=[gmask])
        k.dma("sp", cmask[:], c.inp["c_cmask"][:, :, :], writes=[cmask])
        ls = k.sb([128, 128], F32, "ls")
        k.op("dve", lambda e: e.memset(ls[:], 0.0), [], [ls])
        lre = k.sb([128, 64], F32, "lre")
        lim = k.sb([128, 64], F32, "lim")
        for nm, dst in (("s5_lam_re", lre), ("s5_lam_im", lim)):
            for hh in range(2):
                k.dma("sp", ls[0:64, hh * 64:(hh + 1) * 64], c.inp[nm][0], writes=[ls])
            p = psf(c)
            k.op("pe", lambda e, p=p: e.transpose(out=p[:, 0:128], in_=ls[:, :], identity=c.identf[:]), [ls, c.identf], [p])
            k.op("dve", lambda e, p=p, dst=dst: e.tensor_copy(out=dst[:], in_=p[:, 0:64]), [p], [dst])
        dt = k.sb([128, 64], F32, "dt")
        load_bc(c, dt, c.inp["s5_log_dt"][0:1, :])
        k.op("act", lambda e: e.activation(out=dt[:], in_=dt[:], func=AF.Exp), [dt], [dt])
        emag = k.sb([128, 64], F32, "emag")
        ang = k.sb([128, 64], F32, "ang")
        k.op("dve", lambda e: e.tensor_tensor(out=emag[:], in0=lre[:], in1=dt[:], op=ALU.mult), [lre, dt], [emag])
        k.op("act", lambda e: e.activation(out=emag[:], in_=emag[:], func=AF.Exp), [emag], [emag])
        k.op("dve", lambda e: e.tensor_tensor(out=ang[:], in0=lim[:], in1=dt[:], op=ALU.mult), [lim, dt], [ang])
        acol = k.sb([128, 64], F32, "acol")
        bcol = k.sb([128, 64], F32, "bcol")
        yv = k.sb([128, 64], F32, "yv")
        yi = k.sb([128, 64], I32, "yi")
        yf = k.sb([128, 64], F32, "yf")
        mk = k.sb([128, 64], F32, "mk")
        for off, dst in ((0.25, acol), (0.0, bcol)):
            k.op("dve", lambda e, off=off: e.tensor_scalar(out=yv[:], in0=ang[:], scalar1=1.0 / TWO_PI, scalar2=off, op0=ALU.mult, op1=ALU.add),
                 [ang], [yv])
            k.op("dve", lambda e: e.tensor_copy(out=yi[:], in_=yv[:]), [yv], [yi])
            k.op("dve", lambda e: e.tensor_copy(out=yf[:], in_=yi[:]), [yi], [yf])
            k.op("dve", lambda e: e.tensor_tensor(out=yv[:], in0=yv[:], in1=yf[:], op=ALU.subtract), [yv, yf], [yv])
            k.op("dve", lambda e: e.tensor_single_scalar(out=mk[:], in_=yv[:], scalar=0.5, op=ALU.is_gt), [yv], [mk])
            k.op("dve", lambda e: e.tensor_tensor(out=yv[:], in0=yv[:], in1=mk[:], op=ALU.subtract), [yv, mk], [yv])
            k.op("dve", lambda e: e.tensor_single_scalar(out=mk[:], in_=yv[:], scalar=-0.5, op=ALU.is_lt), [yv], [mk])
            k.op("dve", lambda e: e.tensor_tensor(out=yv[:], in0=yv[:], in1=mk[:], op=ALU.add), [yv, mk], [yv])
            k.op("act", lambda e, dst=dst: e.activation(out=dst[:], in_=yv[:], func=AF.Sin, scale=TWO_PI), [yv], [dst])
            k.op("dve", lambda e, dst=dst: e.tensor_tensor(out=dst[:], in0=dst[:], in1=emag[:], op=ALU.mult), [dst, emag], [dst])
        am1 = k.sb([128, 64], F32, "am1")
        d2 = k.sb([128, 64], F32, "d2")
        t1 = k.sb([128, 64], F32, "t1")
        cr = k.sb([128, 64], F32, "cr")
        cis = k.sb([128, 64], F32, "cis")
        k.op("dve", lambda e: e.tensor_scalar(out=am1[:], in0=acol[:], scalar1=-1.0, scalar2=None, op0=ALU.add), [acol], [am1])
        k.op("dve", lambda e: e.tensor_tensor(out=d2[:], in0=lre[:], in1=lre[:], op=ALU.mult), [lre], [d2])
        k.op("dve", lambda e: e.tensor_tensor(out=t1[:], in0=lim[:], in1=lim[:], op=ALU.mult), [lim], [t1])
        k.op("dve", lambda e: e.tensor_tensor(out=d2[:], in0=d2[:], in1=t1[:], op=ALU.add), [d2, t1], [d2])
        k.op("dve", lambda e: e.reciprocal(out=d2[:], in_=d2[:]), [d2], [d2])
        k.op("dve", lambda e: e.tensor_tensor(out=cr[:], in0=am1[:], in1=lre[:], op=ALU.mult), [am1, lre], [cr])
        k.op("dve", lambda e: e.tensor_tensor(out=t1[:], in0=bcol[:], in1=lim[:], op=ALU.mult), [bcol, lim], [t1])
        k.op("dve", lambda e: e.tensor_tensor(out=cr[:], in0=cr[:], in1=t1[:], op=ALU.add), [cr, t1], [cr])
        k.op("dve", lambda e: e.tensor_tensor(out=cr[:], in0=cr[:], in1=d2[:], op=ALU.mult), [cr, d2], [cr])
        k.op("dve", lambda e: e.tensor_tensor(out=cis[:], in0=bcol[:], in1=lre[:], op=ALU.mult), [bcol, lre], [cis])
        k.op("dve", lambda e: e.tensor_tensor(out=t1[:], in0=am1[:], in1=lim[:], op=ALU.mult), [am1, lim], [t1])
        k.op("dve", lambda e: e.tensor_tensor(out=cis[:], in0=cis[:], in1=t1[:], op=ALU.subtract), [cis, t1], [cis])
        k.op("dve", lambda e: e.tensor_tensor(out=cis[:], in0=cis[:], in1=d2[:], op=ALU.mult), [cis, d2], [cis])
        k.op("dve", lambda e: e.tensor_scalar(out=cis[:], in0=cis[:], scalar1=sgn[:, 0:1], scalar2=None, op0=ALU.mult), [cis, sgn], [cis])
        if 's5pre0' in DBG:
            return
        BA = k.sb([128, 64, 16], F32, "BA")
        BB = k.sb([128, 64, 16], F32, "BB")
        bre = c.inp["s5_b_re"][0].rearrange("g p c -> p g c")
        bim = c.inp["s5_b_im"][0].rearrange("g p c -> p g c")
        k.dma("sp", BA[0:64], bre, writes=[BA])
        k.dma("sp", BA[64:128], bim, writes=[BA])
        k.dma("sp", BB[0:64], bim, writes=[BB])
        k.dma("sp", BB[64:128], bre, writes=[BB])
        k.op("dve", lambda e: e.tensor_tensor(out=BA[:], in0=BA[:], in1=cr[:].unsqueeze(2).broadcast_to([128, 64, 16]), op=ALU.mult), [BA, cr], [BA])
        k.op("dve", lambda e: e.tensor_tensor(out=BB[:], in0=BB[:], in1=cis[:].unsqueeze(2).broadcast_to([128, 64, 16]), op=ALU.mult), [BB, cis], [BB])
        k.op("dve", lambda e: e.tensor_tensor(out=BA[:], in0=BA[:], in1=BB[:], op=ALU.add), [BA, BB], [BA])
        W0 = k.sb([128, 64, 128], BF16, "W0")
        WC = k.sb([128, 64, 128], BF16, "WC")
        CN = k.sb([128, 8, 128], F32, "CN")
        k.dma("sp", CN[:, :, 0:64], c.inp["s5_c_re"][0].rearrange("(cc g) c p -> (g c) cc p", cc=8), writes=[CN])
        k.dma("sp", CN[:, :, 64:128], c.inp["s5_c_im"][0].rearrange("(cc g) c p -> (g c) cc p", cc=8), writes=[CN])
        k.op("dve", lambda e: e.tensor_scalar(out=CN[:, :, 64:128], in0=CN[:, :, 64:128], scalar1=-1.0, scalar2=None, op0=ALU.mult), [CN], [CN])
        tpf = k.sb([128, 128], F32, "tpf")
        for cc in range(8):
            p = psf(c)
            k.op("pe", lambda e, p=p, cc=cc: e.transpose(out=p[:, 0:128], in_=BA[:, cc * 8:(cc + 1) * 8, :].rearrange("p g c -> p (g c)"),
                                                        identity=c.identf[:]), [BA, c.identf], [p])
            k.op("act", lambda e, p=p: e.copy(out=tpf[:], in_=p[:, 0:128]), [p], [tpf])
            for j in range(8):
                k.op("dve", lambda e, cc=cc, j=j: e.tensor_scalar(out=W0[:, cc * 8 + j, :], in0=tpf[:], scalar1=gmask[:, j:j + 1], scalar2=None,
                                                                  op0=ALU.mult), [tpf, gmask], [W0])
            p = psf(c)
            k.op("pe", lambda e, p=p, cc=cc: e.transpose(out=p[:, 0:128], in_=CN[:, cc, :], identity=c.identf[:]), [CN, c.identf], [p])
            k.op("act", lambda e, p=p: e.copy(out=tpf[:], in_=p[:, 0:128]), [p], [tpf])
            for j in range(8):
                k.op("pool", lambda e, cc=cc, j=j: e.tensor_tensor(out=WC[:, cc * 8 + j, :], in0=tpf[:], in1=cmask[:, j, :], op=ALU.mult),
                     [tpf, cmask], [WC])
        dcol = k.sb([128, 8], F32, "dcol")
        k.dma("sp", dcol[:], c.inp["s5_d"][0].rearrange("(cc p) -> p cc", p=128), writes=[dcol], allow_slow_non_contiguous=True) if False else None
        dtmp = k.sb([128, 128], F32, "dtmp")
        k.op("dve", lambda e: e.memset(dtmp[:], 0.0), [], [dtmp])
        k.dma("sp", dtmp[0:8, :], c.inp["s5_d"][0].rearrange("(cc p) -> cc p", p=128), writes=[dtmp])
        p = psf(c)
        k.op("pe", lambda e, p=p: e.transpose(out=p[:, 0:128], in_=dtmp[:, :], identity=c.identf[:]), [dtmp, c.identf], [p])
        k.op("dve", lambda e, p=p: e.tensor_copy(out=dcol[:], in_=p[:, 0:8]), [p], [dcol])
        if 's5prep' in DBG:
            return
        xst = k.sb([128, T_ // 128, 128], F32, "xst")
        xTf = k.sb([128, T_], F32, "xTf")
        xTb = k.sb([128, T_], BF16, "xTb")
        SA = [k.sb([128, T_], BF16, "SA") for _ in range(8)]
        SB = [k.sb([128, T_], BF16, "SB") for _ in range(2)]
        Xf = [k.sb([128, 128], F32, "Xf") for _ in range(2)]
        XTf = [k.sb([128, 128], F32, "XTf") for _ in range(2)]
        PK = [k.sb([128, nlev, 128], BF16, "PK") for _ in range(2)]
        gtb = [k.sb([128, 512], BF16, "gtb") for _ in range(2)]
        ytmp = [k.sb([128, 512], F32, "ytmp") for _ in range(2)]
        s5tmp = [k.sb([128, 1024], BF16, "s5tmp") for _ in range(2)]
        ev = 0
        for s in range(c.NSEQ):
            for cc in range(8):
                k.dma("sp", xst[:], h_in[s * T_:(s + 1) * T_, cc * 128:(cc + 1) * 128].rearrange("(n p) c -> p n c", p=128), writes=[xst])
                if 's5ma' in DBG:
                    return
                for n4 in range(T_ // 512):
                    p = psf(c)
                    for q4 in range(4):
                        n = n4 * 4 + q4
                        k.op("pe", lambda e, p=p, n=n, q4=q4: e.transpose(out=p[:, q4 * 128:(q4 + 1) * 128], in_=xst[:, n, :], identity=c.identf[:]),
                             [xst, c.identf], [p])
                    if 's5mb' in DBG:
                        return
                    k.op("act", lambda e, p=p, n4=n4: e.copy(out=xTf[:, n4 * 512:(n4 + 1) * 512], in_=p[:, 0:512]), [p], [xTf])
                    if 's5mc' in DBG:
                        return
                    k.op("dve", lambda e, n4=n4: e.tensor_copy(out=xTb[:, n4 * 512:(n4 + 1) * 512], in_=xTf[:, n4 * 512:(n4 + 1) * 512]), [xTf], [xTb])
                if 's5m1' in DBG:
                    return
                finals = []
                for j in range(8):
                    g = cc * 8 + j
                    gi = g % 2
                    X, XT, pk = Xf[gi], XTf[gi], PK[gi]
                    k.op("dve", lambda e, X=X, g=g: e.tensor_scalar(out=X[:], in0=c.identf[:], scalar1=acol[:, g:g + 1], scalar2=None, op0=ALU.mult),
                         [c.identf, acol], [X])
                    k.op("dve", lambda e, XT=XT, X=X: e.tensor_copy(out=XT[:], in_=X[:]), [X], [XT])
                    k.op("dve", lambda e, X=X, g=g: e.scalar_tensor_tensor(out=X[:], in0=Jm[:], scalar=bcol[:, g:g + 1], in1=X[:], op0=ALU.mult, op1=ALU.add),
                         [Jm, bcol, X], [X])
                    k.op("dve", lambda e, XT=XT, g=g: e.tensor_scalar(out=mk[:, 0:1], in0=bcol[:, g:g + 1], scalar1=-1.0, scalar2=None, op0=ALU.mult),
                         [bcol], [mk])
                    k.op("dve", lambda e, XT=XT: e.scalar_tensor_tensor(out=XT[:], in0=Jm[:], scalar=mk[:, 0:1], in1=XT[:], op0=ALU.mult, op1=ALU.add),
                         [Jm, mk, XT], [XT])
                    for lv in range(nlev):
                        k.op("act", lambda e, pk=pk, XT=XT, lv=lv: e.copy(out=pk[:, lv, :], in_=XT[:]), [XT], [pk])
                        if lv + 1 < nlev:
                            p = psf(c)
                            k.op("pe", lambda e, p=p, X=X, XT=XT: e.matmul(out=p[:, 0:128], lhsT=XT[:], rhs=X[:], start=True, stop=True), [X, XT], [p])
                            k.op("pe", lambda e, p=p, X=X, XT=XT: e.matmul(out=p[:, 512:640], lhsT=X[:], rhs=XT[:], start=True, stop=True), [X, XT], [p])
                            k.op("dve", lambda e, p=p, X=X: e.tensor_copy(out=X[:], in_=p[:, 0:128]), [p], [X])
                            k.op("act", lambda e, p=p, XT=XT: e.copy(out=XT[:], in_=p[:, 512:640]), [p], [XT])
                    if 's5m2' in DBG:
                        return
                    cur, oth = (SA[j], SB[gi]) if nlev % 2 == 0 else (SB[gi], SA[j])
                    for hb_ in range(0, NB, 2):
                        p = psf(c)
                        for q2 in range(2):
                            blk = hb_ + q2
                            if blk >= NB:
                                continue
                            k.op("pe", lambda e, p=p, q2=q2, blk=blk, g=g: e.matmul(out=p[:, q2 * 512:(q2 + 1) * 512], lhsT=W0[:, g, :],
                                                                                    rhs=xTb[:, blk * 512:(blk + 1) * 512], start=True, stop=True),
                                 [W0, xTb], [p])
                        w = min(2, NB - hb_) * 512
                        eng = "act" if ev % 2 == 0 else "dve"
                        ev += 1
                        if eng == "act":
                            k.op("act", lambda e, p=p, cur=cur, hb_=hb_, w=w: e.copy(out=cur[:, hb_ * 512:hb_ * 512 + w], in_=p[:, 0:w]), [p], [cur])
                        else:
                            k.op("dve", lambda e, p=p, cur=cur, hb_=hb_, w=w: e.tensor_copy(out=cur[:, hb_ * 512:hb_ * 512 + w], in_=p[:, 0:w]), [p], [cur])
                    if 's5m3' in DBG:
                        return
                    for lv in range(nlev):
                        sh = 1 << lv
                        for hb_ in range(0, NB, 2):
                            p = psf(c)
                            for q2 in range(2):
                                blk = hb_ + q2
                                if blk >= NB:
                                    continue
                                t0 = blk * 512
                                lo = max(t0, sh)
                                has2 = lo < t0 + 512
                                if has2:
                                    k.op("pe", lambda e, p=p, q2=q2, t0=t0, lo=lo, sh=sh, cur=cur, pk=pk, lv=lv: e.matmul(
                                        out=p[:, q2 * 512 + (lo - t0):(q2 + 1) * 512], lhsT=pk[:, lv, :], rhs=cur[:, lo - sh:t0 + 512 - sh],
                                        start=True, stop=True), [pk, cur], [p])
                            w = min(2, NB - hb_) * 512
                            c0 = hb_ * 512
                            lo_all = min(max(c0, sh), c0 + w)
                            if lo_all > c0:
                                k.op("pool", lambda e, oth=oth, cur=cur, c0=c0, lo_all=lo_all: e.tensor_copy(out=oth[:, c0:lo_all], in_=cur[:, c0:lo_all]), [cur], [oth])
                            if lo_all < c0 + w:
                                if ev % 2 == 0:
                                    tb = s5tmp[(ev // 2) % 2]
                                    k.op("act", lambda e, p=p, tb=tb, c0=c0, lo_all=lo_all, w=w: e.copy(out=tb[:, lo_all - c0:w], in_=p[:, lo_all - c0:w]), [p], [tb])
                                    k.op("pool", lambda e, tb=tb, oth=oth, cur=cur, c0=c0, lo_all=lo_all, w=w: e.tensor_tensor(
                                        out=oth[:, lo_all:c0 + w], in0=tb[:, lo_all - c0:w], in1=cur[:, lo_all:c0 + w], op=ALU.add), [tb, cur], [oth])
                                else:
                                    k.op("dve", lambda e, p=p, oth=oth, cur=cur, c0=c0, lo_all=lo_all, w=w: e.tensor_tensor(
                                        out=oth[:, lo_all:c0 + w], in0=p[:, lo_all - c0:w], in1=cur[:, lo_all:c0 + w], op=ALU.add), [p, cur], [oth])
                                ev += 1
                        cur, oth = oth, cur
                    finals.append(cur)
                if 's5m4' in DBG:
                    return
                for blk in range(NB):
                    p = psf(c)
                    for j in range(8):
                        k.op("pe", lambda e, p=p, j=j, blk=blk, cc=cc: e.matmul(out=p[:, 0:512], lhsT=WC[:, cc * 8 + j, :],
                                                                                rhs=finals[j][:, blk * 512:(blk + 1) * 512], start=(j == 0), stop=(j == 7)),
                             [WC, finals[j]], [p])
                    yt = ytmp[blk % 2]
                    gb_ = gtb[blk % 2]
                    k.op("dve", lambda e, p=p, yt=yt, blk=blk, cc=cc: e.scalar_tensor_tensor(out=yt[:], in0=xTf[:, blk * 512:(blk + 1) * 512], scalar=dcol[:, cc:cc + 1],
                                                                                             in1=p[:, 0:512], op0=ALU.mult, op1=ALU.add), [xTf, dcol, p], [yt])
                    k.op("act", lambda e, yt=yt, gb_=gb_: e.activation(out=gb_[:], in_=yt[:], func=AF.Gelu_apprx_tanh), [yt], [gb_])
                    k.dma("sp", GT[s, cc, :, blk * 512:(blk + 1) * 512], gb_[:], reads=[gb_])
    if 's5m5' in DBG:
        return
    with k.scope():
        wg = k.sb([128, 8, 2048], BF16, "wg")
        load_w(c, wg, c.inp["s5_w_glu"][0], 1024, 2048)
        bg = k.sb([128, 2048], F32, "bg")
        load_bc(c, bg, c.inp["s5_b_glu"][0:1, :])
        tl = Tail(c, li, 0)
        hf = [k.sb([128, D], F32, "hf") for _ in range(2)]
        gT = [k.sb([128, 8, 128], BF16, "gT") for _ in range(2)]
        vg = [k.sb([128, 2048], F32, "vg") for _ in range(2)]
        tps = T_ // 128

        def load(ti):
            s, tt = ti // tps, ti % tps
            k.dma("sp", hf[ti % 2][:], h_in[ti * 128:(ti + 1) * 128, :], writes=[hf[ti % 2]])
            k.dma("sp", gT[ti % 2][:], GT[s, :, :, tt * 128:(tt + 1) * 128].rearrange("c p t -> p c t"), writes=[gT[ti % 2]])

        load(0)
        for ti in range(c.NTILES):
            if ti + 1 < c.NTILES:
                load(ti + 1)
            b = ti % 2
            for half in range(2):
                p = psf(c)
                for nb in range(2):
                    n0 = half * 1024 + nb * 512
                    for kc in range(8):
                        k.op("pe", lambda e, p=p, nb=nb, n0=n0, kc=kc, b=b: e.matmul(out=p[:, nb * 512:(nb + 1) * 512], lhsT=gT[b][:, kc, :],
                                                                                    rhs=wg[:, kc, n0:n0 + 512], start=(kc == 0), stop=(kc == 7)),
                             [gT[b], wg], [p])
                k.op("dve", lambda e, p=p, half=half, b=b: e.tensor_tensor(out=vg[b][:, half * 1024:(half + 1) * 1024], in0=p[:],
                                                                           in1=bg[:, half * 1024:(half + 1) * 1024], op=ALU.add), [p, bg], [vg[b]])
            k.op("act", lambda e, b=b: e.activation(out=vg[b][:, 1024:2048], in_=vg[b][:, 1024:2048], func=AF.Sigmoid), [vg[b]], [vg[b]])
            k.op("pool", lambda e, b=b: e.tensor_tensor(out=vg[b][:, 0:1024], in0=vg[b][:, 0:1024], in1=vg[b][:, 1024:2048], op=ALU.mult), [vg[b]], [vg[b]])
            tl.run(hf[b], lambda hf_, b=b: vg[b][:, hf_ * 512:(hf_ + 1) * 512], vg[b], h_out, ti)


def phase_da(c, li, h_in, h_out):
    k = c.k
    T_ = c.T
    NQ = T_ // 128
    lam_init = 0.8 - 0.6 * float(np.exp(-0.3 * li))
    QT = [k.dram("da_QT%d" % j, [c.NSEQ, 8, 128, T_], BF16) for j in range(2)]
    KTd = k.dram("da_KT", [c.NSEQ, 8, 128, T_], BF16)
    Vd = k.dram("da_V", [c.NT, 1024], BF16)
    AO = k.dram("da_AO", [c.NT, 1024], BF16)
    with k.scope():
        w = k.sb([128, 8, 3072], BF16, "wqkv")
        load_w(c, w, c.inp["da_w_qkv"][0], 1024, 3072)
        hmask = k.sb([128, 2], F32, "hmask")
        k.dma("sp", hmask[:], c.inp["c_hmask"][:, :], writes=[hmask])
        hf = [k.sb([128, D], F32, "hf") for _ in range(2)]
        hb = k.sb([128, D], BF16, "hb")
        hT = k.sb([128, 8, 128], BF16, "hT")
        qt = [[k.sb([128, 8, 128], BF16, "qt") for _ in range(2)] for _ in range(2)]
        kt = [k.sb([128, 8, 128], BF16, "kt") for _ in range(2)]
        vt = [k.sb([128, 1024], BF16, "vt") for _ in range(2)]
        tps = T_ // 128

        def load(ti):
            k.dma("sp", hf[ti % 2][:], h_in[ti * 128:(ti + 1) * 128, :], writes=[hf[ti % 2]])

        load(0)
        for ti in range(c.NTILES):
            if ti + 1 < c.NTILES:
                load(ti + 1)
            b = ti % 2
            s, tt = ti // tps, ti % tps
            k.op("dve", lambda e, b=b: e.tensor_copy(out=hb[:], in_=hf[b][:]), [hf[b]], [hb])
            transpose_into(c, lambda: hT[:].rearrange("p a b -> p (a b)"), lambda ci: hb[:, ci * 128:(ci + 1) * 128], 8, hb, hT)
            for part in range(2):
                p = psf(c)
                for fc in range(8):
                    for kc in range(8):
                        k.op("pe", lambda e, p=p, fc=fc, kc=kc, part=part: e.matmul(
                            out=p[:, fc * 128:(fc + 1) * 128], lhsT=w[:, kc, part * 1024 + fc * 128: part * 1024 + (fc + 1) * 128],
                            rhs=hT[:, kc, :], start=(kc == 0), stop=(kc == 7)), [w, hT], [p])
                if part == 0:
                    for j in range(2):
                        k.op("act", lambda e, p=p, j=j, b=b: e.activation(out=qt[j][b][:].rearrange("p a b -> p (a b)"), in_=p[:], func=AF.Copy,
                                                                         scale=hmask[:, j:j + 1]), [p, hmask], [qt[j][b]])
                        k.dma("sp", QT[j][s, :, :, tt * 128:(tt + 1) * 128].rearrange("h p t -> p h t"), qt[j][b][:], reads=[qt[j][b]])
                else:
                    k.op("dve", lambda e, p=p, b=b: e.tensor_copy(out=kt[b][:].rearrange("p a b -> p (a b)"), in_=p[:]), [p], [kt[b]])
                    k.dma("sp", KTd[s, :, :, tt * 128:(tt + 1) * 128].rearrange("h p t -> p h t"), kt[b][:], reads=[kt[b]])
            p = psf(c)
            for nb in range(2):
                for kc in range(8):
                    k.op("pe", lambda e, p=p, nb=nb, kc=kc: e.matmul(out=p[:, nb * 512:(nb + 1) * 512], lhsT=hT[:, kc, :],
                                                                    rhs=w[:, kc, 2048 + nb * 512:2048 + (nb + 1) * 512], start=(kc == 0), stop=(kc == 7)),
                         [w, hT], [p])
            k.op("act", lambda e, p=p, b=b: e.copy(out=vt[b][:], in_=p[:]), [p], [vt[b]])
            k.dma("sp", Vd[ti * 128:(ti + 1) * 128, :], vt[b][:], reads=[vt[b]])
    with k.scope():
        r0 = k.sb([128, T_], F32, "r0")
        k.dma("sp", r0[:], c.inp["c_r0"][:, :], writes=[r0])
        dbase = k.sb([128, 128], F32, "dbase")
        dmsk = k.sb([128, 128], F32, "dmsk")
        k.dma("sp", dbase[:], c.inp["c_dbase"][:, :], writes=[dbase])
        k.dma("sp", dmsk[:], c.inp["c_dmask"][:, :], writes=[dmsk])
        lm = k.sb([128, 4, 64], F32, "lm")
        k.dma("sp", lm[:].rearrange("p a b -> p (a b)"), c.inp["da_lambda"][0:1].rearrange("o a b -> o (a b)").broadcast_to([128, 256]), writes=[lm])
        lt = k.sb([128, 2, 64], F32, "lt")
        l2 = k.sb([128, 2], F32, "l2")
        nlam = k.sb([128, 1], F32, "nlam")
        k.op("dve", lambda e: e.tensor_tensor(out=lt[:], in0=lm[:, 0::2, :], in1=lm[:, 1::2, :], op=ALU.mult), [lm], [lt])
        k.op("dve", lambda e: e.tensor_reduce(out=l2[:], in_=lt[:], axis=AX.X, op=ALU.add), [lt], [l2])
        k.op("act", lambda e: e.activation(out=l2[:], in_=l2[:], func=AF.Exp), [l2], [l2])
        k.op("dve", lambda e: e.tensor_tensor(out=nlam[:], in0=l2[:, 1:2], in1=l2[:, 0:1], op=ALU.subtract), [l2], [nlam])
        k.op("dve", lambda e: e.tensor_scalar(out=nlam[:], in0=nlam[:], scalar1=-lam_init, scalar2=None, op0=ALU.add), [nlam], [nlam])
        sg = k.sb([128, 128], F32, "sg")
        load_bc(c, sg, c.inp["da_subln_g"][0:1, :])
        k.op("dve", lambda e: e.tensor_scalar(out=sg[:], in0=sg[:], scalar1=(1.0 - lam_init), scalar2=None, op0=ALU.mult), [sg], [sg])
        kTh = [k.sb([128, T_], BF16, "kTh") for _ in range(2)]
        vh = [k.sb([128, NQ, 128], BF16, "vh") for _ in range(2)]
        qTh = [[k.sb([128, T_], BF16, "qTh") for _ in range(2)] for _ in range(2)]
        dh = [k.sb([128, 128], F32, "dh") for _ in range(2)]
        NBUF = 3
        ssb = [k.sb([128, T_], F32, "ssb") for _ in range(NBUF)]
        P = [k.sb([128, T_], BF16, "P") for _ in range(NBUF)]
        PT = [k.sb([128, NQ, 128], BF16, "PT") for _ in range(2)]
        mx = [k.sb([128, 1], F32, "mx") for _ in range(NBUF)]
        sm = [k.sb([128, 1], F32, "sm") for _ in range(NBUF)]
        o0 = [k.sb([128, 128], F32, "o0") for _ in range(2)]
        oo = [k.sb([128, 128], F32, "oo") for _ in range(2)]
        jk = [k.sb([128, 128], F32, "jk") for _ in range(2)]
        ms = [k.sb([128, 1], F32, "ms") for _ in range(2)]
        ob = [k.sb([128, 128], BF16, "ob") for _ in range(2)]
        units = [(s, h, qi, j) for s in range(c.NSEQ) for h in range(8) for qi in range(NQ) for j in range(2)]

        def stage_a(ui):
            s, h, qi, j = units[ui]
            hb_ = (s * 8 + h) % 2
            slope = 2.0 ** (-(h + 1))
            dhh = dh[hb_]
            if qi == 0 and j == 0:
                k.dma("sp", kTh[hb_][:], KTd[s, h], writes=[kTh[hb_]])
                k.dma("sp", vh[hb_][:], Vd[s * T_:(s + 1) * T_, h * 128:(h + 1) * 128].rearrange("(n p) c -> p n c", p=128), writes=[vh[hb_]])
                for jj in range(2):
                    k.dma("sp", qTh[jj][hb_][:], QT[jj][s, h], writes=[qTh[jj][hb_]])
                k.op("dve", lambda e: e.scalar_tensor_tensor(out=dhh[:], in0=dbase[:], scalar=slope, in1=dmsk[:], op0=ALU.mult, op1=ALU.add),
                     [dbase, dmsk], [dhh])
            q0 = qi * 128
            nk = q0 + 128
            ub = ui % NBUF
            sb_, Pb, mxb, smb = ssb[ub], P[ub], mx[ub], sm[ub]
            for k0 in range(0, q0, 1024):
                p = psf(c)
                w_ = min(1024, q0 - k0)
                for c0 in range(0, w_, 512):
                    cw = min(512, w_ - c0)
                    k.op("pe", lambda e, p=p, c0=c0, cw=cw, k0=k0: e.matmul(
                        out=p[:, c0:c0 + cw], lhsT=qTh[j][hb_][:, q0:q0 + 128], rhs=kTh[hb_][:, k0 + c0:k0 + c0 + cw], start=True, stop=True),
                        [qTh[j][hb_], kTh[hb_]], [p])
                    off = T_ - q0 + k0 + c0
                    k.op("dve", lambda e, p=p, c0=c0, cw=cw, k0=k0, off=off: e.scalar_tensor_tensor(
                        out=sb_[:, k0 + c0:k0 + c0 + cw], in0=r0[:, off:off + cw], scalar=slope, in1=p[:, c0:c0 + cw], op0=ALU.mult, op1=ALU.add),
                        [r0, p], [sb_])
            p = psf(c)
            k.op("pe", lambda e, p=p: e.matmul(out=p[:, 0:128], lhsT=qTh[j][hb_][:, q0:q0 + 128], rhs=kTh[hb_][:, q0:q0 + 128],
                                               start=True, stop=True), [qTh[j][hb_], kTh[hb_]], [p])
            k.op("dve", lambda e, p=p: e.tensor_tensor(out=sb_[:, q0:q0 + 128], in0=p[:, 0:128], in1=dhh[:], op=ALU.add),
                 [p, dhh], [sb_])
            k.op("dve", lambda e: e.tensor_reduce(out=mxb[:], in_=sb_[:, 0:nk], axis=AX.X, op=ALU.max, negate=True), [sb_], [mxb])
            k.op("act", lambda e: e.activation(out=Pb[:, 0:nk], in_=sb_[:, 0:nk], func=AF.Exp, bias=mxb[:, 0:1],
                                               scale=1.0, accum_out=smb[:, 0:1]), [sb_, mxb], [Pb, smb])
            k.op("dve", lambda e: e.reciprocal(out=smb[:], in_=smb[:]), [smb], [smb])

        def stage_b(ui):
            s, h, qi, j = units[ui]
            hb_ = (s * 8 + h) % 2
            q0 = qi * 128
            nk = q0 + 128
            ub = ui % NBUF
            Pb, PTb, smb = P[ub], PT[ui % 2], sm[ub]
            nblk = nk // 128
            for b0 in range(0, nblk, 8):
                nb_ = min(8, nblk - b0)
                transpose_into(c, lambda b0=b0, nb_=nb_: PTb[:, b0:b0 + nb_, :].rearrange("p a b -> p (a b)"),
                               lambda ci, b0=b0: Pb[:, (b0 + ci) * 128:(b0 + ci + 1) * 128], nb_, Pb, PTb,
                               evac=("act" if (b0 // 8) % 2 == 0 else "dve"))
            p = psf(c)
            for bk in range(nblk):
                k.op("pe", lambda e, p=p, bk=bk: e.matmul(out=p[:, 0:128], lhsT=PTb[:, bk, :], rhs=vh[hb_][:, bk, :],
                                                         start=(bk == 0), stop=(bk == nblk - 1)), [PTb, vh[hb_]], [p])
            qb = qi % 2
            if j == 0:
                k.op("act", lambda e, p=p: e.activation(out=o0[qb][:], in_=p[:, 0:128], func=AF.Copy, scale=smb[:, 0:1]), [p, smb], [o0[qb]])
            else:
                k.op("dve", lambda e: e.tensor_tensor(out=smb[:], in0=smb[:], in1=nlam[:], op=ALU.mult), [smb, nlam], [smb])
                k.op("dve", lambda e, p=p: e.scalar_tensor_tensor(out=oo[qb][:], in0=p[:, 0:128], scalar=smb[:, 0:1], in1=o0[qb][:],
                                                                 op0=ALU.mult, op1=ALU.add), [p, smb, o0[qb]], [oo[qb]])
                k.op("act", lambda e: e.activation(out=jk[qb][:], in_=oo[qb][:], func=AF.Square, accum_out=ms[qb][:, 0:1]), [oo[qb]], [jk[qb], ms[qb]])
                k.op("act", lambda e: e.activation(out=ms[qb][:], in_=ms[qb][:], func=AF.Sqrt, scale=1.0 / 128.0, bias=LN_EPS), [ms[qb]], [ms[qb]])
                k.op("dve", lambda e: e.reciprocal(out=ms[qb][:], in_=ms[qb][:]), [ms[qb]], [ms[qb]])
                k.op("dve", lambda e: e.scalar_tensor_tensor(out=ob[qb][:], in0=oo[qb][:], scalar=ms[qb][:, 0:1], in1=sg[:], op0=ALU.mult, op1=ALU.mult),
                     [oo[qb], ms[qb], sg], [ob[qb]])
                k.dma("sp", AO[s * T_ + q0:s * T_ + q0 + 128, h * 128:(h + 1) * 128], ob[qb][:], reads=[ob[qb]])

        for ui in range(len(units) + 1):
            if ui < len(units):
                stage_a(ui)
            if ui >= 1:
                stage_b(ui - 1)
    with k.scope():
        wo = k.sb([128, 8, 1024], BF16, "wo")
        load_w(c, wo, c.inp["da_w_o"][0], 1024, 1024)
        tl = Tail(c, li, 0)
        hf = [k.sb([128, D], F32, "hf") for _ in range(2)]
        ab = [k.sb([128, D], BF16, "ab") for _ in range(2)]
        aT = [k.sb([128, 8, 128], BF16, "aT") for _ in range(2)]

        def load(ti):
            k.dma("sp", hf[ti % 2][:], h_in[ti * 128:(ti + 1) * 128, :], writes=[hf[ti % 2]])
            k.dma("sp", ab[ti % 2][:], AO[ti * 128:(ti + 1) * 128, :], writes=[ab[ti % 2]])

        load(0)
        for ti in range(c.NTILES):
            if ti + 1 < c.NTILES:
                load(ti + 1)
            b = ti % 2
            transpose_into(c, lambda b=b: aT[b][:].rearrange("p a b -> p (a b)"), lambda ci, b=b: ab[b][:, ci * 128:(ci + 1) * 128], 8, ab[b], aT[b])
            p = psf(c)
            for nb in range(2):
                for kc in range(8):
                    k.op("pe", lambda e, p=p, nb=nb, kc=kc, b=b: e.matmul(out=p[:, nb * 512:(nb + 1) * 512], lhsT=aT[b][:, kc, :], rhs=wo[:, kc, nb * 512:(nb + 1) * 512],
                                                                         start=(kc == 0), stop=(kc == 7)), [aT[b], wo], [p])
            tl.run(hf[b], lambda hf_, p=p: p[:, hf_ * 512:(hf_ + 1) * 512], p, h_out, ti)


def phase_m2(c, li, h_in, h_out):
    k = c.k
    T_ = c.T
    NCH = T_ // 128
    Zd = k.dram("m2_Z", [c.NT, 2048], F32)
    XBC = k.dram("m2_XBC", [c.NSEQ, 24, 128, T_], F32)
    DTd = k.dram("m2_DT", [c.NT, 32], F32)
    XS = k.dram("m2_XS", [c.NT, 2048], F32)
    BTd = k.dram("m2_BT", [c.NSEQ, 4, 128, T_], BF16)
    CTd = k.dram("m2_CT", [c.NSEQ, 4, 128, T_], BF16)
    BTOK = k.dram("m2_BTOK", [c.NT, 512], BF16)
    tps = T_ // 128
    with k.scope():
        w = k.sb([128, 8, 5152], BF16, "w_in")
        load_w(c, w, c.inp["m2_w_in"][0], 1024, 5152)
        hf = [k.sb([128, D], F32, "hf") for _ in range(2)]
        hb = k.sb([128, D], BF16, "hb")
        hT = k.sb([128, 8, 128], BF16, "hT")
        zt = [k.sb([128, 1024], F32, "zt") for _ in range(2)]
        xt = [k.sb([128, 8, 128], F32, "xt") for _ in range(2)]
        dtt = [k.sb([128, 32], F32, "dtt") for _ in range(2)]

        def load(ti):
            k.dma("sp", hf[ti % 2][:], h_in[ti * 128:(ti + 1) * 128, :], writes=[hf[ti % 2]])

        load(0)
        ev = 0
        for ti in range(c.NTILES):
            if ti + 1 < c.NTILES:
                load(ti + 1)
            b = ti % 2
            s, tt = ti // tps, ti % tps
            k.op("dve", lambda e, b=b: e.tensor_copy(out=hb[:], in_=hf[b][:]), [hf[b]], [hb])
            transpose_into(c, lambda: hT[:].rearrange("p a b -> p (a b)"), lambda ci: hb[:, ci * 128:(ci + 1) * 128], 8, hb, hT)
            for half in range(2):
                p = psf(c)
                for nb in range(2):
                    n0 = half * 1024 + nb * 512
                    for kc in range(8):
                        k.op("pe", lambda e, p=p, nb=nb, n0=n0, kc=kc: e.matmul(out=p[:, nb * 512:(nb + 1) * 512], lhsT=hT[:, kc, :], rhs=w[:, kc, n0:n0 + 512],
                                                                               start=(kc == 0), stop=(kc == 7)), [hT, w], [p])
                zb = zt[ev % 2]
                ev += 1
                k.op("act", lambda e, p=p, zb=zb: e.copy(out=zb[:], in_=p[:]), [p], [zb])
                k.dma("sp", Zd[ti * 128:(ti + 1) * 128, half * 1024:(half + 1) * 1024], zb[:], reads=[zb])
            for third in range(3):
                p = psf(c)
                for fc in range(8):
                    col = 2048 + (third * 8 + fc) * 128
                    for kc in range(8):
                        k.op("pe", lambda e, p=p, fc=fc, col=col, kc=kc: e.matmul(out=p[:, fc * 128:(fc + 1) * 128], lhsT=w[:, kc, col:col + 128], rhs=hT[:, kc, :],
                                                                                 start=(kc == 0), stop=(kc == 7)), [hT, w], [p])
                xb_ = xt[ev % 2]
                ev += 1
                k.op("dve", lambda e, p=p, xb_=xb_: e.tensor_copy(out=xb_[:].rearrange("p a b -> p (a b)"), in_=p[:]), [p], [xb_])
                k.dma("sp", XBC[s, third * 8:(third + 1) * 8, :, tt * 128:(tt + 1) * 128].rearrange("n p t -> p n t"), xb_[:], reads=[xb_])
            p = psf(c)
            for kc in range(8):
                k.op("pe", lambda e, p=p, kc=kc: e.matmul(out=p[:, 0:32], lhsT=hT[:, kc, :], rhs=w[:, kc, 5120:5152], start=(kc == 0), stop=(kc == 7)), [hT, w], [p])
            k.op("act", lambda e, p=p, b=b: e.copy(out=dtt[b][:], in_=p[:, 0:32]), [p], [dtt[b]])
            k.dma("sp", DTd[ti * 128:(ti + 1) * 128, :], dtt[b][:], reads=[dtt[b]])
    with k.scope():
        cwp = k.sb([128, 3072], F32, "cwp")
        k.op("dve", lambda e: e.memset(cwp[:], 0.0), [], [cwp])
        k.dma("sp", cwp[0:4, :], c.inp["m2_conv_w"][0], writes=[cwp])
        k.dma("sp", cwp[4:5, :], c.inp["m2_conv_b"][0:1, :], writes=[cwp])
        cw = k.sb([128, 24, 8], F32, "cw")
        for cch in range(24):
            p = psf(c)
            k.op("pe", lambda e, p=p, cch=cch: e.transpose(out=p[:, 0:128], in_=cwp[:, cch * 128:(cch + 1) * 128], identity=c.identf[:]), [cwp, c.identf], [p])
            k.op("act", lambda e, p=p, cch=cch: e.copy(out=cw[:, cch, :], in_=p[:, 0:8]), [p], [cw])
        xin = [k.sb([128, T_], F32, "xin") for _ in range(2)]
        acc = [k.sb([128, T_], F32, "acc") for _ in range(2)]
        accb = [k.sb([128, T_], BF16, "accb") for _ in range(2)]
        stg = [k.sb([128, 4, 128], F32, "stg") for _ in range(2)]
        stgb = [k.sb([128, 8, 128], BF16, "stgb") for _ in range(2)]
        u = 0
        for s in range(c.NSEQ):
            for cch in range(24):
                ub = u % 2
                u += 1
                xi, ac = xin[ub], acc[ub]
                k.dma("sp", xi[:], XBC[s, cch], writes=[xi])
                k.op("dve", lambda e, xi=xi, ac=ac, cch=cch: e.tensor_scalar(out=ac[:], in0=xi[:], scalar1=cw[:, cch, 3:4], scalar2=None, op0=ALU.mult), [xi, cw], [ac])
                for kk in range(3):
                    shf = 3 - kk
                    k.op("dve", lambda e, xi=xi, ac=ac, cch=cch, kk=kk, shf=shf: e.scalar_tensor_tensor(out=ac[:, shf:T_], in0=xi[:, 0:T_ - shf], scalar=cw[:, cch, kk:kk + 1],
                                                                                                  in1=ac[:, shf:T_], op0=ALU.mult, op1=ALU.add), [xi, cw, ac], [ac])
                k.op("act", lambda e, ac=ac, cch=cch: e.activation(out=ac[:], in_=ac[:], func=AF.Silu, bias=cw[:, cch, 4:5], scale=1.0), [ac, cw], [ac])
                if cch < 16:
                    for n4 in range(T_ // 512):
                        p = psf(c)
                        for q4 in range(4):
                            t0 = n4 * 512 + q4 * 128
                            k.op("pe", lambda e, p=p, q4=q4, t0=t0, ac=ac: e.transpose(out=p[:, q4 * 128:(q4 + 1) * 128], in_=ac[:, t0:t0 + 128], identity=c.identf[:]),
                                 [ac, c.identf], [p])
                        sg_ = stg[n4 % 2]
                        k.op("act", lambda e, p=p, sg_=sg_: e.copy(out=sg_[:].rearrange("p a b -> p (a b)"), in_=p[:, 0:512]), [p], [sg_])
                        k.dma("sp", XS[s * T_ + n4 * 512:s * T_ + (n4 + 1) * 512, cch * 128:(cch + 1) * 128].rearrange("(n p) c -> p n c", p=128), sg_[:], reads=[sg_])
                else:
                    abf = accb[ub]
                    k.op("dve", lambda e, ac=ac, abf=abf: e.tensor_copy(out=abf[:], in_=ac[:]), [ac], [abf])
                    if cch < 20:
                        g = cch - 16
                        k.dma("sp", BTd[s, g], abf[:], reads=[abf])
                        for n8 in range(0, T_ // 128, 8):
                            nb_ = min(8, T_ // 128 - n8)
                            sb8 = stgb[(n8 // 8) % 2]
                            transpose_into(c, lambda sb8=sb8, nb_=nb_: sb8[:, 0:nb_, :].rearrange("p a b -> p (a b)"),
                                           lambda ci, n8=n8, abf=abf: abf[:, (n8 + ci) * 128:(n8 + ci + 1) * 128], nb_, abf, sb8)
                            k.dma("sp", BTOK[s * T_ + n8 * 128:s * T_ + (n8 + nb_) * 128, g * 128:(g + 1) * 128].rearrange("(n p) c -> p n c", p=128),
                                  sb8[:, 0:nb_, :], reads=[sb8])
                    else:
                        g = cch - 20
                        k.dma("sp", CTd[s, g], abf[:], reads=[abf])
    with k.scope():
        wo = k.sb([128, 16, 1024], BF16, "w_out")
        load_w(c, wo, c.inp["m2_w_out"][0], 2048, 1024)
        tl = Tail(c, li, 0)
        triU = k.sb([128, 128], F32, "triU")
        Lst = k.sb([128, 128], F32, "Lst")
        ones = k.sb([128, 128], F32, "ones")
        k.dma("sp", triU[:], c.inp["c_triU"][:, :], writes=[triU])
        k.dma("sp", Lst[:], c.inp["c_Lst"][:, :], writes=[Lst])
        k.op("dve", lambda e: e.memset(ones[:], 1.0), [], [ones])
        dtb = k.sb([128, 32], F32, "dtb")
        aneg = k.sb([128, 32], F32, "aneg")
        load_bc(c, dtb, c.inp["m2_dt_bias"][0:1, :])
        load_bc(c, aneg, c.inp["m2_a_log"][0:1, :])
        k.op("act", lambda e: e.activation(out=aneg[:], in_=aneg[:], func=AF.Exp), [aneg], [aneg])
        k.op("dve", lambda e: e.tensor_scalar(out=aneg[:], in0=aneg[:], scalar1=-1.0, scalar2=None, op0=ALU.mult), [aneg], [aneg])
        dsk = k.sb([128, 32], F32, "dsk")
        load_bc(c, dsk, c.inp["m2_d"][0:1, :])
        ng = k.sb([128, 2048], F32, "ng")
        load_bc(c, ng, c.inp["m2_norm_g"][0:1, :])
        hst = k.sb([128, 32, 64], F32, "hst")
        hstb = k.sb([128, 32, 64], BF16, "hstb")
        hf = [k.sb([128, D], F32, "hf") for _ in range(2)]
        xs = [k.sb([128, 32, 64], F32, "xs") for _ in range(2)]
        zz = [k.sb([128, 2048], F32, "zz") for _ in range(2)]
        dtr = [k.sb([128, 32], F32, "dtr") for _ in range(2)]
        btk = [k.sb([128, 512], BF16, "btk") for _ in range(2)]
        btc = [k.sb([128, 4, 128], BF16, "btc") for _ in range(2)]
        ctc = [k.sb([128, 4, 128], BF16, "ctc") for _ in range(2)]
        dt = k.sb([128, 32], F32, "dt")
        dta = k.sb([128, 32], F32, "dta")
        acs = k.sb([128, 32], F32, "acs")
        ea = k.sb([128, 32], F32, "ea")
        dec = k.sb([128, 32], F32, "dec")
        etot = k.sb([128, 32], F32, "etot")
        Rm = k.sb([128, 32, 128], F32, "Rm")
        LT = [k.sb([128, 8, 128], F32, "LT") for _ in range(2)]
        MT = k.sb([128, 32, 128], BF16, "MT")
        cbm = [k.sb([128, 128], F32, "cbm") for _ in range(2)]
        xdt = k.sb([128, 32, 64], BF16, "xdt")
        xdd = k.sb([128, 32, 64], BF16, "xdd")
        yy = k.sb([128, 32, 64], F32, "yy")
        ytmp = k.sb([128, 8, 64], F32, "ytmp")
        ssq = k.sb([128, 4], F32, "ssq")
        jk = k.sb([128, 512], F32, "jk")
        yb = k.sb([128, 2048], BF16, "yb")
        yT = k.sb([128, 16, 128], BF16, "yT")

        def load(ti):
            b = ti % 2
            s, tt = ti // tps, ti % tps
            k.dma("sp", hf[b][:], h_in[ti * 128:(ti + 1) * 128, :], writes=[hf[b]])
            k.dma("sp", xs[b][:].rearrange("p a b -> p (a b)"), XS[ti * 128:(ti + 1) * 128, :], writes=[xs[b]])
            k.dma("sp", zz[b][:], Zd[ti * 128:(ti + 1) * 128, :], writes=[zz[b]])
            k.dma("sp", dtr[b][:], DTd[ti * 128:(ti + 1) * 128, :], writes=[dtr[b]])
            k.dma("sp", btk[b][:], BTOK[ti * 128:(ti + 1) * 128, :], writes=[btk[b]])
            k.dma("sp", btc[b][:], BTd[s, :, :, tt * 128:(tt + 1) * 128].rearrange("g p t -> p g t"), writes=[btc[b]])
            k.dma("sp", ctc[b][:], CTd[s, :, :, tt * 128:(tt + 1) * 128].rearrange("g p t -> p g t"), writes=[ctc[b]])

        load(0)
        for ti in range(c.NTILES):
            if ti + 1 < c.NTILES:
                load(ti + 1)
            b = ti % 2
            s, tt = ti // tps, ti % tps
            if tt == 0:
                k.op("dve", lambda e: e.memset(hst[:], 0.0), [], [hst])
                k.op("pool", lambda e: e.memset(hstb[:], 0.0), [], [hstb])
            k.op("dve", lambda e, b=b: e.tensor_tensor(out=dt[:], in0=dtr[b][:], in1=dtb[:], op=ALU.add), [dtr[b], dtb], [dt])
            k.op("act", lambda e: e.activation(out=dt[:], in_=dt[:], func=AF.Exp), [dt], [dt])
            k.op("act", lambda e: e.activation(out=dt[:], in_=dt[:], func=AF.Ln, bias=1.0, scale=1.0), [dt], [dt])
            k.op("dve", lambda e: e.tensor_tensor(out=dta[:], in0=dt[:], in1=aneg[:], op=ALU.mult), [dt, aneg], [dta])
            p = psf(c)
            k.op("pe", lambda e, p=p: e.matmul(out=p[:, 0:32], lhsT=triU[:], rhs=dta[:], start=True, stop=True), [triU, dta], [p])
            k.op("pe", lambda e, p=p: e.matmul(out=p[:, 512:544], lhsT=ones[:], rhs=dta[:], start=True, stop=True), [ones, dta], [p])
            k.op("dve", lambda e, p=p: e.tensor_copy(out=acs[:], in_=p[:, 0:32]), [p], [acs])
            k.op("act", lambda e: e.activation(out=ea[:], in_=acs[:], func=AF.Exp), [acs], [ea])
            k.op("dve", lambda e, p=p: e.tensor_tensor(out=dec[:], in0=p[:, 512:544], in1=acs[:], op=ALU.subtract), [p, acs], [dec])
            k.op("act", lambda e: e.activation(out=dec[:], in_=dec[:], func=AF.Exp), [dec], [dec])
            k.op("act", lambda e, p=p: e.activation(out=etot[:], in_=p[:, 512:544], func=AF.Exp), [p], [etot])
            k.op("dve", lambda e: e.tensor_tensor(out=Rm[:], in0=triU[:].unsqueeze(1).broadcast_to([128, 32, 128]), in1=dta[:].unsqueeze(2).broadcast_to([128, 32, 128]),
                                                  op=ALU.mult), [triU, dta], [Rm])
            k.op("dve", lambda e, b=b: e.tensor_tensor(out=xdt[:], in0=xs[b][:], in1=dt[:].unsqueeze(2).broadcast_to([128, 32, 64]), op=ALU.mult), [xs[b], dt], [xdt])
            k.op("pool", lambda e: e.tensor_tensor(out=xdd[:], in0=xdt[:], in1=dec[:].unsqueeze(2).broadcast_to([128, 32, 64]), op=ALU.mult), [xdt, dec], [xdd])
            for g in range(4):
                gb = g % 2
                p = psf(c)
                k.op("pe", lambda e, p=p, g=g, b=b: e.matmul(out=p[:, 0:128], lhsT=btc[b][:, g, :], rhs=ctc[b][:, g, :], start=True, stop=True), [btc[b], ctc[b]], [p])
                k.op("dve", lambda e, p=p, gb=gb: e.tensor_tensor(out=cbm[gb][:], in0=p[:, 0:128], in1=triU[:], op=ALU.mult), [p, triU], [cbm[gb]])
                p = psf(c)
                for hh in range(2):
                    k.op("pe", lambda e, p=p, g=g, hh=hh: e.matmul(out=p[:, hh * 512:(hh + 1) * 512], lhsT=Lst[:],
                                                                  rhs=Rm[:, g * 8 + hh * 4:g * 8 + hh * 4 + 4, :].rearrange("p a b -> p (a b)"), start=True, stop=True),
                         [Lst, Rm], [p])
                k.op("act", lambda e, p=p, gb=gb: e.activation(out=LT[gb][:].rearrange("p a b -> p (a b)"), in_=p[:], func=AF.Exp), [p], [LT[gb]])
                k.op("dve", lambda e, g=g, gb=gb: e.tensor_tensor(out=MT[:, g * 8:(g + 1) * 8, :], in0=LT[gb][:], in1=cbm[gb][:].unsqueeze(1).broadcast_to([128, 8, 128]),
                                                                  op=ALU.mult), [LT[gb], cbm[gb]], [MT])
                p = psf(c)
                k.op("pe", lambda e, p=p, g=g, b=b: e.matmul(out=p[:, 0:512], lhsT=ctc[b][:, g, :], rhs=hstb[:, g * 8:(g + 1) * 8, :].rearrange("p a b -> p (a b)"),
                                                            start=True, stop=True), [ctc[b], hstb], [p])
                for r in range(8):
                    hh_ = g * 8 + r
                    k.op("pe", lambda e, p=p, r=r, hh_=hh_: e.matmul(out=p[:, 512 + r * 64:512 + (r + 1) * 64], lhsT=MT[:, hh_, :], rhs=xdt[:, hh_, :], start=True, stop=True),
                         [MT, xdt], [p])
                k.op("dve", lambda e, p=p, g=g: e.tensor_tensor(out=ytmp[:], in0=p[:, 0:512].rearrange("p (a b) -> p a b", a=8),
                                                                in1=ea[:, g * 8:(g + 1) * 8].unsqueeze(2).broadcast_to([128, 8, 64]), op=ALU.mult), [p, ea], [ytmp])
                k.op("dve", lambda e, p=p, g=g: e.tensor_tensor(out=yy[:, g * 8:(g + 1) * 8, :], in0=ytmp[:], in1=p[:, 512:1024].rearrange("p (a b) -> p a b", a=8), op=ALU.add),
                     [ytmp, p], [yy])
            k.op("dve", lambda e: e.tensor_tensor(out=hst[:], in0=hst[:], in1=etot[:].unsqueeze(2).broadcast_to([128, 32, 64]), op=ALU.mult), [hst, etot], [hst])
            for g2 in range(2):
                p = psf(c)
                for q2 in range(2):
                    g = g2 * 2 + q2
                    k.op("pe", lambda e, p=p, q2=q2, g=g, b=b: e.matmul(out=p[:, q2 * 512:(q2 + 1) * 512], lhsT=btk[b][:, g * 128:(g + 1) * 128],
                                                                       rhs=xdd[:, g * 8:(g + 1) * 8, :].rearrange("p a b -> p (a b)"), start=True, stop=True), [btk[b], xdd], [p])
                k.op("dve", lambda e, p=p, g2=g2: e.tensor_tensor(out=hst[:, g2 * 16:(g2 + 1) * 16, :].rearrange("p a b -> p (a b)"),
                                                                  in0=hst[:, g2 * 16:(g2 + 1) * 16, :].rearrange("p a b -> p (a b)"), in1=p[:], op=ALU.add), [hst, p], [hst])
            k.op("act", lambda e: e.copy(out=hstb[:], in_=hst[:]), [hst], [hstb])
            k.op("pool", lambda e, b=b: e.tensor_tensor(out=xs[b][:], in0=xs[b][:], in1=dsk[:].unsqueeze(2).broadcast_to([128, 32, 64]), op=ALU.mult), [xs[b], dsk], [xs[b]])
            k.op("pool", lambda e, b=b: e.tensor_tensor(out=yy[:], in0=yy[:], in1=xs[b][:], op=ALU.add), [yy, xs[b]], [yy])
            k.op("act", lambda e, b=b: e.activation(out=zz[b][:], in_=zz[b][:], func=AF.Silu), [zz[b]], [zz[b]])
            yyf = lambda: yy[:].rearrange("p a b -> p (a b)")
            k.op("dve", lambda e, b=b: e.tensor_tensor(out=yyf(), in0=yyf(), in1=zz[b][:], op=ALU.mult), [yy, zz[b]], [yy])
            for g in range(4):
                k.op("act", lambda e, g=g: e.activation(out=jk[:], in_=yyf()[:, g * 512:(g + 1) * 512], func=AF.Square, accum_out=ssq[:, g:g + 1]), [yy], [jk, ssq])
            k.op("act", lambda e: e.activation(out=ssq[:], in_=ssq[:], func=AF.Sqrt, scale=1.0 / 512.0, bias=LN_EPS), [ssq], [ssq])
            k.op("dve", lambda e: e.reciprocal(out=ssq[:], in_=ssq[:]), [ssq], [ssq])
            k.op("dve", lambda e: e.tensor_tensor(out=yy[:].rearrange("p (g a) b -> p g (a b)", g=4), in0=yy[:].rearrange("p (g a) b -> p g (a b)", g=4),
                                                  in1=ssq[:].unsqueeze(2).broadcast_to([128, 4, 512]), op=ALU.mult), [yy, ssq], [yy])
            k.op("pool", lambda e: e.tensor_tensor(out=yb[:], in0=yyf(), in1=ng[:], op=ALU.mult), [yy, ng], [yb])
            for h2 in range(2):
                transpose_into(c, lambda h2=h2: yT[:, h2 * 8:(h2 + 1) * 8, :].rearrange("p a b -> p (a b)"), lambda ci, h2=h2: yb[:, (h2 * 8 + ci) * 128:(h2 * 8 + ci + 1) * 128],
                               8, yb, yT, evac=("act" if h2 == 0 else "dve"))
            p = psf(c)
            for nb in range(2):
                for kc in range(16):
                    k.op("pe", lambda e, p=p, nb=nb, kc=kc: e.matmul(out=p[:, nb * 512:(nb + 1) * 512], lhsT=yT[:, kc, :], rhs=wo[:, kc, nb * 512:(nb + 1) * 512],
                                                                    start=(kc == 0), stop=(kc == 15)), [yT, wo], [p])
            tl.run(hf[b], lambda hf_, p=p: p[:, hf_ * 512:(hf_ + 1) * 512], p, h_out, ti)


def phase_ml(c, li, h_in, h_out):
    k = c.k
    T_ = c.T
    NCH = T_ // 128
    tps = NCH
    XM = k.dram("ml_XM", [c.NSEQ, 16, 128, T_], F32)
    OG = k.dram("ml_OG", [c.NT, 2048], F32)
    XCT = k.dram("ml_XCT", [c.NSEQ, 16, 128, T_], BF16)
    XMT = k.dram("ml_XMT", [c.NSEQ, 16, 128, T_], BF16)
    XC = k.dram("ml_XC", [c.NT, 2048], F32)
    QTd = k.dram("ml_QT", [c.NSEQ, 16, 128, T_], BF16)
    KTd = k.dram("ml_KT", [c.NSEQ, 16, 128, T_], BF16)
    KTOK = k.dram("ml_KTOK", [c.NT, 2048], BF16)
    VTOK = k.dram("ml_VTOK", [c.NT, 2048], BF16)
    GI = k.dram("ml_GI", [c.NSEQ, 4, T_], F32)
    GF = k.dram("ml_GF", [c.NSEQ, 4, T_], F32)
    GO = k.dram("ml_GO", [c.NT, 2048], BF16)
    with k.scope():
        w = k.sb([128, 8, 4096], BF16, "w_in")
        load_w(c, w, c.inp["ml_w_in"][0], 1024, 4096)
        hf = [k.sb([128, D], F32, "hf") for _ in range(2)]
        hb = k.sb([128, D], BF16, "hb")
        hT = k.sb([128, 8, 128], BF16, "hT")
        zt = [k.sb([128, 1024], F32, "zt") for _ in range(2)]
        xt = [k.sb([128, 8, 128], F32, "xt") for _ in range(2)]

        def load(ti):
            k.dma("sp", hf[ti % 2][:], h_in[ti * 128:(ti + 1) * 128, :], writes=[hf[ti % 2]])

        load(0)
        ev = 0
        for ti in range(c.NTILES):
            if ti + 1 < c.NTILES:
                load(ti + 1)
            b = ti % 2
            s, tt = ti // tps, ti % tps
            k.op("dve", lambda e, b=b: e.tensor_copy(out=hb[:], in_=hf[b][:]), [hf[b]], [hb])
            transpose_into(c, lambda: hT[:].rearrange("p a b -> p (a b)"), lambda ci: hb[:, ci * 128:(ci + 1) * 128], 8, hb, hT)
            for half in range(2):
                p = psf(c)
                for fc in range(8):
                    col = (half * 8 + fc) * 128
                    for kc in range(8):
                        k.op("pe", lambda e, p=p, fc=fc, col=col, kc=kc: e.matmul(out=p[:, fc * 128:(fc + 1) * 128], lhsT=w[:, kc, col:col + 128], rhs=hT[:, kc, :],
                                                                                 start=(kc == 0), stop=(kc == 7)), [hT, w], [p])
                xb_ = xt[ev % 2]
                k.op("dve", lambda e, p=p, xb_=xb_: e.tensor_copy(out=xb_[:].rearrange("p a b -> p (a b)"), in_=p[:]), [p], [xb_])
                k.dma("sp", XM[s, half * 8:(half + 1) * 8, :, tt * 128:(tt + 1) * 128].rearrange("n p t -> p n t"), xb_[:], reads=[xb_])
                p = psf(c)
                for nb in range(2):
                    n0 = 2048 + half * 1024 + nb * 512
                    for kc in range(8):
                        k.op("pe", lambda e, p=p, nb=nb, n0=n0, kc=kc: e.matmul(out=p[:, nb * 512:(nb + 1) * 512], lhsT=hT[:, kc, :], rhs=w[:, kc, n0:n0 + 512],
                                                                               start=(kc == 0), stop=(kc == 7)), [hT, w], [p])
                zb = zt[ev % 2]
                ev += 1
                k.op("act", lambda e, p=p, zb=zb: e.copy(out=zb[:], in_=p[:]), [p], [zb])
                k.dma("sp", OG[ti * 128:(ti + 1) * 128, half * 1024:(half + 1) * 1024], zb[:], reads=[zb])
    if 'mlA' in DBG:
        return
    with k.scope():
        cwp = k.sb([128, 2048], F32, "cwp")
        k.op("dve", lambda e: e.memset(cwp[:], 0.0), [], [cwp])
        k.dma("sp", cwp[0:4, :], c.inp["ml_conv_w"][0], writes=[cwp])
        k.dma("sp", cwp[4:5, :], c.inp["ml_conv_b"][0:1, :], writes=[cwp])
        cw = k.sb([128, 16, 8], F32, "cw")
        for cch in range(16):
            p = psf(c)
            k.op("pe", lambda e, p=p, cch=cch: e.transpose(out=p[:, 0:128], in_=cwp[:, cch * 128:(cch + 1) * 128], identity=c.identf[:]), [cwp, c.identf], [p])
            k.op("act", lambda e, p=p, cch=cch: e.copy(out=cw[:, cch, :], in_=p[:, 0:8]), [p], [cw])
        xin = [k.sb([128, T_], F32, "xin") for _ in range(2)]
        acc = [k.sb([128, T_], F32, "acc") for _ in range(2)]
        accb = [k.sb([128, T_], BF16, "accb") for _ in range(2)]
        xmb = [k.sb([128, T_], BF16, "xmb") for _ in range(2)]
        stg = [k.sb([128, 4, 128], F32, "stg") for _ in range(2)]
        u = 0
        for s in range(c.NSEQ):
            for cch in range(16):
                ub = u % 2
                u += 1
                xi, ac, abf, xb2 = xin[ub], acc[ub], accb[ub], xmb[ub]
                k.dma("sp", xi[:], XM[s, cch], writes=[xi])
                k.op("pool", lambda e, xi=xi, xb2=xb2: e.tensor_copy(out=xb2[:], in_=xi[:]), [xi], [xb2])
                k.dma("sp", XMT[s, cch], xb2[:], reads=[xb2])
                k.op("dve", lambda e, xi=xi, ac=ac, cch=cch: e.tensor_scalar(out=ac[:], in0=xi[:], scalar1=cw[:, cch, 3:4], scalar2=None, op0=ALU.mult), [xi, cw], [ac])
                for kk in range(3):
                    shf = 3 - kk
                    k.op("dve", lambda e, xi=xi, ac=ac, cch=cch, kk=kk, shf=shf: e.scalar_tensor_tensor(out=ac[:, shf:T_], in0=xi[:, 0:T_ - shf], scalar=cw[:, cch, kk:kk + 1],
                                                                                                  in1=ac[:, shf:T_], op0=ALU.mult, op1=ALU.add), [xi, cw, ac], [ac])
                k.op("act", lambda e, ac=ac, cch=cch: e.activation(out=ac[:], in_=ac[:], func=AF.Silu, bias=cw[:, cch, 4:5], scale=1.0), [ac, cw], [ac])
                k.op("pool", lambda e, ac=ac, abf=abf: e.tensor_copy(out=abf[:], in_=ac[:]), [ac], [abf])
                k.dma("sp", XCT[s, cch], abf[:], reads=[abf])
                for n4 in range(T_ // 512):
                    p = psf(c)
                    for q4 in range(4):
                        t0 = n4 * 512 + q4 * 128
                        k.op("pe", lambda e, p=p, q4=q4, t0=t0, ac=ac: e.transpose(out=p[:, q4 * 128:(q4 + 1) * 128], in_=ac[:, t0:t0 + 128], identity=c.identf[:]),
                             [ac, c.identf], [p])
                    sg_ = stg[n4 % 2]
                    k.op("act", lambda e, p=p, sg_=sg_: e.copy(out=sg_[:].rearrange("p a b -> p (a b)"), in_=p[:, 0:512]), [p], [sg_])
                    k.dma("sp", XC[s * T_ + n4 * 512:s * T_ + (n4 + 1) * 512, cch * 128:(cch + 1) * 128].rearrange("(n p) c -> p n c", p=128), sg_[:], reads=[sg_])
    if 'mlB' in DBG:
        return
    with k.scope():
        wq = k.sb([128, 16, 512], BF16, "wq")
        wk = k.sb([128, 16, 512], BF16, "wk")
        wv = k.sb([128, 16, 512], BF16, "wv")
        load_w(c, wq, c.inp["ml_w_q"][0].rearrange("h d e -> (h d) e"), 2048, 512)
        load_w(c, wk, c.inp["ml_w_k"][0].rearrange("h d e -> (h d) e"), 2048, 512)
        load_w(c, wv, c.inp["ml_w_v"][0].rearrange("h d e -> (h d) e"), 2048, 512)
        wgs = k.sb([128, 48, 8], F32, "wgs")
        for gs in range(3):
            k.dma("sp", wgs[:, gs * 16:(gs + 1) * 16, :], c.inp["ml_w_gates"][0, gs].rearrange("(c p) n -> p c n", p=128), writes=[wgs])
        wgp = k.sb([128, 48, 128], BF16, "wgp")
        k.op("dve", lambda e: e.memset(wgp[:], 0.0), [], [wgp])
        k.op("dve", lambda e: e.tensor_copy(out=wgp[:, :, 0:4], in_=wgs[:, :, 0:4]), [wgs], [wgp])
        k.op("dve", lambda e: e.tensor_copy(out=wgp[:, :, 32:36], in_=wgs[:, :, 4:8]), [wgs], [wgp])
        bcol = k.sb([128, 1], F32, "bcol")
        nbcol = k.sb([128, 1], F32, "nbcol")
        k.op("dve", lambda e: e.memset(bcol[:], 0.0), [], [bcol])
        k.dma("sp", bcol[0:4, :], c.inp["ml_b_gates"][0:1, 0:4].rearrange("o n -> n o"), writes=[bcol])
        k.dma("sp", bcol[32:36, :], c.inp["ml_b_gates"][0:1, 4:8].rearrange("o n -> n o"), writes=[bcol])
        k.op("dve", lambda e: e.tensor_scalar(out=nbcol[:], in0=bcol[:], scalar1=-1.0, scalar2=None, op0=ALU.mult), [bcol], [nbcol])
        xcT = [k.sb([128, 16, 128], BF16, "xcT") for _ in range(2)]
        xmT = [k.sb([128, 16, 128], BF16, "xmT") for _ in range(2)]
        qT = [k.sb([128, 16, 128], BF16, "qT") for _ in range(2)]
        kT = [k.sb([128, 16, 128], BF16, "kT") for _ in range(2)]
        kt = [k.sb([128, 2048], BF16, "kt") for _ in range(2)]
        vt = [k.sb([128, 2048], BF16, "vt") for _ in range(2)]
        vT = k.sb([128, 16, 128], BF16, "vT")
        gt = [k.sb([128, 128], F32, "gt") for _ in range(2)]
        lf = [k.sb([128, 128], F32, "lf") for _ in range(2)]
        KS = 512.0 ** -0.5

        def load(ti):
            s, tt = ti // tps, ti % tps
            k.dma("sp", xcT[ti % 2][:], XCT[s, :, :, tt * 128:(tt + 1) * 128].rearrange("n p t -> p n t"), writes=[xcT[ti % 2]])
            k.dma("sp", xmT[ti % 2][:], XMT[s, :, :, tt * 128:(tt + 1) * 128].rearrange("n p t -> p n t"), writes=[xmT[ti % 2]])

        load(0)
        for ti in range(c.NTILES):
            if ti + 1 < c.NTILES:
                load(ti + 1)
            b = ti % 2
            s, tt = ti // tps, ti % tps
            for which in range(2):
                wsrc = wq if which == 0 else wk
                dstT = qT[b] if which == 0 else kT[b]
                for half in range(2):
                    p = psf(c)
                    for oc in range(8):
                        o16 = half * 8 + oc
                        hh, ec = o16 // 4, o16 % 4
                        for dc in range(4):
                            k.op("pe", lambda e, p=p, oc=oc, hh=hh, ec=ec, dc=dc, wsrc=wsrc, b=b: e.matmul(
                                out=p[:, oc * 128:(oc + 1) * 128], lhsT=wsrc[:, hh * 4 + dc, ec * 128:(ec + 1) * 128], rhs=xcT[b][:, hh * 4 + dc, :],
                                start=(dc == 0), stop=(dc == 3)), [wsrc, xcT[b]], [p])
                    if which == 0:
                        k.op("act", lambda e, p=p, dstT=dstT, half=half: e.copy(out=dstT[:, half * 8:(half + 1) * 8, :].rearrange("p a b -> p (a b)"), in_=p[:]), [p], [dstT])
                    else:
                        k.op("act", lambda e, p=p, dstT=dstT, half=half: e.activation(out=dstT[:, half * 8:(half + 1) * 8, :].rearrange("p a b -> p (a b)"), in_=p[:],
                                                                                      func=AF.Copy, scale=KS), [p], [dstT])
                dd = QTd if which == 0 else KTd
                k.dma("sp", dd[s, :, :, tt * 128:(tt + 1) * 128].rearrange("n p t -> p n t"), dstT[:], reads=[dstT])
            for which in range(2):
                wsrc = wk if which == 0 else wv
                src = xcT[b] if which == 0 else xmT[b]
                dst = kt[b] if which == 0 else vt[b]
                for half in range(2):
                    p = psf(c)
                    for q2 in range(2):
                        hh = half * 2 + q2
                        for dc in range(4):
                            k.op("pe", lambda e, p=p, q2=q2, hh=hh, dc=dc, wsrc=wsrc, src=src: e.matmul(
                                out=p[:, q2 * 512:(q2 + 1) * 512], lhsT=src[:, hh * 4 + dc, :], rhs=wsrc[:, hh * 4 + dc, :], start=(dc == 0), stop=(dc == 3)), [wsrc, src], [p])
                    if which == 0:
                        k.op("dve", lambda e, p=p, dst=dst, half=half: e.tensor_scalar(out=dst[:, half * 1024:(half + 1) * 1024], in0=p[:], scalar1=KS, scalar2=None, op0=ALU.mult),
                             [p], [dst])
                    else:
                        k.op("dve", lambda e, p=p, dst=dst, half=half: e.tensor_copy(out=dst[:, half * 1024:(half + 1) * 1024], in_=p[:]), [p], [dst])
                dd = KTOK if which == 0 else VTOK
                k.dma("sp", dd[ti * 128:(ti + 1) * 128, :], dst[:], reads=[dst])
            for h2 in range(2):
                transpose_into(c, lambda h2=h2: vT[:, h2 * 8:(h2 + 1) * 8, :].rearrange("p a b -> p (a b)"), lambda ci, h2=h2, b=b: vt[b][:, (h2 * 8 + ci) * 128:(h2 * 8 + ci + 1) * 128],
                               8, vt[b], vT, evac=("act" if h2 == 0 else "dve"))
            p = psf(c)
            n_mm = 0
            for gs, srcT in enumerate((qT[b], kT[b], vT)):
                for fc in range(16):
                    k.op("pe", lambda e, p=p, gs=gs, fc=fc, srcT=srcT, n_mm=n_mm: e.matmul(out=p[:, 0:128], lhsT=wgp[:, gs * 16 + fc, :], rhs=srcT[:, fc, :],
                                                                                       start=(n_mm == 0), stop=(n_mm == 47)), [wgp, srcT], [p])
                    n_mm += 1
            k.op("act", lambda e, p=p, b=b: e.activation(out=gt[b][:], in_=p[:, 0:128], func=AF.Identity, bias=bcol[:, 0:1], scale=1.0), [p, bcol], [gt[b]])
            k.op("act", lambda e, p=p, b=b: e.activation(out=lf[b][:], in_=p[:, 0:128], func=AF.Exp, bias=nbcol[:, 0:1], scale=-1.0), [p, nbcol], [lf[b]])
            k.op("act", lambda e, b=b: e.activation(out=lf[b][:], in_=lf[b][:], func=AF.Ln, bias=1.0, scale=1.0), [lf[b]], [lf[b]])
            k.op("dve", lambda e, b=b: e.tensor_scalar(out=lf[b][:], in0=lf[b][:], scalar1=-1.0, scalar2=None, op0=ALU.mult), [lf[b]], [lf[b]])
            k.dma("sp", GI[s, :, tt * 128:(tt + 1) * 128], gt[b][0:4, :], reads=[gt[b]])
            k.dma("sp", GF[s, :, tt * 128:(tt + 1) * 128], lf[b][32:36, :], reads=[lf[b]])
    if 'mlC' in DBG:
        return
    with k.scope():
        sel = k.sb([128, 4, 128], F32, "sel")
        k.dma("sp", sel[:], c.inp["c_sel"][:, :, :], writes=[sel])
        negm = k.sb([128, 128], F32, "negm")
        k.dma("sp", negm[:], c.inp["c_Lst"][:, :], writes=[negm])
        k.op("dve", lambda e: e.tensor_scalar(out=negm[:], in0=negm[:], scalar1=NEG, scalar2=None, op0=ALU.mult), [negm], [negm])
        ones2 = k.sb([128, 2], BF16, "ones2")
        k.op("dve", lambda e: e.memset(ones2[:], 1.0), [], [ones2])
        ones512 = k.sb([128, 512], F32, "ones512")
        k.op("dve", lambda e: e.memset(ones512[:], 1.0), [], [ones512])
        ngt = k.sb([128, 2048], F32, "ngt")
        skt = k.sb([128, 2048], F32, "skt")
        load_bc(c, ngt, c.inp["ml_norm_g"][0:1, :])
        load_bc(c, skt, c.inp["ml_skip"][0:1, :])
        WROW = k.sb([128, T_], F32, "WROW")
        NCM = k.sb([128, T_], F32, "NCM")
        MROW = k.sb([128, T_], F32, "MROW")
        PM = k.sb([128, 4, NCH + 1], F32, "PM")
        NM = k.sb([128, 4, NCH + 1], F32, "NM")
        Cst = k.sb([128, 4, 4, 512], F32, "Cst")
        Cb = k.sb([128, 4, 4, 512], BF16, "Cb")
        nst = k.sb([128, 4, 4, 2], F32, "nst")
        nstb = k.sb([128, 4, 4, 2], BF16, "nstb")
        qTt = [k.sb([128, 16, 128], BF16, "qTt") for _ in range(2)]
        kTt = [k.sb([128, 16, 128], BF16, "kTt") for _ in range(2)]
        ktt = [k.sb([128, 2048], BF16, "ktt") for _ in range(2)]
        vtt = [k.sb([128, 2048], BF16, "vtt") for _ in range(2)]
        ogt = k.sb([128, 2048], F32, "ogt")
        xct = k.sb([128, 2048], F32, "xct")
        cols = k.sb([128, 3, 4], F32, "cols")
        ET = k.sb([128, 128], F32, "ET")
        ST = k.sb([128, 128], BF16, "ST")
        sc = k.sb([128, 1], F32, "sc")
        wkc = k.sb([128, 1], F32, "wkc")
        cd = k.sb([128, 1], F32, "cd")
        em = k.sb([128, 1], F32, "em")
        den = k.sb([128, 2], F32, "den")
        tmp512 = k.sb([128, 512], F32, "tmp512")
        num = k.sb([128, 512], F32, "num")
        kw = k.sb([128, 512], BF16, "kw")
        st6 = k.sb([128, 6], F32, "st6")
        mv2 = k.sb([128, 2], F32, "mv2")
        rs1 = k.sb([128, 1], F32, "rs1")
        outb = [k.sb([128, 2048], BF16, "outb") for _ in range(2)]

        def load(ti):
            s, tt = ti // tps, ti % tps
            b = ti % 2
            k.dma("sp", qTt[b][:], QTd[s, :, :, tt * 128:(tt + 1) * 128].rearrange("n p t -> p n t"), writes=[qTt[b]])
            k.dma("sp", kTt[b][:], KTd[s, :, :, tt * 128:(tt + 1) * 128].rearrange("n p t -> p n t"), writes=[kTt[b]])
            k.dma("sp", ktt[b][:], KTOK[ti * 128:(ti + 1) * 128, :], writes=[ktt[b]])
            k.dma("sp", vtt[b][:], VTOK[ti * 128:(ti + 1) * 128, :], writes=[vtt[b]])

        for s in range(c.NSEQ):
            with k.scope():
                F2 = k.sb([128, T_], F32, "F2")
                if s == 0:
                    k.op("dve", lambda e: e.memset(WROW[:], 0.0), [], [WROW])
                    k.op("pool", lambda e: e.memset(NCM[:], 0.0), [], [NCM])
                    k.op("pool", lambda e: e.memset(MROW[:], 0.0), [], [MROW])
                k.dma("sp", MROW[0:4, :], GF[s], writes=[MROW])
                k.dma("sp", WROW[0:4, :], GI[s], writes=[WROW])
                for blk in range(T_ // 512):
                    sl = slice(blk * 512, (blk + 1) * 512)
                    init = 0.0 if blk == 0 else F2[0:4, blk * 512 - 1:blk * 512]
                    k.op("dve", lambda e, sl=sl, init=init: e.tensor_tensor_scan(out=F2[0:4, sl], data0=ones512[0:4, :], data1=MROW[0:4, sl], initial=init,
                                                                                op0=ALU.mult, op1=ALU.add), [ones512, MROW, F2], [F2])
                k.op("dve", lambda e: e.tensor_tensor(out=WROW[0:4, :], in0=WROW[0:4, :], in1=F2[0:4, :], op=ALU.subtract), [WROW, F2], [WROW])
                for blk in range(T_ // 512):
                    sl = slice(blk * 512, (blk + 1) * 512)
                    init = 0.0 if blk == 0 else NCM[0:4, blk * 512 - 1:blk * 512]
                    k.op("dve", lambda e, sl=sl, init=init: e.tensor_tensor_scan(out=NCM[0:4, sl], data0=ones512[0:4, :], data1=WROW[0:4, sl], initial=init,
                                                                                op0=ALU.mult, op1=ALU.max), [ones512, WROW, NCM], [NCM])
                k.op("dve", lambda e: e.tensor_tensor(out=MROW[0:4, :], in0=F2[0:4, :], in1=NCM[0:4, :], op=ALU.add), [F2, NCM], [MROW])
                k.op("dve", lambda e: e.tensor_scalar(out=NCM[0:4, :], in0=NCM[0:4, :], scalar1=-1.0, scalar2=None, op0=ALU.mult), [NCM], [NCM])
                k.op("dve", lambda e: e.memset(NM[:], 0.0), [], [NM])
                for hh in range(4):
                    p = psf(c)
                    k.op("pe", lambda e, p=p, hh=hh: e.matmul(out=p[:, 0:NCH], lhsT=sel[:, hh, :], rhs=NCM[:, 127::128], start=True, stop=True), [sel, NCM], [p])
                    k.op("dve", lambda e, p=p, hh=hh: e.tensor_copy(out=NM[:, hh, 1:NCH + 1], in_=p[:, 0:NCH]), [p], [NM])
                k.op("dve", lambda e: e.tensor_scalar(out=PM[:], in0=NM[:], scalar1=-1.0, scalar2=None, op0=ALU.mult), [NM], [PM])
            if 'mlD' in DBG:
                return
            k.op("dve", lambda e: e.memset(Cst[:], 0.0), [], [Cst])
            k.op("pool", lambda e: e.memset(Cb[:], 0.0), [], [Cb])
            k.op("dve", lambda e: e.memset(nst[:], 0.0), [], [nst])
            k.op("dve", lambda e: e.memset(nstb[:], 0.0), [], [nstb])
            load(s * tps)
            for n in range(NCH):
                ti = s * tps + n
                if n + 1 < NCH:
                    load(ti + 1)
                b = ti % 2
                t0 = n * 128
                k.dma("sp", ogt[:], OG[ti * 128:(ti + 1) * 128, :], writes=[ogt])
                k.dma("sp", xct[:], XC[ti * 128:(ti + 1) * 128, :], writes=[xct])
                k.op("act", lambda e: e.activation(out=ogt[:], in_=ogt[:], func=AF.Sigmoid), [ogt], [ogt])
                p = psf(c)
                for a_, src in enumerate((WROW, NCM, MROW)):
                    k.op("pe", lambda e, p=p, a_=a_, src=src, t0=t0: e.transpose(out=p[:, a_ * 128:(a_ + 1) * 128], in_=src[:, t0:t0 + 128], identity=c.identf[:]),
                         [src, c.identf], [p])
                k.op("act", lambda e, p=p: e.copy(out=cols[:], in_=p[:, 0:384].rearrange("p (a b) -> p a b", a=3)[:, :, 0:4]), [p], [cols])
                for hh in range(4):
                    hs = slice(hh * 512, (hh + 1) * 512)
                    p = psf(c)
                    for dc in range(4):
                        k.op("pe", lambda e, p=p, hh=hh, dc=dc, b=b: e.matmul(out=p[:, 0:128], lhsT=kTt[b][:, hh * 4 + dc, :], rhs=qTt[b][:, hh * 4 + dc, :],
                                                                             start=(dc == 0), stop=(dc == 3)), [kTt[b], qTt[b]], [p])
                    k.op("pe", lambda e, p=p, hh=hh, t0=t0: e.matmul(out=p[:, 512:640], lhsT=WROW[:, t0:t0 + 128], rhs=sel[:, hh, :], start=True, stop=False), [WROW, sel], [p])
                    k.op("pe", lambda e, p=p, hh=hh, t0=t0: e.matmul(out=p[:, 512:640], lhsT=sel[:, hh, :], rhs=NCM[:, t0:t0 + 128], start=False, stop=False), [NCM, sel], [p])
                    k.op("pe", lambda e, p=p: e.matmul(out=p[:, 512:640], lhsT=c.identf[:], rhs=negm[:], start=False, stop=True), [c.identf, negm], [p])
                    k.op("act", lambda e, p=p: e.activation(out=ET[:], in_=p[:, 512:640], func=AF.Exp), [p], [ET])
                    k.op("dve", lambda e, p=p: e.tensor_tensor(out=ST[:], in0=ET[:], in1=p[:, 0:128], op=ALU.mult), [ET, p], [ST])
                    p1 = psf(c)
                    k.op("pe", lambda e, p1=p1, hs=hs, b=b: e.matmul(out=p1[:, 0:512], lhsT=ST[:], rhs=vtt[b][:, hs], start=True, stop=True), [ST, vtt[b]], [p1])
                    for dc in range(4):
                        k.op("pe", lambda e, p1=p1, hh=hh, dc=dc, b=b: e.matmul(out=p1[:, 512:1024], lhsT=qTt[b][:, hh * 4 + dc, :], rhs=Cb[:, hh, dc, :],
                                                                               start=(dc == 0), stop=(dc == 3)), [qTt[b], Cb], [p1])
                    p2 = psf(c)
                    k.op("pe", lambda e, p2=p2: e.matmul(out=p2[:, 0:2], lhsT=ST[:], rhs=ones2[:], start=True, stop=True), [ST, ones2], [p2])
                    for dc in range(4):
                        k.op("pe", lambda e, p2=p2, hh=hh, dc=dc, b=b: e.matmul(out=p2[:, 512:514], lhsT=qTt[b][:, hh * 4 + dc, :], rhs=nstb[:, hh, dc, :],
                                                                               start=(dc == 0), stop=(dc == 3)), [qTt[b], nstb], [p2])
                    k.op("act", lambda e, hh=hh, n=n: e.activation(out=sc[:], in_=cols[:, 1, hh:hh + 1], func=AF.Exp, bias=PM[:, hh, n:n + 1], scale=1.0), [cols, PM], [sc])
                    k.op("act", lambda e, p1=p1: e.activation(out=tmp512[:], in_=p1[:, 512:1024], func=AF.Copy, scale=sc[:, 0:1]), [p1, sc], [tmp512])
                    k.op("dve", lambda e, p1=p1: e.tensor_tensor(out=num[:], in0=tmp512[:], in1=p1[:, 0:512], op=ALU.add), [tmp512, p1], [num])
                    k.op("dve", lambda e, p2=p2: e.tensor_scalar(out=den[:, 0:1], in0=p2[:, 512:513], scalar1=sc[:, 0:1], scalar2=None, op0=ALU.mult), [p2, sc], [den])
                    k.op("dve", lambda e, p2=p2: e.tensor_tensor(out=den[:, 0:1], in0=den[:, 0:1], in1=p2[:, 0:1], op=ALU.add), [den, p2], [den])
                    k.op("act", lambda e: e.activation(out=den[:, 0:1], in_=den[:, 0:1], func=AF.Abs), [den], [den])
                    k.op("act", lambda e, hh=hh: e.activation(out=em[:], in_=cols[:, 2, hh:hh + 1], func=AF.Exp, scale=-1.0), [cols], [em])
                    k.op("dve", lambda e: e.tensor_tensor(out=den[:, 0:1], in0=den[:, 0:1], in1=em[:], op=ALU.max), [den, em], [den])
                    k.op("dve", lambda e: e.reciprocal(out=den[:, 0:1], in_=den[:, 0:1]), [den], [den])
                    k.op("dve", lambda e: e.tensor_scalar(out=num[:], in0=num[:], scalar1=den[:, 0:1], scalar2=None, op0=ALU.mult), [num, den], [num])
                    k.op("dve", lambda e: e.bn_stats(out=st6[:], in_=num[:]), [num], [st6])
                    k.op("dve", lambda e: e.bn_aggr(out=mv2[:], in_=st6[:]), [st6], [mv2])
                    k.op("act", lambda e: e.activation(out=rs1[:], in_=mv2[:, 1:2], func=AF.Sqrt, bias=LN_EPS, scale=1.0), [mv2], [rs1])
                    k.op("dve", lambda e: e.reciprocal(out=rs1[:], in_=rs1[:]), [rs1], [rs1])
                    k.op("dve", lambda e: e.tensor_scalar(out=num[:], in0=num[:], scalar1=mv2[:, 0:1], scalar2=rs1[:, 0:1], op0=ALU.subtract, op1=ALU.mult), [num, mv2, rs1], [num])
                    k.op("pool", lambda e, hs=hs: e.tensor_tensor(out=num[:], in0=num[:], in1=ngt[:, hs], op=ALU.mult), [num, ngt], [num])
                    k.op("pool", lambda e, hs=hs: e.tensor_tensor(out=tmp512[:], in0=xct[:, hs], in1=skt[:, hs], op=ALU.mult), [xct, skt], [tmp512])
                    k.op("pool", lambda e: e.tensor_tensor(out=num[:], in0=num[:], in1=tmp512[:], op=ALU.add), [num, tmp512], [num])
                    k.op("dve", lambda e, hs=hs, b=b: e.tensor_tensor(out=outb[b][:, hs], in0=num[:], in1=ogt[:, hs], op=ALU.mult), [num, ogt], [outb[b]])
                    k.op("act", lambda e, hh=hh, n=n: e.activation(out=wkc[:], in_=cols[:, 0, hh:hh + 1], func=AF.Exp, bias=NM[:, hh, n + 1:n + 2], scale=1.0), [cols, NM], [wkc])
                    k.op("act", lambda e, hh=hh, n=n: e.activation(out=cd[:], in_=PM[:, hh, n:n + 1], func=AF.Exp, bias=NM[:, hh, n + 1:n + 2], scale=1.0), [PM, NM], [cd])
                    k.op("dve", lambda e, hs=hs, b=b: e.tensor_scalar(out=kw[:], in0=ktt[b][:, hs], scalar1=wkc[:, 0:1], scalar2=None, op0=ALU.mult), [ktt[b], wkc], [kw])
                    for d2 in range(2):
                        p3 = psf(c)
                        for q2 in range(2):
                            dkc = d2 * 2 + q2
                            k.op("pe", lambda e, p3=p3, q2=q2, dkc=dkc, hs=hs, b=b: e.matmul(out=p3[:, q2 * 512:(q2 + 1) * 512], lhsT=kw[:, dkc * 128:(dkc + 1) * 128],
                                                                                          rhs=vtt[b][:, hs], start=True, stop=True), [kw, vtt[b]], [p3])
                        k.op("dve", lambda e, p3=p3, d2=d2, hh=hh: e.scalar_tensor_tensor(out=Cst[:, hh, d2 * 2:(d2 + 1) * 2, :].rearrange("p a b -> p (a b)"),
                                                                                        in0=Cst[:, hh, d2 * 2:(d2 + 1) * 2, :].rearrange("p a b -> p (a b)"), scalar=cd[:, 0:1],
                                                                                        in1=p3[:], op0=ALU.mult, op1=ALU.add), [Cst, cd, p3], [Cst])
                    k.op("act", lambda e, hh=hh: e.copy(out=Cb[:, hh, :, :], in_=Cst[:, hh, :, :]), [Cst], [Cb])
                    p4 = psf(c)
                    for dkc in range(4):
                        k.op("pe", lambda e, p4=p4, dkc=dkc: e.matmul(out=p4[:, dkc * 2:dkc * 2 + 2], lhsT=kw[:, dkc * 128:(dkc + 1) * 128], rhs=ones2[:], start=True, stop=True),
                             [kw, ones2], [p4])
                    k.op("dve", lambda e, p4=p4, hh=hh: e.scalar_tensor_tensor(out=nst[:, hh, :, :].rearrange("p a b -> p (a b)"), in0=nst[:, hh, :, :].rearrange("p a b -> p (a b)"),
                                                                               scalar=cd[:, 0:1], in1=p4[:, 0:8], op0=ALU.mult, op1=ALU.add), [nst, cd, p4], [nst])
                    k.op("dve", lambda e, hh=hh: e.tensor_copy(out=nstb[:, hh, :, :], in_=nst[:, hh, :, :]), [nst], [nstb])
                k.dma("sp", GO[ti * 128:(ti + 1) * 128, :], outb[b][:], reads=[outb[b]])
    if 'mlE' in DBG:
        return
    with k.scope():
        wd = k.sb([128, 16, 1024], BF16, "w_down")
        load_w(c, wd, c.inp["ml_w_down"][0], 2048, 1024)
        tl = Tail(c, li, 0)
        hf = [k.sb([128, D], F32, "hf") for _ in range(2)]
        ab = [k.sb([128, 2048], BF16, "ab") for _ in range(2)]
        aT = [k.sb([128, 16, 128], BF16, "aT") for _ in range(2)]

        def load2(ti):
            k.dma("sp", hf[ti % 2][:], h_in[ti * 128:(ti + 1) * 128, :], writes=[hf[ti % 2]])
            k.dma("sp", ab[ti % 2][:], GO[ti * 128:(ti + 1) * 128, :], writes=[ab[ti % 2]])

        load2(0)
        for ti in range(c.NTILES):
            if ti + 1 < c.NTILES:
                load2(ti + 1)
            b = ti % 2
            for h2 in range(2):
                transpose_into(c, lambda h2=h2, b=b: aT[b][:, h2 * 8:(h2 + 1) * 8, :].rearrange("p a b -> p (a b)"),
                               lambda ci, h2=h2, b=b: ab[b][:, (h2 * 8 + ci) * 128:(h2 * 8 + ci + 1) * 128], 8, ab[b], aT[b], evac=("act" if h2 == 0 else "dve"))
            p = psf(c)
            for nb in range(2):
                for kc in range(16):
                    k.op("pe", lambda e, p=p, nb=nb, kc=kc, b=b: e.matmul(out=p[:, nb * 512:(nb + 1) * 512], lhsT=aT[b][:, kc, :], rhs=wd[:, kc, nb * 512:(nb + 1) * 512],
                                                                         start=(kc == 0), stop=(kc == 15)), [aT[b], wd], [p])
            tl.run(hf[b], lambda hf_, p=p: p[:, hf_ * 512:(hf_ + 1) * 512], p, h_out, ti)


def build(T_, NSEQ, plan, needed):
    c = setup(T_, NSEQ, needed)
    k = c.k
    cur = c.inp["x"]
    bufs = [c.hA, c.hB]
    bi = 0
    for pi, (kind, li) in enumerate(plan):
        dst = c.out if pi == len(plan) - 1 else bufs[bi]
        if kind == "xa":
            phase_xa(c, li, cur, dst)
        elif kind == "peer":
            phase_peer(c, li, cur, dst)
        elif kind == "s5":
            phase_s5(c, li, cur, dst)
        elif kind == "da":
            phase_da(c, li, cur, dst)
        elif kind == "m2":
            phase_m2(c, li, cur, dst)
        elif kind == "ml":
            phase_ml(c, li, cur, dst)
        cur = dst
        bi ^= 1
    return c, k.finish()


N_CORES = 8
SEQ_FULL = 4096
PLAN = []
for _i, _mx in enumerate(["s5", "da", "m2", "ml"]):
    PLAN += [(_mx, _i), ("xa", _i), ("peer", _i)]


def kernel(**inputs):
    nseq = 16 // N_CORES
    needed = set(n for n, _ in INPUT_SPECS)
    c, nc = build(SEQ_FULL, nseq, PLAN, needed)
    consts = host_consts(SEQ_FULL)
    x = np.ascontiguousarray(np.asarray(inputs["x"], dtype=np.float32))
    mem = np.ascontiguousarray(np.asarray(inputs["mem"], dtype=np.float32))
    shared = {}
    for name in c.inp:
        if name in ("x", "mem"):
            continue
        if name.startswith("c_"):
            shared[name] = consts[name]
        else:
            shared[name] = np.ascontiguousarray(np.asarray(inputs[name], dtype=np.float32))
    in_maps = []
    for ci in range(N_CORES):
        m = dict(shared)
        m["x"] = x[ci * nseq:(ci + 1) * nseq].reshape(nseq * SEQ_FULL, D)
        m["mem"] = mem[ci * nseq:(ci + 1) * nseq].reshape(nseq * 256, D)
        in_maps.append(m)
    res = run_bass_kernel_spmd(nc, in_maps, core_ids=list(range(N_CORES)))
    outs = [np.asarray(r["out"]).reshape(nseq, SEQ_FULL, D) for r in res.results]
    return np.concatenate(outs, axis=0).astype(np.float32)
```

```python
import numpy as np
import ml_dtypes
from contextlib import ExitStack
import concourse.bass as bass
import concourse.mybir as mybir
from concourse.bass_utils import run_bass_kernel_spmd

F32 = mybir.dt.float32
BF16 = mybir.dt.bfloat16
I32 = mybir.dt.int32
U32 = mybir.dt.uint32
AF = mybir.ActivationFunctionType
ALU = mybir.AluOpType
AX = mybir.AxisListType

ENGS = ("pe", "dve", "act", "pool", "sp")
NDSEM = 8


class Res:
    __slots__ = ("name", "w", "r")

    def __init__(self, name):
        self.name = name
        self.w = None
        self.r = {}


class T:
    __slots__ = ("t", "res")

    def __init__(self, t, res):
        self.t = t
        self.res = res

    def __getitem__(self, key):
        return self.t[key]

    def ap(self):
        return self.t.ap()


class KB:
    def __init__(self):
        self.nc = bass.Bass("TRN2", target_bir_lowering=False)
        self.stack = ExitStack()
        self.stacks = [self.stack]
        self.q = {e: [] for e in ENGS}
        self.cnt = {e: 0 for e in ENGS}
        self.waited = {e: {} for e in ENGS}
        self.sems = {}
        for e in ENGS:
            self.sems[e] = self.stack.enter_context(self.nc.semaphore("s_" + e))
        self.dsem = {}
        self.dval = {}
        self.dnext = {}
        for qn in ("sp", "pool", "act"):
            self.dsem[qn] = [self.stack.enter_context(self.nc.semaphore("d_%s%d" % (qn, i))) for i in range(NDSEM)]
            self.dval[qn] = [0] * NDSEM
            self.dnext[qn] = 0
        self.n_ins = 0
        self.uid = 0
        nc = self.nc
        self.eng = {"pe": nc.tensor, "dve": nc.vector, "act": nc.scalar, "pool": nc.gpsimd, "sp": nc.sync}

    def sb(self, shape, dtype, name=None):
        self.uid += 1
        name = "%s_%d" % (name or "sb", self.uid)
        t = self.stacks[-1].enter_context(self.nc.sbuf_tensor(name, list(shape), dtype))
        return T(t, Res(name))

    def ps(self, shape, dtype, name=None):
        self.uid += 1
        name = "%s_%d" % (name or "ps", self.uid)
        t = self.stacks[-1].enter_context(self.nc.psum_tensor(name, list(shape), dtype))
        return T(t, Res(name))

    def dram(self, name, shape, dtype, kind="Internal"):
        t = self.nc.dram_tensor(name, list(shape), dtype, kind=kind)
        return T(t, Res(name))

    def _wait(self, e, tok):
        key, val = tok
        if self.waited[e].get(key, 0) >= val:
            return
        self.waited[e][key] = val
        sem = self._sem(key)
        self.eng[e].wait_ge(sem, val)

    def _sem(self, key):
        if isinstance(key, str):
            return self.sems[key]
        return self.dsem[key[0]][key[1]]

    def _deps(self, e, reads, writes, skip_self=False):
        toks = []
        for r in reads:
            if r.w is not None:
                toks.append(r.w)
        for w in writes:
            if w.w is not None:
                toks.append(w.w)
            toks.extend(w.r.items())
        for tok in toks:
            if skip_self and tok[0] == e:
                continue
            self._wait(e, tok)

    def _commit(self, tok, reads, writes):
        for r in reads:
            r.r[tok[0]] = tok[1]
        for w in writes:
            w.w = tok
            w.r = {}

    @staticmethod
    def _res(lst):
        out = []
        for x in lst:
            if x is None:
                continue
            out.append(x.res if isinstance(x, T) else x)
        return out

    def op(self, e, fn, reads=(), writes=()):
        reads = self._res(reads)
        writes = self._res(writes)
        self._deps(e, reads, writes, skip_self=(e == "pe"))
        self.cnt[e] += 1
        tok = (e, self.cnt[e])
        sem = self.sems[e]
        fn(self.eng[e]).then_inc(sem, 1)
        self._commit(tok, reads, writes)
        self.n_ins += 1

    def dma(self, qn, out, in_, reads=(), writes=(), **kw):
        reads = self._res(reads)
        writes = self._res(writes)
        i = self.dnext[qn]
        self.dnext[qn] = (i + 1) % NDSEM
        key = (qn, i)
        if self.dval[qn][i] > 0:
            self._wait(qn, (key, self.dval[qn][i]))
        self._deps(qn, reads, writes)
        self.dval[qn][i] += 16
        tok = (key, self.dval[qn][i])
        sem = self.dsem[qn][i]
        self.eng[qn].dma_start(out=out, in_=in_, **kw).then_inc(sem, 16)
        self._commit(tok, reads, writes)
        self.n_ins += 1

    def barrier(self):
        for e in ENGS:
            for qn in self.dsem:
                for i in range(NDSEM):
                    if self.dval[qn][i] > 0:
                        self._wait(e, ((qn, i), self.dval[qn][i]))
            for e2 in ENGS:
                if e2 != e and self.cnt[e2] > 0:
                    self._wait(e, (e2, self.cnt[e2]))

    def scope(self):
        kb = self

        class _S:
            def __enter__(s2):
                kb.stacks.append(ExitStack())

            def __exit__(s2, *a):
                kb.barrier()
                kb.stacks.pop().close()
                return False
        return _S()

    def finish(self, final_res=()):
        for qn in self.dsem:
            for i in range(NDSEM):
                if self.dval[qn][i] > 0:
                    self._wait("sp", ((qn, i), self.dval[qn][i]))
        for e in ENGS:
            if e != "sp" and self.cnt[e] > 0:
                self._wait("sp", (e, self.cnt[e]))
        self.stack.close()
        return self.nc

import os
DBG = os.environ.get('KDBG', '')

D = 1024
ALPHA = 8 ** 0.25
LN_EPS = 1e-5
NEG = -30000.0

INPUT_SPECS = [
    ("x", None), ("mem", None),
    ("s5_lam_re", (1, 64, 64)), ("s5_lam_im", (1, 64, 64)), ("s5_log_dt", (1, 64)),
    ("s5_b_re", (1, 64, 64, 16)), ("s5_b_im", (1, 64, 64, 16)), ("s5_c_re", (1, 64, 16, 64)), ("s5_c_im", (1, 64, 16, 64)),
    ("s5_d", (1, 1024)), ("s5_w_glu", (1, 1024, 2048)), ("s5_b_glu", (1, 2048)),
    ("da_w_qkv", (1, 1024, 3072)), ("da_lambda", (1, 4, 64)), ("da_subln_g", (1, 128)), ("da_w_o", (1, 1024, 1024)),
    ("m2_w_in", (1, 1024, 5152)), ("m2_conv_w", (1, 4, 3072)), ("m2_conv_b", (1, 3072)), ("m2_dt_bias", (1, 32)),
    ("m2_a_log", (1, 32)), ("m2_d", (1, 32)), ("m2_norm_g", (1, 2048)), ("m2_w_out", (1, 2048, 1024)),
    ("ml_w_in", (1, 1024, 4096)), ("ml_conv_w", (1, 4, 2048)), ("ml_conv_b", (1, 2048)),
    ("ml_w_q", (1, 4, 512, 512)), ("ml_w_k", (1, 4, 512, 512)), ("ml_w_v", (1, 4, 512, 512)),
    ("ml_w_gates", (1, 3, 2048, 8)), ("ml_b_gates", (1, 8)), ("ml_norm_g", (1, 2048)), ("ml_skip", (1, 2048)),
    ("ml_w_down", (1, 2048, 1024)),
    ("xa_w_q", (4, 1024, 1024)), ("xa_w_kv", (4, 1024, 2048)), ("xa_w_o", (4, 1024, 1024)),
    ("pk_w_query", (4, 1024, 2048)), ("pk_sub_keys", (4, 2, 128, 128)), ("pk_u", (4, 16384, 1024)), ("pk_v", (4, 16384, 1024)),
    ("ln_g", (4, 3, 1024)), ("ln_b", (4, 3, 1024)),
]


class Ctx:
    pass


def host_consts(T_=4096):
    c = {}
    hm = np.zeros((128, 2), np.float32)
    hm[:64, 0] = 0.125
    hm[64:, 1] = 0.125
    c["c_hmask"] = hm
    c["c_r0"] = np.tile(-(T_ - np.arange(T_, dtype=np.float32))[None, :], (128, 1)).astype(np.float32)
    qq = np.arange(128, dtype=np.float32)
    c["c_dbase"] = (-np.abs(qq[:, None] - qq[None, :]) + qq[:, None]).astype(np.float32)
    dm = np.zeros((128, 128), np.float32)
    dm[:64, 64:] = NEG
    c["c_dmask"] = dm
    sl = np.zeros((128, 4, 128), np.float32)
    for hh in range(4):
        sl[hh, hh, :] = 1.0
    c["c_sel"] = sl
    c["c_triU"] = np.triu(np.ones((128, 128), np.float32))
    c["c_Lst"] = np.tril(np.ones((128, 128), np.float32), -1)
    c["c_ident"] = np.eye(128, dtype=np.float32)
    c["c_iota16"] = np.tile(np.arange(16, dtype=np.float32)[None, :], (128, 1))
    J = np.zeros((128, 128), np.float32)
    for p in range(64):
        J[p, p + 64] = -1.0
        J[p + 64, p] = 1.0
    c["c_J"] = J
    sg = np.ones((128, 1), np.float32)
    sg[:64] = -1.0
    c["c_sgn"] = sg
    gm = np.zeros((128, 8), np.float32)
    cm = np.zeros((128, 8, 128), np.float32)
    for j in range(8):
        gm[j * 16:(j + 1) * 16, j] = 1.0
        cm[:, j, j * 16:(j + 1) * 16] = 1.0
    c["c_gmask"] = gm
    c["c_cmask"] = cm
    return c


def setup(T_, NSEQ, needed):
    c = Ctx()
    k = KB()
    c.k = k
    c.T = T_
    c.NPF = 3
    c.NSEQ = NSEQ
    c.NT = T_ * NSEQ
    c.NTILES = c.NT // 128
    c.inp = {}
    for name, shp in INPUT_SPECS:
        if name not in needed:
            continue
        if name == "x":
            shp = (c.NT, D)
        elif name == "mem":
            shp = (NSEQ * 256, D)
        c.inp[name] = k.dram(name, list(shp), F32, kind="ExternalInput")
    for name, arr in host_consts(T_).items():
        c.inp[name] = k.dram(name, list(arr.shape), F32, kind="ExternalInput")
    c.out = k.dram("out", [c.NT, D], F32, kind="ExternalOutput")
    c.hA = k.dram("hA", [c.NT, D], F32)
    c.hB = k.dram("hB", [c.NT, D], F32)
    c.identf = k.sb([128, 128], F32, "identf")
    c.identb = k.sb([128, 128], BF16, "identb")
    k.dma("sp", c.identf[:], c.inp["c_ident"][:, :], writes=[c.identf])
    k.op("dve", lambda e: e.tensor_copy(out=c.identb[:], in_=c.identf[:]), [c.identf], [c.identb])
    c.pf = [k.ps([128, 1024], F32, "pf%d" % i) for i in range(c.NPF)]
    c.pb = [k.ps([128, 1024], BF16, "pb%d" % i) for i in range(2)]
    c.pfi = 0
    c.pbi = 0
    c.pf_n = len(c.pf)
    return c


def psf(c):
    c.pfi = (c.pfi + 1) % c.pf_n
    return c.pf[c.pfi]


def psb(c):
    c.pbi = (c.pbi + 1) % len(c.pb)
    return c.pb[c.pbi]


def load_w(c, dst, src_ap, K, N, q="pool"):
    k = c.k
    KC = K // 128
    for n0 in range(0, N, 2048):
        n1 = min(N, n0 + 2048)
        for c0 in range(0, KC, 8):
            c1 = min(KC, c0 + 8)
            k.dma(q, dst[:, c0:c1, n0:n1],
                  src_ap[c0 * 128:c1 * 128, n0:n1].rearrange("(c p) n -> p c n", p=128), writes=[dst])


def load_bc(c, dst, src_ap, q="sp"):
    c.k.dma(q, dst[:], src_ap.broadcast_to([128, src_ap.shape[-1]]), writes=[dst])


def transpose_into(c, dst_fn, src_fn, C, src, dst, evac="act"):
    k = c.k
    p = psb(c)
    for ci in range(C):
        k.op("pe", lambda e, ci=ci: e.transpose(out=p[:, ci * 128:(ci + 1) * 128], in_=src_fn(ci), identity=c.identb[:]),
             [src, c.identb], [p])
    if evac == "act":
        k.op("act", lambda e: e.copy(out=dst_fn(), in_=p[:, 0:C * 128]), [p], [dst])
    else:
        k.op("dve", lambda e: e.tensor_copy(out=dst_fn(), in_=p[:, 0:C * 128]), [p], [dst])


def tail_ln(c, ht, y_fn, ysrc, g_bc, b_bc, outt, z, st, mv, rstd):
    k = c.k
    for hf in range(2):
        sl = slice(hf * 512, (hf + 1) * 512)
        k.op("dve", lambda e, hf=hf, sl=sl: e.scalar_tensor_tensor(out=z[:, sl], in0=ht[:, sl], scalar=ALPHA, in1=y_fn(hf),
                                                                    op0=ALU.mult, op1=ALU.add), [ht, ysrc], [z])
        k.op("dve", lambda e, hf=hf, sl=sl: e.bn_stats(out=st[:, hf, :], in_=z[:, sl]), [z], [st])
    k.op("dve", lambda e: e.bn_aggr(out=mv[:], in_=st[:].rearrange("p a b -> p (a b)")), [st], [mv])
    k.op("act", lambda e: e.activation(out=rstd[:], in_=mv[:, 1:2], func=AF.Sqrt, bias=LN_EPS, scale=1.0), [mv], [rstd])
    k.op("dve", lambda e: e.reciprocal(out=rstd[:], in_=rstd[:]), [rstd], [rstd])
    k.op("dve", lambda e: e.tensor_scalar(out=z[:], in0=z[:], scalar1=mv[:, 0:1], scalar2=rstd[:, 0:1],
                                          op0=ALU.subtract, op1=ALU.mult), [z, mv, rstd], [z])
    k.op("pool", lambda e: e.tensor_tensor(out=z[:], in0=z[:], in1=g_bc[:], op=ALU.mult), [z, g_bc], [z])
    k.op("pool", lambda e: e.tensor_tensor(out=outt[:], in0=z[:], in1=b_bc[:], op=ALU.add), [z, b_bc], [outt])


class Tail:
    def __init__(self, c, li, sub):
        k = c.k
        self.c = c
        self.g = k.sb([128, D], F32, "lng")
        self.b = k.sb([128, D], F32, "lnb")
        load_bc(c, self.g, c.inp["ln_g"][li, sub:sub + 1, :])
        load_bc(c, self.b, c.inp["ln_b"][li, sub:sub + 1, :])
        self.z = [k.sb([128, D], F32, "z") for _ in range(2)]
        self.o = [k.sb([128, D], F32, "ho") for _ in range(2)]
        self.st = [k.sb([128, 2, 6], F32, "st") for _ in range(2)]
        self.mv = [k.sb([128, 2], F32, "mv") for _ in range(2)]
        self.rs = [k.sb([128, 1], F32, "rs") for _ in range(2)]
        self.i = 0

    def run(self, ht, y_fn, ysrc, h_out, ti):
        c = self.c
        i = self.i
        self.i = (i + 1) % 2
        tail_ln(c, ht, y_fn, ysrc, self.g, self.b, self.o[i], self.z[i], self.st[i], self.mv[i], self.rs[i])
        c.k.dma("sp", h_out[ti * 128:(ti + 1) * 128, :], self.o[i][:], reads=[self.o[i]])


def phase_xa(c, li, h_in, h_out):
    k = c.k
    with k.scope():
        wq = k.sb([128, 8, 1024], BF16, "wq")
        wkv = k.sb([128, 8, 2048], BF16, "wkv")
        wo = k.sb([128, 8, 1024], BF16, "wo")
        load_w(c, wq, c.inp["xa_w_q"][li], 1024, 1024)
        load_w(c, wkv, c.inp["xa_w_kv"][li], 1024, 2048)
        load_w(c, wo, c.inp["xa_w_o"][li], 1024, 1024)
        tl = Tail(c, li, 1)
        KT = [k.sb([128, 8, 256], BF16, "KT") for _ in range(c.NSEQ)]
        V = [k.sb([128, 2, 1024], BF16, "V") for _ in range(c.NSEQ)]
        memf = k.sb([128, 1024], F32, "memf")
        memb = k.sb([128, 1024], BF16, "memb")
        memT = k.sb([128, 8, 256], BF16, "memT")
        for s in range(c.NSEQ):
            for mc in range(2):
                k.dma("sp", memf[:], c.inp["mem"][s * 256 + mc * 128: s * 256 + (mc + 1) * 128, :], writes=[memf])
                k.op("dve", lambda e: e.tensor_copy(out=memb[:], in_=memf[:]), [memf], [memb])
                transpose_into(c, lambda mc=mc: memT[:, :, mc * 128:(mc + 1) * 128],
                               lambda ci: memb[:, ci * 128:(ci + 1) * 128], 8, memb, memT)
            for fc in range(8):
                p = psf(c)
                for kc in range(8):
                    k.op("pe", lambda e, fc=fc, kc=kc, p=p: e.matmul(out=p[:, 0:256], lhsT=wkv[:, kc, fc * 128:(fc + 1) * 128],
                                                                     rhs=memT[:, kc, :], start=(kc == 0), stop=(kc == 7)),
                         [wkv, memT], [p])
                k.op("act", lambda e, fc=fc, p=p, s=s: e.copy(out=KT[s][:, fc, :], in_=p[:, 0:256]), [p], [KT[s]])
            for mc in range(2):
                p = psf(c)
                for nb in range(2):
                    for kc in range(8):
                        k.op("pe", lambda e, nb=nb, kc=kc, p=p, mc=mc: e.matmul(
                            out=p[:, nb * 512:(nb + 1) * 512], lhsT=memT[:, kc, mc * 128:(mc + 1) * 128],
                            rhs=wkv[:, kc, 1024 + nb * 512:1024 + (nb + 1) * 512], start=(kc == 0), stop=(kc == 7)),
                            [wkv, memT], [p])
                k.op("dve", lambda e, p=p, mc=mc, s=s: e.tensor_copy(out=V[s][:, mc, :], in_=p[:]), [p], [V[s]])
        hf = [k.sb([128, D], F32, "hf") for _ in range(3)]
        hb = [k.sb([128, D], BF16, "hb") for _ in range(2)]
        hT = [k.sb([128, 8, 128], BF16, "hT") for _ in range(2)]
        qT = [k.sb([128, 8, 128], BF16, "qT") for _ in range(2)]
        ssb = [k.sb([128, 4, 256], F32, "ssb") for _ in range(2)]
        P = [k.sb([128, 4, 256], BF16, "P") for _ in range(2)]
        PT = [k.sb([128, 8, 128], BF16, "PT") for _ in range(2)]
        ob = [k.sb([128, D], BF16, "ob") for _ in range(2)]
        oT = [k.sb([128, 8, 128], BF16, "oT") for _ in range(2)]
        mx = [k.sb([128, 4], F32, "mx") for _ in range(2)]
        sm = [k.sb([128, 4], F32, "sm") for _ in range(2)]
        tps = c.T // 128

        def load(ti):
            k.dma("sp", hf[ti % 3][:], h_in[ti * 128:(ti + 1) * 128, :], writes=[hf[ti % 3]])

        def stage_a(ti):
            if ti + 1 < c.NTILES:
                load(ti + 1)
            b = ti % 2
            s = ti // tps
            hfb = hf[ti % 3]
            k.op("dve", lambda e: e.tensor_copy(out=hb[b][:], in_=hfb[:]), [hfb], [hb[b]])
            transpose_into(c, lambda: hT[b][:].rearrange("p a b -> p (a b)"),
                           lambda ci: hb[b][:, ci * 128:(ci + 1) * 128], 8, hb[b], hT[b])
            p = psf(c)
            for fc in range(8):
                for kc in range(8):
                    k.op("pe", lambda e, fc=fc, kc=kc, p=p: e.matmul(
                        out=p[:, fc * 128:(fc + 1) * 128], lhsT=wq[:, kc, fc * 128:(fc + 1) * 128], rhs=hT[b][:, kc, :],
                        start=(kc == 0), stop=(kc == 7)), [wq, hT[b]], [p])
            k.op("act", lambda e, p=p: e.activation(out=qT[b][:].rearrange("p a b -> p (a b)"), in_=p[:], func=AF.Copy,
                                                    scale=0.0625), [p], [qT[b]])
            p = psf(c)
            for hd in range(4):
                for cc in range(2):
                    k.op("pe", lambda e, hd=hd, cc=cc, p=p: e.matmul(
                        out=p[:, hd * 256:(hd + 1) * 256], lhsT=qT[b][:, 2 * hd + cc, :], rhs=KT[s][:, 2 * hd + cc, :],
                        start=(cc == 0), stop=(cc == 1)), [qT[b], KT[s]], [p])
            k.op("dve", lambda e, p=p: e.tensor_copy(out=ssb[b][:].rearrange("p a b -> p (a b)"), in_=p[:]), [p], [ssb[b]])
            k.op("dve", lambda e: e.tensor_reduce(out=mx[b][:], in_=ssb[b][:], axis=AX.X, op=ALU.max, negate=True),
                 [ssb[b]], [mx[b]])
            for hd in range(4):
                k.op("act", lambda e, hd=hd: e.activation(out=P[b][:, hd, :], in_=ssb[b][:, hd, :], func=AF.Exp,
                                                         bias=mx[b][:, hd:hd + 1], scale=1.0,
                                                         accum_out=sm[b][:, hd:hd + 1]), [ssb[b], mx[b]], [P[b], sm[b]])

        def stage_b(ti):
            b = ti % 2
            s = ti // tps
            hfb = hf[ti % 3]
            k.op("dve", lambda e: e.reciprocal(out=sm[b][:], in_=sm[b][:]), [sm[b]], [sm[b]])
            transpose_into(c, lambda: PT[b][:].rearrange("p a b -> p (a b)"),
                           lambda ci: P[b][:, ci // 2, (ci % 2) * 128:(ci % 2 + 1) * 128], 8, P[b], PT[b], evac="dve")
            p = psf(c)
            for hd in range(4):
                for mc in range(2):
                    k.op("pe", lambda e, hd=hd, mc=mc, p=p: e.matmul(
                        out=p[:, hd * 256:(hd + 1) * 256], lhsT=PT[b][:, 2 * hd + mc, :], rhs=V[s][:, mc, hd * 256:(hd + 1) * 256],
                        start=(mc == 0), stop=(mc == 1)), [PT[b], V[s]], [p])
            for hd in range(4):
                k.op("act", lambda e, hd=hd, p=p: e.activation(out=ob[b][:, hd * 256:(hd + 1) * 256],
                                                              in_=p[:, hd * 256:(hd + 1) * 256], func=AF.Copy,
                                                              scale=sm[b][:, hd:hd + 1]), [p, sm[b]], [ob[b]])
            transpose_into(c, lambda: oT[b][:].rearrange("p a b -> p (a b)"),
                           lambda ci: ob[b][:, ci * 128:(ci + 1) * 128], 8, ob[b], oT[b], evac="act")
            p = psf(c)
            for nb in range(2):
                for kc in range(8):
                    k.op("pe", lambda e, nb=nb, kc=kc, p=p: e.matmul(
                        out=p[:, nb * 512:(nb + 1) * 512], lhsT=oT[b][:, kc, :], rhs=wo[:, kc, nb * 512:(nb + 1) * 512],
                        start=(kc == 0), stop=(kc == 7)), [oT[b], wo], [p])
            tl.run(hfb, lambda hf_, p=p: p[:, hf_ * 512:(hf_ + 1) * 512], p, h_out, ti)

        load(0)
        for ti in range(c.NTILES + 1):
            if ti < c.NTILES:
                stage_a(ti)
            if ti >= 1:
                stage_b(ti - 1)


def phase_peer(c, li, h_in, h_out):
    k = c.k
    NS = 16
    GRP = 8
    UV = k.dram("pk_UV%d" % li, [16384, 2048], BF16)
    with k.scope():
        stf = [k.sb([128, 8192], F32, "stf") for _ in range(2)]
        stb = [k.sb([128, 8192], BF16, "stb") for _ in range(2)]
        it = 0
        for which, nm in enumerate(("pk_u", "pk_v")):
            src = c.inp[nm][li]
            for r0 in range(0, 16384, 1024):
                a, bb = stf[it % 2], stb[it % 2]
                k.dma("sp", a[:].rearrange("p (r d) -> p r d", r=8), src[r0:r0 + 1024, :].rearrange("(p r) d -> p r d", r=8), writes=[a])
                if it % 2 == 0:
                    k.op("act", lambda e, a=a, bb=bb: e.copy(out=bb[:], in_=a[:]), [a], [bb])
                else:
                    k.op("dve", lambda e, a=a, bb=bb: e.tensor_copy(out=bb[:], in_=a[:]), [a], [bb])
                k.dma("sp", UV[r0:r0 + 1024, which * 1024:(which + 1) * 1024].rearrange("(p r) d -> p r d", r=8),
                      bb[:].rearrange("p (r d) -> p r d", r=8), reads=[bb])
                it += 1
    with k.scope():
        wqy = k.sb([128, 8, 2048], BF16, "wqy")
        load_w(c, wqy, c.inp["pk_w_query"][li], 1024, 2048)
        tl = Tail(c, li, 2)
        iota16 = k.sb([128, 16], F32, "iota16")
        k.dma("sp", iota16[:], c.inp["c_iota16"][:, :], writes=[iota16])
        skf = k.sb([128, 128], F32, "skf")
        skb = k.sb([128, 128], BF16, "skb")
        skT = k.sb([128, 2, 128], BF16, "skT")
        for j in range(2):
            k.dma("sp", skf[:], c.inp["pk_sub_keys"][li, j], writes=[skf])
            k.op("dve", lambda e: e.tensor_copy(out=skb[:], in_=skf[:]), [skf], [skb])
            transpose_into(c, lambda j=j: skT[:, j, :], lambda ci: skb[:, :], 1, skb, skT)
        hf = [k.sb([128, D], F32, "hf") for _ in range(2)]
        hb = k.sb([128, D], BF16, "hb")
        hT = k.sb([128, 8, 128], BF16, "hT")
        qT = k.sb([128, 16, 128], BF16, "qT")
        ssb = k.sb([128, 16, 128], F32, "ssb")
        tmp = k.sb([128, 16, 128], F32, "tmp")
        sv = k.sb([128, 16, 16], F32, "sv")
        si = k.sb([128, 16, 16], U32, "si")
        sif = k.sb([128, 16, 16], F32, "sif")
        cand = k.sb([128, 8, 16, 16], F32, "cand")
        tmpc = k.sb([128, 8, 256], F32, "tmpc")
        cv = k.sb([128, 8, 16], F32, "cv")
        ci = k.sb([128, 8, 16], U32, "ci")
        cab = [k.sb([128, 8, 16], U32, "cab") for _ in range(2)]
        cabf = [k.sb([128, 8, 16], F32, "cabf") for _ in range(2)]
        oh = k.sb([128, 8, 16, 16], F32, "oh")
        k12 = [k.sb([128, 8, 16], F32, "k12") for _ in range(2)]
        eif = k.sb([128, 8, 16], F32, "eif")
        eiu = [k.sb([128, 8, 16], U32, "eiu") for _ in range(2)]
        gt = [k.sb([128, 8, 16], F32, "gt") for _ in range(2)]
        zs = k.sb([128, 8], F32, "zs")
        dots = [k.sb([128, 128], F32, "dots") for _ in range(2)]
        actv = [k.sb([128, 128], F32, "actv") for _ in range(2)]
        junk = k.sb([128, D], BF16, "junk")
        uv = [k.sb([128, 2048], BF16, "uv") for _ in range(NS)]
        dg = [k.sb([128, 128], BF16, "dg") for _ in range(NS)]
        pacc = c.pf[2]
        c.pf_n = 2

        def load(ti):
            k.dma("sp", hf[ti % 2][:], h_in[ti * 128:(ti + 1) * 128, :], writes=[hf[ti % 2]])

        load(0)
        for ti in range(c.NTILES):
            if ti + 1 < c.NTILES:
                load(ti + 1)
            b = ti % 2
            hfb = hf[b]
            gtb, dt_, av_ = gt[b], dots[b], actv[b]
            k.op("dve", lambda e: e.tensor_copy(out=hb[:], in_=hfb[:]), [hfb], [hb])
            transpose_into(c, lambda: hT[:].rearrange("p a b -> p (a b)"), lambda ci_: hb[:, ci_ * 128:(ci_ + 1) * 128], 8, hb, hT)
            for half in range(2):
                p = psf(c)
                for fc in range(8):
                    for kc in range(8):
                        k.op("pe", lambda e, fc=fc, kc=kc, p=p, half=half: e.matmul(
                            out=p[:, fc * 128:(fc + 1) * 128], lhsT=wqy[:, kc, (half * 8 + fc) * 128:(half * 8 + fc + 1) * 128],
                            rhs=hT[:, kc, :], start=(kc == 0), stop=(kc == 7)), [wqy, hT], [p])
                k.op("act", lambda e, p=p, half=half: e.copy(out=qT[:, half * 8:(half + 1) * 8, :].rearrange("p a b -> p (a b)"),
                                                            in_=p[:]), [p], [qT])
            for half in range(2):
                p = psf(c)
                for fc in range(8):
                    cidx = half * 8 + fc
                    k.op("pe", lambda e, fc=fc, cidx=cidx, p=p: e.matmul(
                        out=p[:, fc * 128:(fc + 1) * 128], lhsT=qT[:, cidx, :], rhs=skT[:, cidx % 2, :], start=True, stop=True),
                        [qT, skT], [p])
                k.op("act", lambda e, p=p, half=half: e.copy(out=ssb[:, half * 8:(half + 1) * 8, :].rearrange("p a b -> p (a b)"),
                                                            in_=p[:]), [p], [ssb])
            for cc in range(16):
                k.op("dve", lambda e, cc=cc: e.max(out=sv[:, cc, 0:8], in_=ssb[:, cc, :]), [ssb], [sv])
                k.op("dve", lambda e, cc=cc: e.match_replace(out=tmp[:, cc, :], in_to_replace=sv[:, cc, 0:8], in_values=ssb[:, cc, :],
                                                             imm_value=-1e30), [ssb, sv], [tmp])
                k.op("dve", lambda e, cc=cc: e.max(out=sv[:, cc, 8:16], in_=tmp[:, cc, :]), [tmp], [sv])
                k.op("dve", lambda e, cc=cc: e.max_index(out=si[:, cc, 0:8], in_max=sv[:, cc, 0:8], in_values=ssb[:, cc, :]),
                     [ssb, sv], [si])
                k.op("dve", lambda e, cc=cc: e.max_index(out=si[:, cc, 8:16], in_max=sv[:, cc, 8:16], in_values=ssb[:, cc, :]),
                     [ssb, sv], [si])
            k.op("dve", lambda e: e.tensor_copy(out=sif[:], in_=si[:]), [si], [sif])
            for h in range(8):
                cflat = lambda h=h: cand[:, h, :, :].rearrange("p a b -> p (a b)")
                k.op("dve", lambda e, h=h: e.tensor_tensor(out=cand[:, h, :, :], in0=sv[:, 2 * h, :].unsqueeze(2).broadcast_to([128, 16, 16]),
                                                           in1=sv[:, 2 * h + 1, :].unsqueeze(1).broadcast_to([128, 16, 16]), op=ALU.add),
                     [sv], [cand])
                k.op("dve", lambda e, h=h, cflat=cflat: e.max(out=cv[:, h, 0:8], in_=cflat()), [cand], [cv])
                k.op("dve", lambda e, h=h, cflat=cflat: e.match_replace(out=tmpc[:, h, :], in_to_replace=cv[:, h, 0:8], in_values=cflat(),
                                                                       imm_value=-1e30), [cand, cv], [tmpc])
                k.op("dve", lambda e, h=h: e.max(out=cv[:, h, 8:16], in_=tmpc[:, h, :]), [tmpc], [cv])
                k.op("dve", lambda e, h=h, cflat=cflat: e.max_index(out=ci[:, h, 0:8], in_max=cv[:, h, 0:8], in_values=cflat()),
                     [cand, cv], [ci])
                k.op("dve", lambda e, h=h, cflat=cflat: e.max_index(out=ci[:, h, 8:16], in_max=cv[:, h, 8:16], in_values=cflat()),
                     [cand, cv], [ci])
            k.op("dve", lambda e: e.tensor_single_scalar(out=cab[0][:], in_=ci[:], scalar=4, op=ALU.logical_shift_right), [ci], [cab[0]])
            k.op("dve", lambda e: e.tensor_single_scalar(out=cab[1][:], in_=ci[:], scalar=15, op=ALU.bitwise_and), [ci], [cab[1]])
            for j in range(2):
                k.op("dve", lambda e, j=j: e.tensor_copy(out=cabf[j][:], in_=cab[j][:]), [cab[j]], [cabf[j]])
                k.op("dve", lambda e, j=j: e.tensor_tensor(
                    out=oh[:], in0=cabf[j][:].unsqueeze(3).broadcast_to([128, 8, 16, 16]),
                    in1=iota16[:].unsqueeze(1).unsqueeze(1).broadcast_to([128, 8, 16, 16]), op=ALU.is_equal), [cabf[j], iota16], [oh])
                k.op("dve", lambda e, j=j: e.tensor_tensor(
                    out=oh[:], in0=oh[:], in1=sif[:, j::2, :].unsqueeze(2).broadcast_to([128, 8, 16, 16]), op=ALU.mult), [oh, sif], [oh])
                k.op("dve", lambda e, j=j: e.tensor_reduce(out=k12[j][:], in_=oh[:], axis=AX.X, op=ALU.add), [oh], [k12[j]])
            k.op("dve", lambda e: e.scalar_tensor_tensor(out=eif[:].rearrange("p a b -> p (a b)"), in0=k12[0][:].rearrange("p a b -> p (a b)"),
                                                         scalar=128.0, in1=k12[1][:].rearrange("p a b -> p (a b)"),
                                                         op0=ALU.mult, op1=ALU.add), [k12[0], k12[1]], [eif])
            eb = eiu[b]
            k.op("dve", lambda e: e.tensor_tensor(out=gtb[:], in0=cv[:], in1=cv[:, :, 0:1].broadcast_to([128, 8, 16]), op=ALU.subtract),
                 [cv], [gtb])
            k.op("act", lambda e: e.activation(out=gtb[:], in_=gtb[:], func=AF.Exp), [gtb], [gtb])
            k.op("dve", lambda e: e.tensor_reduce(out=zs[:], in_=gtb[:], axis=AX.X, op=ALU.add), [gtb], [zs])
            k.op("dve", lambda e: e.reciprocal(out=zs[:], in_=zs[:]), [zs], [zs])
            k.op("dve", lambda e: e.tensor_tensor(out=gtb[:], in0=gtb[:], in1=zs[:].unsqueeze(2).broadcast_to([128, 8, 16]), op=ALU.mult),
                 [gtb, zs], [gtb])
            k.op("dve", lambda e: e.tensor_copy(out=eb[:], in_=eif[:]), [eif], [eb])
            for g0 in range(0, 128, GRP):
                for hk in range(g0, g0 + GRP):
                    slot = uv[hk % NS]
                    gather(c, slot, UV.t[:, :], eb, hk // 16, hk % 16)
                    k.op("dve", lambda e, slot=slot, hk=hk: e.scalar_tensor_tensor(out=junk[:], in0=slot[:, 0:1024], scalar=1.0, in1=hfb[:], op0=ALU.mult,
                                                                                   op1=ALU.mult, accum_out=dt_[:, hk:hk + 1]),
                         [slot, hfb], [junk, dt_])
                k.op("act", lambda e, g0=g0: e.activation(out=av_[:, g0:g0 + GRP], in_=dt_[:, g0:g0 + GRP], func=AF.Gelu_apprx_tanh), [dt_], [av_])
                k.op("dve", lambda e, g0=g0: e.tensor_tensor(out=av_[:, g0:g0 + GRP], in0=av_[:, g0:g0 + GRP],
                                                             in1=gtb[:].rearrange("p a b -> p (a b)")[:, g0:g0 + GRP], op=ALU.mult), [av_, gtb], [av_])
                for hk in range(g0, g0 + GRP):
                    slot = uv[hk % NS]
                    dgs = dg[hk % NS]
                    k.op("act", lambda e, dgs=dgs, hk=hk: e.activation(out=dgs[:], in_=c.identb[:], func=AF.Copy, scale=av_[:, hk:hk + 1]), [c.identb, av_], [dgs])
                    for nb in range(2):
                        k.op("pe", lambda e, dgs=dgs, slot=slot, nb=nb, hk=hk: e.matmul(out=pacc[:, nb * 512:(nb + 1) * 512], lhsT=dgs[:],
                                                                                       rhs=slot[:, 1024 + nb * 512:1024 + (nb + 1) * 512],
                                                                                       start=(hk == 0), stop=(hk == 127)), [dgs, slot], [pacc])
            tl.run(hfb, lambda hf_: pacc[:, hf_ * 512:(hf_ + 1) * 512], pacc, h_out, ti)
        c.pf_n = 3


def gather(c, dst, table, eb, h, kk):
    k = c.k
    qn = "pool"
    reads = k._res([eb])
    writes = k._res([dst])
    i = k.dnext[qn]
    k.dnext[qn] = (i + 1) % NDSEM
    key = (qn, i)
    if k.dval[qn][i] > 0:
        k._wait(qn, (key, k.dval[qn][i]))
    k._deps(qn, reads, writes)
    k.dval[qn][i] += 16
    tok = (key, k.dval[qn][i])
    k.nc.gpsimd.indirect_dma_start(out=dst[:], out_offset=None, in_=table,
                                   in_offset=bass.IndirectOffsetOnAxis(ap=eb[:, h, kk:kk + 1], axis=0)).then_inc(k.dsem[qn][i], 16)
    k._commit(tok, reads, writes)
    k.n_ins += 1


def phase_s5(c, li, h_in, h_out):
    k = c.k
    T_ = c.T
    NB = T_ // 512
    nlev = int(np.log2(T_))
    GT = k.dram("s5_GT", [c.NSEQ, 8, 128, T_], BF16)
    TWO_PI = 2.0 * np.pi
    with k.scope():
        Jm = k.sb([128, 128], F32, "Jm")
        sgn = k.sb([128, 1], F32, "sgn")
        gmask = k.sb([128, 8], F32, "gmask")
        cmask = k.sb([128, 8, 128], F32, "cmask")
        k.dma("sp", Jm[:], c.inp["c_J"][:, :], writes=[Jm])
        k.dma("sp", sgn[:], c.inp["c_sgn"][:, :], writes=[sgn])
        k.dma("sp", gmask[:], c.inp["c_gmask"][:, :], writes=[gmask])
        k.dma("sp", cmask[:], c.inp["c_cmask"][:, :, :], writes=[cmask])
        ls = k.sb([128, 128], F32, "ls")
        k.op("dve", lambda e: e.memset(ls[:], 0.0), [], [ls])
        lre = k.sb([128, 64], F32, "lre")
        lim = k.sb([128, 64], F32, "lim")
        for nm, dst in (("s5_lam_re", lre), ("s5_lam_im", lim)):
            for hh in range(2):
                k.dma("sp", ls[0:64, hh * 64:(hh + 1) * 64], c.inp[nm][0], writes=[ls])
            p = psf(c)
            k.op("pe", lambda e, p=p: e.transpose(out=p[:, 0:128], in_=ls[:, :], identity=c.identf[:]), [ls, c.identf], [p])
            k.op("dve", lambda e, p=p, dst=dst: e.tensor_copy(out=dst[:], in_=p[:, 0:64]), [p], [dst])
        dt = k.sb([128, 64], F32, "dt")
        load_bc(c, dt, c.inp["s5_log_dt"][0:1, :])
        k.op("act", lambda e: e.activation(out=dt[:], in_=dt[:], func=AF.Exp), [dt], [dt])
        emag = k.sb([128, 64], F32, "emag")
        ang = k.sb([128, 64], F32, "ang")
        k.op("dve", lambda e: e.tensor_tensor(out=emag[:], in0=lre[:], in1=dt[:], op=ALU.mult), [lre, dt], [emag])
        k.op("act", lambda e: e.activation(out=emag[:], in_=emag[:], func=AF.Exp), [emag], [emag])
        k.op("dve", lambda e: e.tensor_tensor(out=ang[:], in0=lim[:], in1=dt[:], op=ALU.mult), [lim, dt], [ang])
        acol = k.sb([128, 64], F32, "acol")
        bcol = k.sb([128, 64], F32, "bcol")
        yv = k.sb([128, 64], F32, "yv")
        yi = k.sb([128, 64], I32, "yi")
        yf = k.sb([128, 64], F32, "yf")
        mk = k.sb([128, 64], F32, "mk")
        for off, dst in ((0.25, acol), (0.0, bcol)):
            k.op("dve", lambda e, off=off: e.tensor_scalar(out=yv[:], in0=ang[:], scalar1=1.0 / TWO_PI, scalar2=off, op0=ALU.mult, op1=ALU.add),
                 [ang], [yv])
            k.op("dve", lambda e: e.tensor_copy(out=yi[:], in_=yv[:]), [yv], [yi])
            k.op("dve", lambda e: e.tensor_copy(out=yf[:], in_=yi[:]), [yi], [yf])
            k.op("dve", lambda e: e.tensor_tensor(out=yv[:], in0=yv[:], in1=yf[:], op=ALU.subtract), [yv, yf], [yv])
            k.op("dve", lambda e: e.tensor_single_scalar(out=mk[:], in_=yv[:], scalar=0.5, op=ALU.is_gt), [yv], [mk])
            k.op("dve", lambda e: e.tensor_tensor(out=yv[:], in0=yv[:], in1=mk[:], op=ALU.subtract), [yv, mk], [yv])
            k.op("dve", lambda e: e.tensor_single_scalar(out=mk[:], in_=yv[:], scalar=-0.5, op=ALU.is_lt), [yv], [mk])
            k.op("dve", lambda e: e.tensor_tensor(out=yv[:], in0=yv[:], in1=mk[:], op=ALU.add), [yv, mk], [yv])
            k.op("act", lambda e, dst=dst: e.activation(out=dst[:], in_=yv[:], func=AF.Sin, scale=TWO_PI), [yv], [dst])
            k.op("dve", lambda e, dst=dst: e.tensor_tensor(out=dst[:], in0=dst[:], in1=emag[:], op=ALU.mult), [dst, emag], [dst])
        am1 = k.sb([128, 64], F32, "am1")
        d2 = k.sb([128, 64], F32, "d2")
        t1 = k.sb([128, 64], F32, "t1")
        cr = k.sb([128, 64], F32, "cr")
        cis = k.sb([128, 64], F32, "cis")
        k.op("dve", lambda e: e.tensor_scalar(out=am1[:], in0=acol[:], scalar1=-1.0, scalar2=None, op0=ALU.add), [acol], [am1])
        k.op("dve", lambda e: e.tensor_tensor(out=d2[:], in0=lre[:], in1=lre[:], op=ALU.mult), [lre], [d2])
        k.op("dve", lambda e: e.tensor_tensor(out=t1[:], in0=lim[:], in1=lim[:], op=ALU.mult), [lim], [t1])
        k.op("dve", lambda e: e.tensor_tensor(out=d2[:], in0=d2[:], in1=t1[:], op=ALU.add), [d2, t1], [d2])
        k.op("dve", lambda e: e.reciprocal(out=d2[:], in_=d2[:]), [d2], [d2])
        k.op("dve", lambda e: e.tensor_tensor(out=cr[:], in0=am1[:], in1=lre[:], op=ALU.mult), [am1, lre], [cr])
        k.op("dve", lambda e: e.tensor_tensor(out=t1[:], in0=bcol[:], in1=lim[:], op=ALU.mult), [bcol, lim], [t1])
        k.op("dve", lambda e: e.tensor_tensor(out=cr[:], in0=cr[:], in1=t1[:], op=ALU.add), [cr, t1], [cr])
        k.op("dve", lambda e: e.tensor_tensor(out=cr[:], in0=cr[:], in1=d2[:], op=ALU.mult), [cr, d2], [cr])
        k.op("dve", lambda e: e.tensor_tensor(out=cis[:], in0=bcol[:], in1=lre[:], op=ALU.mult), [bcol, lre], [cis])
        k.op("dve", lambda e: e.tensor_tensor(out=t1[:], in0=am1[:], in1=lim[:], op=ALU.mult), [am1, lim], [t1])
        k.op("dve", lambda e: e.tensor_tensor(out=cis[:], in0=cis[:], in1=t1[:], op=ALU.subtract), [cis, t1], [cis])
        k.op("dve", lambda e: e.tensor_tensor(out=cis[:], in0=cis[:], in1=d2[:], op=ALU.mult), [cis, d2], [cis])
        k.op("dve", lambda e: e.tensor_scalar(out=cis[:], in0=cis[:], scalar1=sgn[:, 0:1], scalar2=None, op0=ALU.mult), [cis, sgn], [cis])
        if 's5pre0' in DBG:
            return
        BA = k.sb([128, 64, 16], F32, "BA")
        BB = k.sb([128, 64, 16], F32, "BB")
        bre = c.inp["s5_b_re"][0].rearrange("g p c -> p g c")
        bim = c.inp["s5_b_im"][0].rearrange("g p c -> p g c")
        k.dma("sp", BA[0:64], bre, writes=[BA])
        k.dma("sp", BA[64:128], bim, writes=[BA])
        k.dma("sp", BB[0:64], bim, writes=[BB])
        k.dma("sp", BB[64:128], bre, writes=[BB])
        k.op("dve", lambda e: e.tensor_tensor(out=BA[:], in0=BA[:], in1=cr[:].unsqueeze(2).broadcast_to([128, 64, 16]), op=ALU.mult), [BA, cr], [BA])
        k.op("dve", lambda e: e.tensor_tensor(out=BB[:], in0=BB[:], in1=cis[:].unsqueeze(2).broadcast_to([128, 64, 16]), op=ALU.mult), [BB, cis], [BB])
        k.op("dve", lambda e: e.tensor_tensor(out=BA[:], in0=BA[:], in1=BB[:], op=ALU.add), [BA, BB], [BA])
        W0 = k.sb([128, 64, 128], BF16, "W0")
        WC = k.sb([128, 64, 128], BF16, "WC")
        CN = k.sb([128, 8, 128], F32, "CN")
        k.dma("sp", CN[:, :, 0:64], c.inp["s5_c_re"][0].rearrange("(cc g) c p -> (g c) cc p", cc=8), writes=[CN])
        k.dma("sp", CN[:, :, 64:128], c.inp["s5_c_im"][0].rearrange("(cc g) c p -> (g c) cc p", cc=8), writes=[CN])
        k.op("dve", lambda e: e.tensor_scalar(out=CN[:, :, 64:128], in0=CN[:, :, 64:128], scalar1=-1.0, scalar2=None, op0=ALU.mult), [CN], [CN])
        tpf = k.sb([128, 128], F32, "tpf")
        for cc in range(8):
            p = psf(c)
            k.op("pe", lambda e, p=p, cc=cc: e.transpose(out=p[:, 0:128], in_=BA[:, cc * 8:(cc + 1) * 8, :].rearrange("p g c -> p (g c)"),
                                                        identity=c.identf[:]), [BA, c.identf], [p])
            k.op("act", lambda e, p=p: e.copy(out=tpf[:], in_=p[:, 0:128]), [p], [tpf])
            for j in range(8):
                k.op("dve", lambda e, cc=cc, j=j: e.tensor_scalar(out=W0[:, cc * 8 + j, :], in0=tpf[:], scalar1=gmask[:, j:j + 1], scalar2=None,
                                                                  op0=ALU.mult), [tpf, gmask], [W0])
            p = psf(c)
            k.op("pe", lambda e, p=p, cc=cc: e.transpose(out=p[:, 0:128], in_=CN[:, cc, :], identity=c.identf[:]), [CN, c.identf], [p])
            k.op("act", lambda e, p=p: e.copy(out=tpf[:], in_=p[:, 0:128]), [p], [tpf])
            for j in range(8):
                k.op("pool", lambda e, cc=cc, j=j: e.tensor_tensor(out=WC[:, cc * 8 + j, :], in0=tpf[:], in1=cmask[:, j, :], op=ALU.mult),
                     [tpf, cmask], [WC])
        dcol = k.sb([128, 8], F32, "dcol")
        k.dma("sp", dcol[:], c.inp["s5_d"][0].rearrange("(cc p) -> p cc", p=128), writes=[dcol], allow_slow_non_contiguous=True) if False else None
        dtmp = k.sb([128, 128], F32, "dtmp")
        k.op("dve", lambda e: e.memset(dtmp[:], 0.0), [], [dtmp])
        k.dma("sp", dtmp[0:8, :], c.inp["s5_d"][0].rearrange("(cc p) -> cc p", p=128), writes=[dtmp])
        p = psf(c)
        k.op("pe", lambda e, p=p: e.transpose(out=p[:, 0:128], in_=dtmp[:, :], identity=c.identf[:]), [dtmp, c.identf], [p])
        k.op("dve", lambda e, p=p: e.tensor_copy(out=dcol[:], in_=p[:, 0:8]), [p], [dcol])
        if 's5prep' in DBG:
            return
        xst = k.sb([128, T_ // 128, 128], F32, "xst")
        xTf = k.sb([128, T_], F32, "xTf")
        xTb = k.sb([128, T_], BF16, "xTb")
        SA = [k.sb([128, T_], BF16, "SA") for _ in range(8)]
        SB = [k.sb([128, T_], BF16, "SB") for _ in range(2)]
        Xf = [k.sb([128, 128], F32, "Xf") for _ in range(2)]
        XTf = [k.sb([128, 128], F32, "XTf") for _ in range(2)]
        PK = [k.sb([128, nlev, 128], BF16, "PK") for _ in range(2)]
        gtb = [k.sb([128, 512], BF16, "gtb") for _ in range(2)]
        ytmp = [k.sb([128, 512], F32, "ytmp") for _ in range(2)]
        s5tmp = [k.sb([128, 1024], BF16, "s5tmp") for _ in range(2)]
        ev = 0
        for s in range(c.NSEQ):
            for cc in range(8):
                k.dma("sp", xst[:], h_in[s * T_:(s + 1) * T_, cc * 128:(cc + 1) * 128].rearrange("(n p) c -> p n c", p=128), writes=[xst])
                if 's5ma' in DBG:
                    return
                for n4 in range(T_ // 512):
                    p = psf(c)
                    for q4 in range(4):
                        n = n4 * 4 + q4
                        k.op("pe", lambda e, p=p, n=n, q4=q4: e.transpose(out=p[:, q4 * 128:(q4 + 1) * 128], in_=xst[:, n, :], identity=c.identf[:]),
                             [xst, c.identf], [p])
                    if 's5mb' in DBG:
                        return
                    k.op("act", lambda e, p=p, n4=n4: e.copy(out=xTf[:, n4 * 512:(n4 + 1) * 512], in_=p[:, 0:512]), [p], [xTf])
                    if 's5mc' in DBG:
                        return
                    k.op("dve", lambda e, n4=n4: e.tensor_copy(out=xTb[:, n4 * 512:(n4 + 1) * 512], in_=xTf[:, n4 * 512:(n4 + 1) * 512]), [xTf], [xTb])
                if 's5m1' in DBG:
                    return
                finals = []
                for j in range(8):
                    g = cc * 8 + j
                    gi = g % 2
                    X, XT, pk = Xf[gi], XTf[gi], PK[gi]
                    k.op("dve", lambda e, X=X, g=g: e.tensor_scalar(out=X[:], in0=c.identf[:], scalar1=acol[:, g:g + 1], scalar2=None, op0=ALU.mult),
                         [c.identf, acol], [X])
                    k.op("dve", lambda e, XT=XT, X=X: e.tensor_copy(out=XT[:], in_=X[:]), [X], [XT])
                    k.op("dve", lambda e, X=X, g=g: e.scalar_tensor_tensor(out=X[:], in0=Jm[:], scalar=bcol[:, g:g + 1], in1=X[:], op0=ALU.mult, op1=ALU.add),
                         [Jm, bcol, X], [X])
                    k.op("dve", lambda e, XT=XT, g=g: e.tensor_scalar(out=mk[:, 0:1], in0=bcol[:, g:g + 1], scalar1=-1.0, scalar2=None, op0=ALU.mult),
                         [bcol], [mk])
                    k.op("dve", lambda e, XT=XT: e.scalar_tensor_tensor(out=XT[:], in0=Jm[:], scalar=mk[:, 0:1], in1=XT[:], op0=ALU.mult, op1=ALU.add),
                         [Jm, mk, XT], [XT])
                    for lv in range(nlev):
                        k.op("act", lambda e, pk=pk, XT=XT, lv=lv: e.copy(out=pk[:, lv, :], in_=XT[:]), [XT], [pk])
                        if lv + 1 < nlev:
                            p = psf(c)
                            k.op("pe", lambda e, p=p, X=X, XT=XT: e.matmul(out=p[:, 0:128], lhsT=XT[:], rhs=X[:], start=True, stop=True), [X, XT], [p])
                            k.op("pe", lambda e, p=p, X=X, XT=XT: e.matmul(out=p[:, 512:640], lhsT=X[:], rhs=XT[:], start=True, stop=True), [X, XT], [p])
                            k.op("dve", lambda e, p=p, X=X: e.tensor_copy(out=X[:], in_=p[:, 0:128]), [p], [X])
                            k.op("act", lambda e, p=p, XT=XT: e.copy(out=XT[:], in_=p[:, 512:640]), [p], [XT])
                    if 's5m2' in DBG:
                        return
                    cur, oth = (SA[j], SB[gi]) if nlev % 2 == 0 else (SB[gi], SA[j])
                    for hb_ in range(0, NB, 2):
                        p = psf(c)
                        for q2 in range(2):
                            blk = hb_ + q2
                            if blk >= NB:
                                continue
                            k.op("pe", lambda e, p=p, q2=q2, blk=blk, g=g: e.matmul(out=p[:, q2 * 512:(q2 + 1) * 512], lhsT=W0[:, g, :],
                                                                                    rhs=xTb[:, blk * 512:(blk + 1) * 512], start=True, stop=True),
                                 [W0, xTb], [p])
                        w = min(2, NB - hb_) * 512
                        eng = "act" if ev % 2 == 0 else "dve"
                        ev += 1
                        if eng == "act":
                            k.op("act", lambda e, p=p, cur=cur, hb_=hb_, w=w: e.copy(out=cur[:, hb_ * 512:hb_ * 512 + w], in_=p[:, 0:w]), [p], [cur])
                        else:
                            k.op("dve", lambda e, p=p, cur=cur, hb_=hb_, w=w: e.tensor_copy(out=cur[:, hb_ * 512:hb_ * 512 + w], in_=p[:, 0:w]), [p], [cur])
                    if 's5m3' in DBG:
                        return
                    for lv in range(nlev):
                        sh = 1 << lv
                        for hb_ in range(0, NB, 2):
                            p = psf(c)
                            use_act = (ev % 3 == 2)
                            ev += 1
                            for q2 in range(2):
                                blk = hb_ + q2
                                if blk >= NB:
                                    continue
                                t0 = blk * 512
                                lo = max(t0, sh)
                                has2 = lo < t0 + 512
                                if use_act:
                                    k.op("pe", lambda e, p=p, q2=q2, t0=t0, cur=cur, has2=has2: e.matmul(
                                        out=p[:, q2 * 512:(q2 + 1) * 512], lhsT=c.identb[:], rhs=cur[:, t0:t0 + 512], start=True, stop=not has2),
                                        [c.identb, cur], [p])
                                if has2:
                                    k.op("pe", lambda e, p=p, q2=q2, t0=t0, lo=lo, sh=sh, cur=cur, pk=pk, lv=lv, use_act=use_act: e.matmul(
                                        out=p[:, q2 * 512 + (lo - t0):(q2 + 1) * 512], lhsT=pk[:, lv, :], rhs=cur[:, lo - sh:t0 + 512 - sh],
                                        start=not use_act, stop=True), [pk, cur], [p])
                            w = min(2, NB - hb_) * 512
                            c0 = hb_ * 512
                            if use_act:
                                k.op("act", lambda e, p=p, oth=oth, c0=c0, w=w: e.copy(out=oth[:, c0:c0 + w], in_=p[:, 0:w]), [p], [oth])
                            else:
                                lo_all = min(max(c0, sh), c0 + w)
                                if lo_all > c0:
                                    k.op("dve", lambda e, oth=oth, cur=cur, c0=c0, lo_all=lo_all: e.tensor_copy(out=oth[:, c0:lo_all], in_=cur[:, c0:lo_all]), [cur], [oth])
                                if lo_all < c0 + w:
                                    k.op("dve", lambda e, p=p, oth=oth, cur=cur, c0=c0, lo_all=lo_all, w=w: e.tensor_tensor(
                                        out=oth[:, lo_all:c0 + w], in0=p[:, lo_all - c0:w], in1=cur[:, lo_all:c0 + w], op=ALU.add), [p, cur], [oth])
                        cur, oth = oth, cur
                    finals.append(cur)
                if 's5m4' in DBG:
                    return
                for blk in range(NB):
                    p = psf(c)
                    for j in range(8):
                        k.op("pe", lambda e, p=p, j=j, blk=blk, cc=cc: e.matmul(out=p[:, 0:512], lhsT=WC[:, cc * 8 + j, :],
                                                                                rhs=finals[j][:, blk * 512:(blk + 1) * 512], start=(j == 0), stop=(j == 7)),
                             [WC, finals[j]], [p])
                    yt = ytmp[blk % 2]
                    gb_ = gtb[blk % 2]
                    k.op("dve", lambda e, p=p, yt=yt, blk=blk, cc=cc: e.scalar_tensor_tensor(out=yt[:], in0=xTf[:, blk * 512:(blk + 1) * 512], scalar=dcol[:, cc:cc + 1],
                                                                                             in1=p[:, 0:512], op0=ALU.mult, op1=ALU.add), [xTf, dcol, p], [yt])
                    k.op("act", lambda e, yt=yt, gb_=gb_: e.activation(out=gb_[:], in_=yt[:], func=AF.Gelu_apprx_tanh), [yt], [gb_])
                    k.dma("sp", GT[s, cc, :, blk * 512:(blk + 1) * 512], gb_[:], reads=[gb_])
    if 's5m5' in DBG:
        return
    with k.scope():
        wg = k.sb([128, 8, 2048], BF16, "wg")
        load_w(c, wg, c.inp["s5_w_glu"][0], 1024, 2048)
        bg = k.sb([128, 2048], F32, "bg")
        load_bc(c, bg, c.inp["s5_b_glu"][0:1, :])
        tl = Tail(c, li, 0)
        hf = [k.sb([128, D], F32, "hf") for _ in range(2)]
        gT = [k.sb([128, 8, 128], BF16, "gT") for _ in range(2)]
        vg = [k.sb([128, 2048], F32, "vg") for _ in range(2)]
        tps = T_ // 128

        def load(ti):
            s, tt = ti // tps, ti % tps
            k.dma("sp", hf[ti % 2][:], h_in[ti * 128:(ti + 1) * 128, :], writes=[hf[ti % 2]])
            k.dma("sp", gT[ti % 2][:], GT[s, :, :, tt * 128:(tt + 1) * 128].rearrange("c p t -> p c t"), writes=[gT[ti % 2]])

        load(0)
        for ti in range(c.NTILES):
            if ti + 1 < c.NTILES:
                load(ti + 1)
            b = ti % 2
            for half in range(2):
                p = psf(c)
                for nb in range(2):
                    n0 = half * 1024 + nb * 512
                    for kc in range(8):
                        k.op("pe", lambda e, p=p, nb=nb, n0=n0, kc=kc, b=b: e.matmul(out=p[:, nb * 512:(nb + 1) * 512], lhsT=gT[b][:, kc, :],
                                                                                    rhs=wg[:, kc, n0:n0 + 512], start=(kc == 0), stop=(kc == 7)),
                             [gT[b], wg], [p])
                k.op("dve", lambda e, p=p, half=half, b=b: e.tensor_tensor(out=vg[b][:, half * 1024:(half + 1) * 1024], in0=p[:],
                                                                           in1=bg[:, half * 1024:(half + 1) * 1024], op=ALU.add), [p, bg], [vg[b]])
            k.op("act", lambda e, b=b: e.activation(out=vg[b][:, 1024:2048], in_=vg[b][:, 1024:2048], func=AF.Sigmoid), [vg[b]], [vg[b]])
            k.op("pool", lambda e, b=b: e.tensor_tensor(out=vg[b][:, 0:1024], in0=vg[b][:, 0:1024], in1=vg[b][:, 1024:2048], op=ALU.mult), [vg[b]], [vg[b]])
            tl.run(hf[b], lambda hf_, b=b: vg[b][:, hf_ * 512:(hf_ + 1) * 512], vg[b], h_out, ti)


def phase_da(c, li, h_in, h_out):
    k = c.k
    T_ = c.T
    NQ = T_ // 128
    lam_init = 0.8 - 0.6 * float(np.exp(-0.3 * li))
    QT = [k.dram("da_QT%d" % j, [c.NSEQ, 8, 128, T_], BF16) for j in range(2)]
    KTd = k.dram("da_KT", [c.NSEQ, 8, 128, T_], BF16)
    Vd = k.dram("da_V", [c.NT, 1024], BF16)
    AO = k.dram("da_AO", [c.NT, 1024], BF16)
    with k.scope():
        w = k.sb([128, 8, 3072], BF16, "wqkv")
        load_w(c, w, c.inp["da_w_qkv"][0], 1024, 3072)
        hmask = k.sb([128, 2], F32, "hmask")
        k.dma("sp", hmask[:], c.inp["c_hmask"][:, :], writes=[hmask])
        hf = [k.sb([128, D], F32, "hf") for _ in range(2)]
        hb = k.sb([128, D], BF16, "hb")
        hT = k.sb([128, 8, 128], BF16, "hT")
        qt = [[k.sb([128, 8, 128], BF16, "qt") for _ in range(2)] for _ in range(2)]
        kt = [k.sb([128, 8, 128], BF16, "kt") for _ in range(2)]
        vt = [k.sb([128, 1024], BF16, "vt") for _ in range(2)]
        tps = T_ // 128

        def load(ti):
            k.dma("sp", hf[ti % 2][:], h_in[ti * 128:(ti + 1) * 128, :], writes=[hf[ti % 2]])

        load(0)
        for ti in range(c.NTILES):
            if ti + 1 < c.NTILES:
                load(ti + 1)
            b = ti % 2
            s, tt = ti // tps, ti % tps
            k.op("dve", lambda e, b=b: e.tensor_copy(out=hb[:], in_=hf[b][:]), [hf[b]], [hb])
            transpose_into(c, lambda: hT[:].rearrange("p a b -> p (a b)"), lambda ci: hb[:, ci * 128:(ci + 1) * 128], 8, hb, hT)
            for part in range(2):
                p = psf(c)
                for fc in range(8):
                    for kc in range(8):
                        k.op("pe", lambda e, p=p, fc=fc, kc=kc, part=part: e.matmul(
                            out=p[:, fc * 128:(fc + 1) * 128], lhsT=w[:, kc, part * 1024 + fc * 128: part * 1024 + (fc + 1) * 128],
                            rhs=hT[:, kc, :], start=(kc == 0), stop=(kc == 7)), [w, hT], [p])
                if part == 0:
                    for j in range(2):
                        k.op("act", lambda e, p=p, j=j, b=b: e.activation(out=qt[j][b][:].rearrange("p a b -> p (a b)"), in_=p[:], func=AF.Copy,
                                                                         scale=hmask[:, j:j + 1]), [p, hmask], [qt[j][b]])
                        k.dma("sp", QT[j][s, :, :, tt * 128:(tt + 1) * 128].rearrange("h p t -> p h t"), qt[j][b][:], reads=[qt[j][b]])
                else:
                    k.op("dve", lambda e, p=p, b=b: e.tensor_copy(out=kt[b][:].rearrange("p a b -> p (a b)"), in_=p[:]), [p], [kt[b]])
                    k.dma("sp", KTd[s, :, :, tt * 128:(tt + 1) * 128].rearrange("h p t -> p h t"), kt[b][:], reads=[kt[b]])
            p = psf(c)
            for nb in range(2):
                for kc in range(8):
                    k.op("pe", lambda e, p=p, nb=nb, kc=kc: e.matmul(out=p[:, nb * 512:(nb + 1) * 512], lhsT=hT[:, kc, :],
                                                                    rhs=w[:, kc, 2048 + nb * 512:2048 + (nb + 1) * 512], start=(kc == 0), stop=(kc == 7)),
                         [w, hT], [p])
            k.op("act", lambda e, p=p, b=b: e.copy(out=vt[b][:], in_=p[:]), [p], [vt[b]])
            k.dma("sp", Vd[ti * 128:(ti + 1) * 128, :], vt[b][:], reads=[vt[b]])
    with k.scope():
        r0 = k.sb([128, T_], F32, "r0")
        k.dma("sp", r0[:], c.inp["c_r0"][:, :], writes=[r0])
        dbase = k.sb([128, 128], F32, "dbase")
        dmsk = k.sb([128, 128], F32, "dmsk")
        k.dma("sp", dbase[:], c.inp["c_dbase"][:, :], writes=[dbase])
        k.dma("sp", dmsk[:], c.inp["c_dmask"][:, :], writes=[dmsk])
        lm = k.sb([128, 4, 64], F32, "lm")
        k.dma("sp", lm[:].rearrange("p a b -> p (a b)"), c.inp["da_lambda"][0:1].rearrange("o a b -> o (a b)").broadcast_to([128, 256]), writes=[lm])
        lt = k.sb([128, 2, 64], F32, "lt")
        l2 = k.sb([128, 2], F32, "l2")
        nlam = k.sb([128, 1], F32, "nlam")
        k.op("dve", lambda e: e.tensor_tensor(out=lt[:], in0=lm[:, 0::2, :], in1=lm[:, 1::2, :], op=ALU.mult), [lm], [lt])
        k.op("dve", lambda e: e.tensor_reduce(out=l2[:], in_=lt[:], axis=AX.X, op=ALU.add), [lt], [l2])
        k.op("act", lambda e: e.activation(out=l2[:], in_=l2[:], func=AF.Exp), [l2], [l2])
        k.op("dve", lambda e: e.tensor_tensor(out=nlam[:], in0=l2[:, 1:2], in1=l2[:, 0:1], op=ALU.subtract), [l2], [nlam])
        k.op("dve", lambda e: e.tensor_scalar(out=nlam[:], in0=nlam[:], scalar1=-lam_init, scalar2=None, op0=ALU.add), [nlam], [nlam])
        sg = k.sb([128, 128], F32, "sg")
        load_bc(c, sg, c.inp["da_subln_g"][0:1, :])
        k.op("dve", lambda e: e.tensor_scalar(out=sg[:], in0=sg[:], scalar1=(1.0 - lam_init), scalar2=None, op0=ALU.mult), [sg], [sg])
        kTh = [k.sb([128, T_], BF16, "kTh") for _ in range(2)]
        vh = [k.sb([128, NQ, 128], BF16, "vh") for _ in range(2)]
        qTh = [[k.sb([128, T_], BF16, "qTh") for _ in range(2)] for _ in range(2)]
        dh = [k.sb([128, 128], F32, "dh") for _ in range(2)]
        NBUF = 3
        ssb = [k.sb([128, T_], F32, "ssb") for _ in range(NBUF)]
        P = [k.sb([128, T_], BF16, "P") for _ in range(NBUF)]
        PT = [k.sb([128, NQ, 128], BF16, "PT") for _ in range(2)]
        mx = [k.sb([128, 1], F32, "mx") for _ in range(NBUF)]
        sm = [k.sb([128, 1], F32, "sm") for _ in range(NBUF)]
        o0 = [k.sb([128, 128], F32, "o0") for _ in range(2)]
        oo = [k.sb([128, 128], F32, "oo") for _ in range(2)]
        jk = [k.sb([128, 128], F32, "jk") for _ in range(2)]
        ms = [k.sb([128, 1], F32, "ms") for _ in range(2)]
        ob = [k.sb([128, 128], BF16, "ob") for _ in range(2)]
        units = [(s, h, qi, j) for s in range(c.NSEQ) for h in range(8) for qi in range(NQ) for j in range(2)]

        def stage_a(ui):
            s, h, qi, j = units[ui]
            hb_ = (s * 8 + h) % 2
            slope = 2.0 ** (-(h + 1))
            dhh = dh[hb_]
            if qi == 0 and j == 0:
                k.dma("sp", kTh[hb_][:], KTd[s, h], writes=[kTh[hb_]])
                k.dma("sp", vh[hb_][:], Vd[s * T_:(s + 1) * T_, h * 128:(h + 1) * 128].rearrange("(n p) c -> p n c", p=128), writes=[vh[hb_]])
                for jj in range(2):
                    k.dma("sp", qTh[jj][hb_][:], QT[jj][s, h], writes=[qTh[jj][hb_]])
                k.op("dve", lambda e: e.scalar_tensor_tensor(out=dhh[:], in0=dbase[:], scalar=slope, in1=dmsk[:], op0=ALU.mult, op1=ALU.add),
                     [dbase, dmsk], [dhh])
            q0 = qi * 128
            nk = q0 + 128
            ub = ui % NBUF
            sb_, Pb, mxb, smb = ssb[ub], P[ub], mx[ub], sm[ub]
            for k0 in range(0, q0, 1024):
                p = psf(c)
                w_ = min(1024, q0 - k0)
                for c0 in range(0, w_, 512):
                    cw = min(512, w_ - c0)
                    k.op("pe", lambda e, p=p, c0=c0, cw=cw, k0=k0: e.matmul(
                        out=p[:, c0:c0 + cw], lhsT=qTh[j][hb_][:, q0:q0 + 128], rhs=kTh[hb_][:, k0 + c0:k0 + c0 + cw], start=True, stop=True),
                        [qTh[j][hb_], kTh[hb_]], [p])
                    off = T_ - q0 + k0 + c0
                    k.op("dve", lambda e, p=p, c0=c0, cw=cw, k0=k0, off=off: e.scalar_tensor_tensor(
                        out=sb_[:, k0 + c0:k0 + c0 + cw], in0=r0[:, off:off + cw], scalar=slope, in1=p[:, c0:c0 + cw], op0=ALU.mult, op1=ALU.add),
                        [r0, p], [sb_])
            p = psf(c)
            k.op("pe", lambda e, p=p: e.matmul(out=p[:, 0:128], lhsT=qTh[j][hb_][:, q0:q0 + 128], rhs=kTh[hb_][:, q0:q0 + 128],
                                               start=True, stop=True), [qTh[j][hb_], kTh[hb_]], [p])
            k.op("dve", lambda e, p=p: e.tensor_tensor(out=sb_[:, q0:q0 + 128], in0=p[:, 0:128], in1=dhh[:], op=ALU.add),
                 [p, dhh], [sb_])
            k.op("dve", lambda e: e.tensor_reduce(out=mxb[:], in_=sb_[:, 0:nk], axis=AX.X, op=ALU.max, negate=True), [sb_], [mxb])
            k.op("act", lambda e: e.activation(out=Pb[:, 0:nk], in_=sb_[:, 0:nk], func=AF.Exp, bias=mxb[:, 0:1],
                                               scale=1.0, accum_out=smb[:, 0:1]), [sb_, mxb], [Pb, smb])

        def stage_b(ui):
            s, h, qi, j = units[ui]
            hb_ = (s * 8 + h) % 2
            q0 = qi * 128
            nk = q0 + 128
            ub = ui % NBUF
            Pb, PTb, smb = P[ub], PT[ui % 2], sm[ub]
            nblk = nk // 128
            k.op("dve", lambda e: e.reciprocal(out=smb[:], in_=smb[:]), [smb], [smb])
            for b0 in range(0, nblk, 8):
                nb_ = min(8, nblk - b0)
                transpose_into(c, lambda b0=b0, nb_=nb_: PTb[:, b0:b0 + nb_, :].rearrange("p a b -> p (a b)"),
                               lambda ci, b0=b0: Pb[:, (b0 + ci) * 128:(b0 + ci + 1) * 128], nb_, Pb, PTb,
                               evac=("act" if (b0 // 8) % 2 == 0 else "dve"))
            p = psf(c)
            for bk in range(nblk):
                k.op("pe", lambda e, p=p, bk=bk: e.matmul(out=p[:, 0:128], lhsT=PTb[:, bk, :], rhs=vh[hb_][:, bk, :],
                                                         start=(bk == 0), stop=(bk == nblk - 1)), [PTb, vh[hb_]], [p])
            qb = qi % 2
            if j == 0:
                k.op("act", lambda e, p=p: e.activation(out=o0[qb][:], in_=p[:, 0:128], func=AF.Copy, scale=smb[:, 0:1]), [p, smb], [o0[qb]])
            else:
                k.op("dve", lambda e: e.tensor_tensor(out=smb[:], in0=smb[:], in1=nlam[:], op=ALU.mult), [smb, nlam], [smb])
                k.op("dve", lambda e, p=p: e.scalar_tensor_tensor(out=oo[qb][:], in0=p[:, 0:128], scalar=smb[:, 0:1], in1=o0[qb][:],
                                                                 op0=ALU.mult, op1=ALU.add), [p, smb, o0[qb]], [oo[qb]])
                k.op("act", lambda e: e.activation(out=jk[qb][:], in_=oo[qb][:], func=AF.Square, accum_out=ms[qb][:, 0:1]), [oo[qb]], [jk[qb], ms[qb]])
                k.op("act", lambda e: e.activation(out=ms[qb][:], in_=ms[qb][:], func=AF.Sqrt, scale=1.0 / 128.0, bias=LN_EPS), [ms[qb]], [ms[qb]])
                k.op("dve", lambda e: e.reciprocal(out=ms[qb][:], in_=ms[qb][:]), [ms[qb]], [ms[qb]])
                k.op("dve", lambda e: e.scalar_tensor_tensor(out=ob[qb][:], in0=oo[qb][:], scalar=ms[qb][:, 0:1], in1=sg[:], op0=ALU.mult, op1=ALU.mult),
                     [oo[qb], ms[qb], sg], [ob[qb]])
                k.dma("sp", AO[s * T_ + q0:s * T_ + q0 + 128, h * 128:(h + 1) * 128], ob[qb][:], reads=[ob[qb]])

        for ui in range(len(units) + 1):
            if ui < len(units):
                stage_a(ui)
            if ui >= 1:
                stage_b(ui - 1)
    with k.scope():
        wo = k.sb([128, 8, 1024], BF16, "wo")
        load_w(c, wo, c.inp["da_w_o"][0], 1024, 1024)
        tl = Tail(c, li, 0)
        hf = [k.sb([128, D], F32, "hf") for _ in range(2)]
        ab = [k.sb([128, D], BF16, "ab") for _ in range(2)]
        aT = [k.sb([128, 8, 128], BF16, "aT") for _ in range(2)]

        def load(ti):
            k.dma("sp", hf[ti % 2][:], h_in[ti * 128:(ti + 1) * 128, :], writes=[hf[ti % 2]])
            k.dma("sp", ab[ti % 2][:], AO[ti * 128:(ti + 1) * 128, :], writes=[ab[ti % 2]])

        load(0)
        for ti in range(c.NTILES):
            if ti + 1 < c.NTILES:
                load(ti + 1)
            b = ti % 2
            transpose_into(c, lambda b=b: aT[b][:].rearrange("p a b -> p (a b)"), lambda ci, b=b: ab[b][:, ci * 128:(ci + 1) * 128], 8, ab[b], aT[b])
            p = psf(c)
            for nb in range(2):
                for kc in range(8):
                    k.op("pe", lambda e, p=p, nb=nb, kc=kc, b=b: e.matmul(out=p[:, nb * 512:(nb + 1) * 512], lhsT=aT[b][:, kc, :], rhs=wo[:, kc, nb * 512:(nb + 1) * 512],
                                                                         start=(kc == 0), stop=(kc == 7)), [aT[b], wo], [p])
            tl.run(hf[b], lambda hf_, p=p: p[:, hf_ * 512:(hf_ + 1) * 512], p, h_out, ti)


def phase_m2(c, li, h_in, h_out):
    k = c.k
    T_ = c.T
    NCH = T_ // 128
    Zd = k.dram("m2_Z", [c.NT, 2048], F32)
    XBC = k.dram("m2_XBC", [c.NSEQ, 24, 128, T_], F32)
    DTd = k.dram("m2_DT", [c.NT, 32], F32)
    XS = k.dram("m2_XS", [c.NT, 2048], F32)
    BTd = k.dram("m2_BT", [c.NSEQ, 4, 128, T_], BF16)
    CTd = k.dram("m2_CT", [c.NSEQ, 4, 128, T_], BF16)
    BTOK = k.dram("m2_BTOK", [c.NT, 512], BF16)
    tps = T_ // 128
    with k.scope():
        w = k.sb([128, 8, 5152], BF16, "w_in")
        load_w(c, w, c.inp["m2_w_in"][0], 1024, 5152)
        hf = [k.sb([128, D], F32, "hf") for _ in range(2)]
        hb = k.sb([128, D], BF16, "hb")
        hT = k.sb([128, 8, 128], BF16, "hT")
        zt = [k.sb([128, 1024], F32, "zt") for _ in range(2)]
        xt = [k.sb([128, 8, 128], F32, "xt") for _ in range(2)]
        dtt = [k.sb([128, 32], F32, "dtt") for _ in range(2)]

        def load(ti):
            k.dma("sp", hf[ti % 2][:], h_in[ti * 128:(ti + 1) * 128, :], writes=[hf[ti % 2]])

        load(0)
        ev = 0
        for ti in range(c.NTILES):
            if ti + 1 < c.NTILES:
                load(ti + 1)
            b = ti % 2
            s, tt = ti // tps, ti % tps
            k.op("dve", lambda e, b=b: e.tensor_copy(out=hb[:], in_=hf[b][:]), [hf[b]], [hb])
            transpose_into(c, lambda: hT[:].rearrange("p a b -> p (a b)"), lambda ci: hb[:, ci * 128:(ci + 1) * 128], 8, hb, hT)
            for half in range(2):
                p = psf(c)
                for nb in range(2):
                    n0 = half * 1024 + nb * 512
                    for kc in range(8):
                        k.op("pe", lambda e, p=p, nb=nb, n0=n0, kc=kc: e.matmul(out=p[:, nb * 512:(nb + 1) * 512], lhsT=hT[:, kc, :], rhs=w[:, kc, n0:n0 + 512],
                                                                               start=(kc == 0), stop=(kc == 7)), [hT, w], [p])
                zb = zt[ev % 2]
                ev += 1
                k.op("act", lambda e, p=p, zb=zb: e.copy(out=zb[:], in_=p[:]), [p], [zb])
                k.dma("sp", Zd[ti * 128:(ti + 1) * 128, half * 1024:(half + 1) * 1024], zb[:], reads=[zb])
            for third in range(3):
                p = psf(c)
                for fc in range(8):
                    col = 2048 + (third * 8 + fc) * 128
                    for kc in range(8):
                        k.op("pe", lambda e, p=p, fc=fc, col=col, kc=kc: e.matmul(out=p[:, fc * 128:(fc + 1) * 128], lhsT=w[:, kc, col:col + 128], rhs=hT[:, kc, :],
                                                                                 start=(kc == 0), stop=(kc == 7)), [hT, w], [p])
                xb_ = xt[ev % 2]
                ev += 1
                k.op("dve", lambda e, p=p, xb_=xb_: e.tensor_copy(out=xb_[:].rearrange("p a b -> p (a b)"), in_=p[:]), [p], [xb_])
                k.dma("sp", XBC[s, third * 8:(third + 1) * 8, :, tt * 128:(tt + 1) * 128].rearrange("n p t -> p n t"), xb_[:], reads=[xb_])
            p = psf(c)
            for kc in range(8):
                k.op("pe", lambda e, p=p, kc=kc: e.matmul(out=p[:, 0:32], lhsT=hT[:, kc, :], rhs=w[:, kc, 5120:5152], start=(kc == 0), stop=(kc == 7)), [hT, w], [p])
            k.op("act", lambda e, p=p, b=b: e.copy(out=dtt[b][:], in_=p[:, 0:32]), [p], [dtt[b]])
            k.dma("sp", DTd[ti * 128:(ti + 1) * 128, :], dtt[b][:], reads=[dtt[b]])
    with k.scope():
        cwp = k.sb([128, 3072], F32, "cwp")
        k.op("dve", lambda e: e.memset(cwp[:], 0.0), [], [cwp])
        k.dma("sp", cwp[0:4, :], c.inp["m2_conv_w"][0], writes=[cwp])
        k.dma("sp", cwp[4:5, :], c.inp["m2_conv_b"][0:1, :], writes=[cwp])
        cw = k.sb([128, 24, 8], F32, "cw")
        for cch in range(24):
            p = psf(c)
            k.op("pe", lambda e, p=p, cch=cch: e.transpose(out=p[:, 0:128], in_=cwp[:, cch * 128:(cch + 1) * 128], identity=c.identf[:]), [cwp, c.identf], [p])
            k.op("act", lambda e, p=p, cch=cch: e.copy(out=cw[:, cch, :], in_=p[:, 0:8]), [p], [cw])
        xin = [k.sb([128, T_], F32, "xin") for _ in range(2)]
        acc = [k.sb([128, T_], F32, "acc") for _ in range(2)]
        accb = [k.sb([128, T_], BF16, "accb") for _ in range(2)]
        stg = [k.sb([128, 4, 128], F32, "stg") for _ in range(2)]
        stgb = [k.sb([128, 8, 128], BF16, "stgb") for _ in range(2)]
        u = 0
        for s in range(c.NSEQ):
            for cch in range(24):
                ub = u % 2
                u += 1
                xi, ac = xin[ub], acc[ub]
                k.dma("sp", xi[:], XBC[s, cch], writes=[xi])
                k.op("dve", lambda e, xi=xi, ac=ac, cch=cch: e.tensor_scalar(out=ac[:], in0=xi[:], scalar1=cw[:, cch, 3:4], scalar2=None, op0=ALU.mult), [xi, cw], [ac])
                for kk in range(3):
                    shf = 3 - kk
                    k.op("dve", lambda e, xi=xi, ac=ac, cch=cch, kk=kk, shf=shf: e.scalar_tensor_tensor(out=ac[:, shf:T_], in0=xi[:, 0:T_ - shf], scalar=cw[:, cch, kk:kk + 1],
                                                                                                  in1=ac[:, shf:T_], op0=ALU.mult, op1=ALU.add), [xi, cw, ac], [ac])
                k.op("act", lambda e, ac=ac, cch=cch: e.activation(out=ac[:], in_=ac[:], func=AF.Silu, bias=cw[:, cch, 4:5], scale=1.0), [ac, cw], [ac])
                if cch < 16:
                    for n4 in range(T_ // 512):
                        p = psf(c)
                        for q4 in range(4):
                            t0 = n4 * 512 + q4 * 128
                            k.op("pe", lambda e, p=p, q4=q4, t0=t0, ac=ac: e.transpose(out=p[:, q4 * 128:(q4 + 1) * 128], in_=ac[:, t0:t0 + 128], identity=c.identf[:]),
                                 [ac, c.identf], [p])
                        sg_ = stg[n4 % 2]
                        k.op("act", lambda e, p=p, sg_=sg_: e.copy(out=sg_[:].rearrange("p a b -> p (a b)"), in_=p[:, 0:512]), [p], [sg_])
                        k.dma("sp", XS[s * T_ + n4 * 512:s * T_ + (n4 + 1) * 512, cch * 128:(cch + 1) * 128].rearrange("(n p) c -> p n c", p=128), sg_[:], reads=[sg_])
                else:
                    abf = accb[ub]
                    k.op("dve", lambda e, ac=ac, abf=abf: e.tensor_copy(out=abf[:], in_=ac[:]), [ac], [abf])
                    if cch < 20:
                        g = cch - 16
                        k.dma("sp", BTd[s, g], abf[:], reads=[abf])
                        for n8 in range(0, T_ // 128, 8):
                            nb_ = min(8, T_ // 128 - n8)
                            sb8 = stgb[(n8 // 8) % 2]
                            transpose_into(c, lambda sb8=sb8, nb_=nb_: sb8[:, 0:nb_, :].rearrange("p a b -> p (a b)"),
                                           lambda ci, n8=n8, abf=abf: abf[:, (n8 + ci) * 128:(n8 + ci + 1) * 128], nb_, abf, sb8)
                            k.dma("sp", BTOK[s * T_ + n8 * 128:s * T_ + (n8 + nb_) * 128, g * 128:(g + 1) * 128].rearrange("(n p) c -> p n c", p=128),
                                  sb8[:, 0:nb_, :], reads=[sb8])
                    else:
                        g = cch - 20
                        k.dma("sp", CTd[s, g], abf[:], reads=[abf])
    with k.scope():
        wo = k.sb([128, 16, 1024], BF16, "w_out")
        load_w(c, wo, c.inp["m2_w_out"][0], 2048, 1024)
        tl = Tail(c, li, 0)
        triU = k.sb([128, 128], F32, "triU")
        Lst = k.sb([128, 128], F32, "Lst")
        ones = k.sb([128, 128], F32, "ones")
        k.dma("sp", triU[:], c.inp["c_triU"][:, :], writes=[triU])
        k.dma("sp", Lst[:], c.inp["c_Lst"][:, :], writes=[Lst])
        k.op("dve", lambda e: e.memset(ones[:], 1.0), [], [ones])
        dtb = k.sb([128, 32], F32, "dtb")
        aneg = k.sb([128, 32], F32, "aneg")
        load_bc(c, dtb, c.inp["m2_dt_bias"][0:1, :])
        load_bc(c, aneg, c.inp["m2_a_log"][0:1, :])
        k.op("act", lambda e: e.activation(out=aneg[:], in_=aneg[:], func=AF.Exp), [aneg], [aneg])
        k.op("dve", lambda e: e.tensor_scalar(out=aneg[:], in0=aneg[:], scalar1=-1.0, scalar2=None, op0=ALU.mult), [aneg], [aneg])
        dsk = k.sb([128, 32], F32, "dsk")
        load_bc(c, dsk, c.inp["m2_d"][0:1, :])
        ng = k.sb([128, 2048], F32, "ng")
        load_bc(c, ng, c.inp["m2_norm_g"][0:1, :])
        hst = k.sb([128, 32, 64], F32, "hst")
        hstb = k.sb([128, 32, 64], BF16, "hstb")
        hf = [k.sb([128, D], F32, "hf") for _ in range(2)]
        xs = [k.sb([128, 32, 64], F32, "xs") for _ in range(2)]
        zz = [k.sb([128, 2048], F32, "zz") for _ in range(2)]
        dtr = [k.sb([128, 32], F32, "dtr") for _ in range(2)]
        btk = [k.sb([128, 512], BF16, "btk") for _ in range(2)]
        btc = [k.sb([128, 4, 128], BF16, "btc") for _ in range(2)]
        ctc = [k.sb([128, 4, 128], BF16, "ctc") for _ in range(2)]
        dt = k.sb([128, 32], F32, "dt")
        dta = k.sb([128, 32], F32, "dta")
        acs = k.sb([128, 32], F32, "acs")
        ea = k.sb([128, 32], F32, "ea")
        dec = k.sb([128, 32], F32, "dec")
        etot = k.sb([128, 32], F32, "etot")
        Rm = k.sb([128, 32, 128], F32, "Rm")
        LT = [k.sb([128, 8, 128], F32, "LT") for _ in range(2)]
        MT = k.sb([128, 32, 128], BF16, "MT")
        cbm = [k.sb([128, 128], F32, "cbm") for _ in range(2)]
        xdt = k.sb([128, 32, 64], BF16, "xdt")
        xdd = k.sb([128, 32, 64], BF16, "xdd")
        yy = k.sb([128, 32, 64], F32, "yy")
        ytmp = k.sb([128, 8, 64], F32, "ytmp")
        ssq = k.sb([128, 4], F32, "ssq")
        jk = k.sb([128, 512], F32, "jk")
        yb = k.sb([128, 2048], BF16, "yb")
        yT = k.sb([128, 16, 128], BF16, "yT")

        def load(ti):
            b = ti % 2
            s, tt = ti // tps, ti % tps
            k.dma("sp", hf[b][:], h_in[ti * 128:(ti + 1) * 128, :], writes=[hf[b]])
            k.dma("sp", xs[b][:].rearrange("p a b -> p (a b)"), XS[ti * 128:(ti + 1) * 128, :], writes=[xs[b]])
            k.dma("sp", zz[b][:], Zd[ti * 128:(ti + 1) * 128, :], writes=[zz[b]])
            k.dma("sp", dtr[b][:], DTd[ti * 128:(ti + 1) * 128, :], writes=[dtr[b]])
            k.dma("sp", btk[b][:], BTOK[ti * 128:(ti + 1) * 128, :], writes=[btk[b]])
            k.dma("sp", btc[b][:], BTd[s, :, :, tt * 128:(tt + 1) * 128].rearrange("g p t -> p g t"), writes=[btc[b]])
            k.dma("sp", ctc[b][:], CTd[s, :, :, tt * 128:(tt + 1) * 128].rearrange("g p t -> p g t"), writes=[ctc[b]])

        load(0)
        for ti in range(c.NTILES):
            if ti + 1 < c.NTILES:
                load(ti + 1)
            b = ti % 2
            s, tt = ti // tps, ti % tps
            if tt == 0:
                k.op("dve", lambda e: e.memset(hst[:], 0.0), [], [hst])
                k.op("pool", lambda e: e.memset(hstb[:], 0.0), [], [hstb])
            k.op("dve", lambda e, b=b: e.tensor_tensor(out=dt[:], in0=dtr[b][:], in1=dtb[:], op=ALU.add), [dtr[b], dtb], [dt])
            k.op("act", lambda e: e.activation(out=dt[:], in_=dt[:], func=AF.Exp), [dt], [dt])
            k.op("act", lambda e: e.activation(out=dt[:], in_=dt[:], func=AF.Ln, bias=1.0, scale=1.0), [dt], [dt])
            k.op("dve", lambda e: e.tensor_tensor(out=dta[:], in0=dt[:], in1=aneg[:], op=ALU.mult), [dt, aneg], [dta])
            p = psf(c)
            k.op("pe", lambda e, p=p: e.matmul(out=p[:, 0:32], lhsT=triU[:], rhs=dta[:], start=True, stop=True), [triU, dta], [p])
            k.op("pe", lambda e, p=p: e.matmul(out=p[:, 512:544], lhsT=ones[:], rhs=dta[:], start=True, stop=True), [ones, dta], [p])
            k.op("dve", lambda e, p=p: e.tensor_copy(out=acs[:], in_=p[:, 0:32]), [p], [acs])
            k.op("act", lambda e: e.activation(out=ea[:], in_=acs[:], func=AF.Exp), [acs], [ea])
            k.op("dve", lambda e, p=p: e.tensor_tensor(out=dec[:], in0=p[:, 512:544], in1=acs[:], op=ALU.subtract), [p, acs], [dec])
            k.op("act", lambda e: e.activation(out=dec[:], in_=dec[:], func=AF.Exp), [dec], [dec])
            k.op("act", lambda e, p=p: e.activation(out=etot[:], in_=p[:, 512:544], func=AF.Exp), [p], [etot])
            k.op("dve", lambda e: e.tensor_tensor(out=Rm[:], in0=triU[:].unsqueeze(1).broadcast_to([128, 32, 128]), in1=dta[:].unsqueeze(2).broadcast_to([128, 32, 128]),
                                                  op=ALU.mult), [triU, dta], [Rm])
            k.op("dve", lambda e, b=b: e.tensor_tensor(out=xdt[:], in0=xs[b][:], in1=dt[:].unsqueeze(2).broadcast_to([128, 32, 64]), op=ALU.mult), [xs[b], dt], [xdt])
            k.op("pool", lambda e: e.tensor_tensor(out=xdd[:], in0=xdt[:], in1=dec[:].unsqueeze(2).broadcast_to([128, 32, 64]), op=ALU.mult), [xdt, dec], [xdd])
            for g in range(4):
                gb = g % 2
                p = psf(c)
                k.op("pe", lambda e, p=p, g=g, b=b: e.matmul(out=p[:, 0:128], lhsT=btc[b][:, g, :], rhs=ctc[b][:, g, :], start=True, stop=True), [btc[b], ctc[b]], [p])
                k.op("dve", lambda e, p=p, gb=gb: e.tensor_tensor(out=cbm[gb][:], in0=p[:, 0:128], in1=triU[:], op=ALU.mult), [p, triU], [cbm[gb]])
                p = psf(c)
                for hh in range(2):
                    k.op("pe", lambda e, p=p, g=g, hh=hh: e.matmul(out=p[:, hh * 512:(hh + 1) * 512], lhsT=Lst[:],
                                                                  rhs=Rm[:, g * 8 + hh * 4:g * 8 + hh * 4 + 4, :].rearrange("p a b -> p (a b)"), start=True, stop=True),
                         [Lst, Rm], [p])
                k.op("act", lambda e, p=p, gb=gb: e.activation(out=LT[gb][:].rearrange("p a b -> p (a b)"), in_=p[:], func=AF.Exp), [p], [LT[gb]])
                k.op("dve", lambda e, g=g, gb=gb: e.tensor_tensor(out=MT[:, g * 8:(g + 1) * 8, :], in0=LT[gb][:], in1=cbm[gb][:].unsqueeze(1).broadcast_to([128, 8, 128]),
                                                                  op=ALU.mult), [LT[gb], cbm[gb]], [MT])
                p = psf(c)
                k.op("pe", lambda e, p=p, g=g, b=b: e.matmul(out=p[:, 0:512], lhsT=ctc[b][:, g, :], rhs=hstb[:, g * 8:(g + 1) * 8, :].rearrange("p a b -> p (a b)"),
                                                            start=True, stop=True), [ctc[b], hstb], [p])
                for r in range(8):
                    hh_ = g * 8 + r
                    k.op("pe", lambda e, p=p, r=r, hh_=hh_: e.matmul(out=p[:, 512 + r * 64:512 + (r + 1) * 64], lhsT=MT[:, hh_, :], rhs=xdt[:, hh_, :], start=True, stop=True),
                         [MT, xdt], [p])
                k.op("dve", lambda e, p=p, g=g: e.tensor_tensor(out=ytmp[:], in0=p[:, 0:512].rearrange("p (a b) -> p a b", a=8),
                                                                in1=ea[:, g * 8:(g + 1) * 8].unsqueeze(2).broadcast_to([128, 8, 64]), op=ALU.mult), [p, ea], [ytmp])
                k.op("dve", lambda e, p=p, g=g: e.tensor_tensor(out=yy[:, g * 8:(g + 1) * 8, :], in0=ytmp[:], in1=p[:, 512:1024].rearrange("p (a b) -> p a b", a=8), op=ALU.add),
                     [ytmp, p], [yy])
            k.op("dve", lambda e: e.tensor_tensor(out=hst[:], in0=hst[:], in1=etot[:].unsqueeze(2).broadcast_to([128, 32, 64]), op=ALU.mult), [hst, etot], [hst])
            for g2 in range(2):
                p = psf(c)
                for q2 in range(2):
                    g = g2 * 2 + q2
                    k.op("pe", lambda e, p=p, q2=q2, g=g, b=b: e.matmul(out=p[:, q2 * 512:(q2 + 1) * 512], lhsT=btk[b][:, g * 128:(g + 1) * 128],
                                                                       rhs=xdd[:, g * 8:(g + 1) * 8, :].rearrange("p a b -> p (a b)"), start=True, stop=True), [btk[b], xdd], [p])
                k.op("dve", lambda e, p=p, g2=g2: e.tensor_tensor(out=hst[:, g2 * 16:(g2 + 1) * 16, :].rearrange("p a b -> p (a b)"),
                                                                  in0=hst[:, g2 * 16:(g2 + 1) * 16, :].rearrange("p a b -> p (a b)"), in1=p[:], op=ALU.add), [hst, p], [hst])
            k.op("act", lambda e: e.copy(out=hstb[:], in_=hst[:]), [hst], [hstb])
            k.op("pool", lambda e, b=b: e.tensor_tensor(out=xs[b][:], in0=xs[b][:], in1=dsk[:].unsqueeze(2).broadcast_to([128, 32, 64]), op=ALU.mult), [xs[b], dsk], [xs[b]])
            k.op("pool", lambda e, b=b: e.tensor_tensor(out=yy[:], in0=yy[:], in1=xs[b][:], op=ALU.add), [yy, xs[b]], [yy])
            k.op("act", lambda e, b=b: e.activation(out=zz[b][:], in_=zz[b][:], func=AF.Silu), [zz[b]], [zz[b]])
            yyf = lambda: yy[:].rearrange("p a b -> p (a b)")
            k.op("dve", lambda e, b=b: e.tensor_tensor(out=yyf(), in0=yyf(), in1=zz[b][:], op=ALU.mult), [yy, zz[b]], [yy])
            for g in range(4):
                k.op("act", lambda e, g=g: e.activation(out=jk[:], in_=yyf()[:, g * 512:(g + 1) * 512], func=AF.Square, accum_out=ssq[:, g:g + 1]), [yy], [jk, ssq])
            k.op("act", lambda e: e.activation(out=ssq[:], in_=ssq[:], func=AF.Sqrt, scale=1.0 / 512.0, bias=LN_EPS), [ssq], [ssq])
            k.op("dve", lambda e: e.reciprocal(out=ssq[:], in_=ssq[:]), [ssq], [ssq])
            k.op("dve", lambda e: e.tensor_tensor(out=yy[:].rearrange("p (g a) b -> p g (a b)", g=4), in0=yy[:].rearrange("p (g a) b -> p g (a b)", g=4),
                                                  in1=ssq[:].unsqueeze(2).broadcast_to([128, 4, 512]), op=ALU.mult), [yy, ssq], [yy])
            k.op("pool", lambda e: e.tensor_tensor(out=yb[:], in0=yyf(), in1=ng[:], op=ALU.mult), [yy, ng], [yb])
            for h2 in range(2):
                transpose_into(c, lambda h2=h2: yT[:, h2 * 8:(h2 + 1) * 8, :].rearrange("p a b -> p (a b)"), lambda ci, h2=h2: yb[:, (h2 * 8 + ci) * 128:(h2 * 8 + ci + 1) * 128],
                               8, yb, yT, evac=("act" if h2 == 0 else "dve"))
            p = psf(c)
            for nb in range(2):
                for kc in range(16):
                    k.op("pe", lambda e, p=p, nb=nb, kc=kc: e.matmul(out=p[:, nb * 512:(nb + 1) * 512], lhsT=yT[:, kc, :], rhs=wo[:, kc, nb * 512:(nb + 1) * 512],
                                                                    start=(kc == 0), stop=(kc == 15)), [yT, wo], [p])
            tl.run(hf[b], lambda hf_, p=p: p[:, hf_ * 512:(hf_ + 1) * 512], p, h_out, ti)


def phase_ml(c, li, h_in, h_out):
    k = c.k
    T_ = c.T
    NCH = T_ // 128
    tps = NCH
    XM = k.dram("ml_XM", [c.NSEQ, 16, 128, T_], F32)
    OG = k.dram("ml_OG", [c.NT, 2048], F32)
    XCT = k.dram("ml_XCT", [c.NSEQ, 16, 128, T_], BF16)
    XMT = k.dram("ml_XMT", [c.NSEQ, 16, 128, T_], BF16)
    XC = k.dram("ml_XC", [c.NT, 2048], F32)
    QTd = k.dram("ml_QT", [c.NSEQ, 16, 128, T_], BF16)
    KTd = k.dram("ml_KT", [c.NSEQ, 16, 128, T_], BF16)
    KTOK = k.dram("ml_KTOK", [c.NT, 2048], BF16)
    VTOK = k.dram("ml_VTOK", [c.NT, 2048], BF16)
    GI = k.dram("ml_GI", [c.NSEQ, 4, T_], F32)
    GF = k.dram("ml_GF", [c.NSEQ, 4, T_], F32)
    GO = k.dram("ml_GO", [c.NT, 2048], BF16)
    with k.scope():
        w = k.sb([128, 8, 4096], BF16, "w_in")
        load_w(c, w, c.inp["ml_w_in"][0], 1024, 4096)
        hf = [k.sb([128, D], F32, "hf") for _ in range(2)]
        hb = k.sb([128, D], BF16, "hb")
        hT = k.sb([128, 8, 128], BF16, "hT")
        zt = [k.sb([128, 1024], F32, "zt") for _ in range(2)]
        xt = [k.sb([128, 8, 128], F32, "xt") for _ in range(2)]

        def load(ti):
            k.dma("sp", hf[ti % 2][:], h_in[ti * 128:(ti + 1) * 128, :], writes=[hf[ti % 2]])

        load(0)
        ev = 0
        for ti in range(c.NTILES):
            if ti + 1 < c.NTILES:
                load(ti + 1)
            b = ti % 2
            s, tt = ti // tps, ti % tps
            k.op("dve", lambda e, b=b: e.tensor_copy(out=hb[:], in_=hf[b][:]), [hf[b]], [hb])
            transpose_into(c, lambda: hT[:].rearrange("p a b -> p (a b)"), lambda ci: hb[:, ci * 128:(ci + 1) * 128], 8, hb, hT)
            for half in range(2):
                p = psf(c)
                for fc in range(8):
                    col = (half * 8 + fc) * 128
                    for kc in range(8):
                        k.op("pe", lambda e, p=p, fc=fc, col=col, kc=kc: e.matmul(out=p[:, fc * 128:(fc + 1) * 128], lhsT=w[:, kc, col:col + 128], rhs=hT[:, kc, :],
                                                                                 start=(kc == 0), stop=(kc == 7)), [hT, w], [p])
                xb_ = xt[ev % 2]
                k.op("dve", lambda e, p=p, xb_=xb_: e.tensor_copy(out=xb_[:].rearrange("p a b -> p (a b)"), in_=p[:]), [p], [xb_])
                k.dma("sp", XM[s, half * 8:(half + 1) * 8, :, tt * 128:(tt + 1) * 128].rearrange("n p t -> p n t"), xb_[:], reads=[xb_])
                p = psf(c)
                for nb in range(2):
                    n0 = 2048 + half * 1024 + nb * 512
                    for kc in range(8):
                        k.op("pe", lambda e, p=p, nb=nb, n0=n0, kc=kc: e.matmul(out=p[:, nb * 512:(nb + 1) * 512], lhsT=hT[:, kc, :], rhs=w[:, kc, n0:n0 + 512],
                                                                               start=(kc == 0), stop=(kc == 7)), [hT, w], [p])
                zb = zt[ev % 2]
                ev += 1
                k.op("act", lambda e, p=p, zb=zb: e.copy(out=zb[:], in_=p[:]), [p], [zb])
                k.dma("sp", OG[ti * 128:(ti + 1) * 128, half * 1024:(half + 1) * 1024], zb[:], reads=[zb])
    if 'mlA' in DBG:
        return
    with k.scope():
        cwp = k.sb([128, 2048], F32, "cwp")
        k.op("dve", lambda e: e.memset(cwp[:], 0.0), [], [cwp])
        k.dma("sp", cwp[0:4, :], c.inp["ml_conv_w"][0], writes=[cwp])
        k.dma("sp", cwp[4:5, :], c.inp["ml_conv_b"][0:1, :], writes=[cwp])
        cw = k.sb([128, 16, 8], F32, "cw")
        for cch in range(16):
            p = psf(c)
            k.op("pe", lambda e, p=p, cch=cch: e.transpose(out=p[:, 0:128], in_=cwp[:, cch * 128:(cch + 1) * 128], identity=c.identf[:]), [cwp, c.identf], [p])
            k.op("act", lambda e, p=p, cch=cch: e.copy(out=cw[:, cch, :], in_=p[:, 0:8]), [p], [cw])
        xin = [k.sb([128, T_], F32, "xin") for _ in range(2)]
        acc = [k.sb([128, T_], F32, "acc") for _ in range(2)]
        accb = [k.sb([128, T_], BF16, "accb") for _ in range(2)]
        xmb = [k.sb([128, T_], BF16, "xmb") for _ in range(2)]
        stg = [k.sb([128, 4, 128], F32, "stg") for _ in range(2)]
        u = 0
        for s in range(c.NSEQ):
            for cch in range(16):
                ub = u % 2
                u += 1
                xi, ac, abf, xb2 = xin[ub], acc[ub], accb[ub], xmb[ub]
                k.dma("sp", xi[:], XM[s, cch], writes=[xi])
                k.op("pool", lambda e, xi=xi, xb2=xb2: e.tensor_copy(out=xb2[:], in_=xi[:]), [xi], [xb2])
                k.dma("sp", XMT[s, cch], xb2[:], reads=[xb2])
                k.op("dve", lambda e, xi=xi, ac=ac, cch=cch: e.tensor_scalar(out=ac[:], in0=xi[:], scalar1=cw[:, cch, 3:4], scalar2=None, op0=ALU.mult), [xi, cw], [ac])
                for kk in range(3):
                    shf = 3 - kk
                    k.op("dve", lambda e, xi=xi, ac=ac, cch=cch, kk=kk, shf=shf: e.scalar_tensor_tensor(out=ac[:, shf:T_], in0=xi[:, 0:T_ - shf], scalar=cw[:, cch, kk:kk + 1],
                                                                                                  in1=ac[:, shf:T_], op0=ALU.mult, op1=ALU.add), [xi, cw, ac], [ac])
                k.op("act", lambda e, ac=ac, cch=cch: e.activation(out=ac[:], in_=ac[:], func=AF.Silu, bias=cw[:, cch, 4:5], scale=1.0), [ac, cw], [ac])
                k.op("pool", lambda e, ac=ac, abf=abf: e.tensor_copy(out=abf[:], in_=ac[:]), [ac], [abf])
                k.dma("sp", XCT[s, cch], abf[:], reads=[abf])
                for n4 in range(T_ // 512):
                    p = psf(c)
                    for q4 in range(4):
                        t0 = n4 * 512 + q4 * 128
                        k.op("pe", lambda e, p=p, q4=q4, t0=t0, ac=ac: e.transpose(out=p[:, q4 * 128:(q4 + 1) * 128], in_=ac[:, t0:t0 + 128], identity=c.identf[:]),
                             [ac, c.identf], [p])
                    sg_ = stg[n4 % 2]
                    k.op("act", lambda e, p=p, sg_=sg_: e.copy(out=sg_[:].rearrange("p a b -> p (a b)"), in_=p[:, 0:512]), [p], [sg_])
                    k.dma("sp", XC[s * T_ + n4 * 512:s * T_ + (n4 + 1) * 512, cch * 128:(cch + 1) * 128].rearrange("(n p) c -> p n c", p=128), sg_[:], reads=[sg_])
    if 'mlB' in DBG:
        return
    with k.scope():
        wq = k.sb([128, 16, 512], BF16, "wq")
        wk = k.sb([128, 16, 512], BF16, "wk")
        wv = k.sb([128, 16, 512], BF16, "wv")
        load_w(c, wq, c.inp["ml_w_q"][0].rearrange("h d e -> (h d) e"), 2048, 512)
        load_w(c, wk, c.inp["ml_w_k"][0].rearrange("h d e -> (h d) e"), 2048, 512)
        load_w(c, wv, c.inp["ml_w_v"][0].rearrange("h d e -> (h d) e"), 2048, 512)
        wgs = k.sb([128, 48, 8], F32, "wgs")
        for gs in range(3):
            k.dma("sp", wgs[:, gs * 16:(gs + 1) * 16, :], c.inp["ml_w_gates"][0, gs].rearrange("(c p) n -> p c n", p=128), writes=[wgs])
        wgp = k.sb([128, 48, 128], BF16, "wgp")
        k.op("dve", lambda e: e.memset(wgp[:], 0.0), [], [wgp])
        k.op("dve", lambda e: e.tensor_copy(out=wgp[:, :, 0:4], in_=wgs[:, :, 0:4]), [wgs], [wgp])
        k.op("dve", lambda e: e.tensor_copy(out=wgp[:, :, 32:36], in_=wgs[:, :, 4:8]), [wgs], [wgp])
        bcol = k.sb([128, 1], F32, "bcol")
        nbcol = k.sb([128, 1], F32, "nbcol")
        k.op("dve", lambda e: e.memset(bcol[:], 0.0), [], [bcol])
        k.dma("sp", bcol[0:4, :], c.inp["ml_b_gates"][0:1, 0:4].rearrange("o n -> n o"), writes=[bcol])
        k.dma("sp", bcol[32:36, :], c.inp["ml_b_gates"][0:1, 4:8].rearrange("o n -> n o"), writes=[bcol])
        k.op("dve", lambda e: e.tensor_scalar(out=nbcol[:], in0=bcol[:], scalar1=-1.0, scalar2=None, op0=ALU.mult), [bcol], [nbcol])
        xcT = [k.sb([128, 16, 128], BF16, "xcT") for _ in range(2)]
        xmT = [k.sb([128, 16, 128], BF16, "xmT") for _ in range(2)]
        qT = [k.sb([128, 16, 128], BF16, "qT") for _ in range(2)]
        kT = [k.sb([128, 16, 128], BF16, "kT") for _ in range(2)]
        kt = [k.sb([128, 2048], BF16, "kt") for _ in range(2)]
        vt = [k.sb([128, 2048], BF16, "vt") for _ in range(2)]
        vT = k.sb([128, 16, 128], BF16, "vT")
        gt = [k.sb([128, 128], F32, "gt") for _ in range(2)]
        lf = [k.sb([128, 128], F32, "lf") for _ in range(2)]
        KS = 512.0 ** -0.5

        def load(ti):
            s, tt = ti // tps, ti % tps
            k.dma("sp", xcT[ti % 2][:], XCT[s, :, :, tt * 128:(tt + 1) * 128].rearrange("n p t -> p n t"), writes=[xcT[ti % 2]])
            k.dma("sp", xmT[ti % 2][:], XMT[s, :, :, tt * 128:(tt + 1) * 128].rearrange("n p t -> p n t"), writes=[xmT[ti % 2]])

        load(0)
        for ti in range(c.NTILES):
            if ti + 1 < c.NTILES:
                load(ti + 1)
            b = ti % 2
            s, tt = ti // tps, ti % tps
            for which in range(2):
                wsrc = wq if which == 0 else wk
                dstT = qT[b] if which == 0 else kT[b]
                for half in range(2):
                    p = psf(c)
                    for oc in range(8):
                        o16 = half * 8 + oc
                        hh, ec = o16 // 4, o16 % 4
                        for dc in range(4):
                            k.op("pe", lambda e, p=p, oc=oc, hh=hh, ec=ec, dc=dc, wsrc=wsrc, b=b: e.matmul(
                                out=p[:, oc * 128:(oc + 1) * 128], lhsT=wsrc[:, hh * 4 + dc, ec * 128:(ec + 1) * 128], rhs=xcT[b][:, hh * 4 + dc, :],
                                start=(dc == 0), stop=(dc == 3)), [wsrc, xcT[b]], [p])
                    if which == 0:
                        k.op("act", lambda e, p=p, dstT=dstT, half=half: e.copy(out=dstT[:, half * 8:(half + 1) * 8, :].rearrange("p a b -> p (a b)"), in_=p[:]), [p], [dstT])
                    else:
                        k.op("act", lambda e, p=p, dstT=dstT, half=half: e.activation(out=dstT[:, half * 8:(half + 1) * 8, :].rearrange("p a b -> p (a b)"), in_=p[:],
                                                                                      func=AF.Copy, scale=KS), [p], [dstT])
                dd = QTd if which == 0 else KTd
                k.dma("sp", dd[s, :, :, tt * 128:(tt + 1) * 128].rearrange("n p t -> p n t"), dstT[:], reads=[dstT])
            for which in range(2):
                wsrc = wk if which == 0 else wv
                src = xcT[b] if which == 0 else xmT[b]
                dst = kt[b] if which == 0 else vt[b]
                for half in range(2):
                    p = psf(c)
                    for q2 in range(2):
                        hh = half * 2 + q2
                        for dc in range(4):
                            k.op("pe", lambda e, p=p, q2=q2, hh=hh, dc=dc, wsrc=wsrc, src=src: e.matmul(
                                out=p[:, q2 * 512:(q2 + 1) * 512], lhsT=src[:, hh * 4 + dc, :], rhs=wsrc[:, hh * 4 + dc, :], start=(dc == 0), stop=(dc == 3)), [wsrc, src], [p])
                    if which == 0:
                        k.op("dve", lambda e, p=p, dst=dst, half=half: e.tensor_scalar(out=dst[:, half * 1024:(half + 1) * 1024], in0=p[:], scalar1=KS, scalar2=None, op0=ALU.mult),
                             [p], [dst])
                    else:
                        k.op("dve", lambda e, p=p, dst=dst, half=half: e.tensor_copy(out=dst[:, half * 1024:(half + 1) * 1024], in_=p[:]), [p], [dst])
                dd = KTOK if which == 0 else VTOK
                k.dma("sp", dd[ti * 128:(ti + 1) * 128, :], dst[:], reads=[dst])
            for h2 in range(2):
                transpose_into(c, lambda h2=h2: vT[:, h2 * 8:(h2 + 1) * 8, :].rearrange("p a b -> p (a b)"), lambda ci, h2=h2, b=b: vt[b][:, (h2 * 8 + ci) * 128:(h2 * 8 + ci + 1) * 128],
                               8, vt[b], vT, evac=("act" if h2 == 0 else "dve"))
            p = psf(c)
            n_mm = 0
            for gs, srcT in enumerate((qT[b], kT[b], vT)):
                for fc in range(16):
                    k.op("pe", lambda e, p=p, gs=gs, fc=fc, srcT=srcT, n_mm=n_mm: e.matmul(out=p[:, 0:128], lhsT=wgp[:, gs * 16 + fc, :], rhs=srcT[:, fc, :],
                                                                                       start=(n_mm == 0), stop=(n_mm == 47)), [wgp, srcT], [p])
                    n_mm += 1
            k.op("act", lambda e, p=p, b=b: e.activation(out=gt[b][:], in_=p[:, 0:128], func=AF.Identity, bias=bcol[:, 0:1], scale=1.0), [p, bcol], [gt[b]])
            k.op("act", lambda e, p=p, b=b: e.activation(out=lf[b][:], in_=p[:, 0:128], func=AF.Exp, bias=nbcol[:, 0:1], scale=-1.0), [p, nbcol], [lf[b]])
            k.op("act", lambda e, b=b: e.activation(out=lf[b][:], in_=lf[b][:], func=AF.Ln, bias=1.0, scale=1.0), [lf[b]], [lf[b]])
            k.op("dve", lambda e, b=b: e.tensor_scalar(out=lf[b][:], in0=lf[b][:], scalar1=-1.0, scalar2=None, op0=ALU.mult), [lf[b]], [lf[b]])
            k.dma("sp", GI[s, :, tt * 128:(tt + 1) * 128], gt[b][0:4, :], reads=[gt[b]])
            k.dma("sp", GF[s, :, tt * 128:(tt + 1) * 128], lf[b][32:36, :], reads=[lf[b]])
    if 'mlC' in DBG:
        return
    with k.scope():
        sel = k.sb([128, 4, 128], F32, "sel")
        k.dma("sp", sel[:], c.inp["c_sel"][:, :, :], writes=[sel])
        negm = k.sb([128, 128], F32, "negm")
        k.dma("sp", negm[:], c.inp["c_Lst"][:, :], writes=[negm])
        k.op("dve", lambda e: e.tensor_scalar(out=negm[:], in0=negm[:], scalar1=NEG, scalar2=None, op0=ALU.mult), [negm], [negm])
        ones2 = k.sb([128, 2], BF16, "ones2")
        k.op("dve", lambda e: e.memset(ones2[:], 1.0), [], [ones2])
        ones512 = k.sb([128, 512], F32, "ones512")
        k.op("dve", lambda e: e.memset(ones512[:], 1.0), [], [ones512])
        ngt = k.sb([128, 2048], F32, "ngt")
        skt = k.sb([128, 2048], F32, "skt")
        load_bc(c, ngt, c.inp["ml_norm_g"][0:1, :])
        load_bc(c, skt, c.inp["ml_skip"][0:1, :])
        WROW = k.sb([128, T_], F32, "WROW")
        NCM = k.sb([128, T_], F32, "NCM")
        MROW = k.sb([128, T_], F32, "MROW")
        PM = k.sb([128, 4, NCH + 1], F32, "PM")
        NM = k.sb([128, 4, NCH + 1], F32, "NM")
        Cst = k.sb([128, 4, 4, 512], F32, "Cst")
        Cb = k.sb([128, 4, 4, 512], BF16, "Cb")
        nst = k.sb([128, 4, 4, 2], F32, "nst")
        nstb = k.sb([128, 4, 4, 2], BF16, "nstb")
        qTt = [k.sb([128, 16, 128], BF16, "qTt") for _ in range(2)]
        kTt = [k.sb([128, 16, 128], BF16, "kTt") for _ in range(2)]
        ktt = [k.sb([128, 2048], BF16, "ktt") for _ in range(2)]
        vtt = [k.sb([128, 2048], BF16, "vtt") for _ in range(2)]
        ogt = k.sb([128, 2048], F32, "ogt")
        xct = k.sb([128, 2048], F32, "xct")
        cols = k.sb([128, 3, 4], F32, "cols")
        ET = k.sb([128, 128], F32, "ET")
        ST = k.sb([128, 128], BF16, "ST")
        sc = k.sb([128, 1], F32, "sc")
        wkc = k.sb([128, 1], F32, "wkc")
        cd = k.sb([128, 1], F32, "cd")
        em = k.sb([128, 1], F32, "em")
        den = k.sb([128, 2], F32, "den")
        tmp512 = k.sb([128, 512], F32, "tmp512")
        num = k.sb([128, 512], F32, "num")
        kw = k.sb([128, 512], BF16, "kw")
        st6 = k.sb([128, 6], F32, "st6")
        mv2 = k.sb([128, 2], F32, "mv2")
        rs1 = k.sb([128, 1], F32, "rs1")
        outb = [k.sb([128, 2048], BF16, "outb") for _ in range(2)]

        def load(ti):
            s, tt = ti // tps, ti % tps
            b = ti % 2
            k.dma("sp", qTt[b][:], QTd[s, :, :, tt * 128:(tt + 1) * 128].rearrange("n p t -> p n t"), writes=[qTt[b]])
            k.dma("sp", kTt[b][:], KTd[s, :, :, tt * 128:(tt + 1) * 128].rearrange("n p t -> p n t"), writes=[kTt[b]])
            k.dma("sp", ktt[b][:], KTOK[ti * 128:(ti + 1) * 128, :], writes=[ktt[b]])
            k.dma("sp", vtt[b][:], VTOK[ti * 128:(ti + 1) * 128, :], writes=[vtt[b]])

        for s in range(c.NSEQ):
            with k.scope():
                F2 = k.sb([128, T_], F32, "F2")
                if s == 0:
                    k.op("dve", lambda e: e.memset(WROW[:], 0.0), [], [WROW])
                    k.op("pool", lambda e: e.memset(NCM[:], 0.0), [], [NCM])
                    k.op("pool", lambda e: e.memset(MROW[:], 0.0), [], [MROW])
                k.dma("sp", MROW[0:4, :], GF[s], writes=[MROW])
                k.dma("sp", WROW[0:4, :], GI[s], writes=[WROW])
                for blk in range(T_ // 512):
                    sl = slice(blk * 512, (blk + 1) * 512)
                    init = 0.0 if blk == 0 else F2[0:4, blk * 512 - 1:blk * 512]
                    k.op("dve", lambda e, sl=sl, init=init: e.tensor_tensor_scan(out=F2[0:4, sl], data0=ones512[0:4, :], data1=MROW[0:4, sl], initial=init,
                                                                                op0=ALU.mult, op1=ALU.add), [ones512, MROW, F2], [F2])
                k.op("dve", lambda e: e.tensor_tensor(out=WROW[0:4, :], in0=WROW[0:4, :], in1=F2[0:4, :], op=ALU.subtract), [WROW, F2], [WROW])
                for blk in range(T_ // 512):
                    sl = slice(blk * 512, (blk + 1) * 512)
                    init = 0.0 if blk == 0 else NCM[0:4, blk * 512 - 1:blk * 512]
                    k.op("dve", lambda e, sl=sl, init=init: e.tensor_tensor_scan(out=NCM[0:4, sl], data0=ones512[0:4, :], data1=WROW[0:4, sl], initial=init,
                                                                                op0=ALU.mult, op1=ALU.max), [ones512, WROW, NCM], [NCM])
                k.op("dve", lambda e: e.tensor_tensor(out=MROW[0:4, :], in0=F2[0:4, :], in1=NCM[0:4, :], op=ALU.add), [F2, NCM], [MROW])
                k.op("dve", lambda e: e.tensor_scalar(out=NCM[0:4, :], in0=NCM[0:4, :], scalar1=-1.0, scalar2=None, op0=ALU.mult), [NCM], [NCM])
                k.op("dve", lambda e: e.memset(NM[:], 0.0), [], [NM])
                for hh in range(4):
                    p = psf(c)
                    k.op("pe", lambda e, p=p, hh=hh: e.matmul(out=p[:, 0:NCH], lhsT=sel[:, hh, :], rhs=NCM[:, 127::128], start=True, stop=True), [sel, NCM], [p])
                    k.op("dve", lambda e, p=p, hh=hh: e.tensor_copy(out=NM[:, hh, 1:NCH + 1], in_=p[:, 0:NCH]), [p], [NM])
                k.op("dve", lambda e: e.tensor_scalar(out=PM[:], in0=NM[:], scalar1=-1.0, scalar2=None, op0=ALU.mult), [NM], [PM])
            if 'mlD' in DBG:
                return
            k.op("dve", lambda e: e.memset(Cst[:], 0.0), [], [Cst])
            k.op("pool", lambda e: e.memset(Cb[:], 0.0), [], [Cb])
            k.op("dve", lambda e: e.memset(nst[:], 0.0), [], [nst])
            k.op("dve", lambda e: e.memset(nstb[:], 0.0), [], [nstb])
            load(s * tps)
            for n in range(NCH):
                ti = s * tps + n
                if n + 1 < NCH:
                    load(ti + 1)
                b = ti % 2
                t0 = n * 128
                k.dma("sp", ogt[:], OG[ti * 128:(ti + 1) * 128, :], writes=[ogt])
                k.dma("sp", xct[:], XC[ti * 128:(ti + 1) * 128, :], writes=[xct])
                k.op("act", lambda e: e.activation(out=ogt[:], in_=ogt[:], func=AF.Sigmoid), [ogt], [ogt])
                p = psf(c)
                for a_, src in enumerate((WROW, NCM, MROW)):
                    k.op("pe", lambda e, p=p, a_=a_, src=src, t0=t0: e.transpose(out=p[:, a_ * 128:(a_ + 1) * 128], in_=src[:, t0:t0 + 128], identity=c.identf[:]),
                         [src, c.identf], [p])
                k.op("act", lambda e, p=p: e.copy(out=cols[:], in_=p[:, 0:384].rearrange("p (a b) -> p a b", a=3)[:, :, 0:4]), [p], [cols])
                for hh in range(4):
                    hs = slice(hh * 512, (hh + 1) * 512)
                    p = psf(c)
                    for dc in range(4):
                        k.op("pe", lambda e, p=p, hh=hh, dc=dc, b=b: e.matmul(out=p[:, 0:128], lhsT=kTt[b][:, hh * 4 + dc, :], rhs=qTt[b][:, hh * 4 + dc, :],
                                                                             start=(dc == 0), stop=(dc == 3)), [kTt[b], qTt[b]], [p])
                    k.op("pe", lambda e, p=p, hh=hh, t0=t0: e.matmul(out=p[:, 512:640], lhsT=WROW[:, t0:t0 + 128], rhs=sel[:, hh, :], start=True, stop=False), [WROW, sel], [p])
                    k.op("pe", lambda e, p=p, hh=hh, t0=t0: e.matmul(out=p[:, 512:640], lhsT=sel[:, hh, :], rhs=NCM[:, t0:t0 + 128], start=False, stop=False), [NCM, sel], [p])
                    k.op("pe", lambda e, p=p: e.matmul(out=p[:, 512:640], lhsT=c.identf[:], rhs=negm[:], start=False, stop=True), [c.identf, negm], [p])
                    k.op("act", lambda e, p=p: e.activation(out=ET[:], in_=p[:, 512:640], func=AF.Exp), [p], [ET])
                    k.op("dve", lambda e, p=p: e.tensor_tensor(out=ST[:], in0=ET[:], in1=p[:, 0:128], op=ALU.mult), [ET, p], [ST])
                    p1 = psf(c)
                    k.op("pe", lambda e, p1=p1, hs=hs, b=b: e.matmul(out=p1[:, 0:512], lhsT=ST[:], rhs=vtt[b][:, hs], start=True, stop=True), [ST, vtt[b]], [p1])
                    for dc in range(4):
                        k.op("pe", lambda e, p1=p1, hh=hh, dc=dc, b=b: e.matmul(out=p1[:, 512:1024], lhsT=qTt[b][:, hh * 4 + dc, :], rhs=Cb[:, hh, dc, :],
                                                                               start=(dc == 0), stop=(dc == 3)), [qTt[b], Cb], [p1])
                    p2 = psf(c)
                    k.op("pe", lambda e, p2=p2: e.matmul(out=p2[:, 0:2], lhsT=ST[:], rhs=ones2[:], start=True, stop=True), [ST, ones2], [p2])
                    for dc in range(4):
                        k.op("pe", lambda e, p2=p2, hh=hh, dc=dc, b=b: e.matmul(out=p2[:, 512:514], lhsT=qTt[b][:, hh * 4 + dc, :], rhs=nstb[:, hh, dc, :],
                                                                               start=(dc == 0), stop=(dc == 3)), [qTt[b], nstb], [p2])
                    k.op("act", lambda e, hh=hh, n=n: e.activation(out=sc[:], in_=cols[:, 1, hh:hh + 1], func=AF.Exp, bias=PM[:, hh, n:n + 1], scale=1.0), [cols, PM], [sc])
                    k.op("act", lambda e, p1=p1: e.activation(out=tmp512[:], in_=p1[:, 512:1024], func=AF.Copy, scale=sc[:, 0:1]), [p1, sc], [tmp512])
                    k.op("dve", lambda e, p1=p1: e.tensor_tensor(out=num[:], in0=tmp512[:], in1=p1[:, 0:512], op=ALU.add), [tmp512, p1], [num])
                    k.op("dve", lambda e, p2=p2: e.tensor_scalar(out=den[:, 0:1], in0=p2[:, 512:513], scalar1=sc[:, 0:1], scalar2=None, op0=ALU.mult), [p2, sc], [den])
                    k.op("dve", lambda e, p2=p2: e.tensor_tensor(out=den[:, 0:1], in0=den[:, 0:1], in1=p2[:, 0:1], op=ALU.add), [den, p2], [den])
                    k.op("act", lambda e: e.activation(out=den[:, 0:1], in_=den[:, 0:1], func=AF.Abs), [den], [den])
                    k.op("act", lambda e, hh=hh: e.activation(out=em[:], in_=cols[:, 2, hh:hh + 1], func=AF.Exp, scale=-1.0), [cols], [em])
                    k.op("dve", lambda e: e.tensor_tensor(out=den[:, 0:1], in0=den[:, 0:1], in1=em[:], op=ALU.max), [den, em], [den])
                    k.op("dve", lambda e: e.reciprocal(out=den[:, 0:1], in_=den[:, 0:1]), [den], [den])
                    k.op("dve", lambda e: e.tensor_scalar(out=num[:], in0=num[:], scalar1=den[:, 0:1], scalar2=None, op0=ALU.mult), [num, den], [num])
                    k.op("dve", lambda e: e.bn_stats(out=st6[:], in_=num[:]), [num], [st6])
                    k.op("dve", lambda e: e.bn_aggr(out=mv2[:], in_=st6[:]), [st6], [mv2])
                    k.op("act", lambda e: e.activation(out=rs1[:], in_=mv2[:, 1:2], func=AF.Sqrt, bias=LN_EPS, scale=1.0), [mv2], [rs1])
                    k.op("dve", lambda e: e.reciprocal(out=rs1[:], in_=rs1[:]), [rs1], [rs1])
                    k.op("dve", lambda e: e.tensor_scalar(out=num[:], in0=num[:], scalar1=mv2[:, 0:1], scalar2=rs1[:, 0:1], op0=ALU.subtract, op1=ALU.mult), [num, mv2, rs1], [num])
                    k.op("pool", lambda e, hs=hs: e.tensor_tensor(out=num[:], in0=num[:], in1=ngt[:, hs], op=ALU.mult), [num, ngt], [num])
                    k.op("pool", lambda e, hs=hs: e.tensor_tensor(out=tmp512[:], in0=xct[:, hs], in1=skt[:, hs], op=ALU.mult), [xct, skt], [tmp512])
                    k.op("pool", lambda e: e.tensor_tensor(out=num[:], in0=num[:], in1=tmp512[:], op=ALU.add), [num, tmp512], [num])
                    k.op("dve", lambda e, hs=hs, b=b: e.tensor_tensor(out=outb[b][:, hs], in0=num[:], in1=ogt[:, hs], op=ALU.mult), [num, ogt], [outb[b]])
                    k.op("act", lambda e, hh=hh, n=n: e.activation(out=wkc[:], in_=cols[:, 0, hh:hh + 1], func=AF.Exp, bias=NM[:, hh, n + 1:n + 2], scale=1.0), [cols, NM], [wkc])
                    k.op("act", lambda e, hh=hh, n=n: e.activation(out=cd[:], in_=PM[:, hh, n:n + 1], func=AF.Exp, bias=NM[:, hh, n + 1:n + 2], scale=1.0), [PM, NM], [cd])
                    k.op("dve", lambda e, hs=hs, b=b: e.tensor_scalar(out=kw[:], in0=ktt[b][:, hs], scalar1=wkc[:, 0:1], scalar2=None, op0=ALU.mult), [ktt[b], wkc], [kw])
                    for d2 in range(2):
                        p3 = psf(c)
                        for q2 in range(2):
                            dkc = d2 * 2 + q2
                            k.op("pe", lambda e, p3=p3, q2=q2, dkc=dkc, hs=hs, b=b: e.matmul(out=p3[:, q2 * 512:(q2 + 1) * 512], lhsT=kw[:, dkc * 128:(dkc + 1) * 128],
                                                                                          rhs=vtt[b][:, hs], start=True, stop=True), [kw, vtt[b]], [p3])
                        k.op("dve", lambda e, p3=p3, d2=d2, hh=hh: e.scalar_tensor_tensor(out=Cst[:, hh, d2 * 2:(d2 + 1) * 2, :].rearrange("p a b -> p (a b)"),
                                                                                        in0=Cst[:, hh, d2 * 2:(d2 + 1) * 2, :].rearrange("p a b -> p (a b)"), scalar=cd[:, 0:1],
                                                                                        in1=p3[:], op0=ALU.mult, op1=ALU.add), [Cst, cd, p3], [Cst])
                    k.op("act", lambda e, hh=hh: e.copy(out=Cb[:, hh, :, :], in_=Cst[:, hh, :, :]), [Cst], [Cb])
                    p4 = psf(c)
                    for dkc in range(4):
                        k.op("pe", lambda e, p4=p4, dkc=dkc: e.matmul(out=p4[:, dkc * 2:dkc * 2 + 2], lhsT=kw[:, dkc * 128:(dkc + 1) * 128], rhs=ones2[:], start=True, stop=True),
                             [kw, ones2], [p4])
                    k.op("dve", lambda e, p4=p4, hh=hh: e.scalar_tensor_tensor(out=nst[:, hh, :, :].rearrange("p a b -> p (a b)"), in0=nst[:, hh, :, :].rearrange("p a b -> p (a b)"),
                                                                               scalar=cd[:, 0:1], in1=p4[:, 0:8], op0=ALU.mult, op1=ALU.add), [nst, cd, p4], [nst])
                    k.op("dve", lambda e, hh=hh: e.tensor_copy(out=nstb[:, hh, :, :], in_=nst[:, hh, :, :]), [nst], [nstb])
                k.dma("sp", GO[ti * 128:(ti + 1) * 128, :], outb[b][:], reads=[outb[b]])
    if 'mlE' in DBG:
        return
    with k.scope():
        wd = k.sb([128, 16, 1024], BF16, "w_down")
        load_w(c, wd, c.inp["ml_w_down"][0], 2048, 1024)
        tl = Tail(c, li, 0)
        hf = [k.sb([128, D], F32, "hf") for _ in range(2)]
        ab = [k.sb([128, 2048], BF16, "ab") for _ in range(2)]
        aT = [k.sb([128, 16, 128], BF16, "aT") for _ in range(2)]

        def load2(ti):
            k.dma("sp", hf[ti % 2][:], h_in[ti * 128:(ti + 1) * 128, :], writes=[hf[ti % 2]])
            k.dma("sp", ab[ti % 2][:], GO[ti * 128:(ti + 1) * 128, :], writes=[ab[ti % 2]])

        load2(0)
        for ti in range(c.NTILES):
            if ti + 1 < c.NTILES:
                load2(ti + 1)
            b = ti % 2
            for h2 in range(2):
                transpose_into(c, lambda h2=h2, b=b: aT[b][:, h2 * 8:(h2 + 1) * 8, :].rearrange("p a b -> p (a b)"),
                               lambda ci, h2=h2, b=b: ab[b][:, (h2 * 8 + ci) * 128:(h2 * 8 + ci + 1) * 128], 8, ab[b], aT[b], evac=("act" if h2 == 0 else "dve"))
            p = psf(c)
            for nb in range(2):
                for kc in range(16):
                    k.op("pe", lambda e, p=p, nb=nb, kc=kc, b=b: e.matmul(out=p[:, nb * 512:(nb + 1) * 512], lhsT=aT[b][:, kc, :], rhs=wd[:, kc, nb * 512:(nb + 1) * 512],
                                                                         start=(kc == 0), stop=(kc == 15)), [aT[b], wd], [p])
            tl.run(hf[b], lambda hf_, p=p: p[:, hf_ * 512:(hf_ + 1) * 512], p, h_out, ti)


def build(T_, NSEQ, plan, needed):
    c = setup(T_, NSEQ, needed)
    k = c.k
    cur = c.inp["x"]
    bufs = [c.hA, c.hB]
    bi = 0
    for pi, (kind, li) in enumerate(plan):
        dst = c.out if pi == len(plan) - 1 else bufs[bi]
        if kind == "xa":
            phase_xa(c, li, cur, dst)
        elif kind == "peer":
            phase_peer(c, li, cur, dst)
        elif kind == "s5":
            phase_s5(c, li, cur, dst)
        elif kind == "da":
            phase_da(c, li, cur, dst)
        elif kind == "m2":
            phase_m2(c, li, cur, dst)
        elif kind == "ml":
            phase_ml(c, li, cur, dst)
        cur = dst
        bi ^= 1
    return c, k.finish()


N_CORES = 8
SEQ_FULL = 4096
PLAN = []
for _i, _mx in enumerate(["s5", "da", "m2", "ml"]):
    PLAN += [(_mx, _i), ("xa", _i), ("peer", _i)]


def kernel(**inputs):
    nseq = 16 // N_CORES
    needed = set(n for n, _ in INPUT_SPECS)
    c, nc = build(SEQ_FULL, nseq, PLAN, needed)
    consts = host_consts(SEQ_FULL)
    x = np.ascontiguousarray(np.asarray(inputs["x"], dtype=np.float32))
    mem = np.ascontiguousarray(np.asarray(inputs["mem"], dtype=np.float32))
    shared = {}
    for name in c.inp:
        if name in ("x", "mem"):
            continue
        if name.startswith("c_"):
            shared[name] = consts[name]
        else:
            shared[name] = np.ascontiguousarray(np.asarray(inputs[name], dtype=np.float32))
    in_maps = []
    for ci in range(N_CORES):
        m = dict(shared)
        m["x"] = x[ci * nseq:(ci + 1) * nseq].reshape(nseq * SEQ_FULL, D)
        m["mem"] = mem[ci * nseq:(ci + 1) * nseq].reshape(nseq * 256, D)
        in_maps.append(m)
    res = run_bass_kernel_spmd(nc, in_maps, core_ids=list(range(N_CORES)))
    outs = [np.asarray(r["out"]).reshape(nseq, SEQ_FULL, D) for r in res.results]
    return np.concatenate(outs, axis=0).astype(np.float32)
```

```python
import numpy as np
import ml_dtypes
from contextlib import ExitStack
import concourse.bass as bass
import concourse.mybir as mybir
from concourse.bass_utils import run_bass_kernel_spmd

F32 = mybir.dt.float32
BF16 = mybir.dt.bfloat16
I32 = mybir.dt.int32
U32 = mybir.dt.uint32
AF = mybir.ActivationFunctionType
ALU = mybir.AluOpType
AX = mybir.AxisListType

ENGS = ("pe", "dve", "act", "pool", "sp")
NDSEM = 8


class Res:
    __slots__ = ("name", "w", "r")

    def __init__(self, name):
        self.name = name
        self.w = None
        self.r = {}


class T:
    __slots__ = ("t", "res")

    def __init__(self, t, res):
        self.t = t
        self.res = res

    def __getitem__(self, key):
        return self.t[key]

    def ap(self):
        return self.t.ap()


class KB:
    def __init__(self):
        self.nc = bass.Bass("TRN2", target_bir_lowering=False)
        self.stack = ExitStack()
        self.stacks = [self.stack]
        self.q = {e: [] for e in ENGS}
        self.cnt = {e: 0 for e in ENGS}
        self.waited = {e: {} for e in ENGS}
        self.sems = {}
        for e in ENGS:
            self.sems[e] = self.stack.enter_context(self.nc.semaphore("s_" + e))
        self.dsem = {}
        self.dval = {}
        self.dnext = {}
        for qn in ("sp", "pool", "act"):
            self.dsem[qn] = [self.stack.enter_context(self.nc.semaphore("d_%s%d" % (qn, i))) for i in range(NDSEM)]
            self.dval[qn] = [0] * NDSEM
            self.dnext[qn] = 0
        self.n_ins = 0
        self.uid = 0
        nc = self.nc
        self.eng = {"pe": nc.tensor, "dve": nc.vector, "act": nc.scalar, "pool": nc.gpsimd, "sp": nc.sync}

    def sb(self, shape, dtype, name=None):
        self.uid += 1
        name = "%s_%d" % (name or "sb", self.uid)
        t = self.stacks[-1].enter_context(self.nc.sbuf_tensor(name, list(shape), dtype))
        return T(t, Res(name))

    def ps(self, shape, dtype, name=None):
        self.uid += 1
        name = "%s_%d" % (name or "ps", self.uid)
        t = self.stacks[-1].enter_context(self.nc.psum_tensor(name, list(shape), dtype))
        return T(t, Res(name))

    def dram(self, name, shape, dtype, kind="Internal"):
        t = self.nc.dram_tensor(name, list(shape), dtype, kind=kind)
        return T(t, Res(name))

    def _wait(self, e, tok):
        key, val = tok
        if self.waited[e].get(key, 0) >= val:
            return
        self.waited[e][key] = val
        sem = self._sem(key)
        self.eng[e].wait_ge(sem, val)

    def _sem(self, key):
        if isinstance(key, str):
            return self.sems[key]
        return self.dsem[key[0]][key[1]]

    def _deps(self, e, reads, writes, skip_self=False):
        toks = []
        for r in reads:
            if r.w is not None:
                toks.append(r.w)
        for w in writes:
            if w.w is not None:
                toks.append(w.w)
            toks.extend(w.r.items())
        for tok in toks:
            if skip_self and tok[0] == e:
                continue
            self._wait(e, tok)

    def _commit(self, tok, reads, writes):
        for r in reads:
            r.r[tok[0]] = tok[1]
        for w in writes:
            w.w = tok
            w.r = {}

    @staticmethod
    def _res(lst):
        out = []
        for x in lst:
            if x is None:
                continue
            out.append(x.res if isinstance(x, T) else x)
        return out

    def op(self, e, fn, reads=(), writes=()):
        reads = self._res(reads)
        writes = self._res(writes)
        self._deps(e, reads, writes, skip_self=(e == "pe"))
        self.cnt[e] += 1
        tok = (e, self.cnt[e])
        sem = self.sems[e]
        fn(self.eng[e]).then_inc(sem, 1)
        self._commit(tok, reads, writes)
        self.n_ins += 1

    def dma(self, qn, out, in_, reads=(), writes=(), **kw):
        reads = self._res(reads)
        writes = self._res(writes)
        i = self.dnext[qn]
        self.dnext[qn] = (i + 1) % NDSEM
        key = (qn, i)
        if self.dval[qn][i] > 0:
            self._wait(qn, (key, self.dval[qn][i]))
        self._deps(qn, reads, writes)
        self.dval[qn][i] += 16
        tok = (key, self.dval[qn][i])
        sem = self.dsem[qn][i]
        self.eng[qn].dma_start(out=out, in_=in_, **kw).then_inc(sem, 16)
        self._commit(tok, reads, writes)
        self.n_ins += 1

    def barrier(self):
        for e in ENGS:
            for qn in self.dsem:
                for i in range(NDSEM):
                    if self.dval[qn][i] > 0:
                        self._wait(e, ((qn, i), self.dval[qn][i]))
            for e2 in ENGS:
                if e2 != e and self.cnt[e2] > 0:
                    self._wait(e, (e2, self.cnt[e2]))

    def scope(self):
        kb = self

        class _S:
            def __enter__(s2):
                kb.stacks.append(ExitStack())

            def __exit__(s2, *a):
                kb.barrier()
                kb.stacks.pop().close()
                return False
        return _S()

    def finish(self, final_res=()):
        for qn in self.dsem:
            for i in range(NDSEM):
                if self.dval[qn][i] > 0:
                    self._wait("sp", ((qn, i), self.dval[qn][i]))
        for e in ENGS:
            if e != "sp" and self.cnt[e] > 0:
                self._wait("sp", (e, self.cnt[e]))
        self.stack.close()
        return self.nc

import os
DBG = os.environ.get('KDBG', '')

D = 1024
ALPHA = 8 ** 0.25
LN_EPS = 1e-5
NEG = -30000.0

INPUT_SPECS = [
    ("x", None), ("mem", None),
    ("s5_lam_re", (1, 64, 64)), ("s5_lam_im", (1, 64, 64)), ("s5_log_dt", (1, 64)),
    ("s5_b_re", (1, 64, 64, 16)), ("s5_b_im", (1, 64, 64, 16)), ("s5_c_re", (1, 64, 16, 64)), ("s5_c_im", (1, 64, 16, 64)),
    ("s5_d", (1, 1024)), ("s5_w_glu", (1, 1024, 2048)), ("s5_b_glu", (1, 2048)),
    ("da_w_qkv", (1, 1024, 3072)), ("da_lambda", (1, 4, 64)), ("da_subln_g", (1, 128)), ("da_w_o", (1, 1024, 1024)),
    ("m2_w_in", (1, 1024, 5152)), ("m2_conv_w", (1, 4, 3072)), ("m2_conv_b", (1, 3072)), ("m2_dt_bias", (1, 32)),
    ("m2_a_log", (1, 32)), ("m2_d", (1, 32)), ("m2_norm_g", (1, 2048)), ("m2_w_out", (1, 2048, 1024)),
    ("ml_w_in", (1, 1024, 4096)), ("ml_conv_w", (1, 4, 2048)), ("ml_conv_b", (1, 2048)),
    ("ml_w_q", (1, 4, 512, 512)), ("ml_w_k", (1, 4, 512, 512)), ("ml_w_v", (1, 4, 512, 512)),
    ("ml_w_gates", (1, 3, 2048, 8)), ("ml_b_gates", (1, 8)), ("ml_norm_g", (1, 2048)), ("ml_skip", (1, 2048)),
    ("ml_w_down", (1, 2048, 1024)),
    ("xa_w_q", (4, 1024, 1024)), ("xa_w_kv", (4, 1024, 2048)), ("xa_w_o", (4, 1024, 1024)),
    ("pk_w_query", (4, 1024, 2048)), ("pk_sub_keys", (4, 2, 128, 128)), ("pk_u", (4, 16384, 1024)), ("pk_v", (4, 16384, 1024)),
    ("ln_g", (4, 3, 1024)), ("ln_b", (4, 3, 1024)),
]


class Ctx:
    pass


def host_consts(T_=4096):
    c = {}
    hm = np.zeros((128, 2), np.float32)
    hm[:64, 0] = 0.125
    hm[64:, 1] = 0.125
    c["c_hmask"] = hm
    c["c_r0"] = np.tile(-(T_ - np.arange(T_, dtype=np.float32))[None, :], (128, 1)).astype(np.float32)
    qq = np.arange(128, dtype=np.float32)
    c["c_dbase"] = (-np.abs(qq[:, None] - qq[None, :]) + qq[:, None]).astype(np.float32)
    dm = np.zeros((128, 128), np.float32)
    dm[:64, 64:] = NEG
    c["c_dmask"] = dm
    sl = np.zeros((128, 4, 128), np.float32)
    for hh in range(4):
        sl[hh, hh, :] = 1.0
    c["c_sel"] = sl
    c["c_triU"] = np.triu(np.ones((128, 128), np.float32))
    c["c_Lst"] = np.tril(np.ones((128, 128), np.float32), -1)
    c["c_ident"] = np.eye(128, dtype=np.float32)
    c["c_iota16"] = np.tile(np.arange(16, dtype=np.float32)[None, :], (128, 1))
    J = np.zeros((128, 128), np.float32)
    for p in range(64):
        J[p, p + 64] = -1.0
        J[p + 64, p] = 1.0
    c["c_J"] = J
    sg = np.ones((128, 1), np.float32)
    sg[:64] = -1.0
    c["c_sgn"] = sg
    gm = np.zeros((128, 8), np.float32)
    cm = np.zeros((128, 8, 128), np.float32)
    for j in range(8):
        gm[j * 16:(j + 1) * 16, j] = 1.0
        cm[:, j, j * 16:(j + 1) * 16] = 1.0
    c["c_gmask"] = gm
    c["c_cmask"] = cm
    return c


def setup(T_, NSEQ, needed):
    c = Ctx()
    k = KB()
    c.k = k
    c.T = T_
    c.NPF = 3
    c.NSEQ = NSEQ
    c.NT = T_ * NSEQ
    c.NTILES = c.NT // 128
    c.inp = {}
    for name, shp in INPUT_SPECS:
        if name not in needed:
            continue
        if name == "x":
            shp = (c.NT, D)
        elif name == "mem":
            shp = (NSEQ * 256, D)
        c.inp[name] = k.dram(name, list(shp), F32, kind="ExternalInput")
    for name, arr in host_consts(T_).items():
        c.inp[name] = k.dram(name, list(arr.shape), F32, kind="ExternalInput")
    c.out = k.dram("out", [c.NT, D], F32, kind="ExternalOutput")
    c.hA = k.dram("hA", [c.NT, D], F32)
    c.hB = k.dram("hB", [c.NT, D], F32)
    c.identf = k.sb([128, 128], F32, "identf")
    c.identb = k.sb([128, 128], BF16, "identb")
    k.dma("sp", c.identf[:], c.inp["c_ident"][:, :], writes=[c.identf])
    k.op("dve", lambda e: e.tensor_copy(out=c.identb[:], in_=c.identf[:]), [c.identf], [c.identb])
    c.pf = [k.ps([128, 1024], F32, "pf%d" % i) for i in range(c.NPF)]
    c.pb = [k.ps([128, 1024], BF16, "pb%d" % i) for i in range(2)]
    c.pfi = 0
    c.pbi = 0
    c.pf_n = len(c.pf)
    return c


def psf(c):
    c.pfi = (c.pfi + 1) % c.pf_n
    return c.pf[c.pfi]


def psb(c):
    c.pbi = (c.pbi + 1) % len(c.pb)
    return c.pb[c.pbi]


def load_w(c, dst, src_ap, K, N, q="pool"):
    k = c.k
    KC = K // 128
    for n0 in range(0, N, 2048):
        n1 = min(N, n0 + 2048)
        for c0 in range(0, KC, 8):
            c1 = min(KC, c0 + 8)
            k.dma(q, dst[:, c0:c1, n0:n1],
                  src_ap[c0 * 128:c1 * 128, n0:n1].rearrange("(c p) n -> p c n", p=128), writes=[dst])


def load_bc(c, dst, src_ap, q="sp"):
    c.k.dma(q, dst[:], src_ap.broadcast_to([128, src_ap.shape[-1]]), writes=[dst])


def transpose_into(c, dst_fn, src_fn, C, src, dst, evac="act"):
    k = c.k
    p = psb(c)
    for ci in range(C):
        k.op("pe", lambda e, ci=ci: e.transpose(out=p[:, ci * 128:(ci + 1) * 128], in_=src_fn(ci), identity=c.identb[:]),
             [src, c.identb], [p])
    if evac == "act":
        k.op("act", lambda e: e.copy(out=dst_fn(), in_=p[:, 0:C * 128]), [p], [dst])
    else:
        k.op("dve", lambda e: e.tensor_copy(out=dst_fn(), in_=p[:, 0:C * 128]), [p], [dst])


def tail_ln(c, ht, y_fn, ysrc, g_bc, b_bc, outt, z, st, mv, rstd):
    k = c.k
    for hf in range(2):
        sl = slice(hf * 512, (hf + 1) * 512)
        k.op("dve", lambda e, hf=hf, sl=sl: e.scalar_tensor_tensor(out=z[:, sl], in0=ht[:, sl], scalar=ALPHA, in1=y_fn(hf),
                                                                    op0=ALU.mult, op1=ALU.add), [ht, ysrc], [z])
        k.op("dve", lambda e, hf=hf, sl=sl: e.bn_stats(out=st[:, hf, :], in_=z[:, sl]), [z], [st])
    k.op("dve", lambda e: e.bn_aggr(out=mv[:], in_=st[:].rearrange("p a b -> p (a b)")), [st], [mv])
    k.op("act", lambda e: e.activation(out=rstd[:], in_=mv[:, 1:2], func=AF.Sqrt, bias=LN_EPS, scale=1.0), [mv], [rstd])
    k.op("dve", lambda e: e.reciprocal(out=rstd[:], in_=rstd[:]), [rstd], [rstd])
    k.op("dve", lambda e: e.tensor_scalar(out=z[:], in0=z[:], scalar1=mv[:, 0:1], scalar2=rstd[:, 0:1],
                                          op0=ALU.subtract, op1=ALU.mult), [z, mv, rstd], [z])
    k.op("pool", lambda e: e.tensor_tensor(out=z[:], in0=z[:], in1=g_bc[:], op=ALU.mult), [z, g_bc], [z])
    k.op("pool", lambda e: e.tensor_tensor(out=outt[:], in0=z[:], in1=b_bc[:], op=ALU.add), [z, b_bc], [outt])


class Tail:
    def __init__(self, c, li, sub):
        k = c.k
        self.c = c
        self.g = k.sb([128, D], F32, "lng")
        self.b = k.sb([128, D], F32, "lnb")
        load_bc(c, self.g, c.inp["ln_g"][li, sub:sub + 1, :])
        load_bc(c, self.b, c.inp["ln_b"][li, sub:sub + 1, :])
        self.z = [k.sb([128, D], F32, "z") for _ in range(2)]
        self.o = [k.sb([128, D], F32, "ho") for _ in range(2)]
        self.st = [k.sb([128, 2, 6], F32, "st") for _ in range(2)]
        self.mv = [k.sb([128, 2], F32, "mv") for _ in range(2)]
        self.rs = [k.sb([128, 1], F32, "rs") for _ in range(2)]
        self.i = 0

    def run(self, ht, y_fn, ysrc, h_out, ti):
        c = self.c
        i = self.i
        self.i = (i + 1) % 2
        tail_ln(c, ht, y_fn, ysrc, self.g, self.b, self.o[i], self.z[i], self.st[i], self.mv[i], self.rs[i])
        c.k.dma("sp", h_out[ti * 128:(ti + 1) * 128, :], self.o[i][:], reads=[self.o[i]])


def phase_xa(c, li, h_in, h_out):
    k = c.k
    with k.scope():
        wq = k.sb([128, 8, 1024], BF16, "wq")
        wkv = k.sb([128, 8, 2048], BF16, "wkv")
        wo = k.sb([128, 8, 1024], BF16, "wo")
        load_w(c, wq, c.inp["xa_w_q"][li], 1024, 1024)
        load_w(c, wkv, c.inp["xa_w_kv"][li], 1024, 2048)
        load_w(c, wo, c.inp["xa_w_o"][li], 1024, 1024)
        tl = Tail(c, li, 1)
        KT = [k.sb([128, 8, 256], BF16, "KT") for _ in range(c.NSEQ)]
        V = [k.sb([128, 2, 1024], BF16, "V") for _ in range(c.NSEQ)]
        memf = k.sb([128, 1024], F32, "memf")
        memb = k.sb([128, 1024], BF16, "memb")
        memT = k.sb([128, 8, 256], BF16, "memT")
        for s in range(c.NSEQ):
            for mc in range(2):
                k.dma("sp", memf[:], c.inp["mem"][s * 256 + mc * 128: s * 256 + (mc + 1) * 128, :], writes=[memf])
                k.op("dve", lambda e: e.tensor_copy(out=memb[:], in_=memf[:]), [memf], [memb])
                transpose_into(c, lambda mc=mc: memT[:, :, mc * 128:(mc + 1) * 128],
                               lambda ci: memb[:, ci * 128:(ci + 1) * 128], 8, memb, memT)
            for fc in range(8):
                p = psf(c)
                for kc in range(8):
                    k.op("pe", lambda e, fc=fc, kc=kc, p=p: e.matmul(out=p[:, 0:256], lhsT=wkv[:, kc, fc * 128:(fc + 1) * 128],
                                                                     rhs=memT[:, kc, :], start=(kc == 0), stop=(kc == 7)),
                         [wkv, memT], [p])
                k.op("act", lambda e, fc=fc, p=p, s=s: e.copy(out=KT[s][:, fc, :], in_=p[:, 0:256]), [p], [KT[s]])
            for mc in range(2):
                p = psf(c)
                for nb in range(2):
                    for kc in range(8):
                        k.op("pe", lambda e, nb=nb, kc=kc, p=p, mc=mc: e.matmul(
                            out=p[:, nb * 512:(nb + 1) * 512], lhsT=memT[:, kc, mc * 128:(mc + 1) * 128],
                            rhs=wkv[:, kc, 1024 + nb * 512:1024 + (nb + 1) * 512], start=(kc == 0), stop=(kc == 7)),
                            [wkv, memT], [p])
                k.op("dve", lambda e, p=p, mc=mc, s=s: e.tensor_copy(out=V[s][:, mc, :], in_=p[:]), [p], [V[s]])
        hf = [k.sb([128, D], F32, "hf") for _ in range(3)]
        hb = [k.sb([128, D], BF16, "hb") for _ in range(2)]
        hT = [k.sb([128, 8, 128], BF16, "hT") for _ in range(2)]
        qT = [k.sb([128, 8, 128], BF16, "qT") for _ in range(2)]
        ssb = [k.sb([128, 4, 256], F32, "ssb") for _ in range(2)]
        P = [k.sb([128, 4, 256], BF16, "P") for _ in range(2)]
        PT = [k.sb([128, 8, 128], BF16, "PT") for _ in range(2)]
        ob = [k.sb([128, D], BF16, "ob") for _ in range(2)]
        oT = [k.sb([128, 8, 128], BF16, "oT") for _ in range(2)]
        mx = [k.sb([128, 4], F32, "mx") for _ in range(2)]
        sm = [k.sb([128, 4], F32, "sm") for _ in range(2)]
        tps = c.T // 128

        def load(ti):
            k.dma("sp", hf[ti % 3][:], h_in[ti * 128:(ti + 1) * 128, :], writes=[hf[ti % 3]])

        def stage_a(ti):
            if ti + 1 < c.NTILES:
                load(ti + 1)
            b = ti % 2
            s = ti // tps
            hfb = hf[ti % 3]
            k.op("dve", lambda e: e.tensor_copy(out=hb[b][:], in_=hfb[:]), [hfb], [hb[b]])
            transpose_into(c, lambda: hT[b][:].rearrange("p a b -> p (a b)"),
                           lambda ci: hb[b][:, ci * 128:(ci + 1) * 128], 8, hb[b], hT[b])
            p = psf(c)
            for fc in range(8):
                for kc in range(8):
                    k.op("pe", lambda e, fc=fc, kc=kc, p=p: e.matmul(
                        out=p[:, fc * 128:(fc + 1) * 128], lhsT=wq[:, kc, fc * 128:(fc + 1) * 128], rhs=hT[b][:, kc, :],
                        start=(kc == 0), stop=(kc == 7)), [wq, hT[b]], [p])
            k.op("act", lambda e, p=p: e.activation(out=qT[b][:].rearrange("p a b -> p (a b)"), in_=p[:], func=AF.Copy,
                                                    scale=0.0625), [p], [qT[b]])
            p = psf(c)
            for hd in range(4):
                for cc in range(2):
                    k.op("pe", lambda e, hd=hd, cc=cc, p=p: e.matmul(
                        out=p[:, hd * 256:(hd + 1) * 256], lhsT=qT[b][:, 2 * hd + cc, :], rhs=KT[s][:, 2 * hd + cc, :],
                        start=(cc == 0), stop=(cc == 1)), [qT[b], KT[s]], [p])
            k.op("dve", lambda e, p=p: e.tensor_copy(out=ssb[b][:].rearrange("p a b -> p (a b)"), in_=p[:]), [p], [ssb[b]])
            k.op("dve", lambda e: e.tensor_reduce(out=mx[b][:], in_=ssb[b][:], axis=AX.X, op=ALU.max, negate=True),
                 [ssb[b]], [mx[b]])
            for hd in range(4):
                k.op("act", lambda e, hd=hd: e.activation(out=P[b][:, hd, :], in_=ssb[b][:, hd, :], func=AF.Exp,
                                                         bias=mx[b][:, hd:hd + 1], scale=1.0,
                                                         accum_out=sm[b][:, hd:hd + 1]), [ssb[b], mx[b]], [P[b], sm[b]])

        def stage_b(ti):
            b = ti % 2
            s = ti // tps
            hfb = hf[ti % 3]
            k.op("dve", lambda e: e.reciprocal(out=sm[b][:], in_=sm[b][:]), [sm[b]], [sm[b]])
            transpose_into(c, lambda: PT[b][:].rearrange("p a b -> p (a b)"),
                           lambda ci: P[b][:, ci // 2, (ci % 2) * 128:(ci % 2 + 1) * 128], 8, P[b], PT[b], evac="dve")
            p = psf(c)
            for hd in range(4):
                for mc in range(2):
                    k.op("pe", lambda e, hd=hd, mc=mc, p=p: e.matmul(
                        out=p[:, hd * 256:(hd + 1) * 256], lhsT=PT[b][:, 2 * hd + mc, :], rhs=V[s][:, mc, hd * 256:(hd + 1) * 256],
                        start=(mc == 0), stop=(mc == 1)), [PT[b], V[s]], [p])
            for hd in range(4):
                k.op("act", lambda e, hd=hd, p=p: e.activation(out=ob[b][:, hd * 256:(hd + 1) * 256],
                                                              in_=p[:, hd * 256:(hd + 1) * 256], func=AF.Copy,
                                                              scale=sm[b][:, hd:hd + 1]), [p, sm[b]], [ob[b]])
            transpose_into(c, lambda: oT[b][:].rearrange("p a b -> p (a b)"),
                           lambda ci: ob[b][:, ci * 128:(ci + 1) * 128], 8, ob[b], oT[b], evac="act")
            p = psf(c)
            for nb in range(2):
                for kc in range(8):
                    k.op("pe", lambda e, nb=nb, kc=kc, p=p: e.matmul(
                        out=p[:, nb * 512:(nb + 1) * 512], lhsT=oT[b][:, kc, :], rhs=wo[:, kc, nb * 512:(nb + 1) * 512],
                        start=(kc == 0), stop=(kc == 7)), [oT[b], wo], [p])
            tl.run(hfb, lambda hf_, p=p: p[:, hf_ * 512:(hf_ + 1) * 512], p, h_out, ti)

        load(0)
        for ti in range(c.NTILES + 1):
            if ti < c.NTILES:
                stage_a(ti)
            if ti >= 1:
                stage_b(ti - 1)


def phase_peer(c, li, h_in, h_out):
    k = c.k
    NS = 16
    GRP = 8
    UV = k.dram("pk_UV%d" % li, [16384, 2048], BF16)
    with k.scope():
        stf = [k.sb([128, 8192], F32, "stf") for _ in range(2)]
        stb = [k.sb([128, 8192], BF16, "stb") for _ in range(2)]
        it = 0
        for which, nm in enumerate(("pk_u", "pk_v")):
            src = c.inp[nm][li]
            for r0 in range(0, 16384, 1024):
                a, bb = stf[it % 2], stb[it % 2]
                k.dma("sp", a[:].rearrange("p (r d) -> p r d", r=8), src[r0:r0 + 1024, :].rearrange("(p r) d -> p r d", r=8), writes=[a])
                if it % 2 == 0:
                    k.op("act", lambda e, a=a, bb=bb: e.copy(out=bb[:], in_=a[:]), [a], [bb])
                else:
                    k.op("dve", lambda e, a=a, bb=bb: e.tensor_copy(out=bb[:], in_=a[:]), [a], [bb])
                k.dma("sp", UV[r0:r0 + 1024, which * 1024:(which + 1) * 1024].rearrange("(p r) d -> p r d", r=8),
                      bb[:].rearrange("p (r d) -> p r d", r=8), reads=[bb])
                it += 1
    with k.scope():
        wqy = k.sb([128, 8, 2048], BF16, "wqy")
        load_w(c, wqy, c.inp["pk_w_query"][li], 1024, 2048)
        tl = Tail(c, li, 2)
        iota16 = k.sb([128, 16], F32, "iota16")
        k.dma("sp", iota16[:], c.inp["c_iota16"][:, :], writes=[iota16])
        skf = k.sb([128, 128], F32, "skf")
        skb = k.sb([128, 128], BF16, "skb")
        skT = k.sb([128, 2, 128], BF16, "skT")
        for j in range(2):
            k.dma("sp", skf[:], c.inp["pk_sub_keys"][li, j], writes=[skf])
            k.op("dve", lambda e: e.tensor_copy(out=skb[:], in_=skf[:]), [skf], [skb])
            transpose_into(c, lambda j=j: skT[:, j, :], lambda ci: skb[:, :], 1, skb, skT)
        hf = [k.sb([128, D], F32, "hf") for _ in range(2)]
        hb = k.sb([128, D], BF16, "hb")
        hT = k.sb([128, 8, 128], BF16, "hT")
        qT = k.sb([128, 16, 128], BF16, "qT")
        ssb = k.sb([128, 16, 128], F32, "ssb")
        tmp = k.sb([128, 16, 128], F32, "tmp")
        sv = k.sb([128, 16, 16], F32, "sv")
        si = k.sb([128, 16, 16], U32, "si")
        sif = k.sb([128, 16, 16], F32, "sif")
        cand = k.sb([128, 8, 16, 16], F32, "cand")
        tmpc = k.sb([128, 8, 256], F32, "tmpc")
        cv = k.sb([128, 8, 16], F32, "cv")
        ci = k.sb([128, 8, 16], U32, "ci")
        cab = [k.sb([128, 8, 16], U32, "cab") for _ in range(2)]
        cabf = [k.sb([128, 8, 16], F32, "cabf") for _ in range(2)]
        oh = k.sb([128, 8, 16, 16], F32, "oh")
        k12 = [k.sb([128, 8, 16], F32, "k12") for _ in range(2)]
        eif = k.sb([128, 8, 16], F32, "eif")
        eiu = [k.sb([128, 8, 16], U32, "eiu") for _ in range(2)]
        gt = [k.sb([128, 8, 16], F32, "gt") for _ in range(2)]
        zs = k.sb([128, 8], F32, "zs")
        dots = [k.sb([128, 128], F32, "dots") for _ in range(2)]
        actv = [k.sb([128, 128], F32, "actv") for _ in range(2)]
        junk = k.sb([128, D], BF16, "junk")
        uv = [k.sb([128, 2048], BF16, "uv") for _ in range(NS)]
        dg = [k.sb([128, 128], BF16, "dg") for _ in range(NS)]
        pacc = c.pf[2]
        c.pf_n = 2

        def load(ti):
            k.dma("sp", hf[ti % 2][:], h_in[ti * 128:(ti + 1) * 128, :], writes=[hf[ti % 2]])

        load(0)
        for ti in range(c.NTILES):
            if ti + 1 < c.NTILES:
                load(ti + 1)
            b = ti % 2
            hfb = hf[b]
            gtb, dt_, av_ = gt[b], dots[b], actv[b]
            k.op("dve", lambda e: e.tensor_copy(out=hb[:], in_=hfb[:]), [hfb], [hb])
            transpose_into(c, lambda: hT[:].rearrange("p a b -> p (a b)"), lambda ci_: hb[:, ci_ * 128:(ci_ + 1) * 128], 8, hb, hT)
            for half in range(2):
                p = psf(c)
                for fc in range(8):
                    for kc in range(8):
                        k.op("pe", lambda e, fc=fc, kc=kc, p=p, half=half: e.matmul(
                            out=p[:, fc * 128:(fc + 1) * 128], lhsT=wqy[:, kc, (half * 8 + fc) * 128:(half * 8 + fc + 1) * 128],
                            rhs=hT[:, kc, :], start=(kc == 0), stop=(kc == 7)), [wqy, hT], [p])
                k.op("act", lambda e, p=p, half=half: e.copy(out=qT[:, half * 8:(half + 1) * 8, :].rearrange("p a b -> p (a b)"),
                                                            in_=p[:]), [p], [qT])
            for half in range(2):
                p = psf(c)
                for fc in range(8):
                    cidx = half * 8 + fc
                    k.op("pe", lambda e, fc=fc, cidx=cidx, p=p: e.matmul(
                        out=p[:, fc * 128:(fc + 1) * 128], lhsT=qT[:, cidx, :], rhs=skT[:, cidx % 2, :], start=True, stop=True),
                        [qT, skT], [p])
                k.op("act", lambda e, p=p, half=half: e.copy(out=ssb[:, half * 8:(half + 1) * 8, :].rearrange("p a b -> p (a b)"),
                                                            in_=p[:]), [p], [ssb])
            for cc in range(16):
                k.op("dve", lambda e, cc=cc: e.max(out=sv[:, cc, 0:8], in_=ssb[:, cc, :]), [ssb], [sv])
                k.op("dve", lambda e, cc=cc: e.match_replace(out=tmp[:, cc, :], in_to_replace=sv[:, cc, 0:8], in_values=ssb[:, cc, :],
                                                             imm_value=-1e30), [ssb, sv], [tmp])
                k.op("dve", lambda e, cc=cc: e.max(out=sv[:, cc, 8:16], in_=tmp[:, cc, :]), [tmp], [sv])
                k.op("dve", lambda e, cc=cc: e.max_index(out=si[:, cc, 0:8], in_max=sv[:, cc, 0:8], in_values=ssb[:, cc, :]),
                     [ssb, sv], [si])
                k.op("dve", lambda e, cc=cc: e.max_index(out=si[:, cc, 8:16], in_max=sv[:, cc, 8:16], in_values=ssb[:, cc, :]),
                     [ssb, sv], [si])
            k.op("dve", lambda e: e.tensor_copy(out=sif[:], in_=si[:]), [si], [sif])
            for h in range(8):
                cflat = lambda h=h: cand[:, h, :, :].rearrange("p a b -> p (a b)")
                k.op("dve", lambda e, h=h: e.tensor_tensor(out=cand[:, h, :, :], in0=sv[:, 2 * h, :].unsqueeze(2).broadcast_to([128, 16, 16]),
                                                           in1=sv[:, 2 * h + 1, :].unsqueeze(1).broadcast_to([128, 16, 16]), op=ALU.add),
                     [sv], [cand])
                k.op("dve", lambda e, h=h, cflat=cflat: e.max(out=cv[:, h, 0:8], in_=cflat()), [cand], [cv])
                k.op("dve", lambda e, h=h, cflat=cflat: e.match_replace(out=tmpc[:, h, :], in_to_replace=cv[:, h, 0:8], in_values=cflat(),
                                                                       imm_value=-1e30), [cand, cv], [tmpc])
                k.op("dve", lambda e, h=h: e.max(out=cv[:, h, 8:16], in_=tmpc[:, h, :]), [tmpc], [cv])
                k.op("dve", lambda e, h=h, cflat=cflat: e.max_index(out=ci[:, h, 0:8], in_max=cv[:, h, 0:8], in_values=cflat()),
                     [cand, cv], [ci])
                k.op("dve", lambda e, h=h, cflat=cflat: e.max_index(out=ci[:, h, 8:16], in_max=cv[:, h, 8:16], in_values=cflat()),
                     [cand, cv], [ci])
            k.op("dve", lambda e: e.tensor_single_scalar(out=cab[0][:], in_=ci[:], scalar=4, op=ALU.logical_shift_right), [ci], [cab[0]])
            k.op("dve", lambda e: e.tensor_single_scalar(out=cab[1][:], in_=ci[:], scalar=15, op=ALU.bitwise_and), [ci], [cab[1]])
            for j in range(2):
                k.op("dve", lambda e, j=j: e.tensor_copy(out=cabf[j][:], in_=cab[j][:]), [cab[j]], [cabf[j]])
                k.op("dve", lambda e, j=j: e.tensor_tensor(
                    out=oh[:], in0=cabf[j][:].unsqueeze(3).broadcast_to([128, 8, 16, 16]),
                    in1=iota16[:].unsqueeze(1).unsqueeze(1).broadcast_to([128, 8, 16, 16]), op=ALU.is_equal), [cabf[j], iota16], [oh])
                k.op("dve", lambda e, j=j: e.tensor_tensor(
                    out=oh[:], in0=oh[:], in1=sif[:, j::2, :].unsqueeze(2).broadcast_to([128, 8, 16, 16]), op=ALU.mult), [oh, sif], [oh])
                k.op("dve", lambda e, j=j: e.tensor_reduce(out=k12[j][:], in_=oh[:], axis=AX.X, op=ALU.add), [oh], [k12[j]])
            k.op("dve", lambda e: e.scalar_tensor_tensor(out=eif[:].rearrange("p a b -> p (a b)"), in0=k12[0][:].rearrange("p a b -> p (a b)"),
                                                         scalar=128.0, in1=k12[1][:].rearrange("p a b -> p (a b)"),
                                                         op0=ALU.mult, op1=ALU.add), [k12[0], k12[1]], [eif])
            eb = eiu[b]
            k.op("dve", lambda e: e.tensor_tensor(out=gtb[:], in0=cv[:], in1=cv[:, :, 0:1].broadcast_to([128, 8, 16]), op=ALU.subtract),
                 [cv], [gtb])
            k.op("act", lambda e: e.activation(out=gtb[:], in_=gtb[:], func=AF.Exp), [gtb], [gtb])
            k.op("dve", lambda e: e.tensor_reduce(out=zs[:], in_=gtb[:], axis=AX.X, op=ALU.add), [gtb], [zs])
            k.op("dve", lambda e: e.reciprocal(out=zs[:], in_=zs[:]), [zs], [zs])
            k.op("dve", lambda e: e.tensor_tensor(out=gtb[:], in0=gtb[:], in1=zs[:].unsqueeze(2).broadcast_to([128, 8, 16]), op=ALU.mult),
                 [gtb, zs], [gtb])
            k.op("dve", lambda e: e.tensor_copy(out=eb[:], in_=eif[:]), [eif], [eb])
            for g0 in range(0, 128, GRP):
                for hk in range(g0, g0 + GRP):
                    slot = uv[hk % NS]
                    gather(c, slot, UV.t[:, :], eb, hk // 16, hk % 16)
                    k.op("dve", lambda e, slot=slot, hk=hk: e.scalar_tensor_tensor(out=junk[:], in0=slot[:, 0:1024], scalar=1.0, in1=hfb[:], op0=ALU.mult,
                                                                                   op1=ALU.mult, accum_out=dt_[:, hk:hk + 1]),
                         [slot, hfb], [junk, dt_])
                k.op("act", lambda e, g0=g0: e.activation(out=av_[:, g0:g0 + GRP], in_=dt_[:, g0:g0 + GRP], func=AF.Gelu_apprx_tanh), [dt_], [av_])
                k.op("dve", lambda e, g0=g0: e.tensor_tensor(out=av_[:, g0:g0 + GRP], in0=av_[:, g0:g0 + GRP],
                                                             in1=gtb[:].rearrange("p a b -> p (a b)")[:, g0:g0 + GRP], op=ALU.mult), [av_, gtb], [av_])
                for hk in range(g0, g0 + GRP):
                    slot = uv[hk % NS]
                    dgs = dg[hk % NS]
                    k.op("act", lambda e, dgs=dgs, hk=hk: e.activation(out=dgs[:], in_=c.identb[:], func=AF.Copy, scale=av_[:, hk:hk + 1]), [c.identb, av_], [dgs])
                    for nb in range(2):
                        k.op("pe", lambda e, dgs=dgs, slot=slot, nb=nb, hk=hk: e.matmul(out=pacc[:, nb * 512:(nb + 1) * 512], lhsT=dgs[:],
                                                                                       rhs=slot[:, 1024 + nb * 512:1024 + (nb + 1) * 512],
                                                                                       start=(hk == 0), stop=(hk == 127)), [dgs, slot], [pacc])
            tl.run(hfb, lambda hf_: pacc[:, hf_ * 512:(hf_ + 1) * 512], pacc, h_out, ti)
        c.pf_n = 3


def gather(c, dst, table, eb, h, kk):
    k = c.k
    qn = "pool"
    reads = k._res([eb])
    writes = k._res([dst])
    i = k.dnext[qn]
    k.dnext[qn] = (i + 1) % NDSEM
    key = (qn, i)
    if k.dval[qn][i] > 0:
        k._wait(qn, (key, k.dval[qn][i]))
    k._deps(qn, reads, writes)
    k.dval[qn][i] += 16
    tok = (key, k.dval[qn][i])
    k.nc.gpsimd.indirect_dma_start(out=dst[:], out_offset=None, in_=table,
                                   in_offset=bass.IndirectOffsetOnAxis(ap=eb[:, h, kk:kk + 1], axis=0)).then_inc(k.dsem[qn][i], 16)
    k._commit(tok, reads, writes)
    k.n_ins += 1


def phase_s5(c, li, h_in, h_out):
    k = c.k
    T_ = c.T
    NB = T_ // 512
    nlev = int(np.log2(T_))
    GT = k.dram("s5_GT", [c.NSEQ, 8, 128, T_], BF16)
    TWO_PI = 2.0 * np.pi
    with k.scope():
        Jm = k.sb([128, 128], F32, "Jm")
        sgn = k.sb([128, 1], F32, "sgn")
        gmask = k.sb([128, 8], F32, "gmask")
        cmask = k.sb([128, 8, 128], F32, "cmask")
        k.dma("sp", Jm[:], c.inp["c_J"][:, :], writes=[Jm])
        k.dma("sp", sgn[:], c.inp["c_sgn"][:, :], writes=[sgn])
        k.dma("sp", gmask[:], c.inp["c_gmask"][:, :], writes=[gmask])
        k.dma("sp", cmask[:], c.inp["c_cmask"][:, :, :], writes=[cmask])
        ls = k.sb([128, 128], F32, "ls")
        k.op("dve", lambda e: e.memset(ls[:], 0.0), [], [ls])
        lre = k.sb([128, 64], F32, "lre")
        lim = k.sb([128, 64], F32, "lim")
        for nm, dst in (("s5_lam_re", lre), ("s5_lam_im", lim)):
            for hh in range(2):
                k.dma("sp", ls[0:64, hh * 64:(hh + 1) * 64], c.inp[nm][0], writes=[ls])
            p = psf(c)
            k.op("pe", lambda e, p=p: e.transpose(out=p[:, 0:128], in_=ls[:, :], identity=c.identf[:]), [ls, c.identf], [p])
            k.op("dve", lambda e, p=p, dst=dst: e.tensor_copy(out=dst[:], in_=p[:, 0:64]), [p], [dst])
        dt = k.sb([128, 64], F32, "dt")
        load_bc(c, dt, c.inp["s5_log_dt"][0:1, :])
        k.op("act", lambda e: e.activation(out=dt[:], in_=dt[:], func=AF.Exp), [dt], [dt])
        emag = k.sb([128, 64], F32, "emag")
        ang = k.sb([128, 64], F32, "ang")
        k.op("dve", lambda e: e.tensor_tensor(out=emag[:], in0=lre[:], in1=dt[:], op=ALU.mult), [lre, dt], [emag])
        k.op("act", lambda e: e.activation(out=emag[:], in_=emag[:], func=AF.Exp), [emag], [emag])
        k.op("dve", lambda e: e.tensor_tensor(out=ang[:], in0=lim[:], in1=dt[:], op=ALU.mult), [lim, dt], [ang])
        acol = k.sb([128, 64], F32, "acol")
        bcol = k.sb([128, 64], F32, "bcol")
        yv = k.sb([128, 64], F32, "yv")
        yi = k.sb([128, 64], I32, "yi")
        yf = k.sb([128, 64], F32, "yf")
        mk = k.sb([128, 64], F32, "mk")
        for off, dst in ((0.25, acol), (0.0, bcol)):
            k.op("dve", lambda e, off=off: e.tensor_scalar(out=yv[:], in0=ang[:], scalar1=1.0 / TWO_PI, scalar2=off, op0=ALU.mult, op1=ALU.add),
                 [ang], [yv])
            k.op("dve", lambda e: e.tensor_copy(out=yi[:], in_=yv[:]), [yv], [yi])
            k.op("dve", lambda e: e.tensor_copy(out=yf[:], in_=yi[:]), [yi], [yf])
            k.op("dve", lambda e: e.tensor_tensor(out=yv[:], in0=yv[:], in1=yf[:], op=ALU.subtract), [yv, yf], [yv])
            k.op("dve", lambda e: e.tensor_single_scalar(out=mk[:], in_=yv[:], scalar=0.5, op=ALU.is_gt), [yv], [mk])
            k.op("dve", lambda e: e.tensor_tensor(out=yv[:], in0=yv[:], in1=mk[:], op=ALU.subtract), [yv, mk], [yv])
            k.op("dve", lambda e: e.tensor_single_scalar(out=mk[:], in_=yv[:], scalar=-0.5, op=ALU.is_lt), [yv], [mk])
            k.op("dve", lambda e: e.tensor_tensor(out=yv[:], in0=yv[:], in1=mk[:], op=ALU.add), [yv, mk], [yv])
            k.op("act", lambda e, dst=dst: e.activation(out=dst[:], in_=yv[:], func=AF.Sin, scale=TWO_PI), [yv], [dst])
            k.op("dve", lambda e, dst=dst: e.tensor_tensor(out=dst[:], in0=dst[:], in1=emag[:], op=ALU.mult), [dst, emag], [dst])
        am1 = k.sb([128, 64], F32, "am1")
        d2 = k.sb([128, 64], F32, "d2")
        t1 = k.sb([128, 64], F32, "t1")
        cr = k.sb([128, 64], F32, "cr")
        cis = k.sb([128, 64], F32, "cis")
        k.op("dve", lambda e: e.tensor_scalar(out=am1[:], in0=acol[:], scalar1=-1.0, scalar2=None, op0=ALU.add), [acol], [am1])
        k.op("dve", lambda e: e.tensor_tensor(out=d2[:], in0=lre[:], in1=lre[:], op=ALU.mult), [lre], [d2])
        k.op("dve", lambda e: e.tensor_tensor(out=t1[:], in0=lim[:], in1=lim[:], op=ALU.mult), [lim], [t1])
        k.op("dve", lambda e: e.tensor_tensor(out=d2[:], in0=d2[:], in1=t1[:], op=ALU.add), [d2, t1], [d2])
        k.op("dve", lambda e: e.reciprocal(out=d2[:], in_=d2[:]), [d2], [d2])
        k.op("dve", lambda e: e.tensor_tensor(out=cr[:], in0=am1[:], in1=lre[:], op=ALU.mult), [am1, lre], [cr])
        k.op("dve", lambda e: e.tensor_tensor(out=t1[:], in0=bcol[:], in1=lim[:], op=ALU.mult), [bcol, lim], [t1])
        k.op("dve", lambda e: e.tensor_tensor(out=cr[:], in0=cr[:], in1=t1[:], op=ALU.add), [cr, t1], [cr])
        k.op("dve", lambda e: e.tensor_tensor(out=cr[:], in0=cr[:], in1=d2[:], op=ALU.mult), [cr, d2], [cr])
        k.op("dve", lambda e: e.tensor_tensor(out=cis[:], in0=bcol[:], in1=lre[:], op=ALU.mult), [bcol, lre], [cis])
        k.op("dve", lambda e: e.tensor_tensor(out=t1[:], in0=am1[:], in1=lim[:], op=ALU.mult), [am1, lim], [t1])
        k.op("dve", lambda e: e.tensor_tensor(out=cis[:], in0=cis[:], in1=t1[:], op=ALU.subtract), [cis, t1], [cis])
        k.op("dve", lambda e: e.tensor_tensor(out=cis[:], in0=cis[:], in1=d2[:], op=ALU.mult), [cis, d2], [cis])
        k.op("dve", lambda e: e.tensor_scalar(out=cis[:], in0=cis[:], scalar1=sgn[:, 0:1], scalar2=None, op0=ALU.mult), [cis, sgn], [cis])
        if 's5pre0' in DBG:
            return
        BA = k.sb([128, 64, 16], F32, "BA")
        BB = k.sb([128, 64, 16], F32, "BB")
        bre = c.inp["s5_b_re"][0].rearrange("g p c -> p g c")
        bim = c.inp["s5_b_im"][0].rearrange("g p c -> p g c")
        k.dma("sp", BA[0:64], bre, writes=[BA])
        k.dma("sp", BA[64:128], bim, writes=[BA])
        k.dma("sp", BB[0:64], bim, writes=[BB])
        k.dma("sp", BB[64:128], bre, writes=[BB])
        k.op("dve", lambda e: e.tensor_tensor(out=BA[:], in0=BA[:], in1=cr[:].unsqueeze(2).broadcast_to([128, 64, 16]), op=ALU.mult), [BA, cr], [BA])
        k.op("dve", lambda e: e.tensor_tensor(out=BB[:], in0=BB[:], in1=cis[:].unsqueeze(2).broadcast_to([128, 64, 16]), op=ALU.mult), [BB, cis], [BB])
        k.op("dve", lambda e: e.tensor_tensor(out=BA[:], in0=BA[:], in1=BB[:], op=ALU.add), [BA, BB], [BA])
        W0 = k.sb([128, 64, 128], BF16, "W0")
        WC = k.sb([128, 64, 128], BF16, "WC")
        CN = k.sb([128, 8, 128], F32, "CN")
        k.dma("sp", CN[:, :, 0:64], c.inp["s5_c_re"][0].rearrange("(cc g) c p -> (g c) cc p", cc=8), writes=[CN])
        k.dma("sp", CN[:, :, 64:128], c.inp["s5_c_im"][0].rearrange("(cc g) c p -> (g c) cc p", cc=8), writes=[CN])
        k.op("dve", lambda e: e.tensor_scalar(out=CN[:, :, 64:128], in0=CN[:, :, 64:128], scalar1=-1.0, scalar2=None, op0=ALU.mult), [CN], [CN])
        tpf = k.sb([128, 128], F32, "tpf")
        for cc in range(8):
            p = psf(c)
            k.op("pe", lambda e, p=p, cc=cc: e.transpose(out=p[:, 0:128], in_=BA[:, cc * 8:(cc + 1) * 8, :].rearrange("p g c -> p (g c)"),
                                                        identity=c.identf[:]), [BA, c.identf], [p])
            k.op("act", lambda e, p=p: e.copy(out=tpf[:], in_=p[:, 0:128]), [p], [tpf])
            for j in range(8):
                k.op("dve", lambda e, cc=cc, j=j: e.tensor_scalar(out=W0[:, cc * 8 + j, :], in0=tpf[:], scalar1=gmask[:, j:j + 1], scalar2=None,
                                                                  op0=ALU.mult), [tpf, gmask], [W0])
            p = psf(c)
            k.op("pe", lambda e, p=p, cc=cc: e.transpose(out=p[:, 0:128], in_=CN[:, cc, :], identity=c.identf[:]), [CN, c.identf], [p])
            k.op("act", lambda e, p=p: e.copy(out=tpf[:], in_=p[:, 0:128]), [p], [tpf])
            for j in range(8):
                k.op("pool", lambda e, cc=cc, j=j: e.tensor_tensor(out=WC[:, cc * 8 + j, :], in0=tpf[:], in1=cmask[:, j, :], op=ALU.mult),
                     [tpf, cmask], [WC])
        dcol = k.sb([128, 8], F32, "dcol")
        k.dma("sp", dcol[:], c.inp["s5_d"][0].rearrange("(cc p) -> p cc", p=128), writes=[dcol], allow_slow_non_contiguous=True) if False else None
        dtmp = k.sb([128, 128], F32, "dtmp")
        k.op("dve", lambda e: e.memset(dtmp[:], 0.0), [], [dtmp])
        k.dma("sp", dtmp[0:8, :], c.inp["s5_d"][0].rearrange("(cc p) -> cc p", p=128), writes=[dtmp])
        p = psf(c)
        k.op("pe", lambda e, p=p: e.transpose(out=p[:, 0:128], in_=dtmp[:, :], identity=c.identf[:]), [dtmp, c.identf], [p])
        k.op("dve", lambda e, p=p: e.tensor_copy(out=dcol[:], in_=p[:, 0:8]), [p], [dcol])
        if 's5prep' in DBG:
            return
        xst = k.sb([128, T_ // 128, 128], F32, "xst")
        xTf = k.sb([128, T_], F32, "xTf")
        xTb = k.sb([128, T_], BF16, "xTb")
        SA = [k.sb([128, T_], BF16, "SA") for _ in range(8)]
        SB = [k.sb([128, T_], BF16, "SB") for _ in range(2)]
        Xf = [k.sb([128, 128], F32, "Xf") for _ in range(2)]
        XTf = [k.sb([128, 128], F32, "XTf") for _ in range(2)]
        PK = [k.sb([128, nlev, 128], BF16, "PK") for _ in range(2)]
        gtb = [k.sb([128, 512], BF16, "gtb") for _ in range(2)]
        ytmp = [k.sb([128, 512], F32, "ytmp") for _ in range(2)]
        s5tmp = [k.sb([128, 1024], BF16, "s5tmp") for _ in range(2)]
        ev = 0
        for s in range(c.NSEQ):
            for cc in range(8):
                k.dma("sp", xst[:], h_in[s * T_:(s + 1) * T_, cc * 128:(cc + 1) * 128].rearrange("(n p) c -> p n c", p=128), writes=[xst])
                if 's5ma' in DBG:
                    return
                for n4 in range(T_ // 512):
                    p = psf(c)
                    for q4 in range(4):
                        n = n4 * 4 + q4
                        k.op("pe", lambda e, p=p, n=n, q4=q4: e.transpose(out=p[:, q4 * 128:(q4 + 1) * 128], in_=xst[:, n, :], identity=c.identf[:]),
                             [xst, c.identf], [p])
                    if 's5mb' in DBG:
                        return
                    k.op("act", lambda e, p=p, n4=n4: e.copy(out=xTf[:, n4 * 512:(n4 + 1) * 512], in_=p[:, 0:512]), [p], [xTf])
                    if 's5mc' in DBG:
                        return
                    k.op("dve", lambda e, n4=n4: e.tensor_copy(out=xTb[:, n4 * 512:(n4 + 1) * 512], in_=xTf[:, n4 * 512:(n4 + 1) * 512]), [xTf], [xTb])
                if 's5m1' in DBG:
                    return
                finals = [None] * 8
                state = {}

                def powers(j):
                    g = cc * 8 + j
                    gi = g % 2
                    X, XT, pk = Xf[gi], XTf[gi], PK[gi]
                    k.op("dve", lambda e: e.tensor_scalar(out=X[:], in0=c.identf[:], scalar1=acol[:, g:g + 1], scalar2=None, op0=ALU.mult),
                         [c.identf, acol], [X])
                    k.op("dve", lambda e: e.tensor_copy(out=XT[:], in_=X[:]), [X], [XT])
                    k.op("dve", lambda e: e.scalar_tensor_tensor(out=X[:], in0=Jm[:], scalar=bcol[:, g:g + 1], in1=X[:], op0=ALU.mult, op1=ALU.add),
                         [Jm, bcol, X], [X])
                    k.op("dve", lambda e: e.tensor_scalar(out=mk[:, gi:gi + 1], in0=bcol[:, g:g + 1], scalar1=-1.0, scalar2=None, op0=ALU.mult),
                         [bcol], [mk])
                    k.op("dve", lambda e: e.scalar_tensor_tensor(out=XT[:], in0=Jm[:], scalar=mk[:, gi:gi + 1], in1=XT[:], op0=ALU.mult, op1=ALU.add),
                         [Jm, mk, XT], [XT])
                    for lv in range(nlev):
                        k.op("act", lambda e, lv=lv: e.copy(out=pk[:, lv, :], in_=XT[:]), [XT], [pk])
                        if lv + 1 < nlev:
                            p = psf(c)
                            k.op("pe", lambda e, p=p: e.matmul(out=p[:, 0:128], lhsT=XT[:], rhs=X[:], start=True, stop=True), [X, XT], [p])
                            k.op("pe", lambda e, p=p: e.matmul(out=p[:, 512:640], lhsT=X[:], rhs=XT[:], start=True, stop=True), [X, XT], [p])
                            k.op("dve", lambda e, p=p: e.tensor_copy(out=X[:], in_=p[:, 0:128]), [p], [X])
                            k.op("act", lambda e, p=p: e.copy(out=XT[:], in_=p[:, 512:640]), [p], [XT])

                def level0(j):
                    nonlocal ev
                    g = cc * 8 + j
                    gi = g % 2
                    cur, oth = (SA[j], SB[gi]) if nlev % 2 == 0 else (SB[gi], SA[j])
                    for hb_ in range(0, NB, 2):
                        p = psf(c)
                        for q2 in range(2):
                            blk = hb_ + q2
                            if blk >= NB:
                                continue
                            k.op("pe", lambda e, p=p, q2=q2, blk=blk: e.matmul(out=p[:, q2 * 512:(q2 + 1) * 512], lhsT=W0[:, g, :],
                                                                               rhs=xTb[:, blk * 512:(blk + 1) * 512], start=True, stop=True),
                                 [W0, xTb], [p])
                        w = min(2, NB - hb_) * 512
                        if ev % 2 == 0:
                            k.op("act", lambda e, p=p, hb_=hb_, w=w: e.copy(out=cur[:, hb_ * 512:hb_ * 512 + w], in_=p[:, 0:w]), [p], [cur])
                        else:
                            k.op("dve", lambda e, p=p, hb_=hb_, w=w: e.tensor_copy(out=cur[:, hb_ * 512:hb_ * 512 + w], in_=p[:, 0:w]), [p], [cur])
                        ev += 1
                    state[j] = (cur, oth)

                def level(j, lv):
                    nonlocal ev
                    g = cc * 8 + j
                    gi = g % 2
                    pk = PK[gi]
                    cur, oth = state[j]
                    sh = 1 << lv
                    for hb_ in range(0, NB, 2):
                        p = psf(c)
                        use_act = (ev % 3 == 2)
                        ev += 1
                        for q2 in range(2):
                            blk = hb_ + q2
                            if blk >= NB:
                                continue
                            t0 = blk * 512
                            lo = max(t0, sh)
                            has2 = lo < t0 + 512
                            if use_act:
                                k.op("pe", lambda e, p=p, q2=q2, t0=t0, has2=has2: e.matmul(
                                    out=p[:, q2 * 512:(q2 + 1) * 512], lhsT=c.identb[:], rhs=cur[:, t0:t0 + 512], start=True, stop=not has2),
                                    [c.identb, cur], [p])
                            if has2:
                                k.op("pe", lambda e, p=p, q2=q2, t0=t0, lo=lo: e.matmul(
                                    out=p[:, q2 * 512 + (lo - t0):(q2 + 1) * 512], lhsT=pk[:, lv, :], rhs=cur[:, lo - sh:t0 + 512 - sh],
                                    start=not use_act, stop=True), [pk, cur], [p])
                        w = min(2, NB - hb_) * 512
                        c0 = hb_ * 512
                        if use_act:
                            k.op("act", lambda e, p=p, c0=c0, w=w: e.copy(out=oth[:, c0:c0 + w], in_=p[:, 0:w]), [p], [oth])
                        else:
                            lo_all = min(max(c0, sh), c0 + w)
                            if lo_all > c0:
                                k.op("dve", lambda e, c0=c0, lo_all=lo_all: e.tensor_copy(out=oth[:, c0:lo_all], in_=cur[:, c0:lo_all]), [cur], [oth])
                            if lo_all < c0 + w:
                                k.op("dve", lambda e, p=p, c0=c0, lo_all=lo_all, w=w: e.tensor_tensor(
                                    out=oth[:, lo_all:c0 + w], in0=p[:, lo_all - c0:w], in1=cur[:, lo_all:c0 + w], op=ALU.add), [p, cur], [oth])
                    state[j] = (oth, cur)

                for j0 in range(0, 8, 2):
                    for j in (j0, j0 + 1):
                        powers(j)
                    for j in (j0, j0 + 1):
                        level0(j)
                    for lv in range(nlev):
                        for j in (j0, j0 + 1):
                            level(j, lv)
                    for j in (j0, j0 + 1):
                        finals[j] = state[j][0]
                if 's5m4' in DBG:
                    return
                for blk in range(NB):
                    p = psf(c)
                    for j in range(8):
                        k.op("pe", lambda e, p=p, j=j, blk=blk, cc=cc: e.matmul(out=p[:, 0:512], lhsT=WC[:, cc * 8 + j, :],
                                                                                rhs=finals[j][:, blk * 512:(blk + 1) * 512], start=(j == 0), stop=(j == 7)),
                             [WC, finals[j]], [p])
                    yt = ytmp[blk % 2]
                    gb_ = gtb[blk % 2]
                    k.op("dve", lambda e, p=p, yt=yt, blk=blk, cc=cc: e.scalar_tensor_tensor(out=yt[:], in0=xTf[:, blk * 512:(blk + 1) * 512], scalar=dcol[:, cc:cc + 1],
                                                                                             in1=p[:, 0:512], op0=ALU.mult, op1=ALU.add), [xTf, dcol, p], [yt])
                    k.op("act", lambda e, yt=yt, gb_=gb_: e.activation(out=gb_[:], in_=yt[:], func=AF.Gelu_apprx_tanh), [yt], [gb_])
                    k.dma("sp", GT[s, cc, :, blk * 512:(blk + 1) * 512], gb_[:], reads=[gb_])
    if 's5m5' in DBG:
        return
    with k.scope():
        wg = k.sb([128, 8, 2048], BF16, "wg")
        load_w(c, wg, c.inp["s5_w_glu"][0], 1024, 2048)
        bg = k.sb([128, 2048], F32, "bg")
        load_bc(c, bg, c.inp["s5_b_glu"][0:1, :])
        tl = Tail(c, li, 0)
        hf = [k.sb([128, D], F32, "hf") for _ in range(2)]
        gT = [k.sb([128, 8, 128], BF16, "gT") for _ in range(2)]
        vg = [k.sb([128, 2048], F32, "vg") for _ in range(2)]
        tps = T_ // 128

        def load(ti):
            s, tt = ti // tps, ti % tps
            k.dma("sp", hf[ti % 2][:], h_in[ti * 128:(ti + 1) * 128, :], writes=[hf[ti % 2]])
            k.dma("sp", gT[ti % 2][:], GT[s, :, :, tt * 128:(tt + 1) * 128].rearrange("c p t -> p c t"), writes=[gT[ti % 2]])

        load(0)
        for ti in range(c.NTILES):
            if ti + 1 < c.NTILES:
                load(ti + 1)
            b = ti % 2
            for half in range(2):
                p = psf(c)
                for nb in range(2):
                    n0 = half * 1024 + nb * 512
                    for kc in range(8):
                        k.op("pe", lambda e, p=p, nb=nb, n0=n0, kc=kc, b=b: e.matmul(out=p[:, nb * 512:(nb + 1) * 512], lhsT=gT[b][:, kc, :],
                                                                                    rhs=wg[:, kc, n0:n0 + 512], start=(kc == 0), stop=(kc == 7)),
                             [gT[b], wg], [p])
                k.op("dve", lambda e, p=p, half=half, b=b: e.tensor_tensor(out=vg[b][:, half * 1024:(half + 1) * 1024], in0=p[:],
                                                                           in1=bg[:, half * 1024:(half + 1) * 1024], op=ALU.add), [p, bg], [vg[b]])
            k.op("act", lambda e, b=b: e.activation(out=vg[b][:, 1024:2048], in_=vg[b][:, 1024:2048], func=AF.Sigmoid), [vg[b]], [vg[b]])
            k.op("pool", lambda e, b=b: e.tensor_tensor(out=vg[b][:, 0:1024], in0=vg[b][:, 0:1024], in1=vg[b][:, 1024:2048], op=ALU.mult), [vg[b]], [vg[b]])
            tl.run(hf[b], lambda hf_, b=b: vg[b][:, hf_ * 512:(hf_ + 1) * 512], vg[b], h_out, ti)


def phase_da(c, li, h_in, h_out):
    k = c.k
    T_ = c.T
    NQ = T_ // 128
    lam_init = 0.8 - 0.6 * float(np.exp(-0.3 * li))
    QT = [k.dram("da_QT%d" % j, [c.NSEQ, 8, 128, T_], BF16) for j in range(2)]
    KTd = k.dram("da_KT", [c.NSEQ, 8, 128, T_], BF16)
    Vd = k.dram("da_V", [c.NT, 1024], BF16)
    AO = k.dram("da_AO", [c.NT, 1024], BF16)
    with k.scope():
        w = k.sb([128, 8, 3072], BF16, "wqkv")
        load_w(c, w, c.inp["da_w_qkv"][0], 1024, 3072)
        hmask = k.sb([128, 2], F32, "hmask")
        k.dma("sp", hmask[:], c.inp["c_hmask"][:, :], writes=[hmask])
        hf = [k.sb([128, D], F32, "hf") for _ in range(2)]
        hb = k.sb([128, D], BF16, "hb")
        hT = k.sb([128, 8, 128], BF16, "hT")
        qt = [[k.sb([128, 8, 128], BF16, "qt") for _ in range(2)] for _ in range(2)]
        kt = [k.sb([128, 8, 128], BF16, "kt") for _ in range(2)]
        vt = [k.sb([128, 1024], BF16, "vt") for _ in range(2)]
        tps = T_ // 128

        def load(ti):
            k.dma("sp", hf[ti % 2][:], h_in[ti * 128:(ti + 1) * 128, :], writes=[hf[ti % 2]])

        load(0)
        for ti in range(c.NTILES):
            if ti + 1 < c.NTILES:
                load(ti + 1)
            b = ti % 2
            s, tt = ti // tps, ti % tps
            k.op("dve", lambda e, b=b: e.tensor_copy(out=hb[:], in_=hf[b][:]), [hf[b]], [hb])
            transpose_into(c, lambda: hT[:].rearrange("p a b -> p (a b)"), lambda ci: hb[:, ci * 128:(ci + 1) * 128], 8, hb, hT)
            for part in range(2):
                p = psf(c)
                for fc in range(8):
                    for kc in range(8):
                        k.op("pe", lambda e, p=p, fc=fc, kc=kc, part=part: e.matmul(
                            out=p[:, fc * 128:(fc + 1) * 128], lhsT=w[:, kc, part * 1024 + fc * 128: part * 1024 + (fc + 1) * 128],
                            rhs=hT[:, kc, :], start=(kc == 0), stop=(kc == 7)), [w, hT], [p])
                if part == 0:
                    for j in range(2):
                        k.op("act", lambda e, p=p, j=j, b=b: e.activation(out=qt[j][b][:].rearrange("p a b -> p (a b)"), in_=p[:], func=AF.Copy,
                                                                         scale=hmask[:, j:j + 1]), [p, hmask], [qt[j][b]])
                        k.dma("sp", QT[j][s, :, :, tt * 128:(tt + 1) * 128].rearrange("h p t -> p h t"), qt[j][b][:], reads=[qt[j][b]])
                else:
                    k.op("dve", lambda e, p=p, b=b: e.tensor_copy(out=kt[b][:].rearrange("p a b -> p (a b)"), in_=p[:]), [p], [kt[b]])
                    k.dma("sp", KTd[s, :, :, tt * 128:(tt + 1) * 128].rearrange("h p t -> p h t"), kt[b][:], reads=[kt[b]])
            p = psf(c)
            for nb in range(2):
                for kc in range(8):
                    k.op("pe", lambda e, p=p, nb=nb, kc=kc: e.matmul(out=p[:, nb * 512:(nb + 1) * 512], lhsT=hT[:, kc, :],
                                                                    rhs=w[:, kc, 2048 + nb * 512:2048 + (nb + 1) * 512], start=(kc == 0), stop=(kc == 7)),
                         [w, hT], [p])
            k.op("act", lambda e, p=p, b=b: e.copy(out=vt[b][:], in_=p[:]), [p], [vt[b]])
            k.dma("sp", Vd[ti * 128:(ti + 1) * 128, :], vt[b][:], reads=[vt[b]])
    with k.scope():
        r0 = k.sb([128, T_], F32, "r0")
        k.dma("sp", r0[:], c.inp["c_r0"][:, :], writes=[r0])
        dbase = k.sb([128, 128], F32, "dbase")
        dmsk = k.sb([128, 128], F32, "dmsk")
        k.dma("sp", dbase[:], c.inp["c_dbase"][:, :], writes=[dbase])
        k.dma("sp", dmsk[:], c.inp["c_dmask"][:, :], writes=[dmsk])
        lm = k.sb([128, 4, 64], F32, "lm")
        k.dma("sp", lm[:].rearrange("p a b -> p (a b)"), c.inp["da_lambda"][0:1].rearrange("o a b -> o (a b)").broadcast_to([128, 256]), writes=[lm])
        lt = k.sb([128, 2, 64], F32, "lt")
        l2 = k.sb([128, 2], F32, "l2")
        nlam = k.sb([128, 1], F32, "nlam")
        k.op("dve", lambda e: e.tensor_tensor(out=lt[:], in0=lm[:, 0::2, :], in1=lm[:, 1::2, :], op=ALU.mult), [lm], [lt])
        k.op("dve", lambda e: e.tensor_reduce(out=l2[:], in_=lt[:], axis=AX.X, op=ALU.add), [lt], [l2])
        k.op("act", lambda e: e.activation(out=l2[:], in_=l2[:], func=AF.Exp), [l2], [l2])
        k.op("dve", lambda e: e.tensor_tensor(out=nlam[:], in0=l2[:, 1:2], in1=l2[:, 0:1], op=ALU.subtract), [l2], [nlam])
        k.op("dve", lambda e: e.tensor_scalar(out=nlam[:], in0=nlam[:], scalar1=-lam_init, scalar2=None, op0=ALU.add), [nlam], [nlam])
        sg = k.sb([128, 128], F32, "sg")
        load_bc(c, sg, c.inp["da_subln_g"][0:1, :])
        k.op("dve", lambda e: e.tensor_scalar(out=sg[:], in0=sg[:], scalar1=(1.0 - lam_init), scalar2=None, op0=ALU.mult), [sg], [sg])
        kTh = [k.sb([128, T_], BF16, "kTh") for _ in range(2)]
        vh = [k.sb([128, NQ, 128], BF16, "vh") for _ in range(2)]
        qTh = [[k.sb([128, T_], BF16, "qTh") for _ in range(2)] for _ in range(2)]
        dh = [k.sb([128, 128], F32, "dh") for _ in range(2)]
        NBUF = 3
        ssb = [k.sb([128, T_], F32, "ssb") for _ in range(NBUF)]
        P = [k.sb([128, T_], BF16, "P") for _ in range(NBUF)]
        PT = [k.sb([128, NQ, 128], BF16, "PT") for _ in range(2)]
        mx = [k.sb([128, 1], F32, "mx") for _ in range(NBUF)]
        sm = [k.sb([128, 1], F32, "sm") for _ in range(NBUF)]
        o0 = [k.sb([128, 128], F32, "o0") for _ in range(2)]
        oo = [k.sb([128, 128], F32, "oo") for _ in range(2)]
        jk = [k.sb([128, 128], F32, "jk") for _ in range(2)]
        ms = [k.sb([128, 1], F32, "ms") for _ in range(2)]
        ob = [k.sb([128, 128], BF16, "ob") for _ in range(2)]
        units = [(s, h, qi, j) for s in range(c.NSEQ) for h in range(8) for qi in range(NQ) for j in range(2)]

        def stage_a(ui):
            s, h, qi, j = units[ui]
            hb_ = (s * 8 + h) % 2
            slope = 2.0 ** (-(h + 1))
            dhh = dh[hb_]
            if qi == 0 and j == 0:
                k.dma("sp", kTh[hb_][:], KTd[s, h], writes=[kTh[hb_]])
                k.dma("sp", vh[hb_][:], Vd[s * T_:(s + 1) * T_, h * 128:(h + 1) * 128].rearrange("(n p) c -> p n c", p=128), writes=[vh[hb_]])
                for jj in range(2):
                    k.dma("sp", qTh[jj][hb_][:], QT[jj][s, h], writes=[qTh[jj][hb_]])
                k.op("dve", lambda e: e.scalar_tensor_tensor(out=dhh[:], in0=dbase[:], scalar=slope, in1=dmsk[:], op0=ALU.mult, op1=ALU.add),
                     [dbase, dmsk], [dhh])
            q0 = qi * 128
            nk = q0 + 128
            ub = ui % NBUF
            sb_, Pb, mxb, smb = ssb[ub], P[ub], mx[ub], sm[ub]
            for k0 in range(0, q0, 1024):
                p = psf(c)
                w_ = min(1024, q0 - k0)
                for c0 in range(0, w_, 512):
                    cw = min(512, w_ - c0)
                    k.op("pe", lambda e, p=p, c0=c0, cw=cw, k0=k0: e.matmul(
                        out=p[:, c0:c0 + cw], lhsT=qTh[j][hb_][:, q0:q0 + 128], rhs=kTh[hb_][:, k0 + c0:k0 + c0 + cw], start=True, stop=True),
                        [qTh[j][hb_], kTh[hb_]], [p])
                    off = T_ - q0 + k0 + c0
                    k.op("dve", lambda e, p=p, c0=c0, cw=cw, k0=k0, off=off: e.scalar_tensor_tensor(
                        out=sb_[:, k0 + c0:k0 + c0 + cw], in0=r0[:, off:off + cw], scalar=slope, in1=p[:, c0:c0 + cw], op0=ALU.mult, op1=ALU.add),
                        [r0, p], [sb_])
            p = psf(c)
            k.op("pe", lambda e, p=p: e.matmul(out=p[:, 0:128], lhsT=qTh[j][hb_][:, q0:q0 + 128], rhs=kTh[hb_][:, q0:q0 + 128],
                                               start=True, stop=True), [qTh[j][hb_], kTh[hb_]], [p])
            k.op("dve", lambda e, p=p: e.tensor_tensor(out=sb_[:, q0:q0 + 128], in0=p[:, 0:128], in1=dhh[:], op=ALU.add),
                 [p, dhh], [sb_])
            k.op("dve", lambda e: e.tensor_reduce(out=mxb[:], in_=sb_[:, 0:nk], axis=AX.X, op=ALU.max, negate=True), [sb_], [mxb])

        def stage_a2(ui):
            s, h, qi, j = units[ui]
            nk = qi * 128 + 128
            ub = ui % NBUF
            sb_, Pb, mxb, smb = ssb[ub], P[ub], mx[ub], sm[ub]
            k.op("act", lambda e: e.activation(out=Pb[:, 0:nk], in_=sb_[:, 0:nk], func=AF.Exp, bias=mxb[:, 0:1],
                                               scale=1.0, accum_out=smb[:, 0:1]), [sb_, mxb], [Pb, smb])

        def stage_b(ui):
            s, h, qi, j = units[ui]
            hb_ = (s * 8 + h) % 2
            q0 = qi * 128
            nk = q0 + 128
            ub = ui % NBUF
            Pb, PTb, smb = P[ub], PT[ui % 2], sm[ub]
            nblk = nk // 128
            k.op("dve", lambda e: e.reciprocal(out=smb[:], in_=smb[:]), [smb], [smb])
            for b0 in range(0, nblk, 8):
                nb_ = min(8, nblk - b0)
                transpose_into(c, lambda b0=b0, nb_=nb_: PTb[:, b0:b0 + nb_, :].rearrange("p a b -> p (a b)"),
                               lambda ci, b0=b0: Pb[:, (b0 + ci) * 128:(b0 + ci + 1) * 128], nb_, Pb, PTb,
                               evac=("act" if (b0 // 8) % 2 == 0 else "dve"))

        def stage_b2(ui):
            s, h, qi, j = units[ui]
            hb_ = (s * 8 + h) % 2
            q0 = qi * 128
            nk = q0 + 128
            ub = ui % NBUF
            Pb, PTb, smb = P[ub], PT[ui % 2], sm[ub]
            nblk = nk // 128
            p = psf(c)
            for bk in range(nblk):
                k.op("pe", lambda e, p=p, bk=bk: e.matmul(out=p[:, 0:128], lhsT=PTb[:, bk, :], rhs=vh[hb_][:, bk, :],
                                                         start=(bk == 0), stop=(bk == nblk - 1)), [PTb, vh[hb_]], [p])
            qb = qi % 2
            if j == 0:
                k.op("act", lambda e, p=p: e.activation(out=o0[qb][:], in_=p[:, 0:128], func=AF.Copy, scale=smb[:, 0:1]), [p, smb], [o0[qb]])
            else:
                k.op("dve", lambda e: e.tensor_tensor(out=smb[:], in0=smb[:], in1=nlam[:], op=ALU.mult), [smb, nlam], [smb])
                k.op("dve", lambda e, p=p: e.scalar_tensor_tensor(out=oo[qb][:], in0=p[:, 0:128], scalar=smb[:, 0:1], in1=o0[qb][:],
                                                                 op0=ALU.mult, op1=ALU.add), [p, smb, o0[qb]], [oo[qb]])
                k.op("act", lambda e: e.activation(out=jk[qb][:], in_=oo[qb][:], func=AF.Square, accum_out=ms[qb][:, 0:1]), [oo[qb]], [jk[qb], ms[qb]])
                k.op("act", lambda e: e.activation(out=ms[qb][:], in_=ms[qb][:], func=AF.Sqrt, scale=1.0 / 128.0, bias=LN_EPS), [ms[qb]], [ms[qb]])
                k.op("dve", lambda e: e.reciprocal(out=ms[qb][:], in_=ms[qb][:]), [ms[qb]], [ms[qb]])
                k.op("dve", lambda e: e.scalar_tensor_tensor(out=ob[qb][:], in0=oo[qb][:], scalar=ms[qb][:, 0:1], in1=sg[:], op0=ALU.mult, op1=ALU.mult),
                     [oo[qb], ms[qb], sg], [ob[qb]])
                k.dma("sp", AO[s * T_ + q0:s * T_ + q0 + 128, h * 128:(h + 1) * 128], ob[qb][:], reads=[ob[qb]])

        for ui in range(len(units) + 1):
            if ui < len(units):
                stage_a(ui)
            if ui >= 1:
                stage_b(ui - 1)
            if ui < len(units):
                stage_a2(ui)
            if ui >= 1:
                stage_b2(ui - 1)
    with k.scope():
        wo = k.sb([128, 8, 1024], BF16, "wo")
        load_w(c, wo, c.inp["da_w_o"][0], 1024, 1024)
        tl = Tail(c, li, 0)
        hf = [k.sb([128, D], F32, "hf") for _ in range(2)]
        ab = [k.sb([128, D], BF16, "ab") for _ in range(2)]
        aT = [k.sb([128, 8, 128], BF16, "aT") for _ in range(2)]

        def load(ti):
            k.dma("sp", hf[ti % 2][:], h_in[ti * 128:(ti + 1) * 128, :], writes=[hf[ti % 2]])
            k.dma("sp", ab[ti % 2][:], AO[ti * 128:(ti + 1) * 128, :], writes=[ab[ti % 2]])

        load(0)
        for ti in range(c.NTILES):
            if ti + 1 < c.NTILES:
                load(ti + 1)
            b = ti % 2
            transpose_into(c, lambda b=b: aT[b][:].rearrange("p a b -> p (a b)"), lambda ci, b=b: ab[b][:, ci * 128:(ci + 1) * 128], 8, ab[b], aT[b])
            p = psf(c)
            for nb in range(2):
                for kc in range(8):
                    k.op("pe", lambda e, p=p, nb=nb, kc=kc, b=b: e.matmul(out=p[:, nb * 512:(nb + 1) * 512], lhsT=aT[b][:, kc, :], rhs=wo[:, kc, nb * 512:(nb + 1) * 512],
                                                                         start=(kc == 0), stop=(kc == 7)), [aT[b], wo], [p])
            tl.run(hf[b], lambda hf_, p=p: p[:, hf_ * 512:(hf_ + 1) * 512], p, h_out, ti)


def phase_m2(c, li, h_in, h_out):
    k = c.k
    T_ = c.T
    NCH = T_ // 128
    Zd = k.dram("m2_Z", [c.NT, 2048], F32)
    XBC = k.dram("m2_XBC", [c.NSEQ, 24, 128, T_], F32)
    DTd = k.dram("m2_DT", [c.NT, 32], F32)
    XS = k.dram("m2_XS", [c.NT, 2048], F32)
    BTd = k.dram("m2_BT", [c.NSEQ, 4, 128, T_], BF16)
    CTd = k.dram("m2_CT", [c.NSEQ, 4, 128, T_], BF16)
    BTOK = k.dram("m2_BTOK", [c.NT, 512], BF16)
    tps = T_ // 128
    with k.scope():
        w = k.sb([128, 8, 5152], BF16, "w_in")
        load_w(c, w, c.inp["m2_w_in"][0], 1024, 5152)
        hf = [k.sb([128, D], F32, "hf") for _ in range(2)]
        hb = k.sb([128, D], BF16, "hb")
        hT = k.sb([128, 8, 128], BF16, "hT")
        zt = [k.sb([128, 1024], F32, "zt") for _ in range(2)]
        xt = [k.sb([128, 8, 128], F32, "xt") for _ in range(2)]
        dtt = [k.sb([128, 32], F32, "dtt") for _ in range(2)]

        def load(ti):
            k.dma("sp", hf[ti % 2][:], h_in[ti * 128:(ti + 1) * 128, :], writes=[hf[ti % 2]])

        load(0)
        ev = 0
        for ti in range(c.NTILES):
            if ti + 1 < c.NTILES:
                load(ti + 1)
            b = ti % 2
            s, tt = ti // tps, ti % tps
            k.op("dve", lambda e, b=b: e.tensor_copy(out=hb[:], in_=hf[b][:]), [hf[b]], [hb])
            transpose_into(c, lambda: hT[:].rearrange("p a b -> p (a b)"), lambda ci: hb[:, ci * 128:(ci + 1) * 128], 8, hb, hT)
            for half in range(2):
                p = psf(c)
                for nb in range(2):
                    n0 = half * 1024 + nb * 512
                    for kc in range(8):
                        k.op("pe", lambda e, p=p, nb=nb, n0=n0, kc=kc: e.matmul(out=p[:, nb * 512:(nb + 1) * 512], lhsT=hT[:, kc, :], rhs=w[:, kc, n0:n0 + 512],
                                                                               start=(kc == 0), stop=(kc == 7)), [hT, w], [p])
                zb = zt[ev % 2]
                ev += 1
                k.op("act", lambda e, p=p, zb=zb: e.copy(out=zb[:], in_=p[:]), [p], [zb])
                k.dma("sp", Zd[ti * 128:(ti + 1) * 128, half * 1024:(half + 1) * 1024], zb[:], reads=[zb])
            for third in range(3):
                p = psf(c)
                for fc in range(8):
                    col = 2048 + (third * 8 + fc) * 128
                    for kc in range(8):
                        k.op("pe", lambda e, p=p, fc=fc, col=col, kc=kc: e.matmul(out=p[:, fc * 128:(fc + 1) * 128], lhsT=w[:, kc, col:col + 128], rhs=hT[:, kc, :],
                                                                                 start=(kc == 0), stop=(kc == 7)), [hT, w], [p])
                xb_ = xt[ev % 2]
                ev += 1
                k.op("dve", lambda e, p=p, xb_=xb_: e.tensor_copy(out=xb_[:].rearrange("p a b -> p (a b)"), in_=p[:]), [p], [xb_])
                k.dma("sp", XBC[s, third * 8:(third + 1) * 8, :, tt * 128:(tt + 1) * 128].rearrange("n p t -> p n t"), xb_[:], reads=[xb_])
            p = psf(c)
            for kc in range(8):
                k.op("pe", lambda e, p=p, kc=kc: e.matmul(out=p[:, 0:32], lhsT=hT[:, kc, :], rhs=w[:, kc, 5120:5152], start=(kc == 0), stop=(kc == 7)), [hT, w], [p])
            k.op("act", lambda e, p=p, b=b: e.copy(out=dtt[b][:], in_=p[:, 0:32]), [p], [dtt[b]])
            k.dma("sp", DTd[ti * 128:(ti + 1) * 128, :], dtt[b][:], reads=[dtt[b]])
    with k.scope():
        cwp = k.sb([128, 3072], F32, "cwp")
        k.op("dve", lambda e: e.memset(cwp[:], 0.0), [], [cwp])
        k.dma("sp", cwp[0:4, :], c.inp["m2_conv_w"][0], writes=[cwp])
        k.dma("sp", cwp[4:5, :], c.inp["m2_conv_b"][0:1, :], writes=[cwp])
        cw = k.sb([128, 24, 8], F32, "cw")
        for cch in range(24):
            p = psf(c)
            k.op("pe", lambda e, p=p, cch=cch: e.transpose(out=p[:, 0:128], in_=cwp[:, cch * 128:(cch + 1) * 128], identity=c.identf[:]), [cwp, c.identf], [p])
            k.op("act", lambda e, p=p, cch=cch: e.copy(out=cw[:, cch, :], in_=p[:, 0:8]), [p], [cw])
        xin = [k.sb([128, T_], F32, "xin") for _ in range(2)]
        acc = [k.sb([128, T_], F32, "acc") for _ in range(2)]
        accb = [k.sb([128, T_], BF16, "accb") for _ in range(2)]
        stg = [k.sb([128, 4, 128], F32, "stg") for _ in range(2)]
        stgb = [k.sb([128, 8, 128], BF16, "stgb") for _ in range(2)]
        u = 0
        for s in range(c.NSEQ):
            for cch in range(24):
                ub = u % 2
                u += 1
                xi, ac = xin[ub], acc[ub]
                k.dma("sp", xi[:], XBC[s, cch], writes=[xi])
                k.op("dve", lambda e, xi=xi, ac=ac, cch=cch: e.tensor_scalar(out=ac[:], in0=xi[:], scalar1=cw[:, cch, 3:4], scalar2=None, op0=ALU.mult), [xi, cw], [ac])
                for kk in range(3):
                    shf = 3 - kk
                    k.op("dve", lambda e, xi=xi, ac=ac, cch=cch, kk=kk, shf=shf: e.scalar_tensor_tensor(out=ac[:, shf:T_], in0=xi[:, 0:T_ - shf], scalar=cw[:, cch, kk:kk + 1],
                                                                                                  in1=ac[:, shf:T_], op0=ALU.mult, op1=ALU.add), [xi, cw, ac], [ac])
                k.op("act", lambda e, ac=ac, cch=cch: e.activation(out=ac[:], in_=ac[:], func=AF.Silu, bias=cw[:, cch, 4:5], scale=1.0), [ac, cw], [ac])
                if cch < 16:
                    for n4 in range(T_ // 512):
                        p = psf(c)
                        for q4 in range(4):
                            t0 = n4 * 512 + q4 * 128
                            k.op("pe", lambda e, p=p, q4=q4, t0=t0, ac=ac: e.transpose(out=p[:, q4 * 128:(q4 + 1) * 128], in_=ac[:, t0:t0 + 128], identity=c.identf[:]),
                                 [ac, c.identf], [p])
                        sg_ = stg[n4 % 2]
                        k.op("act", lambda e, p=p, sg_=sg_: e.copy(out=sg_[:].rearrange("p a b -> p (a b)"), in_=p[:, 0:512]), [p], [sg_])
                        k.dma("sp", XS[s * T_ + n4 * 512:s * T_ + (n4 + 1) * 512, cch * 128:(cch + 1) * 128].rearrange("(n p) c -> p n c", p=128), sg_[:], reads=[sg_])
                else:
                    abf = accb[ub]
                    k.op("dve", lambda e, ac=ac, abf=abf: e.tensor_copy(out=abf[:], in_=ac[:]), [ac], [abf])
                    if cch < 20:
                        g = cch - 16
                        k.dma("sp", BTd[s, g], abf[:], reads=[abf])
                        for n8 in range(0, T_ // 128, 8):
                            nb_ = min(8, T_ // 128 - n8)
                            sb8 = stgb[(n8 // 8) % 2]
                            transpose_into(c, lambda sb8=sb8, nb_=nb_: sb8[:, 0:nb_, :].rearrange("p a b -> p (a b)"),
                                           lambda ci, n8=n8, abf=abf: abf[:, (n8 + ci) * 128:(n8 + ci + 1) * 128], nb_, abf, sb8)
                            k.dma("sp", BTOK[s * T_ + n8 * 128:s * T_ + (n8 + nb_) * 128, g * 128:(g + 1) * 128].rearrange("(n p) c -> p n c", p=128),
                                  sb8[:, 0:nb_, :], reads=[sb8])
                    else:
                        g = cch - 20
                        k.dma("sp", CTd[s, g], abf[:], reads=[abf])
    with k.scope():
        wo = k.sb([128, 16, 1024], BF16, "w_out")
        load_w(c, wo, c.inp["m2_w_out"][0], 2048, 1024)
        tl = Tail(c, li, 0)
        triU = k.sb([128, 128], F32, "triU")
        Lst = k.sb([128, 128], F32, "Lst")
        ones = k.sb([128, 128], F32, "ones")
        k.dma("sp", triU[:], c.inp["c_triU"][:, :], writes=[triU])
        k.dma("sp", Lst[:], c.inp["c_Lst"][:, :], writes=[Lst])
        k.op("dve", lambda e: e.memset(ones[:], 1.0), [], [ones])
        dtb = k.sb([128, 32], F32, "dtb")
        aneg = k.sb([128, 32], F32, "aneg")
        load_bc(c, dtb, c.inp["m2_dt_bias"][0:1, :])
        load_bc(c, aneg, c.inp["m2_a_log"][0:1, :])
        k.op("act", lambda e: e.activation(out=aneg[:], in_=aneg[:], func=AF.Exp), [aneg], [aneg])
        k.op("dve", lambda e: e.tensor_scalar(out=aneg[:], in0=aneg[:], scalar1=-1.0, scalar2=None, op0=ALU.mult), [aneg], [aneg])
        dsk = k.sb([128, 32], F32, "dsk")
        load_bc(c, dsk, c.inp["m2_d"][0:1, :])
        ng = k.sb([128, 2048], F32, "ng")
        load_bc(c, ng, c.inp["m2_norm_g"][0:1, :])
        hst = k.sb([128, 32, 64], F32, "hst")
        hstb = k.sb([128, 32, 64], BF16, "hstb")
        hf = [k.sb([128, D], F32, "hf") for _ in range(2)]
        xs = [k.sb([128, 32, 64], F32, "xs") for _ in range(2)]
        zz = [k.sb([128, 2048], F32, "zz") for _ in range(2)]
        dtr = [k.sb([128, 32], F32, "dtr") for _ in range(2)]
        btk = [k.sb([128, 512], BF16, "btk") for _ in range(2)]
        btc = [k.sb([128, 4, 128], BF16, "btc") for _ in range(2)]
        ctc = [k.sb([128, 4, 128], BF16, "ctc") for _ in range(2)]
        dt = k.sb([128, 32], F32, "dt")
        dta = k.sb([128, 32], F32, "dta")
        acs = k.sb([128, 32], F32, "acs")
        ea = k.sb([128, 32], F32, "ea")
        dec = k.sb([128, 32], F32, "dec")
        etot = k.sb([128, 32], F32, "etot")
        Rm = k.sb([128, 32, 128], F32, "Rm")
        LT = [k.sb([128, 8, 128], F32, "LT") for _ in range(2)]
        MT = k.sb([128, 32, 128], BF16, "MT")
        cbm = [k.sb([128, 128], F32, "cbm") for _ in range(2)]
        xdt = k.sb([128, 32, 64], BF16, "xdt")
        xdd = k.sb([128, 32, 64], BF16, "xdd")
        yy = k.sb([128, 32, 64], F32, "yy")
        ytmp = k.sb([128, 8, 64], F32, "ytmp")
        ssq = k.sb([128, 4], F32, "ssq")
        jk = k.sb([128, 512], F32, "jk")
        yb = k.sb([128, 2048], BF16, "yb")
        yT = k.sb([128, 16, 128], BF16, "yT")

        def load(ti):
            b = ti % 2
            s, tt = ti // tps, ti % tps
            k.dma("sp", hf[b][:], h_in[ti * 128:(ti + 1) * 128, :], writes=[hf[b]])
            k.dma("sp", xs[b][:].rearrange("p a b -> p (a b)"), XS[ti * 128:(ti + 1) * 128, :], writes=[xs[b]])
            k.dma("sp", zz[b][:], Zd[ti * 128:(ti + 1) * 128, :], writes=[zz[b]])
            k.dma("sp", dtr[b][:], DTd[ti * 128:(ti + 1) * 128, :], writes=[dtr[b]])
            k.dma("sp", btk[b][:], BTOK[ti * 128:(ti + 1) * 128, :], writes=[btk[b]])
            k.dma("sp", btc[b][:], BTd[s, :, :, tt * 128:(tt + 1) * 128].rearrange("g p t -> p g t"), writes=[btc[b]])
            k.dma("sp", ctc[b][:], CTd[s, :, :, tt * 128:(tt + 1) * 128].rearrange("g p t -> p g t"), writes=[ctc[b]])

        load(0)
        for ti in range(c.NTILES):
            if ti + 1 < c.NTILES:
                load(ti + 1)
            b = ti % 2
            s, tt = ti // tps, ti % tps
            if tt == 0:
                k.op("dve", lambda e: e.memset(hst[:], 0.0), [], [hst])
                k.op("pool", lambda e: e.memset(hstb[:], 0.0), [], [hstb])
            k.op("dve", lambda e, b=b: e.tensor_tensor(out=dt[:], in0=dtr[b][:], in1=dtb[:], op=ALU.add), [dtr[b], dtb], [dt])
            k.op("act", lambda e: e.activation(out=dt[:], in_=dt[:], func=AF.Exp), [dt], [dt])
            k.op("act", lambda e: e.activation(out=dt[:], in_=dt[:], func=AF.Ln, bias=1.0, scale=1.0), [dt], [dt])
            k.op("dve", lambda e: e.tensor_tensor(out=dta[:], in0=dt[:], in1=aneg[:], op=ALU.mult), [dt, aneg], [dta])
            p = psf(c)
            k.op("pe", lambda e, p=p: e.matmul(out=p[:, 0:32], lhsT=triU[:], rhs=dta[:], start=True, stop=True), [triU, dta], [p])
            k.op("pe", lambda e, p=p: e.matmul(out=p[:, 512:544], lhsT=ones[:], rhs=dta[:], start=True, stop=True), [ones, dta], [p])
            k.op("dve", lambda e, p=p: e.tensor_copy(out=acs[:], in_=p[:, 0:32]), [p], [acs])
            k.op("act", lambda e: e.activation(out=ea[:], in_=acs[:], func=AF.Exp), [acs], [ea])
            k.op("dve", lambda e, p=p: e.tensor_tensor(out=dec[:], in0=p[:, 512:544], in1=acs[:], op=ALU.subtract), [p, acs], [dec])
            k.op("act", lambda e: e.activation(out=dec[:], in_=dec[:], func=AF.Exp), [dec], [dec])
            k.op("act", lambda e, p=p: e.activation(out=etot[:], in_=p[:, 512:544], func=AF.Exp), [p], [etot])
            k.op("dve", lambda e: e.tensor_tensor(out=Rm[:], in0=triU[:].unsqueeze(1).broadcast_to([128, 32, 128]), in1=dta[:].unsqueeze(2).broadcast_to([128, 32, 128]),
                                                  op=ALU.mult), [triU, dta], [Rm])
            k.op("dve", lambda e, b=b: e.tensor_tensor(out=xdt[:], in0=xs[b][:], in1=dt[:].unsqueeze(2).broadcast_to([128, 32, 64]), op=ALU.mult), [xs[b], dt], [xdt])
            k.op("pool", lambda e: e.tensor_tensor(out=xdd[:], in0=xdt[:], in1=dec[:].unsqueeze(2).broadcast_to([128, 32, 64]), op=ALU.mult), [xdt, dec], [xdd])
            for g in range(4):
                gb = g % 2
                p = psf(c)
                k.op("pe", lambda e, p=p, g=g, b=b: e.matmul(out=p[:, 0:128], lhsT=btc[b][:, g, :], rhs=ctc[b][:, g, :], start=True, stop=True), [btc[b], ctc[b]], [p])
                k.op("dve", lambda e, p=p, gb=gb: e.tensor_tensor(out=cbm[gb][:], in0=p[:, 0:128], in1=triU[:], op=ALU.mult), [p, triU], [cbm[gb]])
                p = psf(c)
                for hh in range(2):
                    k.op("pe", lambda e, p=p, g=g, hh=hh: e.matmul(out=p[:, hh * 512:(hh + 1) * 512], lhsT=Lst[:],
                                                                  rhs=Rm[:, g * 8 + hh * 4:g * 8 + hh * 4 + 4, :].rearrange("p a b -> p (a b)"), start=True, stop=True),
                         [Lst, Rm], [p])
                k.op("act", lambda e, p=p, gb=gb: e.activation(out=LT[gb][:].rearrange("p a b -> p (a b)"), in_=p[:], func=AF.Exp), [p], [LT[gb]])
                k.op("dve", lambda e, g=g, gb=gb: e.tensor_tensor(out=MT[:, g * 8:(g + 1) * 8, :], in0=LT[gb][:], in1=cbm[gb][:].unsqueeze(1).broadcast_to([128, 8, 128]),
                                                                  op=ALU.mult), [LT[gb], cbm[gb]], [MT])
                p = psf(c)
                k.op("pe", lambda e, p=p, g=g, b=b: e.matmul(out=p[:, 0:512], lhsT=ctc[b][:, g, :], rhs=hstb[:, g * 8:(g + 1) * 8, :].rearrange("p a b -> p (a b)"),
                                                            start=True, stop=True), [ctc[b], hstb], [p])
                for r in range(8):
                    hh_ = g * 8 + r
                    k.op("pe", lambda e, p=p, r=r, hh_=hh_: e.matmul(out=p[:, 512 + r * 64:512 + (r + 1) * 64], lhsT=MT[:, hh_, :], rhs=xdt[:, hh_, :], start=True, stop=True),
                         [MT, xdt], [p])
                k.op("dve", lambda e, p=p, g=g: e.tensor_tensor(out=ytmp[:], in0=p[:, 0:512].rearrange("p (a b) -> p a b", a=8),
                                                                in1=ea[:, g * 8:(g + 1) * 8].unsqueeze(2).broadcast_to([128, 8, 64]), op=ALU.mult), [p, ea], [ytmp])
                k.op("dve", lambda e, p=p, g=g: e.tensor_tensor(out=yy[:, g * 8:(g + 1) * 8, :], in0=ytmp[:], in1=p[:, 512:1024].rearrange("p (a b) -> p a b", a=8), op=ALU.add),
                     [ytmp, p], [yy])
            k.op("dve", lambda e: e.tensor_tensor(out=hst[:], in0=hst[:], in1=etot[:].unsqueeze(2).broadcast_to([128, 32, 64]), op=ALU.mult), [hst, etot], [hst])
            for g2 in range(2):
                p = psf(c)
                for q2 in range(2):
                    g = g2 * 2 + q2
                    k.op("pe", lambda e, p=p, q2=q2, g=g, b=b: e.matmul(out=p[:, q2 * 512:(q2 + 1) * 512], lhsT=btk[b][:, g * 128:(g + 1) * 128],
                                                                       rhs=xdd[:, g * 8:(g + 1) * 8, :].rearrange("p a b -> p (a b)"), start=True, stop=True), [btk[b], xdd], [p])
                k.op("dve", lambda e, p=p, g2=g2: e.tensor_tensor(out=hst[:, g2 * 16:(g2 + 1) * 16, :].rearrange("p a b -> p (a b)"),
                                                                  in0=hst[:, g2 * 16:(g2 + 1) * 16, :].rearrange("p a b -> p (a b)"), in1=p[:], op=ALU.add), [hst, p], [hst])
            k.op("act", lambda e: e.copy(out=hstb[:], in_=hst[:]), [hst], [hstb])
            k.op("pool", lambda e, b=b: e.tensor_tensor(out=xs[b][:], in0=xs[b][:], in1=dsk[:].unsqueeze(2).broadcast_to([128, 32, 64]), op=ALU.mult), [xs[b], dsk], [xs[b]])
            k.op("pool", lambda e, b=b: e.tensor_tensor(out=yy[:], in0=yy[:], in1=xs[b][:], op=ALU.add), [yy, xs[b]], [yy])
            k.op("act", lambda e, b=b: e.activation(out=zz[b][:], in_=zz[b][:], func=AF.Silu), [zz[b]], [zz[b]])
            yyf = lambda: yy[:].rearrange("p a b -> p (a b)")
            k.op("dve", lambda e, b=b: e.tensor_tensor(out=yyf(), in0=yyf(), in1=zz[b][:], op=ALU.mult), [yy, zz[b]], [yy])
            for g in range(4):
                k.op("act", lambda e, g=g: e.activation(out=jk[:], in_=yyf()[:, g * 512:(g + 1) * 512], func=AF.Square, accum_out=ssq[:, g:g + 1]), [yy], [jk, ssq])
            k.op("act", lambda e: e.activation(out=ssq[:], in_=ssq[:], func=AF.Sqrt, scale=1.0 / 512.0, bias=LN_EPS), [ssq], [ssq])
            k.op("dve", lambda e: e.reciprocal(out=ssq[:], in_=ssq[:]), [ssq], [ssq])
            k.op("dve", lambda e: e.tensor_tensor(out=yy[:].rearrange("p (g a) b -> p g (a b)", g=4), in0=yy[:].rearrange("p (g a) b -> p g (a b)", g=4),
                                                  in1=ssq[:].unsqueeze(2).broadcast_to([128, 4, 512]), op=ALU.mult), [yy, ssq], [yy])
            k.op("pool", lambda e: e.tensor_tensor(out=yb[:], in0=yyf(), in1=ng[:], op=ALU.mult), [yy, ng], [yb])
            for h2 in range(2):
                transpose_into(c, lambda h2=h2: yT[:, h2 * 8:(h2 + 1) * 8, :].rearrange("p a b -> p (a b)"), lambda ci, h2=h2: yb[:, (h2 * 8 + ci) * 128:(h2 * 8 + ci + 1) * 128],
                               8, yb, yT, evac=("act" if h2 == 0 else "dve"))
            p = psf(c)
            for nb in range(2):
                for kc in range(16):
                    k.op("pe", lambda e, p=p, nb=nb, kc=kc: e.matmul(out=p[:, nb * 512:(nb + 1) * 512], lhsT=yT[:, kc, :], rhs=wo[:, kc, nb * 512:(nb + 1) * 512],
                                                                    start=(kc == 0), stop=(kc == 15)), [yT, wo], [p])
            tl.run(hf[b], lambda hf_, p=p: p[:, hf_ * 512:(hf_ + 1) * 512], p, h_out, ti)


def phase_ml(c, li, h_in, h_out):
    k = c.k
    T_ = c.T
    NCH = T_ // 128
    tps = NCH
    XM = k.dram("ml_XM", [c.NSEQ, 16, 128, T_], F32)
    OG = k.dram("ml_OG", [c.NT, 2048], F32)
    XCT = k.dram("ml_XCT", [c.NSEQ, 16, 128, T_], BF16)
    XMT = k.dram("ml_XMT", [c.NSEQ, 16, 128, T_], BF16)
    XC = k.dram("ml_XC", [c.NT, 2048], F32)
    QTd = k.dram("ml_QT", [c.NSEQ, 16, 128, T_], BF16)
    KTd = k.dram("ml_KT", [c.NSEQ, 16, 128, T_], BF16)
    KTOK = k.dram("ml_KTOK", [c.NT, 2048], BF16)
    VTOK = k.dram("ml_VTOK", [c.NT, 2048], BF16)
    GI = k.dram("ml_GI", [c.NSEQ, 4, T_], F32)
    GF = k.dram("ml_GF", [c.NSEQ, 4, T_], F32)
    GO = k.dram("ml_GO", [c.NT, 2048], BF16)
    with k.scope():
        w = k.sb([128, 8, 4096], BF16, "w_in")
        load_w(c, w, c.inp["ml_w_in"][0], 1024, 4096)
        hf = [k.sb([128, D], F32, "hf") for _ in range(2)]
        hb = k.sb([128, D], BF16, "hb")
        hT = k.sb([128, 8, 128], BF16, "hT")
        zt = [k.sb([128, 1024], F32, "zt") for _ in range(2)]
        xt = [k.sb([128, 8, 128], F32, "xt") for _ in range(2)]

        def load(ti):
            k.dma("sp", hf[ti % 2][:], h_in[ti * 128:(ti + 1) * 128, :], writes=[hf[ti % 2]])

        load(0)
        ev = 0
        for ti in range(c.NTILES):
            if ti + 1 < c.NTILES:
                load(ti + 1)
            b = ti % 2
            s, tt = ti // tps, ti % tps
            k.op("dve", lambda e, b=b: e.tensor_copy(out=hb[:], in_=hf[b][:]), [hf[b]], [hb])
            transpose_into(c, lambda: hT[:].rearrange("p a b -> p (a b)"), lambda ci: hb[:, ci * 128:(ci + 1) * 128], 8, hb, hT)
            for half in range(2):
                p = psf(c)
                for fc in range(8):
                    col = (half * 8 + fc) * 128
                    for kc in range(8):
                        k.op("pe", lambda e, p=p, fc=fc, col=col, kc=kc: e.matmul(out=p[:, fc * 128:(fc + 1) * 128], lhsT=w[:, kc, col:col + 128], rhs=hT[:, kc, :],
                                                                                 start=(kc == 0), stop=(kc == 7)), [hT, w], [p])
                xb_ = xt[ev % 2]
                k.op("dve", lambda e, p=p, xb_=xb_: e.tensor_copy(out=xb_[:].rearrange("p a b -> p (a b)"), in_=p[:]), [p], [xb_])
                k.dma("sp", XM[s, half * 8:(half + 1) * 8, :, tt * 128:(tt + 1) * 128].rearrange("n p t -> p n t"), xb_[:], reads=[xb_])
                p = psf(c)
                for nb in range(2):
                    n0 = 2048 + half * 1024 + nb * 512
                    for kc in range(8):
                        k.op("pe", lambda e, p=p, nb=nb, n0=n0, kc=kc: e.matmul(out=p[:, nb * 512:(nb + 1) * 512], lhsT=hT[:, kc, :], rhs=w[:, kc, n0:n0 + 512],
                                                                               start=(kc == 0), stop=(kc == 7)), [hT, w], [p])
                zb = zt[ev % 2]
                ev += 1
                k.op("act", lambda e, p=p, zb=zb: e.copy(out=zb[:], in_=p[:]), [p], [zb])
                k.dma("sp", OG[ti * 128:(ti + 1) * 128, half * 1024:(half + 1) * 1024], zb[:], reads=[zb])
    if 'mlA' in DBG:
        return
    with k.scope():
        cwp = k.sb([128, 2048], F32, "cwp")
        k.op("dve", lambda e: e.memset(cwp[:], 0.0), [], [cwp])
        k.dma("sp", cwp[0:4, :], c.inp["ml_conv_w"][0], writes=[cwp])
        k.dma("sp", cwp[4:5, :], c.inp["ml_conv_b"][0:1, :], writes=[cwp])
        cw = k.sb([128, 16, 8], F32, "cw")
        for cch in range(16):
            p = psf(c)
            k.op("pe", lambda e, p=p, cch=cch: e.transpose(out=p[:, 0:128], in_=cwp[:, cch * 128:(cch + 1) * 128], identity=c.identf[:]), [cwp, c.identf], [p])
            k.op("act", lambda e, p=p, cch=cch: e.copy(out=cw[:, cch, :], in_=p[:, 0:8]), [p], [cw])
        xin = [k.sb([128, T_], F32, "xin") for _ in range(2)]
        acc = [k.sb([128, T_], F32, "acc") for _ in range(2)]
        accb = [k.sb([128, T_], BF16, "accb") for _ in range(2)]
        xmb = [k.sb([128, T_], BF16, "xmb") for _ in range(2)]
        stg = [k.sb([128, 4, 128], F32, "stg") for _ in range(2)]
        u = 0
        for s in range(c.NSEQ):
            for cch in range(16):
                ub = u % 2
                u += 1
                xi, ac, abf, xb2 = xin[ub], acc[ub], accb[ub], xmb[ub]
                k.dma("sp", xi[:], XM[s, cch], writes=[xi])
                k.op("pool", lambda e, xi=xi, xb2=xb2: e.tensor_copy(out=xb2[:], in_=xi[:]), [xi], [xb2])
                k.dma("sp", XMT[s, cch], xb2[:], reads=[xb2])
                k.op("dve", lambda e, xi=xi, ac=ac, cch=cch: e.tensor_scalar(out=ac[:], in0=xi[:], scalar1=cw[:, cch, 3:4], scalar2=None, op0=ALU.mult), [xi, cw], [ac])
                for kk in range(3):
                    shf = 3 - kk
                    k.op("dve", lambda e, xi=xi, ac=ac, cch=cch, kk=kk, shf=shf: e.scalar_tensor_tensor(out=ac[:, shf:T_], in0=xi[:, 0:T_ - shf], scalar=cw[:, cch, kk:kk + 1],
                                                                                                  in1=ac[:, shf:T_], op0=ALU.mult, op1=ALU.add), [xi, cw, ac], [ac])
                k.op("act", lambda e, ac=ac, cch=cch: e.activation(out=ac[:], in_=ac[:], func=AF.Silu, bias=cw[:, cch, 4:5], scale=1.0), [ac, cw], [ac])
                k.op("pool", lambda e, ac=ac, abf=abf: e.tensor_copy(out=abf[:], in_=ac[:]), [ac], [abf])
                k.dma("sp", XCT[s, cch], abf[:], reads=[abf])
                for n4 in range(T_ // 512):
                    p = psf(c)
                    for q4 in range(4):
                        t0 = n4 * 512 + q4 * 128
                        k.op("pe", lambda e, p=p, q4=q4, t0=t0, ac=ac: e.transpose(out=p[:, q4 * 128:(q4 + 1) * 128], in_=ac[:, t0:t0 + 128], identity=c.identf[:]),
                             [ac, c.identf], [p])
                    sg_ = stg[n4 % 2]
                    k.op("act", lambda e, p=p, sg_=sg_: e.copy(out=sg_[:].rearrange("p a b -> p (a b)"), in_=p[:, 0:512]), [p], [sg_])
                    k.dma("sp", XC[s * T_ + n4 * 512:s * T_ + (n4 + 1) * 512, cch * 128:(cch + 1) * 128].rearrange("(n p) c -> p n c", p=128), sg_[:], reads=[sg_])
    if 'mlB' in DBG:
        return
    with k.scope():
        wq = k.sb([128, 16, 512], BF16, "wq")
        wk = k.sb([128, 16, 512], BF16, "wk")
        wv = k.sb([128, 16, 512], BF16, "wv")
        load_w(c, wq, c.inp["ml_w_q"][0].rearrange("h d e -> (h d) e"), 2048, 512)
        load_w(c, wk, c.inp["ml_w_k"][0].rearrange("h d e -> (h d) e"), 2048, 512)
        load_w(c, wv, c.inp["ml_w_v"][0].rearrange("h d e -> (h d) e"), 2048, 512)
        wgs = k.sb([128, 48, 8], F32, "wgs")
        for gs in range(3):
            k.dma("sp", wgs[:, gs * 16:(gs + 1) * 16, :], c.inp["ml_w_gates"][0, gs].rearrange("(c p) n -> p c n", p=128), writes=[wgs])
        wgp = k.sb([128, 48, 128], BF16, "wgp")
        k.op("dve", lambda e: e.memset(wgp[:], 0.0), [], [wgp])
        k.op("dve", lambda e: e.tensor_copy(out=wgp[:, :, 0:4], in_=wgs[:, :, 0:4]), [wgs], [wgp])
        k.op("dve", lambda e: e.tensor_copy(out=wgp[:, :, 32:36], in_=wgs[:, :, 4:8]), [wgs], [wgp])
        bcol = k.sb([128, 1], F32, "bcol")
        nbcol = k.sb([128, 1], F32, "nbcol")
        k.op("dve", lambda e: e.memset(bcol[:], 0.0), [], [bcol])
        k.dma("sp", bcol[0:4, :], c.inp["ml_b_gates"][0:1, 0:4].rearrange("o n -> n o"), writes=[bcol])
        k.dma("sp", bcol[32:36, :], c.inp["ml_b_gates"][0:1, 4:8].rearrange("o n -> n o"), writes=[bcol])
        k.op("dve", lambda e: e.tensor_scalar(out=nbcol[:], in0=bcol[:], scalar1=-1.0, scalar2=None, op0=ALU.mult), [bcol], [nbcol])
        xcT = [k.sb([128, 16, 128], BF16, "xcT") for _ in range(2)]
        xmT = [k.sb([128, 16, 128], BF16, "xmT") for _ in range(2)]
        qT = [k.sb([128, 16, 128], BF16, "qT") for _ in range(2)]
        kT = [k.sb([128, 16, 128], BF16, "kT") for _ in range(2)]
        kt = [k.sb([128, 2048], BF16, "kt") for _ in range(2)]
        vt = [k.sb([128, 2048], BF16, "vt") for _ in range(2)]
        vT = k.sb([128, 16, 128], BF16, "vT")
        gt = [k.sb([128, 128], F32, "gt") for _ in range(2)]
        lf = [k.sb([128, 128], F32, "lf") for _ in range(2)]
        KS = 512.0 ** -0.5

        def load(ti):
            s, tt = ti // tps, ti % tps
            k.dma("sp", xcT[ti % 2][:], XCT[s, :, :, tt * 128:(tt + 1) * 128].rearrange("n p t -> p n t"), writes=[xcT[ti % 2]])
            k.dma("sp", xmT[ti % 2][:], XMT[s, :, :, tt * 128:(tt + 1) * 128].rearrange("n p t -> p n t"), writes=[xmT[ti % 2]])

        load(0)
        for ti in range(c.NTILES):
            if ti + 1 < c.NTILES:
                load(ti + 1)
            b = ti % 2
            s, tt = ti // tps, ti % tps
            for which in range(2):
                wsrc = wq if which == 0 else wk
                dstT = qT[b] if which == 0 else kT[b]
                for half in range(2):
                    p = psf(c)
                    for oc in range(8):
                        o16 = half * 8 + oc
                        hh, ec = o16 // 4, o16 % 4
                        for dc in range(4):
                            k.op("pe", lambda e, p=p, oc=oc, hh=hh, ec=ec, dc=dc, wsrc=wsrc, b=b: e.matmul(
                                out=p[:, oc * 128:(oc + 1) * 128], lhsT=wsrc[:, hh * 4 + dc, ec * 128:(ec + 1) * 128], rhs=xcT[b][:, hh * 4 + dc, :],
                                start=(dc == 0), stop=(dc == 3)), [wsrc, xcT[b]], [p])
                    if which == 0:
                        k.op("act", lambda e, p=p, dstT=dstT, half=half: e.copy(out=dstT[:, half * 8:(half + 1) * 8, :].rearrange("p a b -> p (a b)"), in_=p[:]), [p], [dstT])
                    else:
                        k.op("act", lambda e, p=p, dstT=dstT, half=half: e.activation(out=dstT[:, half * 8:(half + 1) * 8, :].rearrange("p a b -> p (a b)"), in_=p[:],
                                                                                      func=AF.Copy, scale=KS), [p], [dstT])
                dd = QTd if which == 0 else KTd
                k.dma("sp", dd[s, :, :, tt * 128:(tt + 1) * 128].rearrange("n p t -> p n t"), dstT[:], reads=[dstT])
            for which in range(2):
                wsrc = wk if which == 0 else wv
                src = xcT[b] if which == 0 else xmT[b]
                dst = kt[b] if which == 0 else vt[b]
                for half in range(2):
                    p = psf(c)
                    for q2 in range(2):
                        hh = half * 2 + q2
                        for dc in range(4):
                            k.op("pe", lambda e, p=p, q2=q2, hh=hh, dc=dc, wsrc=wsrc, src=src: e.matmul(
                                out=p[:, q2 * 512:(q2 + 1) * 512], lhsT=src[:, hh * 4 + dc, :], rhs=wsrc[:, hh * 4 + dc, :], start=(dc == 0), stop=(dc == 3)), [wsrc, src], [p])
                    if which == 0:
                        k.op("dve", lambda e, p=p, dst=dst, half=half: e.tensor_scalar(out=dst[:, half * 1024:(half + 1) * 1024], in0=p[:], scalar1=KS, scalar2=None, op0=ALU.mult),
                             [p], [dst])
                    else:
                        k.op("dve", lambda e, p=p, dst=dst, half=half: e.tensor_copy(out=dst[:, half * 1024:(half + 1) * 1024], in_=p[:]), [p], [dst])
                dd = KTOK if which == 0 else VTOK
                k.dma("sp", dd[ti * 128:(ti + 1) * 128, :], dst[:], reads=[dst])
            for h2 in range(2):
                transpose_into(c, lambda h2=h2: vT[:, h2 * 8:(h2 + 1) * 8, :].rearrange("p a b -> p (a b)"), lambda ci, h2=h2, b=b: vt[b][:, (h2 * 8 + ci) * 128:(h2 * 8 + ci + 1) * 128],
                               8, vt[b], vT, evac=("act" if h2 == 0 else "dve"))
            p = psf(c)
            n_mm = 0
            for gs, srcT in enumerate((qT[b], kT[b], vT)):
                for fc in range(16):
                    k.op("pe", lambda e, p=p, gs=gs, fc=fc, srcT=srcT, n_mm=n_mm: e.matmul(out=p[:, 0:128], lhsT=wgp[:, gs * 16 + fc, :], rhs=srcT[:, fc, :],
                                                                                       start=(n_mm == 0), stop=(n_mm == 47)), [wgp, srcT], [p])
                    n_mm += 1
            k.op("act", lambda e, p=p, b=b: e.activation(out=gt[b][:], in_=p[:, 0:128], func=AF.Identity, bias=bcol[:, 0:1], scale=1.0), [p, bcol], [gt[b]])
            k.op("act", lambda e, p=p, b=b: e.activation(out=lf[b][:], in_=p[:, 0:128], func=AF.Exp, bias=nbcol[:, 0:1], scale=-1.0), [p, nbcol], [lf[b]])
            k.op("act", lambda e, b=b: e.activation(out=lf[b][:], in_=lf[b][:], func=AF.Ln, bias=1.0, scale=1.0), [lf[b]], [lf[b]])
            k.op("dve", lambda e, b=b: e.tensor_scalar(out=lf[b][:], in0=lf[b][:], scalar1=-1.0, scalar2=None, op0=ALU.mult), [lf[b]], [lf[b]])
            k.dma("sp", GI[s, :, tt * 128:(tt + 1) * 128], gt[b][0:4, :], reads=[gt[b]])
            k.dma("sp", GF[s, :, tt * 128:(tt + 1) * 128], lf[b][32:36, :], reads=[lf[b]])
    if 'mlC' in DBG:
        return
    with k.scope():
        sel = k.sb([128, 4, 128], F32, "sel")
        k.dma("sp", sel[:], c.inp["c_sel"][:, :, :], writes=[sel])
        negm = k.sb([128, 128], F32, "negm")
        k.dma("sp", negm[:], c.inp["c_Lst"][:, :], writes=[negm])
        k.op("dve", lambda e: e.tensor_scalar(out=negm[:], in0=negm[:], scalar1=NEG, scalar2=None, op0=ALU.mult), [negm], [negm])
        ones2 = k.sb([128, 2], BF16, "ones2")
        k.op("dve", lambda e: e.memset(ones2[:], 1.0), [], [ones2])
        ones512 = k.sb([128, 512], F32, "ones512")
        k.op("dve", lambda e: e.memset(ones512[:], 1.0), [], [ones512])
        ngt = k.sb([128, 2048], F32, "ngt")
        skt = k.sb([128, 2048], F32, "skt")
        load_bc(c, ngt, c.inp["ml_norm_g"][0:1, :])
        load_bc(c, skt, c.inp["ml_skip"][0:1, :])
        WROW = k.sb([128, T_], F32, "WROW")
        NCM = k.sb([128, T_], F32, "NCM")
        MROW = k.sb([128, T_], F32, "MROW")
        PM = k.sb([128, 4, NCH + 1], F32, "PM")
        NM = k.sb([128, 4, NCH + 1], F32, "NM")
        Cst = k.sb([128, 4, 4, 512], F32, "Cst")
        Cb = k.sb([128, 4, 4, 512], BF16, "Cb")
        nst = k.sb([128, 4, 4, 2], F32, "nst")
        nstb = k.sb([128, 4, 4, 2], BF16, "nstb")
        qTt = [k.sb([128, 16, 128], BF16, "qTt") for _ in range(2)]
        kTt = [k.sb([128, 16, 128], BF16, "kTt") for _ in range(2)]
        ktt = [k.sb([128, 2048], BF16, "ktt") for _ in range(2)]
        vtt = [k.sb([128, 2048], BF16, "vtt") for _ in range(2)]
        ogt = k.sb([128, 2048], F32, "ogt")
        xct = k.sb([128, 2048], F32, "xct")
        cols = k.sb([128, 3, 4], F32, "cols")
        ET = k.sb([128, 128], F32, "ET")
        ST = k.sb([128, 128], BF16, "ST")
        sc = k.sb([128, 1], F32, "sc")
        wkc = k.sb([128, 1], F32, "wkc")
        cd = k.sb([128, 1], F32, "cd")
        em = k.sb([128, 1], F32, "em")
        den = k.sb([128, 2], F32, "den")
        tmp512 = k.sb([128, 512], F32, "tmp512")
        num = k.sb([128, 512], F32, "num")
        kw = k.sb([128, 512], BF16, "kw")
        st6 = k.sb([128, 6], F32, "st6")
        mv2 = k.sb([128, 2], F32, "mv2")
        rs1 = k.sb([128, 1], F32, "rs1")
        outb = [k.sb([128, 2048], BF16, "outb") for _ in range(2)]

        def load(ti):
            s, tt = ti // tps, ti % tps
            b = ti % 2
            k.dma("sp", qTt[b][:], QTd[s, :, :, tt * 128:(tt + 1) * 128].rearrange("n p t -> p n t"), writes=[qTt[b]])
            k.dma("sp", kTt[b][:], KTd[s, :, :, tt * 128:(tt + 1) * 128].rearrange("n p t -> p n t"), writes=[kTt[b]])
            k.dma("sp", ktt[b][:], KTOK[ti * 128:(ti + 1) * 128, :], writes=[ktt[b]])
            k.dma("sp", vtt[b][:], VTOK[ti * 128:(ti + 1) * 128, :], writes=[vtt[b]])

        for s in range(c.NSEQ):
            with k.scope():
                F2 = k.sb([128, T_], F32, "F2")
                if s == 0:
                    k.op("dve", lambda e: e.memset(WROW[:], 0.0), [], [WROW])
                    k.op("pool", lambda e: e.memset(NCM[:], 0.0), [], [NCM])
                    k.op("pool", lambda e: e.memset(MROW[:], 0.0), [], [MROW])
                k.dma("sp", MROW[0:4, :], GF[s], writes=[MROW])
                k.dma("sp", WROW[0:4, :], GI[s], writes=[WROW])
                for blk in range(T_ // 512):
                    sl = slice(blk * 512, (blk + 1) * 512)
                    init = 0.0 if blk == 0 else F2[0:4, blk * 512 - 1:blk * 512]
                    k.op("dve", lambda e, sl=sl, init=init: e.tensor_tensor_scan(out=F2[0:4, sl], data0=ones512[0:4, :], data1=MROW[0:4, sl], initial=init,
                                                                                op0=ALU.mult, op1=ALU.add), [ones512, MROW, F2], [F2])
                k.op("dve", lambda e: e.tensor_tensor(out=WROW[0:4, :], in0=WROW[0:4, :], in1=F2[0:4, :], op=ALU.subtract), [WROW, F2], [WROW])
                for blk in range(T_ // 512):
                    sl = slice(blk * 512, (blk + 1) * 512)
                    init = 0.0 if blk == 0 else NCM[0:4, blk * 512 - 1:blk * 512]
                    k.op("dve", lambda e, sl=sl, init=init: e.tensor_tensor_scan(out=NCM[0:4, sl], data0=ones512[0:4, :], data1=WROW[0:4, sl], initial=init,
                                                                                op0=ALU.mult, op1=ALU.max), [ones512, WROW, NCM], [NCM])
                k.op("dve", lambda e: e.tensor_tensor(out=MROW[0:4, :], in0=F2[0:4, :], in1=NCM[0:4, :], op=ALU.add), [F2, NCM], [MROW])
                k.op("dve", lambda e: e.tensor_scalar(out=NCM[0:4, :], in0=NCM[0:4, :], scalar1=-1.0, scalar2=None, op0=ALU.mult), [NCM], [NCM])
                k.op("dve", lambda e: e.memset(NM[:], 0.0), [], [NM])
                for hh in range(4):
                    p = psf(c)
                    k.op("pe", lambda e, p=p, hh=hh: e.matmul(out=p[:, 0:NCH], lhsT=sel[:, hh, :], rhs=NCM[:, 127::128], start=True, stop=True), [sel, NCM], [p])
                    k.op("dve", lambda e, p=p, hh=hh: e.tensor_copy(out=NM[:, hh, 1:NCH + 1], in_=p[:, 0:NCH]), [p], [NM])
                k.op("dve", lambda e: e.tensor_scalar(out=PM[:], in0=NM[:], scalar1=-1.0, scalar2=None, op0=ALU.mult), [NM], [PM])
            if 'mlD' in DBG:
                return
            k.op("dve", lambda e: e.memset(Cst[:], 0.0), [], [Cst])
            k.op("pool", lambda e: e.memset(Cb[:], 0.0), [], [Cb])
            k.op("dve", lambda e: e.memset(nst[:], 0.0), [], [nst])
            k.op("dve", lambda e: e.memset(nstb[:], 0.0), [], [nstb])
            load(s * tps)
            for n in range(NCH):
                ti = s * tps + n
                if n + 1 < NCH:
                    load(ti + 1)
                b = ti % 2
                t0 = n * 128
                k.dma("sp", ogt[:], OG[ti * 128:(ti + 1) * 128, :], writes=[ogt])
                k.dma("sp", xct[:], XC[ti * 128:(ti + 1) * 128, :], writes=[xct])
                k.op("act", lambda e: e.activation(out=ogt[:], in_=ogt[:], func=AF.Sigmoid), [ogt], [ogt])
                p = psf(c)
                for a_, src in enumerate((WROW, NCM, MROW)):
                    k.op("pe", lambda e, p=p, a_=a_, src=src, t0=t0: e.transpose(out=p[:, a_ * 128:(a_ + 1) * 128], in_=src[:, t0:t0 + 128], identity=c.identf[:]),
                         [src, c.identf], [p])
                k.op("act", lambda e, p=p: e.copy(out=cols[:], in_=p[:, 0:384].rearrange("p (a b) -> p a b", a=3)[:, :, 0:4]), [p], [cols])
                for hh in range(4):
                    hs = slice(hh * 512, (hh + 1) * 512)
                    p = psf(c)
                    for dc in range(4):
                        k.op("pe", lambda e, p=p, hh=hh, dc=dc, b=b: e.matmul(out=p[:, 0:128], lhsT=kTt[b][:, hh * 4 + dc, :], rhs=qTt[b][:, hh * 4 + dc, :],
                                                                             start=(dc == 0), stop=(dc == 3)), [kTt[b], qTt[b]], [p])
                    k.op("pe", lambda e, p=p, hh=hh, t0=t0: e.matmul(out=p[:, 512:640], lhsT=WROW[:, t0:t0 + 128], rhs=sel[:, hh, :], start=True, stop=False), [WROW, sel], [p])
                    k.op("pe", lambda e, p=p, hh=hh, t0=t0: e.matmul(out=p[:, 512:640], lhsT=sel[:, hh, :], rhs=NCM[:, t0:t0 + 128], start=False, stop=False), [NCM, sel], [p])
                    k.op("pe", lambda e, p=p: e.matmul(out=p[:, 512:640], lhsT=c.identf[:], rhs=negm[:], start=False, stop=True), [c.identf, negm], [p])
                    k.op("act", lambda e, p=p: e.activation(out=ET[:], in_=p[:, 512:640], func=AF.Exp), [p], [ET])
                    k.op("dve", lambda e, p=p: e.tensor_tensor(out=ST[:], in0=ET[:], in1=p[:, 0:128], op=ALU.mult), [ET, p], [ST])
                    p1 = psf(c)
                    k.op("pe", lambda e, p1=p1, hs=hs, b=b: e.matmul(out=p1[:, 0:512], lhsT=ST[:], rhs=vtt[b][:, hs], start=True, stop=True), [ST, vtt[b]], [p1])
                    for dc in range(4):
                        k.op("pe", lambda e, p1=p1, hh=hh, dc=dc, b=b: e.matmul(out=p1[:, 512:1024], lhsT=qTt[b][:, hh * 4 + dc, :], rhs=Cb[:, hh, dc, :],
                                                                               start=(dc == 0), stop=(dc == 3)), [qTt[b], Cb], [p1])
                    p2 = psf(c)
                    k.op("pe", lambda e, p2=p2: e.matmul(out=p2[:, 0:2], lhsT=ST[:], rhs=ones2[:], start=True, stop=True), [ST, ones2], [p2])
                    for dc in range(4):
                        k.op("pe", lambda e, p2=p2, hh=hh, dc=dc, b=b: e.matmul(out=p2[:, 512:514], lhsT=qTt[b][:, hh * 4 + dc, :], rhs=nstb[:, hh, dc, :],
                                                                               start=(dc == 0), stop=(dc == 3)), [qTt[b], nstb], [p2])
                    k.op("act", lambda e, hh=hh, n=n: e.activation(out=sc[:], in_=cols[:, 1, hh:hh + 1], func=AF.Exp, bias=PM[:, hh, n:n + 1], scale=1.0), [cols, PM], [sc])
                    k.op("act", lambda e, p1=p1: e.activation(out=tmp512[:], in_=p1[:, 512:1024], func=AF.Copy, scale=sc[:, 0:1]), [p1, sc], [tmp512])
                    k.op("dve", lambda e, p1=p1: e.tensor_tensor(out=num[:], in0=tmp512[:], in1=p1[:, 0:512], op=ALU.add), [tmp512, p1], [num])
                    k.op("dve", lambda e, p2=p2: e.tensor_scalar(out=den[:, 0:1], in0=p2[:, 512:513], scalar1=sc[:, 0:1], scalar2=None, op0=ALU.mult), [p2, sc], [den])
                    k.op("dve", lambda e, p2=p2: e.tensor_tensor(out=den[:, 0:1], in0=den[:, 0:1], in1=p2[:, 0:1], op=ALU.add), [den, p2], [den])
                    k.op("act", lambda e: e.activation(out=den[:, 0:1], in_=den[:, 0:1], func=AF.Abs), [den], [den])
                    k.op("act", lambda e, hh=hh: e.activation(out=em[:], in_=cols[:, 2, hh:hh + 1], func=AF.Exp, scale=-1.0), [cols], [em])
                    k.op("dve", lambda e: e.tensor_tensor(out=den[:, 0:1], in0=den[:, 0:1], in1=em[:], op=ALU.max), [den, em], [den])
                    k.op("dve", lambda e: e.reciprocal(out=den[:, 0:1], in_=den[:, 0:1]), [den], [den])
                    k.op("dve", lambda e: e.tensor_scalar(out=num[:], in0=num[:], scalar1=den[:, 0:1], scalar2=None, op0=ALU.mult), [num, den], [num])
                    k.op("dve", lambda e: e.bn_stats(out=st6[:], in_=num[:]), [num], [st6])
                    k.op("dve", lambda e: e.bn_aggr(out=mv2[:], in_=st6[:]), [st6], [mv2])
                    k.op("act", lambda e: e.activation(out=rs1[:], in_=mv2[:, 1:2], func=AF.Sqrt, bias=LN_EPS, scale=1.0), [mv2], [rs1])
                    k.op("dve", lambda e: e.reciprocal(out=rs1[:], in_=rs1[:]), [rs1], [rs1])
                    k.op("dve", lambda e: e.tensor_scalar(out=num[:], in0=num[:], scalar1=mv2[:, 0:1], scalar2=rs1[:, 0:1], op0=ALU.subtract, op1=ALU.mult), [num, mv2, rs1], [num])
                    k.op("pool", lambda e, hs=hs: e.tensor_tensor(out=num[:], in0=num[:], in1=ngt[:, hs], op=ALU.mult), [num, ngt], [num])
                    k.op("pool", lambda e, hs=hs: e.tensor_tensor(out=tmp512[:], in0=xct[:, hs], in1=skt[:, hs], op=ALU.mult), [xct, skt], [tmp512])
                    k.op("pool", lambda e: e.tensor_tensor(out=num[:], in0=num[:], in1=tmp512[:], op=ALU.add), [num, tmp512], [num])
                    k.op("dve", lambda e, hs=hs, b=b: e.tensor_tensor(out=outb[b][:, hs], in0=num[:], in1=ogt[:, hs], op=ALU.mult), [num, ogt], [outb[b]])
                    k.op("act", lambda e, hh=hh, n=n: e.activation(out=wkc[:], in_=cols[:, 0, hh:hh + 1], func=AF.Exp, bias=NM[:, hh, n + 1:n + 2], scale=1.0), [cols, NM], [wkc])
                    k.op("act", lambda e, hh=hh, n=n: e.activation(out=cd[:], in_=PM[:, hh, n:n + 1], func=AF.Exp, bias=NM[:, hh, n + 1:n + 2], scale=1.0), [PM, NM], [cd])
                    k.op("dve", lambda e, hs=hs, b=b: e.tensor_scalar(out=kw[:], in0=ktt[b][:, hs], scalar1=wkc[:, 0:1], scalar2=None, op0=ALU.mult), [ktt[b], wkc], [kw])
                    for d2 in range(2):
                        p3 = psf(c)
                        for q2 in range(2):
                            dkc = d2 * 2 + q2
                            k.op("pe", lambda e, p3=p3, q2=q2, dkc=dkc, hs=hs, b=b: e.matmul(out=p3[:, q2 * 512:(q2 + 1) * 512], lhsT=kw[:, dkc * 128:(dkc + 1) * 128],
                                                                                          rhs=vtt[b][:, hs], start=True, stop=True), [kw, vtt[b]], [p3])
                        k.op("dve", lambda e, p3=p3, d2=d2, hh=hh: e.scalar_tensor_tensor(out=Cst[:, hh, d2 * 2:(d2 + 1) * 2, :].rearrange("p a b -> p (a b)"),
                                                                                        in0=Cst[:, hh, d2 * 2:(d2 + 1) * 2, :].rearrange("p a b -> p (a b)"), scalar=cd[:, 0:1],
                                                                                        in1=p3[:], op0=ALU.mult, op1=ALU.add), [Cst, cd, p3], [Cst])
                    k.op("act", lambda e, hh=hh: e.copy(out=Cb[:, hh, :, :], in_=Cst[:, hh, :, :]), [Cst], [Cb])
                    p4 = psf(c)
                    for dkc in range(4):
                        k.op("pe", lambda e, p4=p4, dkc=dkc: e.matmul(out=p4[:, dkc * 2:dkc * 2 + 2], lhsT=kw[:, dkc * 128:(dkc + 1) * 128], rhs=ones2[:], start=True, stop=True),
                             [kw, ones2], [p4])
                    k.op("dve", lambda e, p4=p4, hh=hh: e.scalar_tensor_tensor(out=nst[:, hh, :, :].rearrange("p a b -> p (a b)"), in0=nst[:, hh, :, :].rearrange("p a b -> p (a b)"),
                                                                               scalar=cd[:, 0:1], in1=p4[:, 0:8], op0=ALU.mult, op1=ALU.add), [nst, cd, p4], [nst])
                    k.op("dve", lambda e, hh=hh: e.tensor_copy(out=nstb[:, hh, :, :], in_=nst[:, hh, :, :]), [nst], [nstb])
                k.dma("sp", GO[ti * 128:(ti + 1) * 128, :], outb[b][:], reads=[outb[b]])
    if 'mlE' in DBG:
        return
    with k.scope():
        wd = k.sb([128, 16, 1024], BF16, "w_down")
        load_w(c, wd, c.inp["ml_w_down"][0], 2048, 1024)
        tl = Tail(c, li, 0)
        hf = [k.sb([128, D], F32, "hf") for _ in range(2)]
        ab = [k.sb([128, 2048], BF16, "ab") for _ in range(2)]
        aT = [k.sb([128, 16, 128], BF16, "aT") for _ in range(2)]

        def load2(ti):
            k.dma("sp", hf[ti % 2][:], h_in[ti * 128:(ti + 1) * 128, :], writes=[hf[ti % 2]])
            k.dma("sp", ab[ti % 2][:], GO[ti * 128:(ti + 1) * 128, :], writes=[ab[ti % 2]])

        load2(0)
        for ti in range(c.NTILES):
            if ti + 1 < c.NTILES:
                load2(ti + 1)
            b = ti % 2
            for h2 in range(2):
                transpose_into(c, lambda h2=h2, b=b: aT[b][:, h2 * 8:(h2 + 1) * 8, :].rearrange("p a b -> p (a b)"),
                               lambda ci, h2=h2, b=b: ab[b][:, (h2 * 8 + ci) * 128:(h2 * 8 + ci + 1) * 128], 8, ab[b], aT[b], evac=("act" if h2 == 0 else "dve"))
            p = psf(c)
            for nb in range(2):
                for kc in range(16):
                    k.op("pe", lambda e, p=p, nb=nb, kc=kc, b=b: e.matmul(out=p[:, nb * 512:(nb + 1) * 512], lhsT=aT[b][:, kc, :], rhs=wd[:, kc, nb * 512:(nb + 1) * 512],
                                                                         start=(kc == 0), stop=(kc == 15)), [aT[b], wd], [p])
            tl.run(hf[b], lambda hf_, p=p: p[:, hf_ * 512:(hf_ + 1) * 512], p, h_out, ti)


def build(T_, NSEQ, plan, needed):
    c = setup(T_, NSEQ, needed)
    k = c.k
    cur = c.inp["x"]
    bufs = [c.hA, c.hB]
    bi = 0
    for pi, (kind, li) in enumerate(plan):
        dst = c.out if pi == len(plan) - 1 else bufs[bi]
        if kind == "xa":
            phase_xa(c, li, cur, dst)
        elif kind == "peer":
            phase_peer(c, li, cur, dst)
        elif kind == "s5":
            phase_s5(c, li, cur, dst)
        elif kind == "da":
            phase_da(c, li, cur, dst)
        elif kind == "m2":
            phase_m2(c, li, cur, dst)
        elif kind == "ml":
            phase_ml(c, li, cur, dst)
        cur = dst
        bi ^= 1
    return c, k.finish()


N_CORES = 8
SEQ_FULL = 4096
PLAN = []
for _i, _mx in enumerate(["s5", "da", "m2", "ml"]):
    PLAN += [(_mx, _i), ("xa", _i), ("peer", _i)]


def kernel(**inputs):
    nseq = 16 // N_CORES
    needed = set(n for n, _ in INPUT_SPECS)
    c, nc = build(SEQ_FULL, nseq, PLAN, needed)
    consts = host_consts(SEQ_FULL)
    x = np.ascontiguousarray(np.asarray(inputs["x"], dtype=np.float32))
    mem = np.ascontiguousarray(np.asarray(inputs["mem"], dtype=np.float32))
    shared = {}
    for name in c.inp:
        if name in ("x", "mem"):
            continue
        if name.startswith("c_"):
            shared[name] = consts[name]
        else:
            shared[name] = np.ascontiguousarray(np.asarray(inputs[name], dtype=np.float32))
    in_maps = []
    for ci in range(N_CORES):
        m = dict(shared)
        m["x"] = x[ci * nseq:(ci + 1) * nseq].reshape(nseq * SEQ_FULL, D)
        m["mem"] = mem[ci * nseq:(ci + 1) * nseq].reshape(nseq * 256, D)
        in_maps.append(m)
    res = run_bass_kernel_spmd(nc, in_maps, core_ids=list(range(N_CORES)))
    outs = [np.asarray(r["out"]).reshape(nseq, SEQ_FULL, D) for r in res.results]
    return np.concatenate(outs, axis=0).astype(np.float32)
```

```python
import numpy as np
import ml_dtypes
from contextlib import ExitStack
import concourse.bass as bass
import concourse.mybir as mybir
from concourse.bass_utils import run_bass_kernel_spmd

F32 = mybir.dt.float32
BF16 = mybir.dt.bfloat16
I32 = mybir.dt.int32
U32 = mybir.dt.uint32
AF = mybir.ActivationFunctionType
ALU = mybir.AluOpType
AX = mybir.AxisListType

ENGS = ("pe", "dve", "act", "pool", "sp")
NDSEM = 8


class Res:
    __slots__ = ("name", "w", "r")

    def __init__(self, name):
        self.name = name
        self.w = None
        self.r = {}


class T:
    __slots__ = ("t", "res")

    def __init__(self, t, res):
        self.t = t
        self.res = res

    def __getitem__(self, key):
        return self.t[key]

    def ap(self):
        return self.t.ap()


class KB:
    def __init__(self):
        self.nc = bass.Bass("TRN2", target_bir_lowering=False)
        self.stack = ExitStack()
        self.stacks = [self.stack]
        self.q = {e: [] for e in ENGS}
        self.cnt = {e: 0 for e in ENGS}
        self.waited = {e: {} for e in ENGS}
        self.sems = {}
        for e in ENGS:
            self.sems[e] = self.stack.enter_context(self.nc.semaphore("s_" + e))
        self.dsem = {}
        self.dval = {}
        self.dnext = {}
        for qn in ("sp", "pool", "act"):
            self.dsem[qn] = [self.stack.enter_context(self.nc.semaphore("d_%s%d" % (qn, i))) for i in range(NDSEM)]
            self.dval[qn] = [0] * NDSEM
            self.dnext[qn] = 0
        self.n_ins = 0
        self.uid = 0
        nc = self.nc
        self.eng = {"pe": nc.tensor, "dve": nc.vector, "act": nc.scalar, "pool": nc.gpsimd, "sp": nc.sync}

    def sb(self, shape, dtype, name=None):
        self.uid += 1
        name = "%s_%d" % (name or "sb", self.uid)
        t = self.stacks[-1].enter_context(self.nc.sbuf_tensor(name, list(shape), dtype))
        return T(t, Res(name))

    def ps(self, shape, dtype, name=None):
        self.uid += 1
        name = "%s_%d" % (name or "ps", self.uid)
        t = self.stacks[-1].enter_context(self.nc.psum_tensor(name, list(shape), dtype))
        return T(t, Res(name))

    def dram(self, name, shape, dtype, kind="Internal"):
        t = self.nc.dram_tensor(name, list(shape), dtype, kind=kind)
        return T(t, Res(name))

    def _wait(self, e, tok):
        key, val = tok
        if self.waited[e].get(key, 0) >= val:
            return
        self.waited[e][key] = val
        sem = self._sem(key)
        self.eng[e].wait_ge(sem, val)

    def _sem(self, key):
        if isinstance(key, str):
            return self.sems[key]
        return self.dsem[key[0]][key[1]]

    def _deps(self, e, reads, writes, skip_self=False):
        toks = []
        for r in reads:
            if r.w is not None:
                toks.append(r.w)
        for w in writes:
            if w.w is not None:
                toks.append(w.w)
            toks.extend(w.r.items())
        for tok in toks:
            if skip_self and tok[0] == e:
                continue
            self._wait(e, tok)

    def _commit(self, tok, reads, writes):
        for r in reads:
            r.r[tok[0]] = tok[1]
        for w in writes:
            w.w = tok
            w.r = {}

    @staticmethod
    def _res(lst):
        out = []
        for x in lst:
            if x is None:
                continue
            out.append(x.res if isinstance(x, T) else x)
        return out

    def op(self, e, fn, reads=(), writes=()):
        reads = self._res(reads)
        writes = self._res(writes)
        self._deps(e, reads, writes, skip_self=(e == "pe"))
        self.cnt[e] += 1
        tok = (e, self.cnt[e])
        sem = self.sems[e]
        fn(self.eng[e]).then_inc(sem, 1)
        self._commit(tok, reads, writes)
        self.n_ins += 1

    def dma(self, qn, out, in_, reads=(), writes=(), **kw):
        reads = self._res(reads)
        writes = self._res(writes)
        i = self.dnext[qn]
        self.dnext[qn] = (i + 1) % NDSEM
        key = (qn, i)
        if self.dval[qn][i] > 0:
            self._wait(qn, (key, self.dval[qn][i]))
        self._deps(qn, reads, writes)
        self.dval[qn][i] += 16
        tok = (key, self.dval[qn][i])
        sem = self.dsem[qn][i]
        self.eng[qn].dma_start(out=out, in_=in_, **kw).then_inc(sem, 16)
        self._commit(tok, reads, writes)
        self.n_ins += 1

    def barrier(self):
        for e in ENGS:
            for qn in self.dsem:
                for i in range(NDSEM):
                    if self.dval[qn][i] > 0:
                        self._wait(e, ((qn, i), self.dval[qn][i]))
            for e2 in ENGS:
                if e2 != e and self.cnt[e2] > 0:
                    self._wait(e, (e2, self.cnt[e2]))

    def scope(self):
        kb = self

        class _S:
            def __enter__(s2):
                kb.stacks.append(ExitStack())

            def __exit__(s2, *a):
                kb.barrier()
                kb.stacks.pop().close()
                return False
        return _S()

    def finish(self, final_res=()):
        for qn in self.dsem:
            for i in range(NDSEM):
                if self.dval[qn][i] > 0:
                    self._wait("sp", ((qn, i), self.dval[qn][i]))
        for e in ENGS:
            if e != "sp" and self.cnt[e] > 0:
                self._wait("sp", (e, self.cnt[e]))
        self.stack.close()
        return self.nc

import os
DBG = os.environ.get('KDBG', '')

D = 1024
ALPHA = 8 ** 0.25
LN_EPS = 1e-5
NEG = -30000.0

INPUT_SPECS = [
    ("x", None), ("mem", None),
    ("s5_lam_re", (1, 64, 64)), ("s5_lam_im", (1, 64, 64)), ("s5_log_dt", (1, 64)),
    ("s5_b_re", (1, 64, 64, 16)), ("s5_b_im", (1, 64, 64, 16)), ("s5_c_re", (1, 64, 16, 64)), ("s5_c_im", (1, 64, 16, 64)),
    ("s5_d", (1, 1024)), ("s5_w_glu", (1, 1024, 2048)), ("s5_b_glu", (1, 2048)),
    ("da_w_qkv", (1, 1024, 3072)), ("da_lambda", (1, 4, 64)), ("da_subln_g", (1, 128)), ("da_w_o", (1, 1024, 1024)),
    ("m2_w_in", (1, 1024, 5152)), ("m2_conv_w", (1, 4, 3072)), ("m2_conv_b", (1, 3072)), ("m2_dt_bias", (1, 32)),
    ("m2_a_log", (1, 32)), ("m2_d", (1, 32)), ("m2_norm_g", (1, 2048)), ("m2_w_out", (1, 2048, 1024)),
    ("ml_w_in", (1, 1024, 4096)), ("ml_conv_w", (1, 4, 2048)), ("ml_conv_b", (1, 2048)),
    ("ml_w_q", (1, 4, 512, 512)), ("ml_w_k", (1, 4, 512, 512)), ("ml_w_v", (1, 4, 512, 512)),
    ("ml_w_gates", (1, 3, 2048, 8)), ("ml_b_gates", (1, 8)), ("ml_norm_g", (1, 2048)), ("ml_skip", (1, 2048)),
    ("ml_w_down", (1, 2048, 1024)),
    ("xa_w_q", (4, 1024, 1024)), ("xa_w_kv", (4, 1024, 2048)), ("xa_w_o", (4, 1024, 1024)),
    ("pk_w_query", (4, 1024, 2048)), ("pk_sub_keys", (4, 2, 128, 128)), ("pk_u", (4, 16384, 1024)), ("pk_v", (4, 16384, 1024)),
    ("ln_g", (4, 3, 1024)), ("ln_b", (4, 3, 1024)),
]


class Ctx:
    pass


def host_consts(T_=4096):
    c = {}
    hm = np.zeros((128, 2), np.float32)
    hm[:64, 0] = 0.125
    hm[64:, 1] = 0.125
    c["c_hmask"] = hm
    c["c_r0"] = np.tile(-(T_ - np.arange(T_, dtype=np.float32))[None, :], (128, 1)).astype(np.float32)
    qq = np.arange(128, dtype=np.float32)
    c["c_dbase"] = (-np.abs(qq[:, None] - qq[None, :]) + qq[:, None]).astype(np.float32)
    dm = np.zeros((128, 128), np.float32)
    dm[:64, 64:] = NEG
    c["c_dmask"] = dm
    sl = np.zeros((128, 4, 128), np.float32)
    for hh in range(4):
        sl[hh, hh, :] = 1.0
    c["c_sel"] = sl
    c["c_triU"] = np.triu(np.ones((128, 128), np.float32))
    c["c_Lst"] = np.tril(np.ones((128, 128), np.float32), -1)
    c["c_ident"] = np.eye(128, dtype=np.float32)
    c["c_iota16"] = np.tile(np.arange(16, dtype=np.float32)[None, :], (128, 1))
    J = np.zeros((128, 128), np.float32)
    for p in range(64):
        J[p, p + 64] = -1.0
        J[p + 64, p] = 1.0
    c["c_J"] = J
    sg = np.ones((128, 1), np.float32)
    sg[:64] = -1.0
    c["c_sgn"] = sg
    gm = np.zeros((128, 8), np.float32)
    cm = np.zeros((128, 8, 128), np.float32)
    for j in range(8):
        gm[j * 16:(j + 1) * 16, j] = 1.0
        cm[:, j, j * 16:(j + 1) * 16] = 1.0
    c["c_gmask"] = gm
    c["c_cmask"] = cm
    return c


def setup(T_, NSEQ, needed):
    c = Ctx()
    k = KB()
    c.k = k
    c.T = T_
    c.NPF = 3
    c.NSEQ = NSEQ
    c.NT = T_ * NSEQ
    c.NTILES = c.NT // 128
    c.inp = {}
    for name, shp in INPUT_SPECS:
        if name not in needed:
            continue
        if name == "x":
            shp = (c.NT, D)
        elif name == "mem":
            shp = (NSEQ * 256, D)
        c.inp[name] = k.dram(name, list(shp), F32, kind="ExternalInput")
    for name, arr in host_consts(T_).items():
        c.inp[name] = k.dram(name, list(arr.shape), F32, kind="ExternalInput")
    c.out = k.dram("out", [c.NT, D], F32, kind="ExternalOutput")
    c.hA = k.dram("hA", [c.NT, D], F32)
    c.hB = k.dram("hB", [c.NT, D], F32)
    c.identf = k.sb([128, 128], F32, "identf")
    c.identb = k.sb([128, 128], BF16, "identb")
    k.dma("sp", c.identf[:], c.inp["c_ident"][:, :], writes=[c.identf])
    k.op("dve", lambda e: e.tensor_copy(out=c.identb[:], in_=c.identf[:]), [c.identf], [c.identb])
    c.pf = [k.ps([128, 1024], F32, "pf%d" % i) for i in range(c.NPF)]
    c.pb = [k.ps([128, 1024], BF16, "pb%d" % i) for i in range(2)]
    c.pfi = 0
    c.pbi = 0
    c.pf_n = len(c.pf)
    return c


def psf(c):
    c.pfi = (c.pfi + 1) % c.pf_n
    return c.pf[c.pfi]


def psb(c):
    c.pbi = (c.pbi + 1) % len(c.pb)
    return c.pb[c.pbi]


def load_w(c, dst, src_ap, K, N, q="pool"):
    k = c.k
    KC = K // 128
    for n0 in range(0, N, 2048):
        n1 = min(N, n0 + 2048)
        for c0 in range(0, KC, 8):
            c1 = min(KC, c0 + 8)
            k.dma(q, dst[:, c0:c1, n0:n1],
                  src_ap[c0 * 128:c1 * 128, n0:n1].rearrange("(c p) n -> p c n", p=128), writes=[dst])


def load_bc(c, dst, src_ap, q="sp"):
    c.k.dma(q, dst[:], src_ap.broadcast_to([128, src_ap.shape[-1]]), writes=[dst])


def transpose_into(c, dst_fn, src_fn, C, src, dst, evac="act"):
    k = c.k
    p = psb(c)
    for ci in range(C):
        k.op("pe", lambda e, ci=ci: e.transpose(out=p[:, ci * 128:(ci + 1) * 128], in_=src_fn(ci), identity=c.identb[:]),
             [src, c.identb], [p])
    if evac == "act":
        k.op("act", lambda e: e.copy(out=dst_fn(), in_=p[:, 0:C * 128]), [p], [dst])
    else:
        k.op("dve", lambda e: e.tensor_copy(out=dst_fn(), in_=p[:, 0:C * 128]), [p], [dst])


def tail_ln(c, ht, y_fn, ysrc, g_bc, b_bc, outt, z, st, mv, rstd):
    k = c.k
    for hf in range(2):
        sl = slice(hf * 512, (hf + 1) * 512)
        k.op("dve", lambda e, hf=hf, sl=sl: e.scalar_tensor_tensor(out=z[:, sl], in0=ht[:, sl], scalar=ALPHA, in1=y_fn(hf),
                                                                    op0=ALU.mult, op1=ALU.add), [ht, ysrc], [z])
        k.op("dve", lambda e, hf=hf, sl=sl: e.bn_stats(out=st[:, hf, :], in_=z[:, sl]), [z], [st])
    k.op("dve", lambda e: e.bn_aggr(out=mv[:], in_=st[:].rearrange("p a b -> p (a b)")), [st], [mv])
    k.op("act", lambda e: e.activation(out=rstd[:], in_=mv[:, 1:2], func=AF.Sqrt, bias=LN_EPS, scale=1.0), [mv], [rstd])
    k.op("dve", lambda e: e.reciprocal(out=rstd[:], in_=rstd[:]), [rstd], [rstd])
    k.op("dve", lambda e: e.tensor_scalar(out=z[:], in0=z[:], scalar1=mv[:, 0:1], scalar2=rstd[:, 0:1],
                                          op0=ALU.subtract, op1=ALU.mult), [z, mv, rstd], [z])
    k.op("pool", lambda e: e.tensor_tensor(out=z[:], in0=z[:], in1=g_bc[:], op=ALU.mult), [z, g_bc], [z])
    k.op("pool", lambda e: e.tensor_tensor(out=outt[:], in0=z[:], in1=b_bc[:], op=ALU.add), [z, b_bc], [outt])


class Tail:
    def __init__(self, c, li, sub):
        k = c.k
        self.c = c
        self.g = k.sb([128, D], F32, "lng")
        self.b = k.sb([128, D], F32, "lnb")
        load_bc(c, self.g, c.inp["ln_g"][li, sub:sub + 1, :])
        load_bc(c, self.b, c.inp["ln_b"][li, sub:sub + 1, :])
        self.z = [k.sb([128, D], F32, "z") for _ in range(2)]
        self.o = [k.sb([128, D], F32, "ho") for _ in range(2)]
        self.st = [k.sb([128, 2, 6], F32, "st") for _ in range(2)]
        self.mv = [k.sb([128, 2], F32, "mv") for _ in range(2)]
        self.rs = [k.sb([128, 1], F32, "rs") for _ in range(2)]
        self.i = 0

    def run(self, ht, y_fn, ysrc, h_out, ti):
        c = self.c
        i = self.i
        self.i = (i + 1) % 2
        tail_ln(c, ht, y_fn, ysrc, self.g, self.b, self.o[i], self.z[i], self.st[i], self.mv[i], self.rs[i])
        c.k.dma("sp", h_out[ti * 128:(ti + 1) * 128, :], self.o[i][:], reads=[self.o[i]])


def phase_xa(c, li, h_in, h_out):
    k = c.k
    with k.scope():
        wq = k.sb([128, 8, 1024], BF16, "wq")
        wkv = k.sb([128, 8, 2048], BF16, "wkv")
        wo = k.sb([128, 8, 1024], BF16, "wo")
        load_w(c, wq, c.inp["xa_w_q"][li], 1024, 1024)
        load_w(c, wkv, c.inp["xa_w_kv"][li], 1024, 2048)
        load_w(c, wo, c.inp["xa_w_o"][li], 1024, 1024)
        tl = Tail(c, li, 1)
        KT = [k.sb([128, 8, 256], BF16, "KT") for _ in range(c.NSEQ)]
        V = [k.sb([128, 2, 1024], BF16, "V") for _ in range(c.NSEQ)]
        memf = k.sb([128, 1024], F32, "memf")
        memb = k.sb([128, 1024], BF16, "memb")
        memT = k.sb([128, 8, 256], BF16, "memT")
        for s in range(c.NSEQ):
            for mc in range(2):
                k.dma("sp", memf[:], c.inp["mem"][s * 256 + mc * 128: s * 256 + (mc + 1) * 128, :], writes=[memf])
                k.op("dve", lambda e: e.tensor_copy(out=memb[:], in_=memf[:]), [memf], [memb])
                transpose_into(c, lambda mc=mc: memT[:, :, mc * 128:(mc + 1) * 128],
                               lambda ci: memb[:, ci * 128:(ci + 1) * 128], 8, memb, memT)
            for fc in range(8):
                p = psf(c)
                for kc in range(8):
                    k.op("pe", lambda e, fc=fc, kc=kc, p=p: e.matmul(out=p[:, 0:256], lhsT=wkv[:, kc, fc * 128:(fc + 1) * 128],
                                                                     rhs=memT[:, kc, :], start=(kc == 0), stop=(kc == 7)),
                         [wkv, memT], [p])
                k.op("act", lambda e, fc=fc, p=p, s=s: e.copy(out=KT[s][:, fc, :], in_=p[:, 0:256]), [p], [KT[s]])
            for mc in range(2):
                p = psf(c)
                for nb in range(2):
                    for kc in range(8):
                        k.op("pe", lambda e, nb=nb, kc=kc, p=p, mc=mc: e.matmul(
                            out=p[:, nb * 512:(nb + 1) * 512], lhsT=memT[:, kc, mc * 128:(mc + 1) * 128],
                            rhs=wkv[:, kc, 1024 + nb * 512:1024 + (nb + 1) * 512], start=(kc == 0), stop=(kc == 7)),
                            [wkv, memT], [p])
                k.op("dve", lambda e, p=p, mc=mc, s=s: e.tensor_copy(out=V[s][:, mc, :], in_=p[:]), [p], [V[s]])
        hf = [k.sb([128, D], F32, "hf") for _ in range(3)]
        hb = [k.sb([128, D], BF16, "hb") for _ in range(2)]
        hT = [k.sb([128, 8, 128], BF16, "hT") for _ in range(2)]
        qT = [k.sb([128, 8, 128], BF16, "qT") for _ in range(2)]
        ssb = [k.sb([128, 4, 256], F32, "ssb") for _ in range(2)]
        P = [k.sb([128, 4, 256], BF16, "P") for _ in range(2)]
        PT = [k.sb([128, 8, 128], BF16, "PT") for _ in range(2)]
        ob = [k.sb([128, D], BF16, "ob") for _ in range(2)]
        oT = [k.sb([128, 8, 128], BF16, "oT") for _ in range(2)]
        mx = [k.sb([128, 4], F32, "mx") for _ in range(2)]
        sm = [k.sb([128, 4], F32, "sm") for _ in range(2)]
        tps = c.T // 128

        def load(ti):
            k.dma("sp", hf[ti % 3][:], h_in[ti * 128:(ti + 1) * 128, :], writes=[hf[ti % 3]])

        def stage_a(ti):
            if ti + 1 < c.NTILES:
                load(ti + 1)
            b = ti % 2
            s = ti // tps
            hfb = hf[ti % 3]
            k.op("dve", lambda e: e.tensor_copy(out=hb[b][:], in_=hfb[:]), [hfb], [hb[b]])
            transpose_into(c, lambda: hT[b][:].rearrange("p a b -> p (a b)"),
                           lambda ci: hb[b][:, ci * 128:(ci + 1) * 128], 8, hb[b], hT[b])
            p = psf(c)
            for fc in range(8):
                for kc in range(8):
                    k.op("pe", lambda e, fc=fc, kc=kc, p=p: e.matmul(
                        out=p[:, fc * 128:(fc + 1) * 128], lhsT=wq[:, kc, fc * 128:(fc + 1) * 128], rhs=hT[b][:, kc, :],
                        start=(kc == 0), stop=(kc == 7)), [wq, hT[b]], [p])
            k.op("act", lambda e, p=p: e.activation(out=qT[b][:].rearrange("p a b -> p (a b)"), in_=p[:], func=AF.Copy,
                                                    scale=0.0625), [p], [qT[b]])
            p = psf(c)
            for hd in range(4):
                for cc in range(2):
                    k.op("pe", lambda e, hd=hd, cc=cc, p=p: e.matmul(
                        out=p[:, hd * 256:(hd + 1) * 256], lhsT=qT[b][:, 2 * hd + cc, :], rhs=KT[s][:, 2 * hd + cc, :],
                        start=(cc == 0), stop=(cc == 1)), [qT[b], KT[s]], [p])
            k.op("dve", lambda e, p=p: e.tensor_copy(out=ssb[b][:].rearrange("p a b -> p (a b)"), in_=p[:]), [p], [ssb[b]])
            k.op("dve", lambda e: e.tensor_reduce(out=mx[b][:], in_=ssb[b][:], axis=AX.X, op=ALU.max, negate=True),
                 [ssb[b]], [mx[b]])
            for hd in range(4):
                k.op("act", lambda e, hd=hd: e.activation(out=P[b][:, hd, :], in_=ssb[b][:, hd, :], func=AF.Exp,
                                                         bias=mx[b][:, hd:hd + 1], scale=1.0,
                                                         accum_out=sm[b][:, hd:hd + 1]), [ssb[b], mx[b]], [P[b], sm[b]])

        def stage_b(ti):
            b = ti % 2
            s = ti // tps
            hfb = hf[ti % 3]
            k.op("dve", lambda e: e.reciprocal(out=sm[b][:], in_=sm[b][:]), [sm[b]], [sm[b]])
            transpose_into(c, lambda: PT[b][:].rearrange("p a b -> p (a b)"),
                           lambda ci: P[b][:, ci // 2, (ci % 2) * 128:(ci % 2 + 1) * 128], 8, P[b], PT[b], evac="dve")
            p = psf(c)
            for hd in range(4):
                for mc in range(2):
                    k.op("pe", lambda e, hd=hd, mc=mc, p=p: e.matmul(
                        out=p[:, hd * 256:(hd + 1) * 256], lhsT=PT[b][:, 2 * hd + mc, :], rhs=V[s][:, mc, hd * 256:(hd + 1) * 256],
                        start=(mc == 0), stop=(mc == 1)), [PT[b], V[s]], [p])
            for hd in range(4):
                k.op("act", lambda e, hd=hd, p=p: e.activation(out=ob[b][:, hd * 256:(hd + 1) * 256],
                                                              in_=p[:, hd * 256:(hd + 1) * 256], func=AF.Copy,
                                                              scale=sm[b][:, hd:hd + 1]), [p, sm[b]], [ob[b]])
            transpose_into(c, lambda: oT[b][:].rearrange("p a b -> p (a b)"),
                           lambda ci: ob[b][:, ci * 128:(ci + 1) * 128], 8, ob[b], oT[b], evac="act")
            p = psf(c)
            for nb in range(2):
                for kc in range(8):
                    k.op("pe", lambda e, nb=nb, kc=kc, p=p: e.matmul(
                        out=p[:, nb * 512:(nb + 1) * 512], lhsT=oT[b][:, kc, :], rhs=wo[:, kc, nb * 512:(nb + 1) * 512],
                        start=(kc == 0), stop=(kc == 7)), [oT[b], wo], [p])
            tl.run(hfb, lambda hf_, p=p: p[:, hf_ * 512:(hf_ + 1) * 512], p, h_out, ti)

        load(0)
        for ti in range(c.NTILES + 1):
            if ti < c.NTILES:
                stage_a(ti)
            if ti >= 1:
                stage_b(ti - 1)


def phase_peer(c, li, h_in, h_out):
    k = c.k
    NS = 16
    GRP = 8
    UV = k.dram("pk_UV%d" % li, [16384, 2048], BF16)
    with k.scope():
        stf = [k.sb([128, 8192], F32, "stf") for _ in range(2)]
        stb = [k.sb([128, 8192], BF16, "stb") for _ in range(2)]
        it = 0
        for which, nm in enumerate(("pk_u", "pk_v")):
            src = c.inp[nm][li]
            for r0 in range(0, 16384, 1024):
                a, bb = stf[it % 2], stb[it % 2]
                k.dma("sp", a[:].rearrange("p (r d) -> p r d", r=8), src[r0:r0 + 1024, :].rearrange("(p r) d -> p r d", r=8), writes=[a])
                if it % 2 == 0:
                    k.op("act", lambda e, a=a, bb=bb: e.copy(out=bb[:], in_=a[:]), [a], [bb])
                else:
                    k.op("dve", lambda e, a=a, bb=bb: e.tensor_copy(out=bb[:], in_=a[:]), [a], [bb])
                k.dma("sp", UV[r0:r0 + 1024, which * 1024:(which + 1) * 1024].rearrange("(p r) d -> p r d", r=8),
                      bb[:].rearrange("p (r d) -> p r d", r=8), reads=[bb])
                it += 1
    with k.scope():
        wqy = k.sb([128, 8, 2048], BF16, "wqy")
        load_w(c, wqy, c.inp["pk_w_query"][li], 1024, 2048)
        tl = Tail(c, li, 2)
        iota16 = k.sb([128, 16], F32, "iota16")
        k.dma("sp", iota16[:], c.inp["c_iota16"][:, :], writes=[iota16])
        skf = k.sb([128, 128], F32, "skf")
        skb = k.sb([128, 128], BF16, "skb")
        skT = k.sb([128, 2, 128], BF16, "skT")
        for j in range(2):
            k.dma("sp", skf[:], c.inp["pk_sub_keys"][li, j], writes=[skf])
            k.op("dve", lambda e: e.tensor_copy(out=skb[:], in_=skf[:]), [skf], [skb])
            transpose_into(c, lambda j=j: skT[:, j, :], lambda ci: skb[:, :], 1, skb, skT)
        hf = [k.sb([128, D], F32, "hf") for _ in range(2)]
        hb = k.sb([128, D], BF16, "hb")
        hT = k.sb([128, 8, 128], BF16, "hT")
        qT = k.sb([128, 16, 128], BF16, "qT")
        ssb = k.sb([128, 16, 128], F32, "ssb")
        tmp = k.sb([128, 16, 128], F32, "tmp")
        sv = k.sb([128, 16, 16], F32, "sv")
        si = k.sb([128, 16, 16], U32, "si")
        sif = k.sb([128, 16, 16], F32, "sif")
        cand = k.sb([128, 8, 16, 16], F32, "cand")
        tmpc = k.sb([128, 8, 256], F32, "tmpc")
        cv = k.sb([128, 8, 16], F32, "cv")
        ci = k.sb([128, 8, 16], U32, "ci")
        cab = [k.sb([128, 8, 16], U32, "cab") for _ in range(2)]
        cabf = [k.sb([128, 8, 16], F32, "cabf") for _ in range(2)]
        oh = k.sb([128, 8, 16, 16], F32, "oh")
        k12 = [k.sb([128, 8, 16], F32, "k12") for _ in range(2)]
        eif = k.sb([128, 8, 16], F32, "eif")
        eiu = [k.sb([128, 8, 16], U32, "eiu") for _ in range(2)]
        gt = [k.sb([128, 8, 16], F32, "gt") for _ in range(2)]
        zs = k.sb([128, 8], F32, "zs")
        dots = [k.sb([128, 128], F32, "dots") for _ in range(2)]
        actv = [k.sb([128, 128], F32, "actv") for _ in range(2)]
        junk = k.sb([128, D], BF16, "junk")
        uv = [k.sb([128, 2048], BF16, "uv") for _ in range(NS)]
        dg = [k.sb([128, 128], BF16, "dg") for _ in range(NS)]
        pacc = c.pf[2]
        c.pf_n = 2

        def load(ti):
            k.dma("sp", hf[ti % 2][:], h_in[ti * 128:(ti + 1) * 128, :], writes=[hf[ti % 2]])

        def proj(ti):
            hfp = hf[ti % 2]
            k.op("act", lambda e: e.copy(out=hb[:], in_=hfp[:]), [hfp], [hb])
            transpose_into(c, lambda: hT[:].rearrange("p a b -> p (a b)"), lambda ci_: hb[:, ci_ * 128:(ci_ + 1) * 128], 8, hb, hT)
            for half in range(2):
                p = psf(c)
                for fc in range(8):
                    for kc in range(8):
                        k.op("pe", lambda e, fc=fc, kc=kc, p=p, half=half: e.matmul(
                            out=p[:, fc * 128:(fc + 1) * 128], lhsT=wqy[:, kc, (half * 8 + fc) * 128:(half * 8 + fc + 1) * 128],
                            rhs=hT[:, kc, :], start=(kc == 0), stop=(kc == 7)), [wqy, hT], [p])
                k.op("act", lambda e, p=p, half=half: e.copy(out=qT[:, half * 8:(half + 1) * 8, :].rearrange("p a b -> p (a b)"),
                                                            in_=p[:]), [p], [qT])
            for half in range(2):
                p = psf(c)
                for fc in range(8):
                    cidx = half * 8 + fc
                    k.op("pe", lambda e, fc=fc, cidx=cidx, p=p: e.matmul(
                        out=p[:, fc * 128:(fc + 1) * 128], lhsT=qT[:, cidx, :], rhs=skT[:, cidx % 2, :], start=True, stop=True),
                        [qT, skT], [p])
                k.op("act", lambda e, p=p, half=half: e.copy(out=ssb[:, half * 8:(half + 1) * 8, :].rearrange("p a b -> p (a b)"),
                                                            in_=p[:]), [p], [ssb])

        load(0)
        proj(0)
        for ti in range(c.NTILES):
            if ti + 1 < c.NTILES:
                load(ti + 1)
            b = ti % 2
            hfb = hf[b]
            gtb, dt_, av_ = gt[b], dots[b], actv[b]
            for cc in range(16):
                k.op("dve", lambda e, cc=cc: e.max(out=sv[:, cc, 0:8], in_=ssb[:, cc, :]), [ssb], [sv])
                k.op("dve", lambda e, cc=cc: e.match_replace(out=tmp[:, cc, :], in_to_replace=sv[:, cc, 0:8], in_values=ssb[:, cc, :],
                                                             imm_value=-1e30), [ssb, sv], [tmp])
                k.op("dve", lambda e, cc=cc: e.max(out=sv[:, cc, 8:16], in_=tmp[:, cc, :]), [tmp], [sv])
                k.op("dve", lambda e, cc=cc: e.max_index(out=si[:, cc, 0:8], in_max=sv[:, cc, 0:8], in_values=ssb[:, cc, :]),
                     [ssb, sv], [si])
                k.op("dve", lambda e, cc=cc: e.max_index(out=si[:, cc, 8:16], in_max=sv[:, cc, 8:16], in_values=ssb[:, cc, :]),
                     [ssb, sv], [si])
            k.op("dve", lambda e: e.tensor_copy(out=sif[:], in_=si[:]), [si], [sif])
            for h in range(8):
                cflat = lambda h=h: cand[:, h, :, :].rearrange("p a b -> p (a b)")
                k.op("dve", lambda e, h=h: e.tensor_tensor(out=cand[:, h, :, :], in0=sv[:, 2 * h, :].unsqueeze(2).broadcast_to([128, 16, 16]),
                                                           in1=sv[:, 2 * h + 1, :].unsqueeze(1).broadcast_to([128, 16, 16]), op=ALU.add),
                     [sv], [cand])
                k.op("dve", lambda e, h=h, cflat=cflat: e.max(out=cv[:, h, 0:8], in_=cflat()), [cand], [cv])
                k.op("dve", lambda e, h=h, cflat=cflat: e.match_replace(out=tmpc[:, h, :], in_to_replace=cv[:, h, 0:8], in_values=cflat(),
                                                                       imm_value=-1e30), [cand, cv], [tmpc])
                k.op("dve", lambda e, h=h: e.max(out=cv[:, h, 8:16], in_=tmpc[:, h, :]), [tmpc], [cv])
                k.op("dve", lambda e, h=h, cflat=cflat: e.max_index(out=ci[:, h, 0:8], in_max=cv[:, h, 0:8], in_values=cflat()),
                     [cand, cv], [ci])
                k.op("dve", lambda e, h=h, cflat=cflat: e.max_index(out=ci[:, h, 8:16], in_max=cv[:, h, 8:16], in_values=cflat()),
                     [cand, cv], [ci])
            k.op("dve", lambda e: e.tensor_single_scalar(out=cab[0][:], in_=ci[:], scalar=4, op=ALU.logical_shift_right), [ci], [cab[0]])
            k.op("dve", lambda e: e.tensor_single_scalar(out=cab[1][:], in_=ci[:], scalar=15, op=ALU.bitwise_and), [ci], [cab[1]])
            for j in range(2):
                k.op("dve", lambda e, j=j: e.tensor_copy(out=cabf[j][:], in_=cab[j][:]), [cab[j]], [cabf[j]])
                k.op("dve", lambda e, j=j: e.tensor_tensor(
                    out=oh[:], in0=cabf[j][:].unsqueeze(3).broadcast_to([128, 8, 16, 16]),
                    in1=iota16[:].unsqueeze(1).unsqueeze(1).broadcast_to([128, 8, 16, 16]), op=ALU.is_equal), [cabf[j], iota16], [oh])
                k.op("dve", lambda e, j=j: e.tensor_tensor(
                    out=oh[:], in0=oh[:], in1=sif[:, j::2, :].unsqueeze(2).broadcast_to([128, 8, 16, 16]), op=ALU.mult), [oh, sif], [oh])
                k.op("dve", lambda e, j=j: e.tensor_reduce(out=k12[j][:], in_=oh[:], axis=AX.X, op=ALU.add), [oh], [k12[j]])
            k.op("dve", lambda e: e.scalar_tensor_tensor(out=eif[:].rearrange("p a b -> p (a b)"), in0=k12[0][:].rearrange("p a b -> p (a b)"),
                                                         scalar=128.0, in1=k12[1][:].rearrange("p a b -> p (a b)"),
                                                         op0=ALU.mult, op1=ALU.add), [k12[0], k12[1]], [eif])
            eb = eiu[b]
            k.op("dve", lambda e: e.tensor_tensor(out=gtb[:], in0=cv[:], in1=cv[:, :, 0:1].broadcast_to([128, 8, 16]), op=ALU.subtract),
                 [cv], [gtb])
            k.op("act", lambda e: e.activation(out=gtb[:], in_=gtb[:], func=AF.Exp), [gtb], [gtb])
            k.op("dve", lambda e: e.tensor_reduce(out=zs[:], in_=gtb[:], axis=AX.X, op=ALU.add), [gtb], [zs])
            k.op("dve", lambda e: e.reciprocal(out=zs[:], in_=zs[:]), [zs], [zs])
            k.op("dve", lambda e: e.tensor_tensor(out=gtb[:], in0=gtb[:], in1=zs[:].unsqueeze(2).broadcast_to([128, 8, 16]), op=ALU.mult),
                 [gtb, zs], [gtb])
            k.op("dve", lambda e: e.tensor_copy(out=eb[:], in_=eif[:]), [eif], [eb])
            for g0 in range(0, 128, GRP):
                if g0 == 2 * GRP and ti + 1 < c.NTILES:
                    proj(ti + 1)
                for hk in range(g0, g0 + GRP):
                    slot = uv[hk % NS]
                    gather(c, slot, UV.t[:, :], eb, hk // 16, hk % 16)
                    k.op("dve", lambda e, slot=slot, hk=hk: e.scalar_tensor_tensor(out=junk[:], in0=slot[:, 0:1024], scalar=1.0, in1=hfb[:], op0=ALU.mult,
                                                                                   op1=ALU.mult, accum_out=dt_[:, hk:hk + 1]),
                         [slot, hfb], [junk, dt_])
                k.op("act", lambda e, g0=g0: e.activation(out=av_[:, g0:g0 + GRP], in_=dt_[:, g0:g0 + GRP], func=AF.Gelu_apprx_tanh), [dt_], [av_])
                k.op("dve", lambda e, g0=g0: e.tensor_tensor(out=av_[:, g0:g0 + GRP], in0=av_[:, g0:g0 + GRP],
                                                             in1=gtb[:].rearrange("p a b -> p (a b)")[:, g0:g0 + GRP], op=ALU.mult), [av_, gtb], [av_])
                for hk in range(g0, g0 + GRP):
                    slot = uv[hk % NS]
                    dgs = dg[hk % NS]
                    k.op("act", lambda e, dgs=dgs, hk=hk: e.activation(out=dgs[:], in_=c.identb[:], func=AF.Copy, scale=av_[:, hk:hk + 1]), [c.identb, av_], [dgs])
                    for nb in range(2):
                        k.op("pe", lambda e, dgs=dgs, slot=slot, nb=nb, hk=hk: e.matmul(out=pacc[:, nb * 512:(nb + 1) * 512], lhsT=dgs[:],
                                                                                       rhs=slot[:, 1024 + nb * 512:1024 + (nb + 1) * 512],
                                                                                       start=(hk == 0), stop=(hk == 127)), [dgs, slot], [pacc])
            tl.run(hfb, lambda hf_: pacc[:, hf_ * 512:(hf_ + 1) * 512], pacc, h_out, ti)
        c.pf_n = 3


def gather(c, dst, table, eb, h, kk):
    k = c.k
    qn = "pool"
    reads = k._res([eb])
    writes = k._res([dst])
    i = k.dnext[qn]
    k.dnext[qn] = (i + 1) % NDSEM
    key = (qn, i)
    if k.dval[qn][i] > 0:
        k._wait(qn, (key, k.dval[qn][i]))
    k._deps(qn, reads, writes)
    k.dval[qn][i] += 16
    tok = (key, k.dval[qn][i])
    k.nc.gpsimd.indirect_dma_start(out=dst[:], out_offset=None, in_=table,
                                   in_offset=bass.IndirectOffsetOnAxis(ap=eb[:, h, kk:kk + 1], axis=0)).then_inc(k.dsem[qn][i], 16)
    k._commit(tok, reads, writes)
    k.n_ins += 1


def phase_s5(c, li, h_in, h_out):
    k = c.k
    T_ = c.T
    NB = T_ // 512
    nlev = int(np.log2(T_))
    GT = k.dram("s5_GT", [c.NSEQ, 8, 128, T_], BF16)
    TWO_PI = 2.0 * np.pi
    with k.scope():
        Jm = k.sb([128, 128], F32, "Jm")
        sgn = k.sb([128, 1], F32, "sgn")
        gmask = k.sb([128, 8], F32, "gmask")
        cmask = k.sb([128, 8, 128], F32, "cmask")
        k.dma("sp", Jm[:], c.inp["c_J"][:, :], writes=[Jm])
        k.dma("sp", sgn[:], c.inp["c_sgn"][:, :], writes=[sgn])
        k.dma("sp", gmask[:], c.inp["c_gmask"][:, :], writes=[gmask])
        k.dma("sp", cmask[:], c.inp["c_cmask"][:, :, :], writes=[cmask])
        ls = k.sb([128, 128], F32, "ls")
        k.op("dve", lambda e: e.memset(ls[:], 0.0), [], [ls])
        lre = k.sb([128, 64], F32, "lre")
        lim = k.sb([128, 64], F32, "lim")
        for nm, dst in (("s5_lam_re", lre), ("s5_lam_im", lim)):
            for hh in range(2):
                k.dma("sp", ls[0:64, hh * 64:(hh + 1) * 64], c.inp[nm][0], writes=[ls])
            p = psf(c)
            k.op("pe", lambda e, p=p: e.transpose(out=p[:, 0:128], in_=ls[:, :], identity=c.identf[:]), [ls, c.identf], [p])
            k.op("dve", lambda e, p=p, dst=dst: e.tensor_copy(out=dst[:], in_=p[:, 0:64]), [p], [dst])
        dt = k.sb([128, 64], F32, "dt")
        load_bc(c, dt, c.inp["s5_log_dt"][0:1, :])
        k.op("act", lambda e: e.activation(out=dt[:], in_=dt[:], func=AF.Exp), [dt], [dt])
        emag = k.sb([128, 64], F32, "emag")
        ang = k.sb([128, 64], F32, "ang")
        k.op("dve", lambda e: e.tensor_tensor(out=emag[:], in0=lre[:], in1=dt[:], op=ALU.mult), [lre, dt], [emag])
        k.op("act", lambda e: e.activation(out=emag[:], in_=emag[:], func=AF.Exp), [emag], [emag])
        k.op("dve", lambda e: e.tensor_tensor(out=ang[:], in0=lim[:], in1=dt[:], op=ALU.mult), [lim, dt], [ang])
        acol = k.sb([128, 64], F32, "acol")
        bcol = k.sb([128, 64], F32, "bcol")
        yv = k.sb([128, 64], F32, "yv")
        yi = k.sb([128, 64], I32, "yi")
        yf = k.sb([128, 64], F32, "yf")
        mk = k.sb([128, 64], F32, "mk")
        for off, dst in ((0.25, acol), (0.0, bcol)):
            k.op("dve", lambda e, off=off: e.tensor_scalar(out=yv[:], in0=ang[:], scalar1=1.0 / TWO_PI, scalar2=off, op0=ALU.mult, op1=ALU.add),
                 [ang], [yv])
            k.op("dve", lambda e: e.tensor_copy(out=yi[:], in_=yv[:]), [yv], [yi])
            k.op("dve", lambda e: e.tensor_copy(out=yf[:], in_=yi[:]), [yi], [yf])
            k.op("dve", lambda e: e.tensor_tensor(out=yv[:], in0=yv[:], in1=yf[:], op=ALU.subtract), [yv, yf], [yv])
            k.op("dve", lambda e: e.tensor_single_scalar(out=mk[:], in_=yv[:], scalar=0.5, op=ALU.is_gt), [yv], [mk])
            k.op("dve", lambda e: e.tensor_tensor(out=yv[:], in0=yv[:], in1=mk[:], op=ALU.subtract), [yv, mk], [yv])
            k.op("dve", lambda e: e.tensor_single_scalar(out=mk[:], in_=yv[:], scalar=-0.5, op=ALU.is_lt), [yv], [mk])
            k.op("dve", lambda e: e.tensor_tensor(out=yv[:], in0=yv[:], in1=mk[:], op=ALU.add), [yv, mk], [yv])
            k.op("act", lambda e, dst=dst: e.activation(out=dst[:], in_=yv[:], func=AF.Sin, scale=TWO_PI), [yv], [dst])
            k.op("dve", lambda e, dst=dst: e.tensor_tensor(out=dst[:], in0=dst[:], in1=emag[:], op=ALU.mult), [dst, emag], [dst])
        am1 = k.sb([128, 64], F32, "am1")
        d2 = k.sb([128, 64], F32, "d2")
        t1 = k.sb([128, 64], F32, "t1")
        cr = k.sb([128, 64], F32, "cr")
        cis = k.sb([128, 64], F32, "cis")
        k.op("dve", lambda e: e.tensor_scalar(out=am1[:], in0=acol[:], scalar1=-1.0, scalar2=None, op0=ALU.add), [acol], [am1])
        k.op("dve", lambda e: e.tensor_tensor(out=d2[:], in0=lre[:], in1=lre[:], op=ALU.mult), [lre], [d2])
        k.op("dve", lambda e: e.tensor_tensor(out=t1[:], in0=lim[:], in1=lim[:], op=ALU.mult), [lim], [t1])
        k.op("dve", lambda e: e.tensor_tensor(out=d2[:], in0=d2[:], in1=t1[:], op=ALU.add), [d2, t1], [d2])
        k.op("dve", lambda e: e.reciprocal(out=d2[:], in_=d2[:]), [d2], [d2])
        k.op("dve", lambda e: e.tensor_tensor(out=cr[:], in0=am1[:], in1=lre[:], op=ALU.mult), [am1, lre], [cr])
        k.op("dve", lambda e: e.tensor_tensor(out=t1[:], in0=bcol[:], in1=lim[:], op=ALU.mult), [bcol, lim], [t1])
        k.op("dve", lambda e: e.tensor_tensor(out=cr[:], in0=cr[:], in1=t1[:], op=ALU.add), [cr, t1], [cr])
        k.op("dve", lambda e: e.tensor_tensor(out=cr[:], in0=cr[:], in1=d2[:], op=ALU.mult), [cr, d2], [cr])
        k.op("dve", lambda e: e.tensor_tensor(out=cis[:], in0=bcol[:], in1=lre[:], op=ALU.mult), [bcol, lre], [cis])
        k.op("dve", lambda e: e.tensor_tensor(out=t1[:], in0=am1[:], in1=lim[:], op=ALU.mult), [am1, lim], [t1])
        k.op("dve", lambda e: e.tensor_tensor(out=cis[:], in0=cis[:], in1=t1[:], op=ALU.subtract), [cis, t1], [cis])
        k.op("dve", lambda e: e.tensor_tensor(out=cis[:], in0=cis[:], in1=d2[:], op=ALU.mult), [cis, d2], [cis])
        k.op("dve", lambda e: e.tensor_scalar(out=cis[:], in0=cis[:], scalar1=sgn[:, 0:1], scalar2=None, op0=ALU.mult), [cis, sgn], [cis])
        if 's5pre0' in DBG:
            return
        BA = k.sb([128, 64, 16], F32, "BA")
        BB = k.sb([128, 64, 16], F32, "BB")
        bre = c.inp["s5_b_re"][0].rearrange("g p c -> p g c")
        bim = c.inp["s5_b_im"][0].rearrange("g p c -> p g c")
        k.dma("sp", BA[0:64], bre, writes=[BA])
        k.dma("sp", BA[64:128], bim, writes=[BA])
        k.dma("sp", BB[0:64], bim, writes=[BB])
        k.dma("sp", BB[64:128], bre, writes=[BB])
        k.op("dve", lambda e: e.tensor_tensor(out=BA[:], in0=BA[:], in1=cr[:].unsqueeze(2).broadcast_to([128, 64, 16]), op=ALU.mult), [BA, cr], [BA])
        k.op("dve", lambda e: e.tensor_tensor(out=BB[:], in0=BB[:], in1=cis[:].unsqueeze(2).broadcast_to([128, 64, 16]), op=ALU.mult), [BB, cis], [BB])
        k.op("dve", lambda e: e.tensor_tensor(out=BA[:], in0=BA[:], in1=BB[:], op=ALU.add), [BA, BB], [BA])
        W0 = k.sb([128, 64, 128], BF16, "W0")
        WC = k.sb([128, 64, 128], BF16, "WC")
        CN = k.sb([128, 8, 128], F32, "CN")
        k.dma("sp", CN[:, :, 0:64], c.inp["s5_c_re"][0].rearrange("(cc g) c p -> (g c) cc p", cc=8), writes=[CN])
        k.dma("sp", CN[:, :, 64:128], c.inp["s5_c_im"][0].rearrange("(cc g) c p -> (g c) cc p", cc=8), writes=[CN])
        k.op("dve", lambda e: e.tensor_scalar(out=CN[:, :, 64:128], in0=CN[:, :, 64:128], scalar1=-1.0, scalar2=None, op0=ALU.mult), [CN], [CN])
        tpf = k.sb([128, 128], F32, "tpf")
        for cc in range(8):
            p = psf(c)
            k.op("pe", lambda e, p=p, cc=cc: e.transpose(out=p[:, 0:128], in_=BA[:, cc * 8:(cc + 1) * 8, :].rearrange("p g c -> p (g c)"),
                                                        identity=c.identf[:]), [BA, c.identf], [p])
            k.op("act", lambda e, p=p: e.copy(out=tpf[:], in_=p[:, 0:128]), [p], [tpf])
            for j in range(8):
                k.op("dve", lambda e, cc=cc, j=j: e.tensor_scalar(out=W0[:, cc * 8 + j, :], in0=tpf[:], scalar1=gmask[:, j:j + 1], scalar2=None,
                                                                  op0=ALU.mult), [tpf, gmask], [W0])
            p = psf(c)
            k.op("pe", lambda e, p=p, cc=cc: e.transpose(out=p[:, 0:128], in_=CN[:, cc, :], identity=c.identf[:]), [CN, c.identf], [p])
            k.op("act", lambda e, p=p: e.copy(out=tpf[:], in_=p[:, 0:128]), [p], [tpf])
            for j in range(8):
                k.op("pool", lambda e, cc=cc, j=j: e.tensor_tensor(out=WC[:, cc * 8 + j, :], in0=tpf[:], in1=cmask[:, j, :], op=ALU.mult),
                     [tpf, cmask], [WC])
        dcol = k.sb([128, 8], F32, "dcol")
        k.dma("sp", dcol[:], c.inp["s5_d"][0].rearrange("(cc p) -> p cc", p=128), writes=[dcol], allow_slow_non_contiguous=True) if False else None
        dtmp = k.sb([128, 128], F32, "dtmp")
        k.op("dve", lambda e: e.memset(dtmp[:], 0.0), [], [dtmp])
        k.dma("sp", dtmp[0:8, :], c.inp["s5_d"][0].rearrange("(cc p) -> cc p", p=128), writes=[dtmp])
        p = psf(c)
        k.op("pe", lambda e, p=p: e.transpose(out=p[:, 0:128], in_=dtmp[:, :], identity=c.identf[:]), [dtmp, c.identf], [p])
        k.op("dve", lambda e, p=p: e.tensor_copy(out=dcol[:], in_=p[:, 0:8]), [p], [dcol])
        if 's5prep' in DBG:
            return
        xst = k.sb([128, T_ // 128, 128], F32, "xst")
        xTf = k.sb([128, T_], F32, "xTf")
        xTb = k.sb([128, T_], BF16, "xTb")
        SA = [k.sb([128, T_], BF16, "SA") for _ in range(8)]
        SB = [k.sb([128, T_], BF16, "SB") for _ in range(2)]
        Xf = [k.sb([128, 128], F32, "Xf") for _ in range(2)]
        XTf = [k.sb([128, 128], F32, "XTf") for _ in range(2)]
        PK = [k.sb([128, nlev, 128], BF16, "PK") for _ in range(2)]
        gtb = [k.sb([128, 512], BF16, "gtb") for _ in range(2)]
        ytmp = [k.sb([128, 512], F32, "ytmp") for _ in range(2)]
        s5tmp = [k.sb([128, 1024], BF16, "s5tmp") for _ in range(2)]
        ev = 0
        for s in range(c.NSEQ):
            for cc in range(8):
                k.dma("sp", xst[:], h_in[s * T_:(s + 1) * T_, cc * 128:(cc + 1) * 128].rearrange("(n p) c -> p n c", p=128), writes=[xst])
                if 's5ma' in DBG:
                    return
                for n4 in range(T_ // 512):
                    p = psf(c)
                    for q4 in range(4):
                        n = n4 * 4 + q4
                        k.op("pe", lambda e, p=p, n=n, q4=q4: e.transpose(out=p[:, q4 * 128:(q4 + 1) * 128], in_=xst[:, n, :], identity=c.identf[:]),
                             [xst, c.identf], [p])
                    if 's5mb' in DBG:
                        return
                    k.op("act", lambda e, p=p, n4=n4: e.copy(out=xTf[:, n4 * 512:(n4 + 1) * 512], in_=p[:, 0:512]), [p], [xTf])
                    if 's5mc' in DBG:
                        return
                    k.op("dve", lambda e, n4=n4: e.tensor_copy(out=xTb[:, n4 * 512:(n4 + 1) * 512], in_=xTf[:, n4 * 512:(n4 + 1) * 512]), [xTf], [xTb])
                if 's5m1' in DBG:
                    return
                finals = [None] * 8
                state = {}

                def powers(j):
                    g = cc * 8 + j
                    gi = g % 2
                    X, XT, pk = Xf[gi], XTf[gi], PK[gi]
                    k.op("dve", lambda e: e.tensor_scalar(out=X[:], in0=c.identf[:], scalar1=acol[:, g:g + 1], scalar2=None, op0=ALU.mult),
                         [c.identf, acol], [X])
                    k.op("dve", lambda e: e.tensor_copy(out=XT[:], in_=X[:]), [X], [XT])
                    k.op("dve", lambda e: e.scalar_tensor_tensor(out=X[:], in0=Jm[:], scalar=bcol[:, g:g + 1], in1=X[:], op0=ALU.mult, op1=ALU.add),
                         [Jm, bcol, X], [X])
                    k.op("dve", lambda e: e.tensor_scalar(out=mk[:, gi:gi + 1], in0=bcol[:, g:g + 1], scalar1=-1.0, scalar2=None, op0=ALU.mult),
                         [bcol], [mk])
                    k.op("dve", lambda e: e.scalar_tensor_tensor(out=XT[:], in0=Jm[:], scalar=mk[:, gi:gi + 1], in1=XT[:], op0=ALU.mult, op1=ALU.add),
                         [Jm, mk, XT], [XT])
                    for lv in range(nlev):
                        k.op("act", lambda e, lv=lv: e.copy(out=pk[:, lv, :], in_=XT[:]), [XT], [pk])
                        if lv + 1 < nlev:
                            p = psf(c)
                            k.op("pe", lambda e, p=p: e.matmul(out=p[:, 0:128], lhsT=XT[:], rhs=X[:], start=True, stop=True), [X, XT], [p])
                            k.op("pe", lambda e, p=p: e.matmul(out=p[:, 512:640], lhsT=X[:], rhs=XT[:], start=True, stop=True), [X, XT], [p])
                            k.op("dve", lambda e, p=p: e.tensor_copy(out=X[:], in_=p[:, 0:128]), [p], [X])
                            k.op("act", lambda e, p=p: e.copy(out=XT[:], in_=p[:, 512:640]), [p], [XT])

                def level0(j):
                    nonlocal ev
                    g = cc * 8 + j
                    gi = g % 2
                    cur, oth = (SA[j], SB[gi]) if nlev % 2 == 0 else (SB[gi], SA[j])
                    for hb_ in range(0, NB, 2):
                        p = psf(c)
                        for q2 in range(2):
                            blk = hb_ + q2
                            if blk >= NB:
                                continue
                            k.op("pe", lambda e, p=p, q2=q2, blk=blk: e.matmul(out=p[:, q2 * 512:(q2 + 1) * 512], lhsT=W0[:, g, :],
                                                                               rhs=xTb[:, blk * 512:(blk + 1) * 512], start=True, stop=True),
                                 [W0, xTb], [p])
                        w = min(2, NB - hb_) * 512
                        if ev % 2 == 0:
                            k.op("act", lambda e, p=p, hb_=hb_, w=w: e.copy(out=cur[:, hb_ * 512:hb_ * 512 + w], in_=p[:, 0:w]), [p], [cur])
                        else:
                            k.op("dve", lambda e, p=p, hb_=hb_, w=w: e.tensor_copy(out=cur[:, hb_ * 512:hb_ * 512 + w], in_=p[:, 0:w]), [p], [cur])
                        ev += 1
                    state[j] = (cur, oth)

                def level(j, lv):
                    nonlocal ev
                    g = cc * 8 + j
                    gi = g % 2
                    pk = PK[gi]
                    cur, oth = state[j]
                    sh = 1 << lv
                    for hb_ in range(0, NB, 2):
                        p = psf(c)
                        use_act = (ev % 3 == 2)
                        ev += 1
                        for q2 in range(2):
                            blk = hb_ + q2
                            if blk >= NB:
                                continue
                            t0 = blk * 512
                            lo = max(t0, sh)
                            has2 = lo < t0 + 512
                            if use_act:
                                k.op("pe", lambda e, p=p, q2=q2, t0=t0, has2=has2: e.matmul(
                                    out=p[:, q2 * 512:(q2 + 1) * 512], lhsT=c.identb[:], rhs=cur[:, t0:t0 + 512], start=True, stop=not has2),
                                    [c.identb, cur], [p])
                            if has2:
                                k.op("pe", lambda e, p=p, q2=q2, t0=t0, lo=lo: e.matmul(
                                    out=p[:, q2 * 512 + (lo - t0):(q2 + 1) * 512], lhsT=pk[:, lv, :], rhs=cur[:, lo - sh:t0 + 512 - sh],
                                    start=not use_act, stop=True), [pk, cur], [p])
                        w = min(2, NB - hb_) * 512
                        c0 = hb_ * 512
                        if use_act:
                            k.op("act", lambda e, p=p, c0=c0, w=w: e.copy(out=oth[:, c0:c0 + w], in_=p[:, 0:w]), [p], [oth])
                        else:
                            lo_all = min(max(c0, sh), c0 + w)
                            if lo_all > c0:
                                k.op("dve", lambda e, c0=c0, lo_all=lo_all: e.tensor_copy(out=oth[:, c0:lo_all], in_=cur[:, c0:lo_all]), [cur], [oth])
                            if lo_all < c0 + w:
                                k.op("dve", lambda e, p=p, c0=c0, lo_all=lo_all, w=w: e.tensor_tensor(
                                    out=oth[:, lo_all:c0 + w], in0=p[:, lo_all - c0:w], in1=cur[:, lo_all:c0 + w], op=ALU.add), [p, cur], [oth])
                    state[j] = (oth, cur)

                for j0 in range(0, 8, 2):
                    for j in (j0, j0 + 1):
                        powers(j)
                    for j in (j0, j0 + 1):
                        level0(j)
                    for lv in range(nlev):
                        for j in (j0, j0 + 1):
                            level(j, lv)
                    for j in (j0, j0 + 1):
                        finals[j] = state[j][0]
                if 's5m4' in DBG:
                    return
                for blk in range(NB):
                    p = psf(c)
                    for j in range(8):
                        k.op("pe", lambda e, p=p, j=j, blk=blk, cc=cc: e.matmul(out=p[:, 0:512], lhsT=WC[:, cc * 8 + j, :],
                                                                                rhs=finals[j][:, blk * 512:(blk + 1) * 512], start=(j == 0), stop=(j == 7)),
                             [WC, finals[j]], [p])
                    yt = ytmp[blk % 2]
                    gb_ = gtb[blk % 2]
                    k.op("dve", lambda e, p=p, yt=yt, blk=blk, cc=cc: e.scalar_tensor_tensor(out=yt[:], in0=xTf[:, blk * 512:(blk + 1) * 512], scalar=dcol[:, cc:cc + 1],
                                                                                             in1=p[:, 0:512], op0=ALU.mult, op1=ALU.add), [xTf, dcol, p], [yt])
                    k.op("act", lambda e, yt=yt, gb_=gb_: e.activation(out=gb_[:], in_=yt[:], func=AF.Gelu_apprx_tanh), [yt], [gb_])
                    k.dma("sp", GT[s, cc, :, blk * 512:(blk + 1) * 512], gb_[:], reads=[gb_])
    if 's5m5' in DBG:
        return
    with k.scope():
        wg = k.sb([128, 8, 2048], BF16, "wg")
        load_w(c, wg, c.inp["s5_w_glu"][0], 1024, 2048)
        bg = k.sb([128, 2048], F32, "bg")
        load_bc(c, bg, c.inp["s5_b_glu"][0:1, :])
        tl = Tail(c, li, 0)
        hf = [k.sb([128, D], F32, "hf") for _ in range(2)]
        gT = [k.sb([128, 8, 128], BF16, "gT") for _ in range(2)]
        vg = [k.sb([128, 2048], F32, "vg") for _ in range(2)]
        tps = T_ // 128

        def load(ti):
            s, tt = ti // tps, ti % tps
            k.dma("sp", hf[ti % 2][:], h_in[ti * 128:(ti + 1) * 128, :], writes=[hf[ti % 2]])
            k.dma("sp", gT[ti % 2][:], GT[s, :, :, tt * 128:(tt + 1) * 128].rearrange("c p t -> p c t"), writes=[gT[ti % 2]])

        load(0)
        for ti in range(c.NTILES):
            if ti + 1 < c.NTILES:
                load(ti + 1)
            b = ti % 2
            for half in range(2):
                p = psf(c)
                for nb in range(2):
                    n0 = half * 1024 + nb * 512
                    for kc in range(8):
                        k.op("pe", lambda e, p=p, nb=nb, n0=n0, kc=kc, b=b: e.matmul(out=p[:, nb * 512:(nb + 1) * 512], lhsT=gT[b][:, kc, :],
                                                                                    rhs=wg[:, kc, n0:n0 + 512], start=(kc == 0), stop=(kc == 7)),
                             [gT[b], wg], [p])
                k.op("dve", lambda e, p=p, half=half, b=b: e.tensor_tensor(out=vg[b][:, half * 1024:(half + 1) * 1024], in0=p[:],
                                                                           in1=bg[:, half * 1024:(half + 1) * 1024], op=ALU.add), [p, bg], [vg[b]])
            k.op("act", lambda e, b=b: e.activation(out=vg[b][:, 1024:2048], in_=vg[b][:, 1024:2048], func=AF.Sigmoid), [vg[b]], [vg[b]])
            k.op("pool", lambda e, b=b: e.tensor_tensor(out=vg[b][:, 0:1024], in0=vg[b][:, 0:1024], in1=vg[b][:, 1024:2048], op=ALU.mult), [vg[b]], [vg[b]])
            tl.run(hf[b], lambda hf_, b=b: vg[b][:, hf_ * 512:(hf_ + 1) * 512], vg[b], h_out, ti)


def phase_da(c, li, h_in, h_out):
    k = c.k
    T_ = c.T
    NQ = T_ // 128
    lam_init = 0.8 - 0.6 * float(np.exp(-0.3 * li))
    QT = [k.dram("da_QT%d" % j, [c.NSEQ, 8, 128, T_], BF16) for j in range(2)]
    KTd = k.dram("da_KT", [c.NSEQ, 8, 128, T_], BF16)
    Vd = k.dram("da_V", [c.NT, 1024], BF16)
    AO = k.dram("da_AO", [c.NT, 1024], BF16)
    with k.scope():
        w = k.sb([128, 8, 3072], BF16, "wqkv")
        load_w(c, w, c.inp["da_w_qkv"][0], 1024, 3072)
        hmask = k.sb([128, 2], F32, "hmask")
        k.dma("sp", hmask[:], c.inp["c_hmask"][:, :], writes=[hmask])
        hf = [k.sb([128, D], F32, "hf") for _ in range(2)]
        hb = k.sb([128, D], BF16, "hb")
        hT = k.sb([128, 8, 128], BF16, "hT")
        qt = [[k.sb([128, 8, 128], BF16, "qt") for _ in range(2)] for _ in range(2)]
        kt = [k.sb([128, 8, 128], BF16, "kt") for _ in range(2)]
        vt = [k.sb([128, 1024], BF16, "vt") for _ in range(2)]
        tps = T_ // 128

        def load(ti):
            k.dma("sp", hf[ti % 2][:], h_in[ti * 128:(ti + 1) * 128, :], writes=[hf[ti % 2]])

        load(0)
        for ti in range(c.NTILES):
            if ti + 1 < c.NTILES:
                load(ti + 1)
            b = ti % 2
            s, tt = ti // tps, ti % tps
            k.op("dve", lambda e, b=b: e.tensor_copy(out=hb[:], in_=hf[b][:]), [hf[b]], [hb])
            transpose_into(c, lambda: hT[:].rearrange("p a b -> p (a b)"), lambda ci: hb[:, ci * 128:(ci + 1) * 128], 8, hb, hT)
            for part in range(2):
                p = psf(c)
                for fc in range(8):
                    for kc in range(8):
                        k.op("pe", lambda e, p=p, fc=fc, kc=kc, part=part: e.matmul(
                            out=p[:, fc * 128:(fc + 1) * 128], lhsT=w[:, kc, part * 1024 + fc * 128: part * 1024 + (fc + 1) * 128],
                            rhs=hT[:, kc, :], start=(kc == 0), stop=(kc == 7)), [w, hT], [p])
                if part == 0:
                    for j in range(2):
                        k.op("act", lambda e, p=p, j=j, b=b: e.activation(out=qt[j][b][:].rearrange("p a b -> p (a b)"), in_=p[:], func=AF.Copy,
                                                                         scale=hmask[:, j:j + 1]), [p, hmask], [qt[j][b]])
                        k.dma("sp", QT[j][s, :, :, tt * 128:(tt + 1) * 128].rearrange("h p t -> p h t"), qt[j][b][:], reads=[qt[j][b]])
                else:
                    k.op("dve", lambda e, p=p, b=b: e.tensor_copy(out=kt[b][:].rearrange("p a b -> p (a b)"), in_=p[:]), [p], [kt[b]])
                    k.dma("sp", KTd[s, :, :, tt * 128:(tt + 1) * 128].rearrange("h p t -> p h t"), kt[b][:], reads=[kt[b]])
            p = psf(c)
            for nb in range(2):
                for kc in range(8):
                    k.op("pe", lambda e, p=p, nb=nb, kc=kc: e.matmul(out=p[:, nb * 512:(nb + 1) * 512], lhsT=hT[:, kc, :],
                                                                    rhs=w[:, kc, 2048 + nb * 512:2048 + (nb + 1) * 512], start=(kc == 0), stop=(kc == 7)),
                         [w, hT], [p])
            k.op("act", lambda e, p=p, b=b: e.copy(out=vt[b][:], in_=p[:]), [p], [vt[b]])
            k.dma("sp", Vd[ti * 128:(ti + 1) * 128, :], vt[b][:], reads=[vt[b]])
    with k.scope():
        r0 = k.sb([128, T_], F32, "r0")
        k.dma("sp", r0[:], c.inp["c_r0"][:, :], writes=[r0])
        dbase = k.sb([128, 128], F32, "dbase")
        dmsk = k.sb([128, 128], F32, "dmsk")
        k.dma("sp", dbase[:], c.inp["c_dbase"][:, :], writes=[dbase])
        k.dma("sp", dmsk[:], c.inp["c_dmask"][:, :], writes=[dmsk])
        lm = k.sb([128, 4, 64], F32, "lm")
        k.dma("sp", lm[:].rearrange("p a b -> p (a b)"), c.inp["da_lambda"][0:1].rearrange("o a b -> o (a b)").broadcast_to([128, 256]), writes=[lm])
        lt = k.sb([128, 2, 64], F32, "lt")
        l2 = k.sb([128, 2], F32, "l2")
        nlam = k.sb([128, 1], F32, "nlam")
        k.op("dve", lambda e: e.tensor_tensor(out=lt[:], in0=lm[:, 0::2, :], in1=lm[:, 1::2, :], op=ALU.mult), [lm], [lt])
        k.op("dve", lambda e: e.tensor_reduce(out=l2[:], in_=lt[:], axis=AX.X, op=ALU.add), [lt], [l2])
        k.op("act", lambda e: e.activation(out=l2[:], in_=l2[:], func=AF.Exp), [l2], [l2])
        k.op("dve", lambda e: e.tensor_tensor(out=nlam[:], in0=l2[:, 1:2], in1=l2[:, 0:1], op=ALU.subtract), [l2], [nlam])
        k.op("dve", lambda e: e.tensor_scalar(out=nlam[:], in0=nlam[:], scalar1=-lam_init, scalar2=None, op0=ALU.add), [nlam], [nlam])
        sg = k.sb([128, 128], F32, "sg")
        load_bc(c, sg, c.inp["da_subln_g"][0:1, :])
        k.op("dve", lambda e: e.tensor_scalar(out=sg[:], in0=sg[:], scalar1=(1.0 - lam_init), scalar2=None, op0=ALU.mult), [sg], [sg])
        kTh = [k.sb([128, T_], BF16, "kTh") for _ in range(2)]
        vh = [k.sb([128, NQ, 128], BF16, "vh") for _ in range(2)]
        qTh = [[k.sb([128, T_], BF16, "qTh") for _ in range(2)] for _ in range(2)]
        dh = [k.sb([128, 128], F32, "dh") for _ in range(2)]
        NBUF = 3
        ssb = [k.sb([128, T_], F32, "ssb") for _ in range(NBUF)]
        P = [k.sb([128, T_], BF16, "P") for _ in range(NBUF)]
        PT = [k.sb([128, NQ, 128], BF16, "PT") for _ in range(2)]
        mx = [k.sb([128, 1], F32, "mx") for _ in range(NBUF)]
        sm = [k.sb([128, 1], F32, "sm") for _ in range(NBUF)]
        o0 = [k.sb([128, 128], F32, "o0") for _ in range(2)]
        oo = [k.sb([128, 128], F32, "oo") for _ in range(2)]
        jk = [k.sb([128, 128], F32, "jk") for _ in range(2)]
        ms = [k.sb([128, 1], F32, "ms") for _ in range(2)]
        ob = [k.sb([128, 128], BF16, "ob") for _ in range(2)]
        units = [(s, h, qi, j) for s in range(c.NSEQ) for h in range(8) for qi in range(NQ) for j in range(2)]

        def stage_a(ui):
            s, h, qi, j = units[ui]
            hb_ = (s * 8 + h) % 2
            slope = 2.0 ** (-(h + 1))
            dhh = dh[hb_]
            if qi == 0 and j == 0:
                k.dma("sp", kTh[hb_][:], KTd[s, h], writes=[kTh[hb_]])
                k.dma("sp", vh[hb_][:], Vd[s * T_:(s + 1) * T_, h * 128:(h + 1) * 128].rearrange("(n p) c -> p n c", p=128), writes=[vh[hb_]])
                for jj in range(2):
                    k.dma("sp", qTh[jj][hb_][:], QT[jj][s, h], writes=[qTh[jj][hb_]])
                k.op("dve", lambda e: e.scalar_tensor_tensor(out=dhh[:], in0=dbase[:], scalar=slope, in1=dmsk[:], op0=ALU.mult, op1=ALU.add),
                     [dbase, dmsk], [dhh])
            q0 = qi * 128
            nk = q0 + 128
            ub = ui % NBUF
            sb_, Pb, mxb, smb = ssb[ub], P[ub], mx[ub], sm[ub]
            for k0 in range(0, q0, 1024):
                p = psf(c)
                w_ = min(1024, q0 - k0)
                for c0 in range(0, w_, 512):
                    cw = min(512, w_ - c0)
                    k.op("pe", lambda e, p=p, c0=c0, cw=cw, k0=k0: e.matmul(
                        out=p[:, c0:c0 + cw], lhsT=qTh[j][hb_][:, q0:q0 + 128], rhs=kTh[hb_][:, k0 + c0:k0 + c0 + cw], start=True, stop=True),
                        [qTh[j][hb_], kTh[hb_]], [p])
                    off = T_ - q0 + k0 + c0
                    k.op("dve", lambda e, p=p, c0=c0, cw=cw, k0=k0, off=off: e.scalar_tensor_tensor(
                        out=sb_[:, k0 + c0:k0 + c0 + cw], in0=r0[:, off:off + cw], scalar=slope, in1=p[:, c0:c0 + cw], op0=ALU.mult, op1=ALU.add),
                        [r0, p], [sb_])
            p = psf(c)
            k.op("pe", lambda e, p=p: e.matmul(out=p[:, 0:128], lhsT=qTh[j][hb_][:, q0:q0 + 128], rhs=kTh[hb_][:, q0:q0 + 128],
                                               start=True, stop=True), [qTh[j][hb_], kTh[hb_]], [p])
            k.op("dve", lambda e, p=p: e.tensor_tensor(out=sb_[:, q0:q0 + 128], in0=p[:, 0:128], in1=dhh[:], op=ALU.add),
                 [p, dhh], [sb_])
            k.op("dve", lambda e: e.tensor_reduce(out=mxb[:], in_=sb_[:, 0:nk], axis=AX.X, op=ALU.max, negate=True), [sb_], [mxb])

        def stage_a2(ui):
            s, h, qi, j = units[ui]
            nk = qi * 128 + 128
            ub = ui % NBUF
            sb_, Pb, mxb, smb = ssb[ub], P[ub], mx[ub], sm[ub]
            k.op("act", lambda e: e.activation(out=Pb[:, 0:nk], in_=sb_[:, 0:nk], func=AF.Exp, bias=mxb[:, 0:1],
                                               scale=1.0, accum_out=smb[:, 0:1]), [sb_, mxb], [Pb, smb])

        def stage_b(ui):
            s, h, qi, j = units[ui]
            hb_ = (s * 8 + h) % 2
            q0 = qi * 128
            nk = q0 + 128
            ub = ui % NBUF
            Pb, PTb, smb = P[ub], PT[ui % 2], sm[ub]
            nblk = nk // 128
            k.op("dve", lambda e: e.reciprocal(out=smb[:], in_=smb[:]), [smb], [smb])
            for b0 in range(0, nblk, 8):
                nb_ = min(8, nblk - b0)
                transpose_into(c, lambda b0=b0, nb_=nb_: PTb[:, b0:b0 + nb_, :].rearrange("p a b -> p (a b)"),
                               lambda ci, b0=b0: Pb[:, (b0 + ci) * 128:(b0 + ci + 1) * 128], nb_, Pb, PTb,
                               evac=("act" if (b0 // 8) % 2 == 0 else "dve"))

        def stage_b2(ui):
            s, h, qi, j = units[ui]
            hb_ = (s * 8 + h) % 2
            q0 = qi * 128
            nk = q0 + 128
            ub = ui % NBUF
            Pb, PTb, smb = P[ub], PT[ui % 2], sm[ub]
            nblk = nk // 128
            p = psf(c)
            for bk in range(nblk):
                k.op("pe", lambda e, p=p, bk=bk: e.matmul(out=p[:, 0:128], lhsT=PTb[:, bk, :], rhs=vh[hb_][:, bk, :],
                                                         start=(bk == 0), stop=(bk == nblk - 1)), [PTb, vh[hb_]], [p])
            qb = qi % 2
            if j == 0:
                k.op("act", lambda e, p=p: e.activation(out=o0[qb][:], in_=p[:, 0:128], func=AF.Copy, scale=smb[:, 0:1]), [p, smb], [o0[qb]])
            else:
                k.op("dve", lambda e: e.tensor_tensor(out=smb[:], in0=smb[:], in1=nlam[:], op=ALU.mult), [smb, nlam], [smb])
                k.op("dve", lambda e, p=p: e.scalar_tensor_tensor(out=oo[qb][:], in0=p[:, 0:128], scalar=smb[:, 0:1], in1=o0[qb][:],
                                                                 op0=ALU.mult, op1=ALU.add), [p, smb, o0[qb]], [oo[qb]])
                k.op("act", lambda e: e.activation(out=jk[qb][:], in_=oo[qb][:], func=AF.Square, accum_out=ms[qb][:, 0:1]), [oo[qb]], [jk[qb], ms[qb]])
                k.op("act", lambda e: e.activation(out=ms[qb][:], in_=ms[qb][:], func=AF.Sqrt, scale=1.0 / 128.0, bias=LN_EPS), [ms[qb]], [ms[qb]])
                k.op("dve", lambda e: e.reciprocal(out=ms[qb][:], in_=ms[qb][:]), [ms[qb]], [ms[qb]])
                k.op("dve", lambda e: e.scalar_tensor_tensor(out=ob[qb][:], in0=oo[qb][:], scalar=ms[qb][:, 0:1], in1=sg[:], op0=ALU.mult, op1=ALU.mult),
                     [oo[qb], ms[qb], sg], [ob[qb]])
                k.dma("sp", AO[s * T_ + q0:s * T_ + q0 + 128, h * 128:(h + 1) * 128], ob[qb][:], reads=[ob[qb]])

        for ui in range(len(units) + 1):
            if ui < len(units):
                stage_a(ui)
            if ui >= 1:
                stage_b(ui - 1)
            if ui < len(units):
                stage_a2(ui)
            if ui >= 1:
                stage_b2(ui - 1)
    with k.scope():
        wo = k.sb([128, 8, 1024], BF16, "wo")
        load_w(c, wo, c.inp["da_w_o"][0], 1024, 1024)
        tl = Tail(c, li, 0)
        hf = [k.sb([128, D], F32, "hf") for _ in range(2)]
        ab = [k.sb([128, D], BF16, "ab") for _ in range(2)]
        aT = [k.sb([128, 8, 128], BF16, "aT") for _ in range(2)]

        def load(ti):
            k.dma("sp", hf[ti % 2][:], h_in[ti * 128:(ti + 1) * 128, :], writes=[hf[ti % 2]])
            k.dma("sp", ab[ti % 2][:], AO[ti * 128:(ti + 1) * 128, :], writes=[ab[ti % 2]])

        load(0)
        for ti in range(c.NTILES):
            if ti + 1 < c.NTILES:
                load(ti + 1)
            b = ti % 2
            transpose_into(c, lambda b=b: aT[b][:].rearrange("p a b -> p (a b)"), lambda ci, b=b: ab[b][:, ci * 128:(ci + 1) * 128], 8, ab[b], aT[b])
            p = psf(c)
            for nb in range(2):
                for kc in range(8):
                    k.op("pe", lambda e, p=p, nb=nb, kc=kc, b=b: e.matmul(out=p[:, nb * 512:(nb + 1) * 512], lhsT=aT[b][:, kc, :], rhs=wo[:, kc, nb * 512:(nb + 1) * 512],
                                                                         start=(kc == 0), stop=(kc == 7)), [aT[b], wo], [p])
            tl.run(hf[b], lambda hf_, p=p: p[:, hf_ * 512:(hf_ + 1) * 512], p, h_out, ti)


def phase_m2(c, li, h_in, h_out):
    k = c.k
    T_ = c.T
    NCH = T_ // 128
    Zd = k.dram("m2_Z", [c.NT, 2048], F32)
    XBC = k.dram("m2_XBC", [c.NSEQ, 24, 128, T_], F32)
    DTd = k.dram("m2_DT", [c.NT, 32], F32)
    XS = k.dram("m2_XS", [c.NT, 2048], F32)
    BTd = k.dram("m2_BT", [c.NSEQ, 4, 128, T_], BF16)
    CTd = k.dram("m2_CT", [c.NSEQ, 4, 128, T_], BF16)
    BTOK = k.dram("m2_BTOK", [c.NT, 512], BF16)
    tps = T_ // 128
    with k.scope():
        w = k.sb([128, 8, 5152], BF16, "w_in")
        load_w(c, w, c.inp["m2_w_in"][0], 1024, 5152)
        hf = [k.sb([128, D], F32, "hf") for _ in range(2)]
        hb = k.sb([128, D], BF16, "hb")
        hT = k.sb([128, 8, 128], BF16, "hT")
        zt = [k.sb([128, 1024], F32, "zt") for _ in range(2)]
        xt = [k.sb([128, 8, 128], F32, "xt") for _ in range(2)]
        dtt = [k.sb([128, 32], F32, "dtt") for _ in range(2)]

        def load(ti):
            k.dma("sp", hf[ti % 2][:], h_in[ti * 128:(ti + 1) * 128, :], writes=[hf[ti % 2]])

        load(0)
        ev = 0
        for ti in range(c.NTILES):
            if ti + 1 < c.NTILES:
                load(ti + 1)
            b = ti % 2
            s, tt = ti // tps, ti % tps
            k.op("dve", lambda e, b=b: e.tensor_copy(out=hb[:], in_=hf[b][:]), [hf[b]], [hb])
            transpose_into(c, lambda: hT[:].rearrange("p a b -> p (a b)"), lambda ci: hb[:, ci * 128:(ci + 1) * 128], 8, hb, hT)
            for half in range(2):
                p = psf(c)
                for nb in range(2):
                    n0 = half * 1024 + nb * 512
                    for kc in range(8):
                        k.op("pe", lambda e, p=p, nb=nb, n0=n0, kc=kc: e.matmul(out=p[:, nb * 512:(nb + 1) * 512], lhsT=hT[:, kc, :], rhs=w[:, kc, n0:n0 + 512],
                                                                               start=(kc == 0), stop=(kc == 7)), [hT, w], [p])
                zb = zt[ev % 2]
                ev += 1
                k.op("act", lambda e, p=p, zb=zb: e.copy(out=zb[:], in_=p[:]), [p], [zb])
                k.dma("sp", Zd[ti * 128:(ti + 1) * 128, half * 1024:(half + 1) * 1024], zb[:], reads=[zb])
            for third in range(3):
                p = psf(c)
                for fc in range(8):
                    col = 2048 + (third * 8 + fc) * 128
                    for kc in range(8):
                        k.op("pe", lambda e, p=p, fc=fc, col=col, kc=kc: e.matmul(out=p[:, fc * 128:(fc + 1) * 128], lhsT=w[:, kc, col:col + 128], rhs=hT[:, kc, :],
                                                                                 start=(kc == 0), stop=(kc == 7)), [hT, w], [p])
                xb_ = xt[ev % 2]
                ev += 1
                k.op("dve", lambda e, p=p, xb_=xb_: e.tensor_copy(out=xb_[:].rearrange("p a b -> p (a b)"), in_=p[:]), [p], [xb_])
                k.dma("sp", XBC[s, third * 8:(third + 1) * 8, :, tt * 128:(tt + 1) * 128].rearrange("n p t -> p n t"), xb_[:], reads=[xb_])
            p = psf(c)
            for kc in range(8):
                k.op("pe", lambda e, p=p, kc=kc: e.matmul(out=p[:, 0:32], lhsT=hT[:, kc, :], rhs=w[:, kc, 5120:5152], start=(kc == 0), stop=(kc == 7)), [hT, w], [p])
            k.op("act", lambda e, p=p, b=b: e.copy(out=dtt[b][:], in_=p[:, 0:32]), [p], [dtt[b]])
            k.dma("sp", DTd[ti * 128:(ti + 1) * 128, :], dtt[b][:], reads=[dtt[b]])
    with k.scope():
        cwp = k.sb([128, 3072], F32, "cwp")
        k.op("dve", lambda e: e.memset(cwp[:], 0.0), [], [cwp])
        k.dma("sp", cwp[0:4, :], c.inp["m2_conv_w"][0], writes=[cwp])
        k.dma("sp", cwp[4:5, :], c.inp["m2_conv_b"][0:1, :], writes=[cwp])
        cw = k.sb([128, 24, 8], F32, "cw")
        for cch in range(24):
            p = psf(c)
            k.op("pe", lambda e, p=p, cch=cch: e.transpose(out=p[:, 0:128], in_=cwp[:, cch * 128:(cch + 1) * 128], identity=c.identf[:]), [cwp, c.identf], [p])
            k.op("act", lambda e, p=p, cch=cch: e.copy(out=cw[:, cch, :], in_=p[:, 0:8]), [p], [cw])
        xin = [k.sb([128, T_], F32, "xin") for _ in range(2)]
        acc = [k.sb([128, T_], F32, "acc") for _ in range(2)]
        accb = [k.sb([128, T_], BF16, "accb") for _ in range(2)]
        stg = [k.sb([128, 4, 128], F32, "stg") for _ in range(2)]
        stgb = [k.sb([128, 8, 128], BF16, "stgb") for _ in range(2)]
        u = 0
        for s in range(c.NSEQ):
            for cch in range(24):
                ub = u % 2
                u += 1
                xi, ac = xin[ub], acc[ub]
                k.dma("sp", xi[:], XBC[s, cch], writes=[xi])
                k.op("dve", lambda e, xi=xi, ac=ac, cch=cch: e.tensor_scalar(out=ac[:], in0=xi[:], scalar1=cw[:, cch, 3:4], scalar2=None, op0=ALU.mult), [xi, cw], [ac])
                for kk in range(3):
                    shf = 3 - kk
                    k.op("dve", lambda e, xi=xi, ac=ac, cch=cch, kk=kk, shf=shf: e.scalar_tensor_tensor(out=ac[:, shf:T_], in0=xi[:, 0:T_ - shf], scalar=cw[:, cch, kk:kk + 1],
                                                                                                  in1=ac[:, shf:T_], op0=ALU.mult, op1=ALU.add), [xi, cw, ac], [ac])
                k.op("act", lambda e, ac=ac, cch=cch: e.activation(out=ac[:], in_=ac[:], func=AF.Silu, bias=cw[:, cch, 4:5], scale=1.0), [ac, cw], [ac])
                if cch < 16:
                    for n4 in range(T_ // 512):
                        p = psf(c)
                        for q4 in range(4):
                            t0 = n4 * 512 + q4 * 128
                            k.op("pe", lambda e, p=p, q4=q4, t0=t0, ac=ac: e.transpose(out=p[:, q4 * 128:(q4 + 1) * 128], in_=ac[:, t0:t0 + 128], identity=c.identf[:]),
                                 [ac, c.identf], [p])
                        sg_ = stg[n4 % 2]
                        k.op("act", lambda e, p=p, sg_=sg_: e.copy(out=sg_[:].rearrange("p a b -> p (a b)"), in_=p[:, 0:512]), [p], [sg_])
                        k.dma("sp", XS[s * T_ + n4 * 512:s * T_ + (n4 + 1) * 512, cch * 128:(cch + 1) * 128].rearrange("(n p) c -> p n c", p=128), sg_[:], reads=[sg_])
                else:
                    abf = accb[ub]
                    k.op("dve", lambda e, ac=ac, abf=abf: e.tensor_copy(out=abf[:], in_=ac[:]), [ac], [abf])
                    if cch < 20:
                        g = cch - 16
                        k.dma("sp", BTd[s, g], abf[:], reads=[abf])
                        for n8 in range(0, T_ // 128, 8):
                            nb_ = min(8, T_ // 128 - n8)
                            sb8 = stgb[(n8 // 8) % 2]
                            transpose_into(c, lambda sb8=sb8, nb_=nb_: sb8[:, 0:nb_, :].rearrange("p a b -> p (a b)"),
                                           lambda ci, n8=n8, abf=abf: abf[:, (n8 + ci) * 128:(n8 + ci + 1) * 128], nb_, abf, sb8)
                            k.dma("sp", BTOK[s * T_ + n8 * 128:s * T_ + (n8 + nb_) * 128, g * 128:(g + 1) * 128].rearrange("(n p) c -> p n c", p=128),
                                  sb8[:, 0:nb_, :], reads=[sb8])
                    else:
                        g = cch - 20
                        k.dma("sp", CTd[s, g], abf[:], reads=[abf])
    with k.scope():
        wo = k.sb([128, 16, 1024], BF16, "w_out")
        load_w(c, wo, c.inp["m2_w_out"][0], 2048, 1024)
        tl = Tail(c, li, 0)
        triU = k.sb([128, 128], F32, "triU")
        Lst = k.sb([128, 128], F32, "Lst")
        ones = k.sb([128, 128], F32, "ones")
        k.dma("sp", triU[:], c.inp["c_triU"][:, :], writes=[triU])
        k.dma("sp", Lst[:], c.inp["c_Lst"][:, :], writes=[Lst])
        k.op("dve", lambda e: e.memset(ones[:], 1.0), [], [ones])
        dtb = k.sb([128, 32], F32, "dtb")
        aneg = k.sb([128, 32], F32, "aneg")
        load_bc(c, dtb, c.inp["m2_dt_bias"][0:1, :])
        load_bc(c, aneg, c.inp["m2_a_log"][0:1, :])
        k.op("act", lambda e: e.activation(out=aneg[:], in_=aneg[:], func=AF.Exp), [aneg], [aneg])
        k.op("dve", lambda e: e.tensor_scalar(out=aneg[:], in0=aneg[:], scalar1=-1.0, scalar2=None, op0=ALU.mult), [aneg], [aneg])
        dsk = k.sb([128, 32], F32, "dsk")
        load_bc(c, dsk, c.inp["m2_d"][0:1, :])
        ng = k.sb([128, 2048], F32, "ng")
        load_bc(c, ng, c.inp["m2_norm_g"][0:1, :])
        hst = k.sb([128, 32, 64], F32, "hst")
        hstb = k.sb([128, 32, 64], BF16, "hstb")
        hf = [k.sb([128, D], F32, "hf") for _ in range(2)]
        xs = [k.sb([128, 32, 64], F32, "xs") for _ in range(2)]
        zz = [k.sb([128, 2048], F32, "zz") for _ in range(2)]
        dtr = [k.sb([128, 32], F32, "dtr") for _ in range(2)]
        btk = [k.sb([128, 512], BF16, "btk") for _ in range(2)]
        btc = [k.sb([128, 4, 128], BF16, "btc") for _ in range(2)]
        ctc = [k.sb([128, 4, 128], BF16, "ctc") for _ in range(2)]
        dt = k.sb([128, 32], F32, "dt")
        dta = k.sb([128, 32], F32, "dta")
        acs = k.sb([128, 32], F32, "acs")
        ea = k.sb([128, 32], F32, "ea")
        dec = k.sb([128, 32], F32, "dec")
        etot = k.sb([128, 32], F32, "etot")
        Rm = k.sb([128, 32, 128], F32, "Rm")
        LT = [k.sb([128, 8, 128], F32, "LT") for _ in range(2)]
        MT = k.sb([128, 32, 128], BF16, "MT")
        cbm = [k.sb([128, 128], F32, "cbm") for _ in range(2)]
        xdt = k.sb([128, 32, 64], BF16, "xdt")
        xdd = k.sb([128, 32, 64], BF16, "xdd")
        yy = k.sb([128, 32, 64], F32, "yy")
        ytmp = k.sb([128, 8, 64], F32, "ytmp")
        ssq = k.sb([128, 4], F32, "ssq")
        jk = k.sb([128, 512], F32, "jk")
        yb = k.sb([128, 2048], BF16, "yb")
        yT = k.sb([128, 16, 128], BF16, "yT")

        def load(ti):
            b = ti % 2
            s, tt = ti // tps, ti % tps
            k.dma("sp", hf[b][:], h_in[ti * 128:(ti + 1) * 128, :], writes=[hf[b]])
            k.dma("sp", xs[b][:].rearrange("p a b -> p (a b)"), XS[ti * 128:(ti + 1) * 128, :], writes=[xs[b]])
            k.dma("sp", zz[b][:], Zd[ti * 128:(ti + 1) * 128, :], writes=[zz[b]])
            k.dma("sp", dtr[b][:], DTd[ti * 128:(ti + 1) * 128, :], writes=[dtr[b]])
            k.dma("sp", btk[b][:], BTOK[ti * 128:(ti + 1) * 128, :], writes=[btk[b]])
            k.dma("sp", btc[b][:], BTd[s, :, :, tt * 128:(tt + 1) * 128].rearrange("g p t -> p g t"), writes=[btc[b]])
            k.dma("sp", ctc[b][:], CTd[s, :, :, tt * 128:(tt + 1) * 128].rearrange("g p t -> p g t"), writes=[ctc[b]])

        load(0)
        for ti in range(c.NTILES):
            if ti + 1 < c.NTILES:
                load(ti + 1)
            b = ti % 2
            s, tt = ti // tps, ti % tps
            if tt == 0:
                k.op("dve", lambda e: e.memset(hst[:], 0.0), [], [hst])
                k.op("pool", lambda e: e.memset(hstb[:], 0.0), [], [hstb])
            k.op("dve", lambda e, b=b: e.tensor_tensor(out=dt[:], in0=dtr[b][:], in1=dtb[:], op=ALU.add), [dtr[b], dtb], [dt])
            k.op("act", lambda e: e.activation(out=dt[:], in_=dt[:], func=AF.Exp), [dt], [dt])
            k.op("act", lambda e: e.activation(out=dt[:], in_=dt[:], func=AF.Ln, bias=1.0, scale=1.0), [dt], [dt])
            k.op("dve", lambda e: e.tensor_tensor(out=dta[:], in0=dt[:], in1=aneg[:], op=ALU.mult), [dt, aneg], [dta])
            p = psf(c)
            k.op("pe", lambda e, p=p: e.matmul(out=p[:, 0:32], lhsT=triU[:], rhs=dta[:], start=True, stop=True), [triU, dta], [p])
            k.op("pe", lambda e, p=p: e.matmul(out=p[:, 512:544], lhsT=ones[:], rhs=dta[:], start=True, stop=True), [ones, dta], [p])
            k.op("dve", lambda e, p=p: e.tensor_copy(out=acs[:], in_=p[:, 0:32]), [p], [acs])
            k.op("act", lambda e: e.activation(out=ea[:], in_=acs[:], func=AF.Exp), [acs], [ea])
            k.op("dve", lambda e, p=p: e.tensor_tensor(out=dec[:], in0=p[:, 512:544], in1=acs[:], op=ALU.subtract), [p, acs], [dec])
            k.op("act", lambda e: e.activation(out=dec[:], in_=dec[:], func=AF.Exp), [dec], [dec])
            k.op("act", lambda e, p=p: e.activation(out=etot[:], in_=p[:, 512:544], func=AF.Exp), [p], [etot])
            k.op("dve", lambda e: e.tensor_tensor(out=Rm[:], in0=triU[:].unsqueeze(1).broadcast_to([128, 32, 128]), in1=dta[:].unsqueeze(2).broadcast_to([128, 32, 128]),
                                                  op=ALU.mult), [triU, dta], [Rm])
            k.op("dve", lambda e, b=b: e.tensor_tensor(out=xdt[:], in0=xs[b][:], in1=dt[:].unsqueeze(2).broadcast_to([128, 32, 64]), op=ALU.mult), [xs[b], dt], [xdt])
            k.op("pool", lambda e: e.tensor_tensor(out=xdd[:], in0=xdt[:], in1=dec[:].unsqueeze(2).broadcast_to([128, 32, 64]), op=ALU.mult), [xdt, dec], [xdd])
            for g in range(4):
                gb = g % 2
                p = psf(c)
                k.op("pe", lambda e, p=p, g=g, b=b: e.matmul(out=p[:, 0:128], lhsT=btc[b][:, g, :], rhs=ctc[b][:, g, :], start=True, stop=True), [btc[b], ctc[b]], [p])
                k.op("dve", lambda e, p=p, gb=gb: e.tensor_tensor(out=cbm[gb][:], in0=p[:, 0:128], in1=triU[:], op=ALU.mult), [p, triU], [cbm[gb]])
                p = psf(c)
                for hh in range(2):
                    k.op("pe", lambda e, p=p, g=g, hh=hh: e.matmul(out=p[:, hh * 512:(hh + 1) * 512], lhsT=Lst[:],
                                                                  rhs=Rm[:, g * 8 + hh * 4:g * 8 + hh * 4 + 4, :].rearrange("p a b -> p (a b)"), start=True, stop=True),
                         [Lst, Rm], [p])
                k.op("act", lambda e, p=p, gb=gb: e.activation(out=LT[gb][:].rearrange("p a b -> p (a b)"), in_=p[:], func=AF.Exp), [p], [LT[gb]])
                k.op("dve", lambda e, g=g, gb=gb: e.tensor_tensor(out=MT[:, g * 8:(g + 1) * 8, :], in0=LT[gb][:], in1=cbm[gb][:].unsqueeze(1).broadcast_to([128, 8, 128]),
                                                                  op=ALU.mult), [LT[gb], cbm[gb]], [MT])
                p = psf(c)
                k.op("pe", lambda e, p=p, g=g, b=b: e.matmul(out=p[:, 0:512], lhsT=ctc[b][:, g, :], rhs=hstb[:, g * 8:(g + 1) * 8, :].rearrange("p a b -> p (a b)"),
                                                            start=True, stop=True), [ctc[b], hstb], [p])
                for r in range(8):
                    hh_ = g * 8 + r
                    k.op("pe", lambda e, p=p, r=r, hh_=hh_: e.matmul(out=p[:, 512 + r * 64:512 + (r + 1) * 64], lhsT=MT[:, hh_, :], rhs=xdt[:, hh_, :], start=True, stop=True),
                         [MT, xdt], [p])
                k.op("dve", lambda e, p=p, g=g: e.tensor_tensor(out=ytmp[:], in0=p[:, 0:512].rearrange("p (a b) -> p a b", a=8),
                                                                in1=ea[:, g * 8:(g + 1) * 8].unsqueeze(2).broadcast_to([128, 8, 64]), op=ALU.mult), [p, ea], [ytmp])
                k.op("dve", lambda e, p=p, g=g: e.tensor_tensor(out=yy[:, g * 8:(g + 1) * 8, :], in0=ytmp[:], in1=p[:, 512:1024].rearrange("p (a b) -> p a b", a=8), op=ALU.add),
                     [ytmp, p], [yy])
            k.op("dve", lambda e: e.tensor_tensor(out=hst[:], in0=hst[:], in1=etot[:].unsqueeze(2).broadcast_to([128, 32, 64]), op=ALU.mult), [hst, etot], [hst])
            for g2 in range(2):
                p = psf(c)
                for q2 in range(2):
                    g = g2 * 2 + q2
                    k.op("pe", lambda e, p=p, q2=q2, g=g, b=b: e.matmul(out=p[:, q2 * 512:(q2 + 1) * 512], lhsT=btk[b][:, g * 128:(g + 1) * 128],
                                                                       rhs=xdd[:, g * 8:(g + 1) * 8, :].rearrange("p a b -> p (a b)"), start=True, stop=True), [btk[b], xdd], [p])
                k.op("dve", lambda e, p=p, g2=g2: e.tensor_tensor(out=hst[:, g2 * 16:(g2 + 1) * 16, :].rearrange("p a b -> p (a b)"),
                                                                  in0=hst[:, g2 * 16:(g2 + 1) * 16, :].rearrange("p a b -> p (a b)"), in1=p[:], op=ALU.add), [hst, p], [hst])
            k.op("act", lambda e: e.copy(out=hstb[:], in_=hst[:]), [hst], [hstb])
            k.op("pool", lambda e, b=b: e.tensor_tensor(out=xs[b][:], in0=xs[b][:], in1=dsk[:].unsqueeze(2).broadcast_to([128, 32, 64]), op=ALU.mult), [xs[b], dsk], [xs[b]])
            k.op("pool", lambda e, b=b: e.tensor_tensor(out=yy[:], in0=yy[:], in1=xs[b][:], op=ALU.add), [yy, xs[b]], [yy])
            k.op("act", lambda e, b=b: e.activation(out=zz[b][:], in_=zz[b][:], func=AF.Silu), [zz[b]], [zz[b]])
            yyf = lambda: yy[:].rearrange("p a b -> p (a b)")
            k.op("dve", lambda e, b=b: e.tensor_tensor(out=yyf(), in0=yyf(), in1=zz[b][:], op=ALU.mult), [yy, zz[b]], [yy])
            for g in range(4):
                k.op("act", lambda e, g=g: e.activation(out=jk[:], in_=yyf()[:, g * 512:(g + 1) * 512], func=AF.Square, accum_out=ssq[:, g:g + 1]), [yy], [jk, ssq])
            k.op("act", lambda e: e.activation(out=ssq[:], in_=ssq[:], func=AF.Sqrt, scale=1.0 / 512.0, bias=LN_EPS), [ssq], [ssq])
            k.op("dve", lambda e: e.reciprocal(out=ssq[:], in_=ssq[:]), [ssq], [ssq])
            k.op("dve", lambda e: e.tensor_tensor(out=yy[:].rearrange("p (g a) b -> p g (a b)", g=4), in0=yy[:].rearrange("p (g a) b -> p g (a b)", g=4),
                                                  in1=ssq[:].unsqueeze(2).broadcast_to([128, 4, 512]), op=ALU.mult), [yy, ssq], [yy])
            k.op("pool", lambda e: e.tensor_tensor(out=yb[:], in0=yyf(), in1=ng[:], op=ALU.mult), [yy, ng], [yb])
            for h2 in range(2):
                transpose_into(c, lambda h2=h2: yT[:, h2 * 8:(h2 + 1) * 8, :].rearrange("p a b -> p (a b)"), lambda ci, h2=h2: yb[:, (h2 * 8 + ci) * 128:(h2 * 8 + ci + 1) * 128],
                               8, yb, yT, evac=("act" if h2 == 0 else "dve"))
            p = psf(c)
            for nb in range(2):
                for kc in range(16):
                    k.op("pe", lambda e, p=p, nb=nb, kc=kc: e.matmul(out=p[:, nb * 512:(nb + 1) * 512], lhsT=yT[:, kc, :], rhs=wo[:, kc, nb * 512:(nb + 1) * 512],
                                                                    start=(kc == 0), stop=(kc == 15)), [yT, wo], [p])
            tl.run(hf[b], lambda hf_, p=p: p[:, hf_ * 512:(hf_ + 1) * 512], p, h_out, ti)


def phase_ml(c, li, h_in, h_out):
    k = c.k
    T_ = c.T
    NCH = T_ // 128
    tps = NCH
    XM = k.dram("ml_XM", [c.NSEQ, 16, 128, T_], F32)
    OG = k.dram("ml_OG", [c.NT, 2048], F32)
    XCT = k.dram("ml_XCT", [c.NSEQ, 16, 128, T_], BF16)
    XMT = k.dram("ml_XMT", [c.NSEQ, 16, 128, T_], BF16)
    XC = k.dram("ml_XC", [c.NT, 2048], F32)
    QTd = k.dram("ml_QT", [c.NSEQ, 16, 128, T_], BF16)
    KTd = k.dram("ml_KT", [c.NSEQ, 16, 128, T_], BF16)
    KTOK = k.dram("ml_KTOK", [c.NT, 2048], BF16)
    VTOK = k.dram("ml_VTOK", [c.NT, 2048], BF16)
    GI = k.dram("ml_GI", [c.NSEQ, 4, T_], F32)
    GF = k.dram("ml_GF", [c.NSEQ, 4, T_], F32)
    GO = k.dram("ml_GO", [c.NT, 2048], BF16)
    with k.scope():
        w = k.sb([128, 8, 4096], BF16, "w_in")
        load_w(c, w, c.inp["ml_w_in"][0], 1024, 4096)
        hf = [k.sb([128, D], F32, "hf") for _ in range(2)]
        hb = k.sb([128, D], BF16, "hb")
        hT = k.sb([128, 8, 128], BF16, "hT")
        zt = [k.sb([128, 1024], F32, "zt") for _ in range(2)]
        xt = [k.sb([128, 8, 128], F32, "xt") for _ in range(2)]

        def load(ti):
            k.dma("sp", hf[ti % 2][:], h_in[ti * 128:(ti + 1) * 128, :], writes=[hf[ti % 2]])

        load(0)
        ev = 0
        for ti in range(c.NTILES):
            if ti + 1 < c.NTILES:
                load(ti + 1)
            b = ti % 2
            s, tt = ti // tps, ti % tps
            k.op("dve", lambda e, b=b: e.tensor_copy(out=hb[:], in_=hf[b][:]), [hf[b]], [hb])
            transpose_into(c, lambda: hT[:].rearrange("p a b -> p (a b)"), lambda ci: hb[:, ci * 128:(ci + 1) * 128], 8, hb, hT)
            for half in range(2):
                p = psf(c)
                for fc in range(8):
                    col = (half * 8 + fc) * 128
                    for kc in range(8):
                        k.op("pe", lambda e, p=p, fc=fc, col=col, kc=kc: e.matmul(out=p[:, fc * 128:(fc + 1) * 128], lhsT=w[:, kc, col:col + 128], rhs=hT[:, kc, :],
                                                                                 start=(kc == 0), stop=(kc == 7)), [hT, w], [p])
                xb_ = xt[ev % 2]
                k.op("dve", lambda e, p=p, xb_=xb_: e.tensor_copy(out=xb_[:].rearrange("p a b -> p (a b)"), in_=p[:]), [p], [xb_])
                k.dma("sp", XM[s, half * 8:(half + 1) * 8, :, tt * 128:(tt + 1) * 128].rearrange("n p t -> p n t"), xb_[:], reads=[xb_])
                p = psf(c)
                for nb in range(2):
                    n0 = 2048 + half * 1024 + nb * 512
                    for kc in range(8):
                        k.op("pe", lambda e, p=p, nb=nb, n0=n0, kc=kc: e.matmul(out=p[:, nb * 512:(nb + 1) * 512], lhsT=hT[:, kc, :], rhs=w[:, kc, n0:n0 + 512],
                                                                               start=(kc == 0), stop=(kc == 7)), [hT, w], [p])
                zb = zt[ev % 2]
                ev += 1
                k.op("act", lambda e, p=p, zb=zb: e.copy(out=zb[:], in_=p[:]), [p], [zb])
                k.dma("sp", OG[ti * 128:(ti + 1) * 128, half * 1024:(half + 1) * 1024], zb[:], reads=[zb])
    if 'mlA' in DBG:
        return
    with k.scope():
        cwp = k.sb([128, 2048], F32, "cwp")
        k.op("dve", lambda e: e.memset(cwp[:], 0.0), [], [cwp])
        k.dma("sp", cwp[0:4, :], c.inp["ml_conv_w"][0], writes=[cwp])
        k.dma("sp", cwp[4:5, :], c.inp["ml_conv_b"][0:1, :], writes=[cwp])
        cw = k.sb([128, 16, 8], F32, "cw")
        for cch in range(16):
            p = psf(c)
            k.op("pe", lambda e, p=p, cch=cch: e.transpose(out=p[:, 0:128], in_=cwp[:, cch * 128:(cch + 1) * 128], identity=c.identf[:]), [cwp, c.identf], [p])
            k.op("act", lambda e, p=p, cch=cch: e.copy(out=cw[:, cch, :], in_=p[:, 0:8]), [p], [cw])
        xin = [k.sb([128, T_], F32, "xin") for _ in range(2)]
        acc = [k.sb([128, T_], F32, "acc") for _ in range(2)]
        accb = [k.sb([128, T_], BF16, "accb") for _ in range(2)]
        xmb = [k.sb([128, T_], BF16, "xmb") for _ in range(2)]
        stg = [k.sb([128, 4, 128], F32, "stg") for _ in range(2)]
        u = 0
        for s in range(c.NSEQ):
            for cch in range(16):
                ub = u % 2
                u += 1
                xi, ac, abf, xb2 = xin[ub], acc[ub], accb[ub], xmb[ub]
                k.dma("sp", xi[:], XM[s, cch], writes=[xi])
                k.op("pool", lambda e, xi=xi, xb2=xb2: e.tensor_copy(out=xb2[:], in_=xi[:]), [xi], [xb2])
                k.dma("sp", XMT[s, cch], xb2[:], reads=[xb2])
                k.op("dve", lambda e, xi=xi, ac=ac, cch=cch: e.tensor_scalar(out=ac[:], in0=xi[:], scalar1=cw[:, cch, 3:4], scalar2=None, op0=ALU.mult), [xi, cw], [ac])
                for kk in range(3):
                    shf = 3 - kk
                    k.op("dve", lambda e, xi=xi, ac=ac, cch=cch, kk=kk, shf=shf: e.scalar_tensor_tensor(out=ac[:, shf:T_], in0=xi[:, 0:T_ - shf], scalar=cw[:, cch, kk:kk + 1],
                                                                                                  in1=ac[:, shf:T_], op0=ALU.mult, op1=ALU.add), [xi, cw, ac], [ac])
                k.op("act", lambda e, ac=ac, cch=cch: e.activation(out=ac[:], in_=ac[:], func=AF.Silu, bias=cw[:, cch, 4:5], scale=1.0), [ac, cw], [ac])
                k.op("pool", lambda e, ac=ac, abf=abf: e.tensor_copy(out=abf[:], in_=ac[:]), [ac], [abf])
                k.dma("sp", XCT[s, cch], abf[:], reads=[abf])
                for n4 in range(T_ // 512):
                    p = psf(c)
                    for q4 in range(4):
                        t0 = n4 * 512 + q4 * 128
                        k.op("pe", lambda e, p=p, q4=q4, t0=t0, ac=ac: e.transpose(out=p[:, q4 * 128:(q4 + 1) * 128], in_=ac[:, t0:t0 + 128], identity=c.identf[:]),
                             [ac, c.identf], [p])
                    sg_ = stg[n4 % 2]
                    k.op("act", lambda e, p=p, sg_=sg_: e.copy(out=sg_[:].rearrange("p a b -> p (a b)"), in_=p[:, 0:512]), [p], [sg_])
                    k.dma("sp", XC[s * T_ + n4 * 512:s * T_ + (n4 + 1) * 512, cch * 128:(cch + 1) * 128].rearrange("(n p) c -> p n c", p=128), sg_[:], reads=[sg_])
    if 'mlB' in DBG:
        return
    with k.scope():
        wq = k.sb([128, 16, 512], BF16, "wq")
        wk = k.sb([128, 16, 512], BF16, "wk")
        wv = k.sb([128, 16, 512], BF16, "wv")
        load_w(c, wq, c.inp["ml_w_q"][0].rearrange("h d e -> (h d) e"), 2048, 512)
        load_w(c, wk, c.inp["ml_w_k"][0].rearrange("h d e -> (h d) e"), 2048, 512)
        load_w(c, wv, c.inp["ml_w_v"][0].rearrange("h d e -> (h d) e"), 2048, 512)
        wgs = k.sb([128, 48, 8], F32, "wgs")
        for gs in range(3):
            k.dma("sp", wgs[:, gs * 16:(gs + 1) * 16, :], c.inp["ml_w_gates"][0, gs].rearrange("(c p) n -> p c n", p=128), writes=[wgs])
        wgp = k.sb([128, 48, 128], BF16, "wgp")
        k.op("dve", lambda e: e.memset(wgp[:], 0.0), [], [wgp])
        k.op("dve", lambda e: e.tensor_copy(out=wgp[:, :, 0:4], in_=wgs[:, :, 0:4]), [wgs], [wgp])
        k.op("dve", lambda e: e.tensor_copy(out=wgp[:, :, 32:36], in_=wgs[:, :, 4:8]), [wgs], [wgp])
        bcol = k.sb([128, 1], F32, "bcol")
        nbcol = k.sb([128, 1], F32, "nbcol")
        k.op("dve", lambda e: e.memset(bcol[:], 0.0), [], [bcol])
        k.dma("sp", bcol[0:4, :], c.inp["ml_b_gates"][0:1, 0:4].rearrange("o n -> n o"), writes=[bcol])
        k.dma("sp", bcol[32:36, :], c.inp["ml_b_gates"][0:1, 4:8].rearrange("o n -> n o"), writes=[bcol])
        k.op("dve", lambda e: e.tensor_scalar(out=nbcol[:], in0=bcol[:], scalar1=-1.0, scalar2=None, op0=ALU.mult), [bcol], [nbcol])
        xcT = [k.sb([128, 16, 128], BF16, "xcT") for _ in range(2)]
        xmT = [k.sb([128, 16, 128], BF16, "xmT") for _ in range(2)]
        qT = [k.sb([128, 16, 128], BF16, "qT") for _ in range(2)]
        kT = [k.sb([128, 16, 128], BF16, "kT") for _ in range(2)]
        kt = [k.sb([128, 2048], BF16, "kt") for _ in range(2)]
        vt = [k.sb([128, 2048], BF16, "vt") for _ in range(2)]
        vT = k.sb([128, 16, 128], BF16, "vT")
        gt = [k.sb([128, 128], F32, "gt") for _ in range(2)]
        lf = [k.sb([128, 128], F32, "lf") for _ in range(2)]
        KS = 512.0 ** -0.5

        def load(ti):
            s, tt = ti // tps, ti % tps
            k.dma("sp", xcT[ti % 2][:], XCT[s, :, :, tt * 128:(tt + 1) * 128].rearrange("n p t -> p n t"), writes=[xcT[ti % 2]])
            k.dma("sp", xmT[ti % 2][:], XMT[s, :, :, tt * 128:(tt + 1) * 128].rearrange("n p t -> p n t"), writes=[xmT[ti % 2]])

        load(0)
        for ti in range(c.NTILES):
            if ti + 1 < c.NTILES:
                load(ti + 1)
            b = ti % 2
            s, tt = ti // tps, ti % tps
            for which in range(2):
                wsrc = wq if which == 0 else wk
                dstT = qT[b] if which == 0 else kT[b]
                for half in range(2):
                    p = psf(c)
                    for oc in range(8):
                        o16 = half * 8 + oc
                        hh, ec = o16 // 4, o16 % 4
                        for dc in range(4):
                            k.op("pe", lambda e, p=p, oc=oc, hh=hh, ec=ec, dc=dc, wsrc=wsrc, b=b: e.matmul(
                                out=p[:, oc * 128:(oc + 1) * 128], lhsT=wsrc[:, hh * 4 + dc, ec * 128:(ec + 1) * 128], rhs=xcT[b][:, hh * 4 + dc, :],
                                start=(dc == 0), stop=(dc == 3)), [wsrc, xcT[b]], [p])
                    if which == 0:
                        k.op("act", lambda e, p=p, dstT=dstT, half=half: e.copy(out=dstT[:, half * 8:(half + 1) * 8, :].rearrange("p a b -> p (a b)"), in_=p[:]), [p], [dstT])
                    else:
                        k.op("act", lambda e, p=p, dstT=dstT, half=half: e.activation(out=dstT[:, half * 8:(half + 1) * 8, :].rearrange("p a b -> p (a b)"), in_=p[:],
                                                                                      func=AF.Copy, scale=KS), [p], [dstT])
                dd = QTd if which == 0 else KTd
                k.dma("sp", dd[s, :, :, tt * 128:(tt + 1) * 128].rearrange("n p t -> p n t"), dstT[:], reads=[dstT])
            for which in range(2):
                wsrc = wk if which == 0 else wv
                src = xcT[b] if which == 0 else xmT[b]
                dst = kt[b] if which == 0 else vt[b]
                for half in range(2):
                    p = psf(c)
                    for q2 in range(2):
                        hh = half * 2 + q2
                        for dc in range(4):
                            k.op("pe", lambda e, p=p, q2=q2, hh=hh, dc=dc, wsrc=wsrc, src=src: e.matmul(
                                out=p[:, q2 * 512:(q2 + 1) * 512], lhsT=src[:, hh * 4 + dc, :], rhs=wsrc[:, hh * 4 + dc, :], start=(dc == 0), stop=(dc == 3)), [wsrc, src], [p])
                    if which == 0:
                        k.op("dve", lambda e, p=p, dst=dst, half=half: e.tensor_scalar(out=dst[:, half * 1024:(half + 1) * 1024], in0=p[:], scalar1=KS, scalar2=None, op0=ALU.mult),
                             [p], [dst])
                    else:
                        k.op("dve", lambda e, p=p, dst=dst, half=half: e.tensor_copy(out=dst[:, half * 1024:(half + 1) * 1024], in_=p[:]), [p], [dst])
                dd = KTOK if which == 0 else VTOK
                k.dma("sp", dd[ti * 128:(ti + 1) * 128, :], dst[:], reads=[dst])
            for h2 in range(2):
                transpose_into(c, lambda h2=h2: vT[:, h2 * 8:(h2 + 1) * 8, :].rearrange("p a b -> p (a b)"), lambda ci, h2=h2, b=b: vt[b][:, (h2 * 8 + ci) * 128:(h2 * 8 + ci + 1) * 128],
                               8, vt[b], vT, evac=("act" if h2 == 0 else "dve"))
            p = psf(c)
            n_mm = 0
            for gs, srcT in enumerate((qT[b], kT[b], vT)):
                for fc in range(16):
                    k.op("pe", lambda e, p=p, gs=gs, fc=fc, srcT=srcT, n_mm=n_mm: e.matmul(out=p[:, 0:128], lhsT=wgp[:, gs * 16 + fc, :], rhs=srcT[:, fc, :],
                                                                                       start=(n_mm == 0), stop=(n_mm == 47)), [wgp, srcT], [p])
                    n_mm += 1
            k.op("act", lambda e, p=p, b=b: e.activation(out=gt[b][:], in_=p[:, 0:128], func=AF.Identity, bias=bcol[:, 0:1], scale=1.0), [p, bcol], [gt[b]])
            k.op("act", lambda e, p=p, b=b: e.activation(out=lf[b][:], in_=p[:, 0:128], func=AF.Exp, bias=nbcol[:, 0:1], scale=-1.0), [p, nbcol], [lf[b]])
            k.op("act", lambda e, b=b: e.activation(out=lf[b][:], in_=lf[b][:], func=AF.Ln, bias=1.0, scale=1.0), [lf[b]], [lf[b]])
            k.op("dve", lambda e, b=b: e.tensor_scalar(out=lf[b][:], in0=lf[b][:], scalar1=-1.0, scalar2=None, op0=ALU.mult), [lf[b]], [lf[b]])
            k.dma("sp", GI[s, :, tt * 128:(tt + 1) * 128], gt[b][0:4, :], reads=[gt[b]])
            k.dma("sp", GF[s, :, tt * 128:(tt + 1) * 128], lf[b][32:36, :], reads=[lf[b]])
    if 'mlC' in DBG:
        return
    with k.scope():
        sel = k.sb([128, 4, 128], F32, "sel")
        k.dma("sp", sel[:], c.inp["c_sel"][:, :, :], writes=[sel])
        negm = k.sb([128, 128], F32, "negm")
        k.dma("sp", negm[:], c.inp["c_Lst"][:, :], writes=[negm])
        k.op("dve", lambda e: e.tensor_scalar(out=negm[:], in0=negm[:], scalar1=NEG, scalar2=None, op0=ALU.mult), [negm], [negm])
        ones2 = k.sb([128, 2], BF16, "ones2")
        k.op("dve", lambda e: e.memset(ones2[:], 1.0), [], [ones2])
        ones512 = k.sb([128, 512], F32, "ones512")
        k.op("dve", lambda e: e.memset(ones512[:], 1.0), [], [ones512])
        ngt = k.sb([128, 2048], F32, "ngt")
        skt = k.sb([128, 2048], F32, "skt")
        load_bc(c, ngt, c.inp["ml_norm_g"][0:1, :])
        load_bc(c, skt, c.inp["ml_skip"][0:1, :])
        WROW = k.sb([128, T_], F32, "WROW")
        NCM = k.sb([128, T_], F32, "NCM")
        MROW = k.sb([128, T_], F32, "MROW")
        PM = k.sb([128, 4, NCH + 1], F32, "PM")
        NM = k.sb([128, 4, NCH + 1], F32, "NM")
        Cst = k.sb([128, 4, 4, 512], F32, "Cst")
        Cb = k.sb([128, 4, 4, 512], BF16, "Cb")
        nst = k.sb([128, 4, 4, 2], F32, "nst")
        nstb = k.sb([128, 4, 4, 2], BF16, "nstb")
        qTt = [k.sb([128, 16, 128], BF16, "qTt") for _ in range(2)]
        kTt = [k.sb([128, 16, 128], BF16, "kTt") for _ in range(2)]
        ktt = [k.sb([128, 2048], BF16, "ktt") for _ in range(2)]
        vtt = [k.sb([128, 2048], BF16, "vtt") for _ in range(2)]
        ogt = k.sb([128, 2048], F32, "ogt")
        xct = k.sb([128, 2048], F32, "xct")
        cols = k.sb([128, 3, 4], F32, "cols")
        ET = k.sb([128, 128], F32, "ET")
        ST = k.sb([128, 128], BF16, "ST")
        sc = k.sb([128, 1], F32, "sc")
        wkc = k.sb([128, 1], F32, "wkc")
        cd = k.sb([128, 1], F32, "cd")
        em = k.sb([128, 1], F32, "em")
        den = k.sb([128, 2], F32, "den")
        tmp512 = k.sb([128, 512], F32, "tmp512")
        num = k.sb([128, 512], F32, "num")
        kw = k.sb([128, 512], BF16, "kw")
        st6 = k.sb([128, 6], F32, "st6")
        mv2 = k.sb([128, 2], F32, "mv2")
        rs1 = k.sb([128, 1], F32, "rs1")
        outb = [k.sb([128, 2048], BF16, "outb") for _ in range(2)]

        def load(ti):
            s, tt = ti // tps, ti % tps
            b = ti % 2
            k.dma("sp", qTt[b][:], QTd[s, :, :, tt * 128:(tt + 1) * 128].rearrange("n p t -> p n t"), writes=[qTt[b]])
            k.dma("sp", kTt[b][:], KTd[s, :, :, tt * 128:(tt + 1) * 128].rearrange("n p t -> p n t"), writes=[kTt[b]])
            k.dma("sp", ktt[b][:], KTOK[ti * 128:(ti + 1) * 128, :], writes=[ktt[b]])
            k.dma("sp", vtt[b][:], VTOK[ti * 128:(ti + 1) * 128, :], writes=[vtt[b]])

        for s in range(c.NSEQ):
            with k.scope():
                F2 = k.sb([128, T_], F32, "F2")
                if s == 0:
                    k.op("dve", lambda e: e.memset(WROW[:], 0.0), [], [WROW])
                    k.op("pool", lambda e: e.memset(NCM[:], 0.0), [], [NCM])
                    k.op("pool", lambda e: e.memset(MROW[:], 0.0), [], [MROW])
                k.dma("sp", MROW[0:4, :], GF[s], writes=[MROW])
                k.dma("sp", WROW[0:4, :], GI[s], writes=[WROW])
                for blk in range(T_ // 512):
                    sl = slice(blk * 512, (blk + 1) * 512)
                    init = 0.0 if blk == 0 else F2[0:4, blk * 512 - 1:blk * 512]
                    k.op("dve", lambda e, sl=sl, init=init: e.tensor_tensor_scan(out=F2[0:4, sl], data0=ones512[0:4, :], data1=MROW[0:4, sl], initial=init,
                                                                                op0=ALU.mult, op1=ALU.add), [ones512, MROW, F2], [F2])
                k.op("dve", lambda e: e.tensor_tensor(out=WROW[0:4, :], in0=WROW[0:4, :], in1=F2[0:4, :], op=ALU.subtract), [WROW, F2], [WROW])
                for blk in range(T_ // 512):
                    sl = slice(blk * 512, (blk + 1) * 512)
                    init = 0.0 if blk == 0 else NCM[0:4, blk * 512 - 1:blk * 512]
                    k.op("dve", lambda e, sl=sl, init=init: e.tensor_tensor_scan(out=NCM[0:4, sl], data0=ones512[0:4, :], data1=WROW[0:4, sl], initial=init,
                                                                                op0=ALU.mult, op1=ALU.max), [ones512, WROW, NCM], [NCM])
                k.op("dve", lambda e: e.tensor_tensor(out=MROW[0:4, :], in0=F2[0:4, :], in1=NCM[0:4, :], op=ALU.add), [F2, NCM], [MROW])
                k.op("dve", lambda e: e.tensor_scalar(out=NCM[0:4, :], in0=NCM[0:4, :], scalar1=-1.0, scalar2=None, op0=ALU.mult), [NCM], [NCM])
                k.op("dve", lambda e: e.memset(NM[:], 0.0), [], [NM])
                for hh in range(4):
                    p = psf(c)
                    k.op("pe", lambda e, p=p, hh=hh: e.matmul(out=p[:, 0:NCH], lhsT=sel[:, hh, :], rhs=NCM[:, 127::128], start=True, stop=True), [sel, NCM], [p])
                    k.op("dve", lambda e, p=p, hh=hh: e.tensor_copy(out=NM[:, hh, 1:NCH + 1], in_=p[:, 0:NCH]), [p], [NM])
                k.op("dve", lambda e: e.tensor_scalar(out=PM[:], in0=NM[:], scalar1=-1.0, scalar2=None, op0=ALU.mult), [NM], [PM])
            if 'mlD' in DBG:
                return
            k.op("dve", lambda e: e.memset(Cst[:], 0.0), [], [Cst])
            k.op("pool", lambda e: e.memset(Cb[:], 0.0), [], [Cb])
            k.op("dve", lambda e: e.memset(nst[:], 0.0), [], [nst])
            k.op("dve", lambda e: e.memset(nstb[:], 0.0), [], [nstb])
            load(s * tps)
            for n in range(NCH):
                ti = s * tps + n
                if n + 1 < NCH:
                    load(ti + 1)
                b = ti % 2
                t0 = n * 128
                k.dma("sp", ogt[:], OG[ti * 128:(ti + 1) * 128, :], writes=[ogt])
                k.dma("sp", xct[:], XC[ti * 128:(ti + 1) * 128, :], writes=[xct])
                k.op("act", lambda e: e.activation(out=ogt[:], in_=ogt[:], func=AF.Sigmoid), [ogt], [ogt])
                p = psf(c)
                for a_, src in enumerate((WROW, NCM, MROW)):
                    k.op("pe", lambda e, p=p, a_=a_, src=src, t0=t0: e.transpose(out=p[:, a_ * 128:(a_ + 1) * 128], in_=src[:, t0:t0 + 128], identity=c.identf[:]),
                         [src, c.identf], [p])
                k.op("act", lambda e, p=p: e.copy(out=cols[:], in_=p[:, 0:384].rearrange("p (a b) -> p a b", a=3)[:, :, 0:4]), [p], [cols])
                for hh in range(4):
                    hs = slice(hh * 512, (hh + 1) * 512)
                    p = psf(c)
                    for dc in range(4):
                        k.op("pe", lambda e, p=p, hh=hh, dc=dc, b=b: e.matmul(out=p[:, 0:128], lhsT=kTt[b][:, hh * 4 + dc, :], rhs=qTt[b][:, hh * 4 + dc, :],
                                                                             start=(dc == 0), stop=(dc == 3)), [kTt[b], qTt[b]], [p])
                    k.op("pe", lambda e, p=p, hh=hh, t0=t0: e.matmul(out=p[:, 512:640], lhsT=WROW[:, t0:t0 + 128], rhs=sel[:, hh, :], start=True, stop=False), [WROW, sel], [p])
                    k.op("pe", lambda e, p=p, hh=hh, t0=t0: e.matmul(out=p[:, 512:640], lhsT=sel[:, hh, :], rhs=NCM[:, t0:t0 + 128], start=False, stop=False), [NCM, sel], [p])
                    k.op("pe", lambda e, p=p: e.matmul(out=p[:, 512:640], lhsT=c.identf[:], rhs=negm[:], start=False, stop=True), [c.identf, negm], [p])
                    k.op("act", lambda e, p=p: e.activation(out=ET[:], in_=p[:, 512:640], func=AF.Exp), [p], [ET])
                    k.op("dve", lambda e, p=p: e.tensor_tensor(out=ST[:], in0=ET[:], in1=p[:, 0:128], op=ALU.mult), [ET, p], [ST])
                    p1 = psf(c)
                    k.op("pe", lambda e, p1=p1, hs=hs, b=b: e.matmul(out=p1[:, 0:512], lhsT=ST[:], rhs=vtt[b][:, hs], start=True, stop=True), [ST, vtt[b]], [p1])
                    for dc in range(4):
                        k.op("pe", lambda e, p1=p1, hh=hh, dc=dc, b=b: e.matmul(out=p1[:, 512:1024], lhsT=qTt[b][:, hh * 4 + dc, :], rhs=Cb[:, hh, dc, :],
                                                                               start=(dc == 0), stop=(dc == 3)), [qTt[b], Cb], [p1])
                    p2 = psf(c)
                    k.op("pe", lambda e, p2=p2: e.matmul(out=p2[:, 0:2], lhsT=ST[:], rhs=ones2[:], start=True, stop=True), [ST, ones2], [p2])
                    for dc in range(4):
                        k.op("pe", lambda e, p2=p2, hh=hh, dc=dc, b=b: e.matmul(out=p2[:, 512:514], lhsT=qTt[b][:, hh * 4 + dc, :], rhs=nstb[:, hh, dc, :],
                                                                               start=(dc == 0), stop=(dc == 3)), [qTt[b], nstb], [p2])
                    k.op("act", lambda e, hh=hh, n=n: e.activation(out=sc[:], in_=cols[:, 1, hh:hh + 1], func=AF.Exp, bias=PM[:, hh, n:n + 1], scale=1.0), [cols, PM], [sc])
                    k.op("act", lambda e, p1=p1: e.activation(out=tmp512[:], in_=p1[:, 512:1024], func=AF.Copy, scale=sc[:, 0:1]), [p1, sc], [tmp512])
                    k.op("dve", lambda e, p1=p1: e.tensor_tensor(out=num[:], in0=tmp512[:], in1=p1[:, 0:512], op=ALU.add), [tmp512, p1], [num])
                    k.op("dve", lambda e, p2=p2: e.tensor_scalar(out=den[:, 0:1], in0=p2[:, 512:513], scalar1=sc[:, 0:1], scalar2=None, op0=ALU.mult), [p2, sc], [den])
                    k.op("dve", lambda e, p2=p2: e.tensor_tensor(out=den[:, 0:1], in0=den[:, 0:1], in1=p2[:, 0:1], op=ALU.add), [den, p2], [den])
                    k.op("act", lambda e: e.activation(out=den[:, 0:1], in_=den[:, 0:1], func=AF.Abs), [den], [den])
                    k.op("act", lambda e, hh=hh: e.activation(out=em[:], in_=cols[:, 2, hh:hh + 1], func=AF.Exp, scale=-1.0), [cols], [em])
                    k.op("dve", lambda e: e.tensor_tensor(out=den[:, 0:1], in0=den[:, 0:1], in1=em[:], op=ALU.max), [den, em], [den])
                    k.op("dve", lambda e: e.reciprocal(out=den[:, 0:1], in_=den[:, 0:1]), [den], [den])
                    k.op("dve", lambda e: e.tensor_scalar(out=num[:], in0=num[:], scalar1=den[:, 0:1], scalar2=None, op0=ALU.mult), [num, den], [num])
                    k.op("dve", lambda e: e.bn_stats(out=st6[:], in_=num[:]), [num], [st6])
                    k.op("dve", lambda e: e.bn_aggr(out=mv2[:], in_=st6[:]), [st6], [mv2])
                    k.op("act", lambda e: e.activation(out=rs1[:], in_=mv2[:, 1:2], func=AF.Sqrt, bias=LN_EPS, scale=1.0), [mv2], [rs1])
                    k.op("dve", lambda e: e.reciprocal(out=rs1[:], in_=rs1[:]), [rs1], [rs1])
                    k.op("dve", lambda e: e.tensor_scalar(out=num[:], in0=num[:], scalar1=mv2[:, 0:1], scalar2=rs1[:, 0:1], op0=ALU.subtract, op1=ALU.mult), [num, mv2, rs1], [num])
                    k.op("pool", lambda e, hs=hs: e.tensor_tensor(out=num[:], in0=num[:], in1=ngt[:, hs], op=ALU.mult), [num, ngt], [num])
                    k.op("pool", lambda e, hs=hs: e.tensor_tensor(out=tmp512[:], in0=xct[:, hs], in1=skt[:, hs], op=ALU.mult), [xct, skt], [tmp512])
                    k.op("pool", lambda e: e.tensor_tensor(out=num[:], in0=num[:], in1=tmp512[:], op=ALU.add), [num, tmp512], [num])
                    k.op("dve", lambda e, hs=hs, b=b: e.tensor_tensor(out=outb[b][:, hs], in0=num[:], in1=ogt[:, hs], op=ALU.mult), [num, ogt], [outb[b]])
                    k.op("act", lambda e, hh=hh, n=n: e.activation(out=wkc[:], in_=cols[:, 0, hh:hh + 1], func=AF.Exp, bias=NM[:, hh, n + 1:n + 2], scale=1.0), [cols, NM], [wkc])
                    k.op("act", lambda e, hh=hh, n=n: e.activation(out=cd[:], in_=PM[:, hh, n:n + 1], func=AF.Exp, bias=NM[:, hh, n + 1:n + 2], scale=1.0), [PM, NM], [cd])
                    k.op("dve", lambda e, hs=hs, b=b: e.tensor_scalar(out=kw[:], in0=ktt[b][:, hs], scalar1=wkc[:, 0:1], scalar2=None, op0=ALU.mult), [ktt[b], wkc], [kw])
                    for d2 in range(2):
                        p3 = psf(c)
                        for q2 in range(2):
                            dkc = d2 * 2 + q2
                            k.op("pe", lambda e, p3=p3, q2=q2, dkc=dkc, hs=hs, b=b: e.matmul(out=p3[:, q2 * 512:(q2 + 1) * 512], lhsT=kw[:, dkc * 128:(dkc + 1) * 128],
                                                                                          rhs=vtt[b][:, hs], start=True, stop=True), [kw, vtt[b]], [p3])
                        k.op("dve", lambda e, p3=p3, d2=d2, hh=hh: e.scalar_tensor_tensor(out=Cst[:, hh, d2 * 2:(d2 + 1) * 2, :].rearrange("p a b -> p (a b)"),
                                                                                        in0=Cst[:, hh, d2 * 2:(d2 + 1) * 2, :].rearrange("p a b -> p (a b)"), scalar=cd[:, 0:1],
                                                                                        in1=p3[:], op0=ALU.mult, op1=ALU.add), [Cst, cd, p3], [Cst])
                    k.op("act", lambda e, hh=hh: e.copy(out=Cb[:, hh, :, :], in_=Cst[:, hh, :, :]), [Cst], [Cb])
                    p4 = psf(c)
                    for dkc in range(4):
                        k.op("pe", lambda e, p4=p4, dkc=dkc: e.matmul(out=p4[:, dkc * 2:dkc * 2 + 2], lhsT=kw[:, dkc * 128:(dkc + 1) * 128], rhs=ones2[:], start=True, stop=True),
                             [kw, ones2], [p4])
                    k.op("dve", lambda e, p4=p4, hh=hh: e.scalar_tensor_tensor(out=nst[:, hh, :, :].rearrange("p a b -> p (a b)"), in0=nst[:, hh, :, :].rearrange("p a b -> p (a b)"),
                                                                               scalar=cd[:, 0:1], in1=p4[:, 0:8], op0=ALU.mult, op1=ALU.add), [nst, cd, p4], [nst])
                    k.op("dve", lambda e, hh=hh: e.tensor_copy(out=nstb[:, hh, :, :], in_=nst[:, hh, :, :]), [nst], [nstb])
                k.dma("sp", GO[ti * 128:(ti + 1) * 128, :], outb[b][:], reads=[outb[b]])
    if 'mlE' in DBG:
        return
    with k.scope():
        wd = k.sb([128, 16, 1024], BF16, "w_down")
        load_w(c, wd, c.inp["ml_w_down"][0], 2048, 1024)
        tl = Tail(c, li, 0)
        hf = [k.sb([128, D], F32, "hf") for _ in range(2)]
        ab = [k.sb([128, 2048], BF16, "ab") for _ in range(2)]
        aT = [k.sb([128, 16, 128], BF16, "aT") for _ in range(2)]

        def load2(ti):
            k.dma("sp", hf[ti % 2][:], h_in[ti * 128:(ti + 1) * 128, :], writes=[hf[ti % 2]])
            k.dma("sp", ab[ti % 2][:], GO[ti * 128:(ti + 1) * 128, :], writes=[ab[ti % 2]])

        load2(0)
        for ti in range(c.NTILES):
            if ti + 1 < c.NTILES:
                load2(ti + 1)
            b = ti % 2
            for h2 in range(2):
                transpose_into(c, lambda h2=h2, b=b: aT[b][:, h2 * 8:(h2 + 1) * 8, :].rearrange("p a b -> p (a b)"),
                               lambda ci, h2=h2, b=b: ab[b][:, (h2 * 8 + ci) * 128:(h2 * 8 + ci + 1) * 128], 8, ab[b], aT[b], evac=("act" if h2 == 0 else "dve"))
            p = psf(c)
            for nb in range(2):
                for kc in range(16):
                    k.op("pe", lambda e, p=p, nb=nb, kc=kc, b=b: e.matmul(out=p[:, nb * 512:(nb + 1) * 512], lhsT=aT[b][:, kc, :], rhs=wd[:, kc, nb * 512:(nb + 1) * 512],
                                                                         start=(kc == 0), stop=(kc == 15)), [aT[b], wd], [p])
            tl.run(hf[b], lambda hf_, p=p: p[:, hf_ * 512:(hf_ + 1) * 512], p, h_out, ti)


def build(T_, NSEQ, plan, needed):
    c = setup(T_, NSEQ, needed)
    k = c.k
    cur = c.inp["x"]
    bufs = [c.hA, c.hB]
    bi = 0
    for pi, (kind, li) in enumerate(plan):
        dst = c.out if pi == len(plan) - 1 else bufs[bi]
        if kind == "xa":
            phase_xa(c, li, cur, dst)
        elif kind == "peer":
            phase_peer(c, li, cur, dst)
        elif kind == "s5":
            phase_s5(c, li, cur, dst)
        elif kind == "da":
            phase_da(c, li, cur, dst)
        elif kind == "m2":
            phase_m2(c, li, cur, dst)
        elif kind == "ml":
            phase_ml(c, li, cur, dst)
        cur = dst
        bi ^= 1
    return c, k.finish()


N_CORES = 8
SEQ_FULL = 4096
PLAN = []
for _i, _mx in enumerate(["s5", "da", "m2", "ml"]):
    PLAN += [(_mx, _i), ("xa", _i), ("peer", _i)]


def kernel(**inputs):
    nseq = 16 // N_CORES
    needed = set(n for n, _ in INPUT_SPECS)
    c, nc = build(SEQ_FULL, nseq, PLAN, needed)
    consts = host_consts(SEQ_FULL)
    x = np.ascontiguousarray(np.asarray(inputs["x"], dtype=np.float32))
    mem = np.ascontiguousarray(np.asarray(inputs["mem"], dtype=np.float32))
    shared = {}
    for name in c.inp:
        if name in ("x", "mem"):
            continue
        if name.startswith("c_"):
            shared[name] = consts[name]
        else:
            shared[name] = np.ascontiguousarray(np.asarray(inputs[name], dtype=np.float32))
    in_maps = []
    for ci in range(N_CORES):
        m = dict(shared)
        m["x"] = x[ci * nseq:(ci + 1) * nseq].reshape(nseq * SEQ_FULL, D)
        m["mem"] = mem[ci * nseq:(ci + 1) * nseq].reshape(nseq * 256, D)
        in_maps.append(m)
    res = run_bass_kernel_spmd(nc, in_maps, core_ids=list(range(N_CORES)))
    outs = [np.asarray(r["out"]).reshape(nseq, SEQ_FULL, D) for r in res.results]
    return np.concatenate(outs, axis=0).astype(np.float32)
```

```python
import numpy as np
import ml_dtypes
from contextlib import ExitStack
import concourse.bass as bass
import concourse.mybir as mybir
from concourse.bass_utils import run_bass_kernel_spmd

F32 = mybir.dt.float32
BF16 = mybir.dt.bfloat16
I32 = mybir.dt.int32
U32 = mybir.dt.uint32
AF = mybir.ActivationFunctionType
ALU = mybir.AluOpType
AX = mybir.AxisListType

ENGS = ("pe", "dve", "act", "pool", "sp")
NDSEM = 8


class Res:
    __slots__ = ("name", "w", "r")

    def __init__(self, name):
        self.name = name
        self.w = None
        self.r = {}


class T:
    __slots__ = ("t", "res")

    def __init__(self, t, res):
        self.t = t
        self.res = res

    def __getitem__(self, key):
        return self.t[key]

    def ap(self):
        return self.t.ap()


class KB:
    def __init__(self):
        self.nc = bass.Bass("TRN2", target_bir_lowering=False)
        self.stack = ExitStack()
        self.stacks = [self.stack]
        self.q = {e: [] for e in ENGS}
        self.cnt = {e: 0 for e in ENGS}
        self.waited = {e: {} for e in ENGS}
        self.sems = {}
        for e in ENGS:
            self.sems[e] = self.stack.enter_context(self.nc.semaphore("s_" + e))
        self.dsem = {}
        self.dval = {}
        self.dnext = {}
        for qn in ("sp", "pool", "act"):
            self.dsem[qn] = [self.stack.enter_context(self.nc.semaphore("d_%s%d" % (qn, i))) for i in range(NDSEM)]
            self.dval[qn] = [0] * NDSEM
            self.dnext[qn] = 0
        self.n_ins = 0
        self.uid = 0
        nc = self.nc
        self.eng = {"pe": nc.tensor, "dve": nc.vector, "act": nc.scalar, "pool": nc.gpsimd, "sp": nc.sync}

    def sb(self, shape, dtype, name=None):
        self.uid += 1
        name = "%s_%d" % (name or "sb", self.uid)
        t = self.stacks[-1].enter_context(self.nc.sbuf_tensor(name, list(shape), dtype))
        return T(t, Res(name))

    def ps(self, shape, dtype, name=None):
        self.uid += 1
        name = "%s_%d" % (name or "ps", self.uid)
        t = self.stacks[-1].enter_context(self.nc.psum_tensor(name, list(shape), dtype))
        return T(t, Res(name))

    def dram(self, name, shape, dtype, kind="Internal"):
        t = self.nc.dram_tensor(name, list(shape), dtype, kind=kind)
        return T(t, Res(name))

    def _wait(self, e, tok):
        key, val = tok
        if self.waited[e].get(key, 0) >= val:
            return
        self.waited[e][key] = val
        sem = self._sem(key)
        self.eng[e].wait_ge(sem, val)

    def _sem(self, key):
        if isinstance(key, str):
            return self.sems[key]
        return self.dsem[key[0]][key[1]]

    def _deps(self, e, reads, writes, skip_self=False):
        toks = []
        for r in reads:
            if r.w is not None:
                toks.append(r.w)
        for w in writes:
            if w.w is not None:
                toks.append(w.w)
            toks.extend(w.r.items())
        for tok in toks:
            if skip_self and tok[0] == e:
                continue
            self._wait(e, tok)

    def _commit(self, tok, reads, writes):
        for r in reads:
            r.r[tok[0]] = tok[1]
        for w in writes:
            w.w = tok
            w.r = {}

    @staticmethod
    def _res(lst):
        out = []
        for x in lst:
            if x is None:
                continue
            out.append(x.res if isinstance(x, T) else x)
        return out

    def op(self, e, fn, reads=(), writes=()):
        reads = self._res(reads)
        writes = self._res(writes)
        self._deps(e, reads, writes, skip_self=(e == "pe"))
        self.cnt[e] += 1
        tok = (e, self.cnt[e])
        sem = self.sems[e]
        fn(self.eng[e]).then_inc(sem, 1)
        self._commit(tok, reads, writes)
        self.n_ins += 1

    def dma(self, qn, out, in_, reads=(), writes=(), **kw):
        reads = self._res(reads)
        writes = self._res(writes)
        i = self.dnext[qn]
        self.dnext[qn] = (i + 1) % NDSEM
        key = (qn, i)
        if self.dval[qn][i] > 0:
            self._wait(qn, (key, self.dval[qn][i]))
        self._deps(qn, reads, writes)
        self.dval[qn][i] += 16
        tok = (key, self.dval[qn][i])
        sem = self.dsem[qn][i]
        self.eng[qn].dma_start(out=out, in_=in_, **kw).then_inc(sem, 16)
        self._commit(tok, reads, writes)
        self.n_ins += 1

    def barrier(self):
        for e in ENGS:
            for qn in self.dsem:
                for i in range(NDSEM):
                    if self.dval[qn][i] > 0:
                        self._wait(e, ((qn, i), self.dval[qn][i]))
            for e2 in ENGS:
                if e2 != e and self.cnt[e2] > 0:
                    self._wait(e, (e2, self.cnt[e2]))

    def scope(self):
        kb = self

        class _S:
            def __enter__(s2):
                kb.stacks.append(ExitStack())

            def __exit__(s2, *a):
                kb.barrier()
                kb.stacks.pop().close()
                return False
        return _S()

    def finish(self, final_res=()):
        for qn in self.dsem:
            for i in range(NDSEM):
                if self.dval[qn][i] > 0:
                    self._wait("sp", ((qn, i), self.dval[qn][i]))
        for e in ENGS:
            if e != "sp" and self.cnt[e] > 0:
                self._wait("sp", (e, self.cnt[e]))
        self.stack.close()
        return self.nc

import os
DBG = os.environ.get('KDBG', '')

D = 1024
ALPHA = 8 ** 0.25
LN_EPS = 1e-5
NEG = -30000.0

INPUT_SPECS = [
    ("x", None), ("mem", None),
    ("s5_lam_re", (1, 64, 64)), ("s5_lam_im", (1, 64, 64)), ("s5_log_dt", (1, 64)),
    ("s5_b_re", (1, 64, 64, 16)), ("s5_b_im", (1, 64, 64, 16)), ("s5_c_re", (1, 64, 16, 64)), ("s5_c_im", (1, 64, 16, 64)),
    ("s5_d", (1, 1024)), ("s5_w_glu", (1, 1024, 2048)), ("s5_b_glu", (1, 2048)),
    ("da_w_qkv", (1, 1024, 3072)), ("da_lambda", (1, 4, 64)), ("da_subln_g", (1, 128)), ("da_w_o", (1, 1024, 1024)),
    ("m2_w_in", (1, 1024, 5152)), ("m2_conv_w", (1, 4, 3072)), ("m2_conv_b", (1, 3072)), ("m2_dt_bias", (1, 32)),
    ("m2_a_log", (1, 32)), ("m2_d", (1, 32)), ("m2_norm_g", (1, 2048)), ("m2_w_out", (1, 2048, 1024)),
    ("ml_w_in", (1, 1024, 4096)), ("ml_conv_w", (1, 4, 2048)), ("ml_conv_b", (1, 2048)),
    ("ml_w_q", (1, 4, 512, 512)), ("ml_w_k", (1, 4, 512, 512)), ("ml_w_v", (1, 4, 512, 512)),
    ("ml_w_gates", (1, 3, 2048, 8)), ("ml_b_gates", (1, 8)), ("ml_norm_g", (1, 2048)), ("ml_skip", (1, 2048)),
    ("ml_w_down", (1, 2048, 1024)),
    ("xa_w_q", (4, 1024, 1024)), ("xa_w_kv", (4, 1024, 2048)), ("xa_w_o", (4, 1024, 1024)),
    ("pk_w_query", (4, 1024, 2048)), ("pk_sub_keys", (4, 2, 128, 128)), ("pk_u", (4, 16384, 1024)), ("pk_v", (4, 16384, 1024)),
    ("ln_g", (4, 3, 1024)), ("ln_b", (4, 3, 1024)),
]


class Ctx:
    pass


def host_consts(T_=4096):
    c = {}
    hm = np.zeros((128, 2), np.float32)
    hm[:64, 0] = 0.125
    hm[64:, 1] = 0.125
    c["c_hmask"] = hm
    c["c_r0"] = np.tile(-(T_ - np.arange(T_, dtype=np.float32))[None, :], (128, 1)).astype(np.float32)
    qq = np.arange(128, dtype=np.float32)
    c["c_dbase"] = (-np.abs(qq[:, None] - qq[None, :]) + qq[:, None]).astype(np.float32)
    dm = np.zeros((128, 128), np.float32)
    dm[:64, 64:] = NEG
    c["c_dmask"] = dm
    sl = np.zeros((128, 4, 128), np.float32)
    for hh in range(4):
        sl[hh, hh, :] = 1.0
    c["c_sel"] = sl
    c["c_triU"] = np.triu(np.ones((128, 128), np.float32))
    c["c_Lst"] = np.tril(np.ones((128, 128), np.float32), -1)
    c["c_ident"] = np.eye(128, dtype=np.float32)
    c["c_iota16"] = np.tile(np.arange(16, dtype=np.float32)[None, :], (128, 1))
    J = np.zeros((128, 128), np.float32)
    for p in range(64):
        J[p, p + 64] = -1.0
        J[p + 64, p] = 1.0
    c["c_J"] = J
    sg = np.ones((128, 1), np.float32)
    sg[:64] = -1.0
    c["c_sgn"] = sg
    gm = np.zeros((128, 8), np.float32)
    cm = np.zeros((128, 8, 128), np.float32)
    for j in range(8):
        gm[j * 16:(j + 1) * 16, j] = 1.0
        cm[:, j, j * 16:(j + 1) * 16] = 1.0
    c["c_gmask"] = gm
    c["c_cmask"] = cm
    return c


def setup(T_, NSEQ, needed):
    c = Ctx()
    k = KB()
    c.k = k
    c.T = T_
    c.NPF = 3
    c.NSEQ = NSEQ
    c.NT = T_ * NSEQ
    c.NTILES = c.NT // 128
    c.inp = {}
    for name, shp in INPUT_SPECS:
        if name not in needed:
            continue
        if name == "x":
            shp = (c.NT, D)
        elif name == "mem":
            shp = (NSEQ * 256, D)
        c.inp[name] = k.dram(name, list(shp), F32, kind="ExternalInput")
    for name, arr in host_consts(T_).items():
        c.inp[name] = k.dram(name, list(arr.shape), F32, kind="ExternalInput")
    c.out = k.dram("out", [c.NT, D], F32, kind="ExternalOutput")
    c.hA = k.dram("hA", [c.NT, D], F32)
    c.hB = k.dram("hB", [c.NT, D], F32)
    c.identf = k.sb([128, 128], F32, "identf")
    c.identb = k.sb([128, 128], BF16, "identb")
    k.dma("sp", c.identf[:], c.inp["c_ident"][:, :], writes=[c.identf])
    k.op("dve", lambda e: e.tensor_copy(out=c.identb[:], in_=c.identf[:]), [c.identf], [c.identb])
    c.pf = [k.ps([128, 1024], F32, "pf%d" % i) for i in range(c.NPF)]
    c.pb = [k.ps([128, 1024], BF16, "pb%d" % i) for i in range(2)]
    c.pfi = 0
    c.pbi = 0
    c.pf_n = len(c.pf)
    c.UV = {}
    c.uv_built = set()
    c.peer_layers = []
    return c


def psf(c):
    c.pfi = (c.pfi + 1) % c.pf_n
    return c.pf[c.pfi]


def psb(c):
    c.pbi = (c.pbi + 1) % len(c.pb)
    return c.pb[c.pbi]


def load_w(c, dst, src_ap, K, N, q="pool"):
    k = c.k
    KC = K // 128
    for n0 in range(0, N, 2048):
        n1 = min(N, n0 + 2048)
        for c0 in range(0, KC, 8):
            c1 = min(KC, c0 + 8)
            k.dma(q, dst[:, c0:c1, n0:n1],
                  src_ap[c0 * 128:c1 * 128, n0:n1].rearrange("(c p) n -> p c n", p=128), writes=[dst])


def load_bc(c, dst, src_ap, q="sp"):
    c.k.dma(q, dst[:], src_ap.broadcast_to([128, src_ap.shape[-1]]), writes=[dst])


def transpose_into(c, dst_fn, src_fn, C, src, dst, evac="act"):
    k = c.k
    p = psb(c)
    for ci in range(C):
        k.op("pe", lambda e, ci=ci: e.transpose(out=p[:, ci * 128:(ci + 1) * 128], in_=src_fn(ci), identity=c.identb[:]),
             [src, c.identb], [p])
    if evac == "act":
        k.op("act", lambda e: e.copy(out=dst_fn(), in_=p[:, 0:C * 128]), [p], [dst])
    else:
        k.op("dve", lambda e: e.tensor_copy(out=dst_fn(), in_=p[:, 0:C * 128]), [p], [dst])


def tail_ln(c, ht, y_fn, ysrc, g_bc, b_bc, outt, z, st, mv, rstd):
    k = c.k
    for hf in range(2):
        sl = slice(hf * 512, (hf + 1) * 512)
        k.op("dve", lambda e, hf=hf, sl=sl: e.scalar_tensor_tensor(out=z[:, sl], in0=ht[:, sl], scalar=ALPHA, in1=y_fn(hf),
                                                                    op0=ALU.mult, op1=ALU.add), [ht, ysrc], [z])
        k.op("dve", lambda e, hf=hf, sl=sl: e.bn_stats(out=st[:, hf, :], in_=z[:, sl]), [z], [st])
    k.op("dve", lambda e: e.bn_aggr(out=mv[:], in_=st[:].rearrange("p a b -> p (a b)")), [st], [mv])
    k.op("act", lambda e: e.activation(out=rstd[:], in_=mv[:, 1:2], func=AF.Sqrt, bias=LN_EPS, scale=1.0), [mv], [rstd])
    k.op("dve", lambda e: e.reciprocal(out=rstd[:], in_=rstd[:]), [rstd], [rstd])
    k.op("dve", lambda e: e.tensor_scalar(out=z[:], in0=z[:], scalar1=mv[:, 0:1], scalar2=rstd[:, 0:1],
                                          op0=ALU.subtract, op1=ALU.mult), [z, mv, rstd], [z])
    k.op("pool", lambda e: e.tensor_tensor(out=z[:], in0=z[:], in1=g_bc[:], op=ALU.mult), [z, g_bc], [z])
    k.op("pool", lambda e: e.tensor_tensor(out=outt[:], in0=z[:], in1=b_bc[:], op=ALU.add), [z, b_bc], [outt])


class Tail:
    def __init__(self, c, li, sub):
        k = c.k
        self.c = c
        self.g = k.sb([128, D], F32, "lng")
        self.b = k.sb([128, D], F32, "lnb")
        load_bc(c, self.g, c.inp["ln_g"][li, sub:sub + 1, :])
        load_bc(c, self.b, c.inp["ln_b"][li, sub:sub + 1, :])
        self.z = [k.sb([128, D], F32, "z") for _ in range(2)]
        self.o = [k.sb([128, D], F32, "ho") for _ in range(2)]
        self.st = [k.sb([128, 2, 6], F32, "st") for _ in range(2)]
        self.mv = [k.sb([128, 2], F32, "mv") for _ in range(2)]
        self.rs = [k.sb([128, 1], F32, "rs") for _ in range(2)]
        self.i = 0

    def run(self, ht, y_fn, ysrc, h_out, ti):
        c = self.c
        i = self.i
        self.i = (i + 1) % 2
        tail_ln(c, ht, y_fn, ysrc, self.g, self.b, self.o[i], self.z[i], self.st[i], self.mv[i], self.rs[i])
        c.k.dma("sp", h_out[ti * 128:(ti + 1) * 128, :], self.o[i][:], reads=[self.o[i]])


def phase_xa(c, li, h_in, h_out):
    k = c.k
    with k.scope():
        wq = k.sb([128, 8, 1024], BF16, "wq")
        wkv = k.sb([128, 8, 2048], BF16, "wkv")
        wo = k.sb([128, 8, 1024], BF16, "wo")
        load_w(c, wq, c.inp["xa_w_q"][li], 1024, 1024)
        load_w(c, wkv, c.inp["xa_w_kv"][li], 1024, 2048)
        load_w(c, wo, c.inp["xa_w_o"][li], 1024, 1024)
        tl = Tail(c, li, 1)
        KT = [k.sb([128, 8, 256], BF16, "KT") for _ in range(c.NSEQ)]
        V = [k.sb([128, 2, 1024], BF16, "V") for _ in range(c.NSEQ)]
        memf = k.sb([128, 1024], F32, "memf")
        memb = k.sb([128, 1024], BF16, "memb")
        memT = k.sb([128, 8, 256], BF16, "memT")
        for s in range(c.NSEQ):
            for mc in range(2):
                k.dma("sp", memf[:], c.inp["mem"][s * 256 + mc * 128: s * 256 + (mc + 1) * 128, :], writes=[memf])
                k.op("dve", lambda e: e.tensor_copy(out=memb[:], in_=memf[:]), [memf], [memb])
                transpose_into(c, lambda mc=mc: memT[:, :, mc * 128:(mc + 1) * 128],
                               lambda ci: memb[:, ci * 128:(ci + 1) * 128], 8, memb, memT)
            for fc in range(8):
                p = psf(c)
                for kc in range(8):
                    k.op("pe", lambda e, fc=fc, kc=kc, p=p: e.matmul(out=p[:, 0:256], lhsT=wkv[:, kc, fc * 128:(fc + 1) * 128],
                                                                     rhs=memT[:, kc, :], start=(kc == 0), stop=(kc == 7)),
                         [wkv, memT], [p])
                k.op("act", lambda e, fc=fc, p=p, s=s: e.copy(out=KT[s][:, fc, :], in_=p[:, 0:256]), [p], [KT[s]])
            for mc in range(2):
                p = psf(c)
                for nb in range(2):
                    for kc in range(8):
                        k.op("pe", lambda e, nb=nb, kc=kc, p=p, mc=mc: e.matmul(
                            out=p[:, nb * 512:(nb + 1) * 512], lhsT=memT[:, kc, mc * 128:(mc + 1) * 128],
                            rhs=wkv[:, kc, 1024 + nb * 512:1024 + (nb + 1) * 512], start=(kc == 0), stop=(kc == 7)),
                            [wkv, memT], [p])
                k.op("dve", lambda e, p=p, mc=mc, s=s: e.tensor_copy(out=V[s][:, mc, :], in_=p[:]), [p], [V[s]])
        hf = [k.sb([128, D], F32, "hf") for _ in range(3)]
        hb = [k.sb([128, D], BF16, "hb") for _ in range(2)]
        hT = [k.sb([128, 8, 128], BF16, "hT") for _ in range(2)]
        qT = [k.sb([128, 8, 128], BF16, "qT") for _ in range(2)]
        ssb = [k.sb([128, 4, 256], F32, "ssb") for _ in range(2)]
        P = [k.sb([128, 4, 256], BF16, "P") for _ in range(2)]
        PT = [k.sb([128, 8, 128], BF16, "PT") for _ in range(2)]
        ob = [k.sb([128, D], BF16, "ob") for _ in range(2)]
        oT = [k.sb([128, 8, 128], BF16, "oT") for _ in range(2)]
        mx = [k.sb([128, 4], F32, "mx") for _ in range(2)]
        sm = [k.sb([128, 4], F32, "sm") for _ in range(2)]
        tps = c.T // 128

        def load(ti):
            k.dma("sp", hf[ti % 3][:], h_in[ti * 128:(ti + 1) * 128, :], writes=[hf[ti % 3]])

        def stage_a(ti):
            if ti + 1 < c.NTILES:
                load(ti + 1)
            b = ti % 2
            s = ti // tps
            hfb = hf[ti % 3]
            k.op("dve", lambda e: e.tensor_copy(out=hb[b][:], in_=hfb[:]), [hfb], [hb[b]])
            transpose_into(c, lambda: hT[b][:].rearrange("p a b -> p (a b)"),
                           lambda ci: hb[b][:, ci * 128:(ci + 1) * 128], 8, hb[b], hT[b])
            p = psf(c)
            for fc in range(8):
                for kc in range(8):
                    k.op("pe", lambda e, fc=fc, kc=kc, p=p: e.matmul(
                        out=p[:, fc * 128:(fc + 1) * 128], lhsT=wq[:, kc, fc * 128:(fc + 1) * 128], rhs=hT[b][:, kc, :],
                        start=(kc == 0), stop=(kc == 7)), [wq, hT[b]], [p])
            k.op("act", lambda e, p=p: e.activation(out=qT[b][:].rearrange("p a b -> p (a b)"), in_=p[:], func=AF.Copy,
                                                    scale=0.0625), [p], [qT[b]])
            p = psf(c)
            for hd in range(4):
                for cc in range(2):
                    k.op("pe", lambda e, hd=hd, cc=cc, p=p: e.matmul(
                        out=p[:, hd * 256:(hd + 1) * 256], lhsT=qT[b][:, 2 * hd + cc, :], rhs=KT[s][:, 2 * hd + cc, :],
                        start=(cc == 0), stop=(cc == 1)), [qT[b], KT[s]], [p])
            k.op("dve", lambda e, p=p: e.tensor_copy(out=ssb[b][:].rearrange("p a b -> p (a b)"), in_=p[:]), [p], [ssb[b]])
            k.op("dve", lambda e: e.tensor_reduce(out=mx[b][:], in_=ssb[b][:], axis=AX.X, op=ALU.max, negate=True),
                 [ssb[b]], [mx[b]])
            for hd in range(4):
                k.op("act", lambda e, hd=hd: e.activation(out=P[b][:, hd, :], in_=ssb[b][:, hd, :], func=AF.Exp,
                                                         bias=mx[b][:, hd:hd + 1], scale=1.0,
                                                         accum_out=sm[b][:, hd:hd + 1]), [ssb[b], mx[b]], [P[b], sm[b]])

        def stage_b(ti):
            b = ti % 2
            s = ti // tps
            hfb = hf[ti % 3]
            k.op("dve", lambda e: e.reciprocal(out=sm[b][:], in_=sm[b][:]), [sm[b]], [sm[b]])
            transpose_into(c, lambda: PT[b][:].rearrange("p a b -> p (a b)"),
                           lambda ci: P[b][:, ci // 2, (ci % 2) * 128:(ci % 2 + 1) * 128], 8, P[b], PT[b], evac="dve")
            p = psf(c)
            for hd in range(4):
                for mc in range(2):
                    k.op("pe", lambda e, hd=hd, mc=mc, p=p: e.matmul(
                        out=p[:, hd * 256:(hd + 1) * 256], lhsT=PT[b][:, 2 * hd + mc, :], rhs=V[s][:, mc, hd * 256:(hd + 1) * 256],
                        start=(mc == 0), stop=(mc == 1)), [PT[b], V[s]], [p])
            for hd in range(4):
                k.op("act", lambda e, hd=hd, p=p: e.activation(out=ob[b][:, hd * 256:(hd + 1) * 256],
                                                              in_=p[:, hd * 256:(hd + 1) * 256], func=AF.Copy,
                                                              scale=sm[b][:, hd:hd + 1]), [p, sm[b]], [ob[b]])
            transpose_into(c, lambda: oT[b][:].rearrange("p a b -> p (a b)"),
                           lambda ci: ob[b][:, ci * 128:(ci + 1) * 128], 8, ob[b], oT[b], evac="act")
            p = psf(c)
            for nb in range(2):
                for kc in range(8):
                    k.op("pe", lambda e, nb=nb, kc=kc, p=p: e.matmul(
                        out=p[:, nb * 512:(nb + 1) * 512], lhsT=oT[b][:, kc, :], rhs=wo[:, kc, nb * 512:(nb + 1) * 512],
                        start=(kc == 0), stop=(kc == 7)), [oT[b], wo], [p])
            tl.run(hfb, lambda hf_, p=p: p[:, hf_ * 512:(hf_ + 1) * 512], p, h_out, ti)

        load(0)
        for ti in range(c.NTILES + 1):
            if ti < c.NTILES:
                stage_a(ti)
            if ti >= 1:
                stage_b(ti - 1)


class UVBuilder:
    def __init__(self, c, layers):
        k = c.k
        self.c = c
        self.jobs = [(li, which, r0) for li in layers for which in range(2) for r0 in range(0, 16384, 128)]
        self.pos = 0
        self.stf = [k.sb([128, 1024], F32, "uvf") for _ in range(2)]
        self.stb = [k.sb([128, 1024], BF16, "uvb") for _ in range(2)]
        for li in layers:
            if li not in c.UV:
                c.UV[li] = k.dram("pk_UV%d" % li, [16384, 2048], BF16)
        self.loaded = -1

    def _load(self, j):
        li, which, r0 = self.jobs[j]
        src = self.c.inp[("pk_u", "pk_v")[which]][li]
        a = self.stf[j % 2]
        self.c.k.dma("sp", a[:], src[r0:r0 + 128, :], writes=[a])
        self.loaded = j

    def step(self, n):
        k = self.c.k
        for _ in range(n):
            j = self.pos
            if j >= len(self.jobs):
                return
            if self.loaded < j:
                self._load(j)
            if j + 1 < len(self.jobs):
                self._load(j + 1)
            li, which, r0 = self.jobs[j]
            a, bb = self.stf[j % 2], self.stb[j % 2]
            k.op("pool", lambda e, a=a, bb=bb: e.tensor_copy(out=bb[:], in_=a[:]), [a], [bb])
            k.dma("sp", self.c.UV[li][r0:r0 + 128, which * 1024:(which + 1) * 1024], bb[:], reads=[bb])
            self.pos += 1
            if self.pos == len(self.jobs) or self.jobs[self.pos][0] != li:
                self.c.uv_built.add(li)

    def finish(self):
        self.step(len(self.jobs))


def phase_peer(c, li, h_in, h_out):
    k = c.k
    NS = 16
    GRP = 8
    if li not in c.UV:
        c.UV[li] = k.dram("pk_UV%d" % li, [16384, 2048], BF16)
    UV = c.UV[li]
    if li in c.uv_built:
        pass
    else:
      with k.scope():
          stf = [k.sb([128, 8192], F32, "stf") for _ in range(2)]
          stb = [k.sb([128, 8192], BF16, "stb") for _ in range(2)]
          it = 0
          for which, nm in enumerate(("pk_u", "pk_v")):
              src = c.inp[nm][li]
              for r0 in range(0, 16384, 1024):
                  a, bb = stf[it % 2], stb[it % 2]
                  k.dma("sp", a[:].rearrange("p (r d) -> p r d", r=8), src[r0:r0 + 1024, :].rearrange("(p r) d -> p r d", r=8), writes=[a])
                  if it % 2 == 0:
                      k.op("act", lambda e, a=a, bb=bb: e.copy(out=bb[:], in_=a[:]), [a], [bb])
                  else:
                      k.op("dve", lambda e, a=a, bb=bb: e.tensor_copy(out=bb[:], in_=a[:]), [a], [bb])
                  k.dma("sp", UV[r0:r0 + 1024, which * 1024:(which + 1) * 1024].rearrange("(p r) d -> p r d", r=8),
                        bb[:].rearrange("p (r d) -> p r d", r=8), reads=[bb])
                  it += 1
    with k.scope():
        wqy = k.sb([128, 8, 2048], BF16, "wqy")
        load_w(c, wqy, c.inp["pk_w_query"][li], 1024, 2048)
        tl = Tail(c, li, 2)
        iota16 = k.sb([128, 16], F32, "iota16")
        k.dma("sp", iota16[:], c.inp["c_iota16"][:, :], writes=[iota16])
        skf = k.sb([128, 128], F32, "skf")
        skb = k.sb([128, 128], BF16, "skb")
        skT = k.sb([128, 2, 128], BF16, "skT")
        for j in range(2):
            k.dma("sp", skf[:], c.inp["pk_sub_keys"][li, j], writes=[skf])
            k.op("dve", lambda e: e.tensor_copy(out=skb[:], in_=skf[:]), [skf], [skb])
            transpose_into(c, lambda j=j: skT[:, j, :], lambda ci: skb[:, :], 1, skb, skT)
        hf = [k.sb([128, D], F32, "hf") for _ in range(2)]
        hb = k.sb([128, D], BF16, "hb")
        hT = k.sb([128, 8, 128], BF16, "hT")
        qT = k.sb([128, 16, 128], BF16, "qT")
        ssb = k.sb([128, 16, 128], F32, "ssb")
        tmp = k.sb([128, 16, 128], F32, "tmp")
        sv = k.sb([128, 16, 16], F32, "sv")
        si = k.sb([128, 16, 16], U32, "si")
        sif = k.sb([128, 16, 16], F32, "sif")
        cand = k.sb([128, 8, 16, 16], F32, "cand")
        tmpc = k.sb([128, 8, 256], F32, "tmpc")
        cv = k.sb([128, 8, 16], F32, "cv")
        ci = k.sb([128, 8, 16], U32, "ci")
        cab = [k.sb([128, 8, 16], U32, "cab") for _ in range(2)]
        cabf = [k.sb([128, 8, 16], F32, "cabf") for _ in range(2)]
        oh = k.sb([128, 8, 16, 16], F32, "oh")
        k12 = [k.sb([128, 8, 16], F32, "k12") for _ in range(2)]
        eif = k.sb([128, 8, 16], F32, "eif")
        eiu = [k.sb([128, 8, 16], U32, "eiu") for _ in range(2)]
        gt = [k.sb([128, 8, 16], F32, "gt") for _ in range(2)]
        zs = k.sb([128, 8], F32, "zs")
        dots = [k.sb([128, 128], F32, "dots") for _ in range(2)]
        actv = [k.sb([128, 128], F32, "actv") for _ in range(2)]
        junk = k.sb([128, D], BF16, "junk")
        uv = [k.sb([128, 2048], BF16, "uv") for _ in range(NS)]
        dg = [k.sb([128, 128], BF16, "dg") for _ in range(NS)]
        pacc = c.pf[2]
        c.pf_n = 2

        def load(ti):
            k.dma("sp", hf[ti % 2][:], h_in[ti * 128:(ti + 1) * 128, :], writes=[hf[ti % 2]])

        def proj(ti):
            hfp = hf[ti % 2]
            k.op("act", lambda e: e.copy(out=hb[:], in_=hfp[:]), [hfp], [hb])
            transpose_into(c, lambda: hT[:].rearrange("p a b -> p (a b)"), lambda ci_: hb[:, ci_ * 128:(ci_ + 1) * 128], 8, hb, hT)
            for half in range(2):
                p = psf(c)
                for fc in range(8):
                    for kc in range(8):
                        k.op("pe", lambda e, fc=fc, kc=kc, p=p, half=half: e.matmul(
                            out=p[:, fc * 128:(fc + 1) * 128], lhsT=wqy[:, kc, (half * 8 + fc) * 128:(half * 8 + fc + 1) * 128],
                            rhs=hT[:, kc, :], start=(kc == 0), stop=(kc == 7)), [wqy, hT], [p])
                k.op("act", lambda e, p=p, half=half: e.copy(out=qT[:, half * 8:(half + 1) * 8, :].rearrange("p a b -> p (a b)"),
                                                            in_=p[:]), [p], [qT])
            for half in range(2):
                p = psf(c)
                for fc in range(8):
                    cidx = half * 8 + fc
                    k.op("pe", lambda e, fc=fc, cidx=cidx, p=p: e.matmul(
                        out=p[:, fc * 128:(fc + 1) * 128], lhsT=qT[:, cidx, :], rhs=skT[:, cidx % 2, :], start=True, stop=True),
                        [qT, skT], [p])
                k.op("act", lambda e, p=p, half=half: e.copy(out=ssb[:, half * 8:(half + 1) * 8, :].rearrange("p a b -> p (a b)"),
                                                            in_=p[:]), [p], [ssb])

        load(0)
        proj(0)
        for ti in range(c.NTILES):
            if ti + 1 < c.NTILES:
                load(ti + 1)
            b = ti % 2
            hfb = hf[b]
            gtb, dt_, av_ = gt[b], dots[b], actv[b]
            for cc in range(16):
                k.op("dve", lambda e, cc=cc: e.max(out=sv[:, cc, 0:8], in_=ssb[:, cc, :]), [ssb], [sv])
                k.op("dve", lambda e, cc=cc: e.match_replace(out=tmp[:, cc, :], in_to_replace=sv[:, cc, 0:8], in_values=ssb[:, cc, :],
                                                             imm_value=-1e30), [ssb, sv], [tmp])
                k.op("dve", lambda e, cc=cc: e.max(out=sv[:, cc, 8:16], in_=tmp[:, cc, :]), [tmp], [sv])
                k.op("dve", lambda e, cc=cc: e.max_index(out=si[:, cc, 0:8], in_max=sv[:, cc, 0:8], in_values=ssb[:, cc, :]),
                     [ssb, sv], [si])
                k.op("dve", lambda e, cc=cc: e.max_index(out=si[:, cc, 8:16], in_max=sv[:, cc, 8:16], in_values=ssb[:, cc, :]),
                     [ssb, sv], [si])
            k.op("dve", lambda e: e.tensor_copy(out=sif[:], in_=si[:]), [si], [sif])
            for h in range(8):
                cflat = lambda h=h: cand[:, h, :, :].rearrange("p a b -> p (a b)")
                k.op("dve", lambda e, h=h: e.tensor_tensor(out=cand[:, h, :, :], in0=sv[:, 2 * h, :].unsqueeze(2).broadcast_to([128, 16, 16]),
                                                           in1=sv[:, 2 * h + 1, :].unsqueeze(1).broadcast_to([128, 16, 16]), op=ALU.add),
                     [sv], [cand])
                k.op("dve", lambda e, h=h, cflat=cflat: e.max(out=cv[:, h, 0:8], in_=cflat()), [cand], [cv])
                k.op("dve", lambda e, h=h, cflat=cflat: e.match_replace(out=tmpc[:, h, :], in_to_replace=cv[:, h, 0:8], in_values=cflat(),
                                                                       imm_value=-1e30), [cand, cv], [tmpc])
                k.op("dve", lambda e, h=h: e.max(out=cv[:, h, 8:16], in_=tmpc[:, h, :]), [tmpc], [cv])
                k.op("dve", lambda e, h=h, cflat=cflat: e.max_index(out=ci[:, h, 0:8], in_max=cv[:, h, 0:8], in_values=cflat()),
                     [cand, cv], [ci])
                k.op("dve", lambda e, h=h, cflat=cflat: e.max_index(out=ci[:, h, 8:16], in_max=cv[:, h, 8:16], in_values=cflat()),
                     [cand, cv], [ci])
            k.op("dve", lambda e: e.tensor_single_scalar(out=cab[0][:], in_=ci[:], scalar=4, op=ALU.logical_shift_right), [ci], [cab[0]])
            k.op("dve", lambda e: e.tensor_single_scalar(out=cab[1][:], in_=ci[:], scalar=15, op=ALU.bitwise_and), [ci], [cab[1]])
            for j in range(2):
                k.op("dve", lambda e, j=j: e.tensor_copy(out=cabf[j][:], in_=cab[j][:]), [cab[j]], [cabf[j]])
                k.op("dve", lambda e, j=j: e.tensor_tensor(
                    out=oh[:], in0=cabf[j][:].unsqueeze(3).broadcast_to([128, 8, 16, 16]),
                    in1=iota16[:].unsqueeze(1).unsqueeze(1).broadcast_to([128, 8, 16, 16]), op=ALU.is_equal), [cabf[j], iota16], [oh])
                k.op("dve", lambda e, j=j: e.tensor_tensor(
                    out=oh[:], in0=oh[:], in1=sif[:, j::2, :].unsqueeze(2).broadcast_to([128, 8, 16, 16]), op=ALU.mult), [oh, sif], [oh])
                k.op("dve", lambda e, j=j: e.tensor_reduce(out=k12[j][:], in_=oh[:], axis=AX.X, op=ALU.add), [oh], [k12[j]])
            k.op("dve", lambda e: e.scalar_tensor_tensor(out=eif[:].rearrange("p a b -> p (a b)"), in0=k12[0][:].rearrange("p a b -> p (a b)"),
                                                         scalar=128.0, in1=k12[1][:].rearrange("p a b -> p (a b)"),
                                                         op0=ALU.mult, op1=ALU.add), [k12[0], k12[1]], [eif])
            eb = eiu[b]
            k.op("dve", lambda e: e.tensor_tensor(out=gtb[:], in0=cv[:], in1=cv[:, :, 0:1].broadcast_to([128, 8, 16]), op=ALU.subtract),
                 [cv], [gtb])
            k.op("act", lambda e: e.activation(out=gtb[:], in_=gtb[:], func=AF.Exp), [gtb], [gtb])
            k.op("dve", lambda e: e.tensor_reduce(out=zs[:], in_=gtb[:], axis=AX.X, op=ALU.add), [gtb], [zs])
            k.op("dve", lambda e: e.reciprocal(out=zs[:], in_=zs[:]), [zs], [zs])
            k.op("dve", lambda e: e.tensor_tensor(out=gtb[:], in0=gtb[:], in1=zs[:].unsqueeze(2).broadcast_to([128, 8, 16]), op=ALU.mult),
                 [gtb, zs], [gtb])
            k.op("dve", lambda e: e.tensor_copy(out=eb[:], in_=eif[:]), [eif], [eb])
            for g0 in range(0, 128, GRP):
                if g0 == 2 * GRP and ti + 1 < c.NTILES:
                    proj(ti + 1)
                for hk in range(g0, g0 + GRP):
                    slot = uv[hk % NS]
                    gather(c, slot, UV.t[:, :], eb, hk // 16, hk % 16)
                    k.op("dve", lambda e, slot=slot, hk=hk: e.scalar_tensor_tensor(out=junk[:], in0=slot[:, 0:1024], scalar=1.0, in1=hfb[:], op0=ALU.mult,
                                                                                   op1=ALU.mult, accum_out=dt_[:, hk:hk + 1]),
                         [slot, hfb], [junk, dt_])
                k.op("act", lambda e, g0=g0: e.activation(out=av_[:, g0:g0 + GRP], in_=dt_[:, g0:g0 + GRP], func=AF.Gelu_apprx_tanh), [dt_], [av_])
                k.op("dve", lambda e, g0=g0: e.tensor_tensor(out=av_[:, g0:g0 + GRP], in0=av_[:, g0:g0 + GRP],
                                                             in1=gtb[:].rearrange("p a b -> p (a b)")[:, g0:g0 + GRP], op=ALU.mult), [av_, gtb], [av_])
                for hk in range(g0, g0 + GRP):
                    slot = uv[hk % NS]
                    dgs = dg[hk % NS]
                    k.op("act", lambda e, dgs=dgs, hk=hk: e.activation(out=dgs[:], in_=c.identb[:], func=AF.Copy, scale=av_[:, hk:hk + 1]), [c.identb, av_], [dgs])
                    for nb in range(2):
                        k.op("pe", lambda e, dgs=dgs, slot=slot, nb=nb, hk=hk: e.matmul(out=pacc[:, nb * 512:(nb + 1) * 512], lhsT=dgs[:],
                                                                                       rhs=slot[:, 1024 + nb * 512:1024 + (nb + 1) * 512],
                                                                                       start=(hk == 0), stop=(hk == 127)), [dgs, slot], [pacc])
            tl.run(hfb, lambda hf_: pacc[:, hf_ * 512:(hf_ + 1) * 512], pacc, h_out, ti)
        c.pf_n = 3


def gather(c, dst, table, eb, h, kk):
    k = c.k
    qn = "pool"
    reads = k._res([eb])
    writes = k._res([dst])
    i = k.dnext[qn]
    k.dnext[qn] = (i + 1) % NDSEM
    key = (qn, i)
    if k.dval[qn][i] > 0:
        k._wait(qn, (key, k.dval[qn][i]))
    k._deps(qn, reads, writes)
    k.dval[qn][i] += 16
    tok = (key, k.dval[qn][i])
    k.nc.gpsimd.indirect_dma_start(out=dst[:], out_offset=None, in_=table,
                                   in_offset=bass.IndirectOffsetOnAxis(ap=eb[:, h, kk:kk + 1], axis=0)).then_inc(k.dsem[qn][i], 16)
    k._commit(tok, reads, writes)
    k.n_ins += 1


def phase_s5(c, li, h_in, h_out):
    k = c.k
    T_ = c.T
    NB = T_ // 512
    nlev = int(np.log2(T_))
    GT = k.dram("s5_GT", [c.NSEQ, 8, 128, T_], BF16)
    TWO_PI = 2.0 * np.pi
    with k.scope():
        Jm = k.sb([128, 128], F32, "Jm")
        sgn = k.sb([128, 1], F32, "sgn")
        gmask = k.sb([128, 8], F32, "gmask")
        cmask = k.sb([128, 8, 128], F32, "cmask")
        k.dma("sp", Jm[:], c.inp["c_J"][:, :], writes=[Jm])
        k.dma("sp", sgn[:], c.inp["c_sgn"][:, :], writes=[sgn])
        k.dma("sp", gmask[:], c.inp["c_gmask"][:, :], writes=[gmask])
        k.dma("sp", cmask[:], c.inp["c_cmask"][:, :, :], writes=[cmask])
        ls = k.sb([128, 128], F32, "ls")
        k.op("dve", lambda e: e.memset(ls[:], 0.0), [], [ls])
        lre = k.sb([128, 64], F32, "lre")
        lim = k.sb([128, 64], F32, "lim")
        for nm, dst in (("s5_lam_re", lre), ("s5_lam_im", lim)):
            for hh in range(2):
                k.dma("sp", ls[0:64, hh * 64:(hh + 1) * 64], c.inp[nm][0], writes=[ls])
            p = psf(c)
            k.op("pe", lambda e, p=p: e.transpose(out=p[:, 0:128], in_=ls[:, :], identity=c.identf[:]), [ls, c.identf], [p])
            k.op("dve", lambda e, p=p, dst=dst: e.tensor_copy(out=dst[:], in_=p[:, 0:64]), [p], [dst])
        dt = k.sb([128, 64], F32, "dt")
        load_bc(c, dt, c.inp["s5_log_dt"][0:1, :])
        k.op("act", lambda e: e.activation(out=dt[:], in_=dt[:], func=AF.Exp), [dt], [dt])
        emag = k.sb([128, 64], F32, "emag")
        ang = k.sb([128, 64], F32, "ang")
        k.op("dve", lambda e: e.tensor_tensor(out=emag[:], in0=lre[:], in1=dt[:], op=ALU.mult), [lre, dt], [emag])
        k.op("act", lambda e: e.activation(out=emag[:], in_=emag[:], func=AF.Exp), [emag], [emag])
        k.op("dve", lambda e: e.tensor_tensor(out=ang[:], in0=lim[:], in1=dt[:], op=ALU.mult), [lim, dt], [ang])
        acol = k.sb([128, 64], F32, "acol")
        bcol = k.sb([128, 64], F32, "bcol")
        yv = k.sb([128, 64], F32, "yv")
        yi = k.sb([128, 64], I32, "yi")
        yf = k.sb([128, 64], F32, "yf")
        mk = k.sb([128, 64], F32, "mk")
        for off, dst in ((0.25, acol), (0.0, bcol)):
            k.op("dve", lambda e, off=off: e.tensor_scalar(out=yv[:], in0=ang[:], scalar1=1.0 / TWO_PI, scalar2=off, op0=ALU.mult, op1=ALU.add),
                 [ang], [yv])
            k.op("dve", lambda e: e.tensor_copy(out=yi[:], in_=yv[:]), [yv], [yi])
            k.op("dve", lambda e: e.tensor_copy(out=yf[:], in_=yi[:]), [yi], [yf])
            k.op("dve", lambda e: e.tensor_tensor(out=yv[:], in0=yv[:], in1=yf[:], op=ALU.subtract), [yv, yf], [yv])
            k.op("dve", lambda e: e.tensor_single_scalar(out=mk[:], in_=yv[:], scalar=0.5, op=ALU.is_gt), [yv], [mk])
            k.op("dve", lambda e: e.tensor_tensor(out=yv[:], in0=yv[:], in1=mk[:], op=ALU.subtract), [yv, mk], [yv])
            k.op("dve", lambda e: e.tensor_single_scalar(out=mk[:], in_=yv[:], scalar=-0.5, op=ALU.is_lt), [yv], [mk])
            k.op("dve", lambda e: e.tensor_tensor(out=yv[:], in0=yv[:], in1=mk[:], op=ALU.add), [yv, mk], [yv])
            k.op("act", lambda e, dst=dst: e.activation(out=dst[:], in_=yv[:], func=AF.Sin, scale=TWO_PI), [yv], [dst])
            k.op("dve", lambda e, dst=dst: e.tensor_tensor(out=dst[:], in0=dst[:], in1=emag[:], op=ALU.mult), [dst, emag], [dst])
        am1 = k.sb([128, 64], F32, "am1")
        d2 = k.sb([128, 64], F32, "d2")
        t1 = k.sb([128, 64], F32, "t1")
        cr = k.sb([128, 64], F32, "cr")
        cis = k.sb([128, 64], F32, "cis")
        k.op("dve", lambda e: e.tensor_scalar(out=am1[:], in0=acol[:], scalar1=-1.0, scalar2=None, op0=ALU.add), [acol], [am1])
        k.op("dve", lambda e: e.tensor_tensor(out=d2[:], in0=lre[:], in1=lre[:], op=ALU.mult), [lre], [d2])
        k.op("dve", lambda e: e.tensor_tensor(out=t1[:], in0=lim[:], in1=lim[:], op=ALU.mult), [lim], [t1])
        k.op("dve", lambda e: e.tensor_tensor(out=d2[:], in0=d2[:], in1=t1[:], op=ALU.add), [d2, t1], [d2])
        k.op("dve", lambda e: e.reciprocal(out=d2[:], in_=d2[:]), [d2], [d2])
        k.op("dve", lambda e: e.tensor_tensor(out=cr[:], in0=am1[:], in1=lre[:], op=ALU.mult), [am1, lre], [cr])
        k.op("dve", lambda e: e.tensor_tensor(out=t1[:], in0=bcol[:], in1=lim[:], op=ALU.mult), [bcol, lim], [t1])
        k.op("dve", lambda e: e.tensor_tensor(out=cr[:], in0=cr[:], in1=t1[:], op=ALU.add), [cr, t1], [cr])
        k.op("dve", lambda e: e.tensor_tensor(out=cr[:], in0=cr[:], in1=d2[:], op=ALU.mult), [cr, d2], [cr])
        k.op("dve", lambda e: e.tensor_tensor(out=cis[:], in0=bcol[:], in1=lre[:], op=ALU.mult), [bcol, lre], [cis])
        k.op("dve", lambda e: e.tensor_tensor(out=t1[:], in0=am1[:], in1=lim[:], op=ALU.mult), [am1, lim], [t1])
        k.op("dve", lambda e: e.tensor_tensor(out=cis[:], in0=cis[:], in1=t1[:], op=ALU.subtract), [cis, t1], [cis])
        k.op("dve", lambda e: e.tensor_tensor(out=cis[:], in0=cis[:], in1=d2[:], op=ALU.mult), [cis, d2], [cis])
        k.op("dve", lambda e: e.tensor_scalar(out=cis[:], in0=cis[:], scalar1=sgn[:, 0:1], scalar2=None, op0=ALU.mult), [cis, sgn], [cis])
        if 's5pre0' in DBG:
            return
        BA = k.sb([128, 64, 16], F32, "BA")
        BB = k.sb([128, 64, 16], F32, "BB")
        bre = c.inp["s5_b_re"][0].rearrange("g p c -> p g c")
        bim = c.inp["s5_b_im"][0].rearrange("g p c -> p g c")
        k.dma("sp", BA[0:64], bre, writes=[BA])
        k.dma("sp", BA[64:128], bim, writes=[BA])
        k.dma("sp", BB[0:64], bim, writes=[BB])
        k.dma("sp", BB[64:128], bre, writes=[BB])
        k.op("dve", lambda e: e.tensor_tensor(out=BA[:], in0=BA[:], in1=cr[:].unsqueeze(2).broadcast_to([128, 64, 16]), op=ALU.mult), [BA, cr], [BA])
        k.op("dve", lambda e: e.tensor_tensor(out=BB[:], in0=BB[:], in1=cis[:].unsqueeze(2).broadcast_to([128, 64, 16]), op=ALU.mult), [BB, cis], [BB])
        k.op("dve", lambda e: e.tensor_tensor(out=BA[:], in0=BA[:], in1=BB[:], op=ALU.add), [BA, BB], [BA])
        W0 = k.sb([128, 64, 128], BF16, "W0")
        WC = k.sb([128, 64, 128], BF16, "WC")
        CN = k.sb([128, 8, 128], F32, "CN")
        k.dma("sp", CN[:, :, 0:64], c.inp["s5_c_re"][0].rearrange("(cc g) c p -> (g c) cc p", cc=8), writes=[CN])
        k.dma("sp", CN[:, :, 64:128], c.inp["s5_c_im"][0].rearrange("(cc g) c p -> (g c) cc p", cc=8), writes=[CN])
        k.op("dve", lambda e: e.tensor_scalar(out=CN[:, :, 64:128], in0=CN[:, :, 64:128], scalar1=-1.0, scalar2=None, op0=ALU.mult), [CN], [CN])
        tpf = k.sb([128, 128], F32, "tpf")
        for cc in range(8):
            p = psf(c)
            k.op("pe", lambda e, p=p, cc=cc: e.transpose(out=p[:, 0:128], in_=BA[:, cc * 8:(cc + 1) * 8, :].rearrange("p g c -> p (g c)"),
                                                        identity=c.identf[:]), [BA, c.identf], [p])
            k.op("act", lambda e, p=p: e.copy(out=tpf[:], in_=p[:, 0:128]), [p], [tpf])
            for j in range(8):
                k.op("dve", lambda e, cc=cc, j=j: e.tensor_scalar(out=W0[:, cc * 8 + j, :], in0=tpf[:], scalar1=gmask[:, j:j + 1], scalar2=None,
                                                                  op0=ALU.mult), [tpf, gmask], [W0])
            p = psf(c)
            k.op("pe", lambda e, p=p, cc=cc: e.transpose(out=p[:, 0:128], in_=CN[:, cc, :], identity=c.identf[:]), [CN, c.identf], [p])
            k.op("act", lambda e, p=p: e.copy(out=tpf[:], in_=p[:, 0:128]), [p], [tpf])
            for j in range(8):
                k.op("pool", lambda e, cc=cc, j=j: e.tensor_tensor(out=WC[:, cc * 8 + j, :], in0=tpf[:], in1=cmask[:, j, :], op=ALU.mult),
                     [tpf, cmask], [WC])
        dcol = k.sb([128, 8], F32, "dcol")
        k.dma("sp", dcol[:], c.inp["s5_d"][0].rearrange("(cc p) -> p cc", p=128), writes=[dcol], allow_slow_non_contiguous=True) if False else None
        dtmp = k.sb([128, 128], F32, "dtmp")
        k.op("dve", lambda e: e.memset(dtmp[:], 0.0), [], [dtmp])
        k.dma("sp", dtmp[0:8, :], c.inp["s5_d"][0].rearrange("(cc p) -> cc p", p=128), writes=[dtmp])
        p = psf(c)
        k.op("pe", lambda e, p=p: e.transpose(out=p[:, 0:128], in_=dtmp[:, :], identity=c.identf[:]), [dtmp, c.identf], [p])
        k.op("dve", lambda e, p=p: e.tensor_copy(out=dcol[:], in_=p[:, 0:8]), [p], [dcol])
        if 's5prep' in DBG:
            return
        xst = k.sb([128, T_ // 128, 128], F32, "xst")
        xTf = k.sb([128, T_], F32, "xTf")
        xTb = k.sb([128, T_], BF16, "xTb")
        SA = [k.sb([128, T_], BF16, "SA") for _ in range(8)]
        SB = [k.sb([128, T_], BF16, "SB") for _ in range(2)]
        Xf = [k.sb([128, 128], F32, "Xf") for _ in range(2)]
        XTf = [k.sb([128, 128], F32, "XTf") for _ in range(2)]
        PK = [k.sb([128, nlev, 128], BF16, "PK") for _ in range(2)]
        gtb = [k.sb([128, 512], BF16, "gtb") for _ in range(2)]
        ytmp = [k.sb([128, 512], F32, "ytmp") for _ in range(2)]
        uvb = UVBuilder(c, c.peer_layers) if (c.peer_layers and "pk_u" in c.inp) else None
        uv_per_it = 0 if uvb is None else -(-len(uvb.jobs) // (c.NSEQ * 8))
        ev = 0
        for s in range(c.NSEQ):
            for cc in range(8):
                k.dma("sp", xst[:], h_in[s * T_:(s + 1) * T_, cc * 128:(cc + 1) * 128].rearrange("(n p) c -> p n c", p=128), writes=[xst])
                if uvb is not None:
                    uvb.step(uv_per_it)
                if 's5ma' in DBG:
                    return
                for n4 in range(T_ // 512):
                    p = psf(c)
                    for q4 in range(4):
                        n = n4 * 4 + q4
                        k.op("pe", lambda e, p=p, n=n, q4=q4: e.transpose(out=p[:, q4 * 128:(q4 + 1) * 128], in_=xst[:, n, :], identity=c.identf[:]),
                             [xst, c.identf], [p])
                    if 's5mb' in DBG:
                        return
                    k.op("act", lambda e, p=p, n4=n4: e.copy(out=xTf[:, n4 * 512:(n4 + 1) * 512], in_=p[:, 0:512]), [p], [xTf])
                    if 's5mc' in DBG:
                        return
                    k.op("dve", lambda e, n4=n4: e.tensor_copy(out=xTb[:, n4 * 512:(n4 + 1) * 512], in_=xTf[:, n4 * 512:(n4 + 1) * 512]), [xTf], [xTb])
                if 's5m1' in DBG:
                    return
                finals = [None] * 8
                state = {}

                def powers(j):
                    g = cc * 8 + j
                    gi = g % 2
                    X, XT, pk = Xf[gi], XTf[gi], PK[gi]
                    k.op("dve", lambda e: e.tensor_scalar(out=X[:], in0=c.identf[:], scalar1=acol[:, g:g + 1], scalar2=None, op0=ALU.mult),
                         [c.identf, acol], [X])
                    k.op("dve", lambda e: e.tensor_copy(out=XT[:], in_=X[:]), [X], [XT])
                    k.op("dve", lambda e: e.scalar_tensor_tensor(out=X[:], in0=Jm[:], scalar=bcol[:, g:g + 1], in1=X[:], op0=ALU.mult, op1=ALU.add),
                         [Jm, bcol, X], [X])
                    k.op("dve", lambda e: e.tensor_scalar(out=mk[:, gi:gi + 1], in0=bcol[:, g:g + 1], scalar1=-1.0, scalar2=None, op0=ALU.mult),
                         [bcol], [mk])
                    k.op("dve", lambda e: e.scalar_tensor_tensor(out=XT[:], in0=Jm[:], scalar=mk[:, gi:gi + 1], in1=XT[:], op0=ALU.mult, op1=ALU.add),
                         [Jm, mk, XT], [XT])
                    for lv in range(nlev):
                        k.op("act", lambda e, lv=lv: e.copy(out=pk[:, lv, :], in_=XT[:]), [XT], [pk])
                        if lv + 1 < nlev:
                            p = psf(c)
                            k.op("pe", lambda e, p=p: e.matmul(out=p[:, 0:128], lhsT=XT[:], rhs=X[:], start=True, stop=True), [X, XT], [p])
                            k.op("pe", lambda e, p=p: e.matmul(out=p[:, 512:640], lhsT=X[:], rhs=XT[:], start=True, stop=True), [X, XT], [p])
                            k.op("dve", lambda e, p=p: e.tensor_copy(out=X[:], in_=p[:, 0:128]), [p], [X])
                            k.op("act", lambda e, p=p: e.copy(out=XT[:], in_=p[:, 512:640]), [p], [XT])

                def level0(j):
                    nonlocal ev
                    g = cc * 8 + j
                    gi = g % 2
                    cur, oth = (SA[j], SB[gi]) if nlev % 2 == 0 else (SB[gi], SA[j])
                    for hb_ in range(0, NB, 2):
                        p = psf(c)
                        for q2 in range(2):
                            blk = hb_ + q2
                            if blk >= NB:
                                continue
                            k.op("pe", lambda e, p=p, q2=q2, blk=blk: e.matmul(out=p[:, q2 * 512:(q2 + 1) * 512], lhsT=W0[:, g, :],
                                                                               rhs=xTb[:, blk * 512:(blk + 1) * 512], start=True, stop=True),
                                 [W0, xTb], [p])
                        w = min(2, NB - hb_) * 512
                        if ev % 2 == 0:
                            k.op("act", lambda e, p=p, hb_=hb_, w=w: e.copy(out=cur[:, hb_ * 512:hb_ * 512 + w], in_=p[:, 0:w]), [p], [cur])
                        else:
                            k.op("dve", lambda e, p=p, hb_=hb_, w=w: e.tensor_copy(out=cur[:, hb_ * 512:hb_ * 512 + w], in_=p[:, 0:w]), [p], [cur])
                        ev += 1
                    state[j] = (cur, oth)

                def level(j, lv):
                    nonlocal ev
                    g = cc * 8 + j
                    gi = g % 2
                    pk = PK[gi]
                    cur, oth = state[j]
                    sh = 1 << lv
                    for hb_ in range(0, NB, 2):
                        p = psf(c)
                        use_act = (ev % 3 == 2)
                        ev += 1
                        for q2 in range(2):
                            blk = hb_ + q2
                            if blk >= NB:
                                continue
                            t0 = blk * 512
                            lo = max(t0, sh)
                            has2 = lo < t0 + 512
                            if use_act:
                                k.op("pe", lambda e, p=p, q2=q2, t0=t0, has2=has2: e.matmul(
                                    out=p[:, q2 * 512:(q2 + 1) * 512], lhsT=c.identb[:], rhs=cur[:, t0:t0 + 512], start=True, stop=not has2),
                                    [c.identb, cur], [p])
                            if has2:
                                k.op("pe", lambda e, p=p, q2=q2, t0=t0, lo=lo: e.matmul(
                                    out=p[:, q2 * 512 + (lo - t0):(q2 + 1) * 512], lhsT=pk[:, lv, :], rhs=cur[:, lo - sh:t0 + 512 - sh],
                                    start=not use_act, stop=True), [pk, cur], [p])
                        w = min(2, NB - hb_) * 512
                        c0 = hb_ * 512
                        if use_act:
                            k.op("act", lambda e, p=p, c0=c0, w=w: e.copy(out=oth[:, c0:c0 + w], in_=p[:, 0:w]), [p], [oth])
                        else:
                            lo_all = min(max(c0, sh), c0 + w)
                            if lo_all > c0:
                                k.op("dve", lambda e, c0=c0, lo_all=lo_all: e.tensor_copy(out=oth[:, c0:lo_all], in_=cur[:, c0:lo_all]), [cur], [oth])
                            if lo_all < c0 + w:
                                k.op("dve", lambda e, p=p, c0=c0, lo_all=lo_all, w=w: e.tensor_tensor(
                                    out=oth[:, lo_all:c0 + w], in0=p[:, lo_all - c0:w], in1=cur[:, lo_all:c0 + w], op=ALU.add), [p, cur], [oth])
                    state[j] = (oth, cur)

                for j0 in range(0, 8, 2):
                    for j in (j0, j0 + 1):
                        powers(j)
                    for j in (j0, j0 + 1):
                        level0(j)
                    for lv in range(nlev):
                        for j in (j0, j0 + 1):
                            level(j, lv)
                    for j in (j0, j0 + 1):
                        finals[j] = state[j][0]
                if 's5m4' in DBG:
                    return
                for blk in range(NB):
                    p = psf(c)
                    for j in range(8):
                        k.op("pe", lambda e, p=p, j=j, blk=blk, cc=cc: e.matmul(out=p[:, 0:512], lhsT=WC[:, cc * 8 + j, :],
                                                                                rhs=finals[j][:, blk * 512:(blk + 1) * 512], start=(j == 0), stop=(j == 7)),
                             [WC, finals[j]], [p])
                    yt = ytmp[blk % 2]
                    gb_ = gtb[blk % 2]
                    k.op("dve", lambda e, p=p, yt=yt, blk=blk, cc=cc: e.scalar_tensor_tensor(out=yt[:], in0=xTf[:, blk * 512:(blk + 1) * 512], scalar=dcol[:, cc:cc + 1],
                                                                                             in1=p[:, 0:512], op0=ALU.mult, op1=ALU.add), [xTf, dcol, p], [yt])
                    k.op("act", lambda e, yt=yt, gb_=gb_: e.activation(out=gb_[:], in_=yt[:], func=AF.Gelu_apprx_tanh), [yt], [gb_])
                    k.dma("sp", GT[s, cc, :, blk * 512:(blk + 1) * 512], gb_[:], reads=[gb_])
                if uvb is not None and s == c.NSEQ - 1 and cc == 7:
                    uvb.finish()
    if 's5m5' in DBG:
        return
    with k.scope():
        wg = k.sb([128, 8, 2048], BF16, "wg")
        load_w(c, wg, c.inp["s5_w_glu"][0], 1024, 2048)
        bg = k.sb([128, 2048], F32, "bg")
        load_bc(c, bg, c.inp["s5_b_glu"][0:1, :])
        tl = Tail(c, li, 0)
        hf = [k.sb([128, D], F32, "hf") for _ in range(2)]
        gT = [k.sb([128, 8, 128], BF16, "gT") for _ in range(2)]
        vg = [k.sb([128, 2048], F32, "vg") for _ in range(2)]
        tps = T_ // 128

        def load(ti):
            s, tt = ti // tps, ti % tps
            k.dma("sp", hf[ti % 2][:], h_in[ti * 128:(ti + 1) * 128, :], writes=[hf[ti % 2]])
            k.dma("sp", gT[ti % 2][:], GT[s, :, :, tt * 128:(tt + 1) * 128].rearrange("c p t -> p c t"), writes=[gT[ti % 2]])

        load(0)
        for ti in range(c.NTILES):
            if ti + 1 < c.NTILES:
                load(ti + 1)
            b = ti % 2
            for half in range(2):
                p = psf(c)
                for nb in range(2):
                    n0 = half * 1024 + nb * 512
                    for kc in range(8):
                        k.op("pe", lambda e, p=p, nb=nb, n0=n0, kc=kc, b=b: e.matmul(out=p[:, nb * 512:(nb + 1) * 512], lhsT=gT[b][:, kc, :],
                                                                                    rhs=wg[:, kc, n0:n0 + 512], start=(kc == 0), stop=(kc == 7)),
                             [gT[b], wg], [p])
                k.op("dve", lambda e, p=p, half=half, b=b: e.tensor_tensor(out=vg[b][:, half * 1024:(half + 1) * 1024], in0=p[:],
                                                                           in1=bg[:, half * 1024:(half + 1) * 1024], op=ALU.add), [p, bg], [vg[b]])
            k.op("act", lambda e, b=b: e.activation(out=vg[b][:, 1024:2048], in_=vg[b][:, 1024:2048], func=AF.Sigmoid), [vg[b]], [vg[b]])
            k.op("pool", lambda e, b=b: e.tensor_tensor(out=vg[b][:, 0:1024], in0=vg[b][:, 0:1024], in1=vg[b][:, 1024:2048], op=ALU.mult), [vg[b]], [vg[b]])
            tl.run(hf[b], lambda hf_, b=b: vg[b][:, hf_ * 512:(hf_ + 1) * 512], vg[b], h_out, ti)


def phase_da(c, li, h_in, h_out):
    k = c.k
    T_ = c.T
    NQ = T_ // 128
    lam_init = 0.8 - 0.6 * float(np.exp(-0.3 * li))
    QT = [k.dram("da_QT%d" % j, [c.NSEQ, 8, 128, T_], BF16) for j in range(2)]
    KTd = k.dram("da_KT", [c.NSEQ, 8, 128, T_], BF16)
    Vd = k.dram("da_V", [c.NT, 1024], BF16)
    AO = k.dram("da_AO", [c.NT, 1024], BF16)
    with k.scope():
        w = k.sb([128, 8, 3072], BF16, "wqkv")
        load_w(c, w, c.inp["da_w_qkv"][0], 1024, 3072)
        hmask = k.sb([128, 2], F32, "hmask")
        k.dma("sp", hmask[:], c.inp["c_hmask"][:, :], writes=[hmask])
        hf = [k.sb([128, D], F32, "hf") for _ in range(2)]
        hb = k.sb([128, D], BF16, "hb")
        hT = k.sb([128, 8, 128], BF16, "hT")
        qt = [[k.sb([128, 8, 128], BF16, "qt") for _ in range(2)] for _ in range(2)]
        kt = [k.sb([128, 8, 128], BF16, "kt") for _ in range(2)]
        vt = [k.sb([128, 1024], BF16, "vt") for _ in range(2)]
        tps = T_ // 128

        def load(ti):
            k.dma("sp", hf[ti % 2][:], h_in[ti * 128:(ti + 1) * 128, :], writes=[hf[ti % 2]])

        load(0)
        for ti in range(c.NTILES):
            if ti + 1 < c.NTILES:
                load(ti + 1)
            b = ti % 2
            s, tt = ti // tps, ti % tps
            k.op("dve", lambda e, b=b: e.tensor_copy(out=hb[:], in_=hf[b][:]), [hf[b]], [hb])
            transpose_into(c, lambda: hT[:].rearrange("p a b -> p (a b)"), lambda ci: hb[:, ci * 128:(ci + 1) * 128], 8, hb, hT)
            for part in range(2):
                p = psf(c)
                for fc in range(8):
                    for kc in range(8):
                        k.op("pe", lambda e, p=p, fc=fc, kc=kc, part=part: e.matmul(
                            out=p[:, fc * 128:(fc + 1) * 128], lhsT=w[:, kc, part * 1024 + fc * 128: part * 1024 + (fc + 1) * 128],
                            rhs=hT[:, kc, :], start=(kc == 0), stop=(kc == 7)), [w, hT], [p])
                if part == 0:
                    for j in range(2):
                        k.op("act", lambda e, p=p, j=j, b=b: e.activation(out=qt[j][b][:].rearrange("p a b -> p (a b)"), in_=p[:], func=AF.Copy,
                                                                         scale=hmask[:, j:j + 1]), [p, hmask], [qt[j][b]])
                        k.dma("sp", QT[j][s, :, :, tt * 128:(tt + 1) * 128].rearrange("h p t -> p h t"), qt[j][b][:], reads=[qt[j][b]])
                else:
                    k.op("dve", lambda e, p=p, b=b: e.tensor_copy(out=kt[b][:].rearrange("p a b -> p (a b)"), in_=p[:]), [p], [kt[b]])
                    k.dma("sp", KTd[s, :, :, tt * 128:(tt + 1) * 128].rearrange("h p t -> p h t"), kt[b][:], reads=[kt[b]])
            p = psf(c)
            for nb in range(2):
                for kc in range(8):
                    k.op("pe", lambda e, p=p, nb=nb, kc=kc: e.matmul(out=p[:, nb * 512:(nb + 1) * 512], lhsT=hT[:, kc, :],
                                                                    rhs=w[:, kc, 2048 + nb * 512:2048 + (nb + 1) * 512], start=(kc == 0), stop=(kc == 7)),
                         [w, hT], [p])
            k.op("act", lambda e, p=p, b=b: e.copy(out=vt[b][:], in_=p[:]), [p], [vt[b]])
            k.dma("sp", Vd[ti * 128:(ti + 1) * 128, :], vt[b][:], reads=[vt[b]])
    with k.scope():
        r0 = k.sb([128, T_], F32, "r0")
        k.dma("sp", r0[:], c.inp["c_r0"][:, :], writes=[r0])
        dbase = k.sb([128, 128], F32, "dbase")
        dmsk = k.sb([128, 128], F32, "dmsk")
        k.dma("sp", dbase[:], c.inp["c_dbase"][:, :], writes=[dbase])
        k.dma("sp", dmsk[:], c.inp["c_dmask"][:, :], writes=[dmsk])
        lm = k.sb([128, 4, 64], F32, "lm")
        k.dma("sp", lm[:].rearrange("p a b -> p (a b)"), c.inp["da_lambda"][0:1].rearrange("o a b -> o (a b)").broadcast_to([128, 256]), writes=[lm])
        lt = k.sb([128, 2, 64], F32, "lt")
        l2 = k.sb([128, 2], F32, "l2")
        nlam = k.sb([128, 1], F32, "nlam")
        k.op("dve", lambda e: e.tensor_tensor(out=lt[:], in0=lm[:, 0::2, :], in1=lm[:, 1::2, :], op=ALU.mult), [lm], [lt])
        k.op("dve", lambda e: e.tensor_reduce(out=l2[:], in_=lt[:], axis=AX.X, op=ALU.add), [lt], [l2])
        k.op("act", lambda e: e.activation(out=l2[:], in_=l2[:], func=AF.Exp), [l2], [l2])
        k.op("dve", lambda e: e.tensor_tensor(out=nlam[:], in0=l2[:, 1:2], in1=l2[:, 0:1], op=ALU.subtract), [l2], [nlam])
        k.op("dve", lambda e: e.tensor_scalar(out=nlam[:], in0=nlam[:], scalar1=-lam_init, scalar2=None, op0=ALU.add), [nlam], [nlam])
        sg = k.sb([128, 128], F32, "sg")
        load_bc(c, sg, c.inp["da_subln_g"][0:1, :])
        k.op("dve", lambda e: e.tensor_scalar(out=sg[:], in0=sg[:], scalar1=(1.0 - lam_init), scalar2=None, op0=ALU.mult), [sg], [sg])
        kTh = [k.sb([128, T_], BF16, "kTh") for _ in range(2)]
        vh = [k.sb([128, NQ, 128], BF16, "vh") for _ in range(2)]
        qTh = [[k.sb([128, T_], BF16, "qTh") for _ in range(2)] for _ in range(2)]
        dh = [k.sb([128, 128], F32, "dh") for _ in range(2)]
        NBUF = 3
        ssb = [k.sb([128, T_], F32, "ssb") for _ in range(NBUF)]
        P = [k.sb([128, T_], BF16, "P") for _ in range(NBUF)]
        PT = [k.sb([128, NQ, 128], BF16, "PT") for _ in range(2)]
        mx = [k.sb([128, 1], F32, "mx") for _ in range(NBUF)]
        sm = [k.sb([128, 1], F32, "sm") for _ in range(NBUF)]
        o0 = [k.sb([128, 128], F32, "o0") for _ in range(2)]
        oo = [k.sb([128, 128], F32, "oo") for _ in range(2)]
        jk = [k.sb([128, 128], F32, "jk") for _ in range(2)]
        ms = [k.sb([128, 1], F32, "ms") for _ in range(2)]
        ob = [k.sb([128, 128], BF16, "ob") for _ in range(2)]
        units = [(s, h, qi, j) for s in range(c.NSEQ) for h in range(8) for qi in range(NQ) for j in range(2)]

        def stage_a(ui):
            s, h, qi, j = units[ui]
            hb_ = (s * 8 + h) % 2
            slope = 2.0 ** (-(h + 1))
            dhh = dh[hb_]
            if qi == 0 and j == 0:
                k.dma("sp", kTh[hb_][:], KTd[s, h], writes=[kTh[hb_]])
                k.dma("sp", vh[hb_][:], Vd[s * T_:(s + 1) * T_, h * 128:(h + 1) * 128].rearrange("(n p) c -> p n c", p=128), writes=[vh[hb_]])
                for jj in range(2):
                    k.dma("sp", qTh[jj][hb_][:], QT[jj][s, h], writes=[qTh[jj][hb_]])
                k.op("dve", lambda e: e.scalar_tensor_tensor(out=dhh[:], in0=dbase[:], scalar=slope, in1=dmsk[:], op0=ALU.mult, op1=ALU.add),
                     [dbase, dmsk], [dhh])
            q0 = qi * 128
            nk = q0 + 128
            ub = ui % NBUF
            sb_, Pb, mxb, smb = ssb[ub], P[ub], mx[ub], sm[ub]
            for k0 in range(0, q0, 1024):
                p = psf(c)
                w_ = min(1024, q0 - k0)
                for c0 in range(0, w_, 512):
                    cw = min(512, w_ - c0)
                    k.op("pe", lambda e, p=p, c0=c0, cw=cw, k0=k0: e.matmul(
                        out=p[:, c0:c0 + cw], lhsT=qTh[j][hb_][:, q0:q0 + 128], rhs=kTh[hb_][:, k0 + c0:k0 + c0 + cw], start=True, stop=True),
                        [qTh[j][hb_], kTh[hb_]], [p])
                    off = T_ - q0 + k0 + c0
                    k.op("dve", lambda e, p=p, c0=c0, cw=cw, k0=k0, off=off: e.scalar_tensor_tensor(
                        out=sb_[:, k0 + c0:k0 + c0 + cw], in0=r0[:, off:off + cw], scalar=slope, in1=p[:, c0:c0 + cw], op0=ALU.mult, op1=ALU.add),
                        [r0, p], [sb_])
            p = psf(c)
            k.op("pe", lambda e, p=p: e.matmul(out=p[:, 0:128], lhsT=qTh[j][hb_][:, q0:q0 + 128], rhs=kTh[hb_][:, q0:q0 + 128],
                                               start=True, stop=True), [qTh[j][hb_], kTh[hb_]], [p])
            k.op("dve", lambda e, p=p: e.tensor_tensor(out=sb_[:, q0:q0 + 128], in0=p[:, 0:128], in1=dhh[:], op=ALU.add),
                 [p, dhh], [sb_])
            k.op("dve", lambda e: e.tensor_reduce(out=mxb[:], in_=sb_[:, 0:nk], axis=AX.X, op=ALU.max, negate=True), [sb_], [mxb])

        def stage_a2(ui):
            s, h, qi, j = units[ui]
            nk = qi * 128 + 128
            ub = ui % NBUF
            sb_, Pb, mxb, smb = ssb[ub], P[ub], mx[ub], sm[ub]
            k.op("act", lambda e: e.activation(out=Pb[:, 0:nk], in_=sb_[:, 0:nk], func=AF.Exp, bias=mxb[:, 0:1],
                                               scale=1.0, accum_out=smb[:, 0:1]), [sb_, mxb], [Pb, smb])

        def stage_b(ui):
            s, h, qi, j = units[ui]
            hb_ = (s * 8 + h) % 2
            q0 = qi * 128
            nk = q0 + 128
            ub = ui % NBUF
            Pb, PTb, smb = P[ub], PT[ui % 2], sm[ub]
            nblk = nk // 128
            k.op("dve", lambda e: e.reciprocal(out=smb[:], in_=smb[:]), [smb], [smb])
            for b0 in range(0, nblk, 8):
                nb_ = min(8, nblk - b0)
                transpose_into(c, lambda b0=b0, nb_=nb_: PTb[:, b0:b0 + nb_, :].rearrange("p a b -> p (a b)"),
                               lambda ci, b0=b0: Pb[:, (b0 + ci) * 128:(b0 + ci + 1) * 128], nb_, Pb, PTb,
                               evac=("act" if (b0 // 8) % 2 == 0 else "dve"))

        def stage_b2(ui):
            s, h, qi, j = units[ui]
            hb_ = (s * 8 + h) % 2
            q0 = qi * 128
            nk = q0 + 128
            ub = ui % NBUF
            Pb, PTb, smb = P[ub], PT[ui % 2], sm[ub]
            nblk = nk // 128
            p = psf(c)
            for bk in range(nblk):
                k.op("pe", lambda e, p=p, bk=bk: e.matmul(out=p[:, 0:128], lhsT=PTb[:, bk, :], rhs=vh[hb_][:, bk, :],
                                                         start=(bk == 0), stop=(bk == nblk - 1)), [PTb, vh[hb_]], [p])
            qb = qi % 2
            if j == 0:
                k.op("act", lambda e, p=p: e.activation(out=o0[qb][:], in_=p[:, 0:128], func=AF.Copy, scale=smb[:, 0:1]), [p, smb], [o0[qb]])
            else:
                k.op("dve", lambda e: e.tensor_tensor(out=smb[:], in0=smb[:], in1=nlam[:], op=ALU.mult), [smb, nlam], [smb])
                k.op("dve", lambda e, p=p: e.scalar_tensor_tensor(out=oo[qb][:], in0=p[:, 0:128], scalar=smb[:, 0:1], in1=o0[qb][:],
                                                                 op0=ALU.mult, op1=ALU.add), [p, smb, o0[qb]], [oo[qb]])
                k.op("act", lambda e: e.activation(out=jk[qb][:], in_=oo[qb][:], func=AF.Square, accum_out=ms[qb][:, 0:1]), [oo[qb]], [jk[qb], ms[qb]])
                k.op("act", lambda e: e.activation(out=ms[qb][:], in_=ms[qb][:], func=AF.Sqrt, scale=1.0 / 128.0, bias=LN_EPS), [ms[qb]], [ms[qb]])
                k.op("dve", lambda e: e.reciprocal(out=ms[qb][:], in_=ms[qb][:]), [ms[qb]], [ms[qb]])
                k.op("dve", lambda e: e.scalar_tensor_tensor(out=ob[qb][:], in0=oo[qb][:], scalar=ms[qb][:, 0:1], in1=sg[:], op0=ALU.mult, op1=ALU.mult),
                     [oo[qb], ms[qb], sg], [ob[qb]])
                k.dma("sp", AO[s * T_ + q0:s * T_ + q0 + 128, h * 128:(h + 1) * 128], ob[qb][:], reads=[ob[qb]])

        for ui in range(len(units) + 1):
            if ui < len(units):
                stage_a(ui)
            if ui >= 1:
                stage_b(ui - 1)
            if ui < len(units):
                stage_a2(ui)
            if ui >= 1:
                stage_b2(ui - 1)
    with k.scope():
        wo = k.sb([128, 8, 1024], BF16, "wo")
        load_w(c, wo, c.inp["da_w_o"][0], 1024, 1024)
        tl = Tail(c, li, 0)
        hf = [k.sb([128, D], F32, "hf") for _ in range(2)]
        ab = [k.sb([128, D], BF16, "ab") for _ in range(2)]
        aT = [k.sb([128, 8, 128], BF16, "aT") for _ in range(2)]

        def load(ti):
            k.dma("sp", hf[ti % 2][:], h_in[ti * 128:(ti + 1) * 128, :], writes=[hf[ti % 2]])
            k.dma("sp", ab[ti % 2][:], AO[ti * 128:(ti + 1) * 128, :], writes=[ab[ti % 2]])

        load(0)
        for ti in range(c.NTILES):
            if ti + 1 < c.NTILES:
                load(ti + 1)
            b = ti % 2
            transpose_into(c, lambda b=b: aT[b][:].rearrange("p a b -> p (a b)"), lambda ci, b=b: ab[b][:, ci * 128:(ci + 1) * 128], 8, ab[b], aT[b])
            p = psf(c)
            for nb in range(2):
                for kc in range(8):
                    k.op("pe", lambda e, p=p, nb=nb, kc=kc, b=b: e.matmul(out=p[:, nb * 512:(nb + 1) * 512], lhsT=aT[b][:, kc, :], rhs=wo[:, kc, nb * 512:(nb + 1) * 512],
                                                                         start=(kc == 0), stop=(kc == 7)), [aT[b], wo], [p])
            tl.run(hf[b], lambda hf_, p=p: p[:, hf_ * 512:(hf_ + 1) * 512], p, h_out, ti)


def phase_m2(c, li, h_in, h_out):
    k = c.k
    T_ = c.T
    NCH = T_ // 128
    Zd = k.dram("m2_Z", [c.NT, 2048], F32)
    XBC = k.dram("m2_XBC", [c.NSEQ, 24, 128, T_], F32)
    DTd = k.dram("m2_DT", [c.NT, 32], F32)
    XS = k.dram("m2_XS", [c.NT, 2048], F32)
    BTd = k.dram("m2_BT", [c.NSEQ, 4, 128, T_], BF16)
    CTd = k.dram("m2_CT", [c.NSEQ, 4, 128, T_], BF16)
    BTOK = k.dram("m2_BTOK", [c.NT, 512], BF16)
    tps = T_ // 128
    with k.scope():
        w = k.sb([128, 8, 5152], BF16, "w_in")
        load_w(c, w, c.inp["m2_w_in"][0], 1024, 5152)
        hf = [k.sb([128, D], F32, "hf") for _ in range(2)]
        hb = k.sb([128, D], BF16, "hb")
        hT = k.sb([128, 8, 128], BF16, "hT")
        zt = [k.sb([128, 1024], F32, "zt") for _ in range(2)]
        xt = [k.sb([128, 8, 128], F32, "xt") for _ in range(2)]
        dtt = [k.sb([128, 32], F32, "dtt") for _ in range(2)]

        def load(ti):
            k.dma("sp", hf[ti % 2][:], h_in[ti * 128:(ti + 1) * 128, :], writes=[hf[ti % 2]])

        load(0)
        ev = 0
        for ti in range(c.NTILES):
            if ti + 1 < c.NTILES:
                load(ti + 1)
            b = ti % 2
            s, tt = ti // tps, ti % tps
            k.op("dve", lambda e, b=b: e.tensor_copy(out=hb[:], in_=hf[b][:]), [hf[b]], [hb])
            transpose_into(c, lambda: hT[:].rearrange("p a b -> p (a b)"), lambda ci: hb[:, ci * 128:(ci + 1) * 128], 8, hb, hT)
            for half in range(2):
                p = psf(c)
                for nb in range(2):
                    n0 = half * 1024 + nb * 512
                    for kc in range(8):
                        k.op("pe", lambda e, p=p, nb=nb, n0=n0, kc=kc: e.matmul(out=p[:, nb * 512:(nb + 1) * 512], lhsT=hT[:, kc, :], rhs=w[:, kc, n0:n0 + 512],
                                                                               start=(kc == 0), stop=(kc == 7)), [hT, w], [p])
                zb = zt[ev % 2]
                ev += 1
                k.op("act", lambda e, p=p, zb=zb: e.copy(out=zb[:], in_=p[:]), [p], [zb])
                k.dma("sp", Zd[ti * 128:(ti + 1) * 128, half * 1024:(half + 1) * 1024], zb[:], reads=[zb])
            for third in range(3):
                p = psf(c)
                for fc in range(8):
                    col = 2048 + (third * 8 + fc) * 128
                    for kc in range(8):
                        k.op("pe", lambda e, p=p, fc=fc, col=col, kc=kc: e.matmul(out=p[:, fc * 128:(fc + 1) * 128], lhsT=w[:, kc, col:col + 128], rhs=hT[:, kc, :],
                                                                                 start=(kc == 0), stop=(kc == 7)), [hT, w], [p])
                xb_ = xt[ev % 2]
                ev += 1
                k.op("dve", lambda e, p=p, xb_=xb_: e.tensor_copy(out=xb_[:].rearrange("p a b -> p (a b)"), in_=p[:]), [p], [xb_])
                k.dma("sp", XBC[s, third * 8:(third + 1) * 8, :, tt * 128:(tt + 1) * 128].rearrange("n p t -> p n t"), xb_[:], reads=[xb_])
            p = psf(c)
            for kc in range(8):
                k.op("pe", lambda e, p=p, kc=kc: e.matmul(out=p[:, 0:32], lhsT=hT[:, kc, :], rhs=w[:, kc, 5120:5152], start=(kc == 0), stop=(kc == 7)), [hT, w], [p])
            k.op("act", lambda e, p=p, b=b: e.copy(out=dtt[b][:], in_=p[:, 0:32]), [p], [dtt[b]])
            k.dma("sp", DTd[ti * 128:(ti + 1) * 128, :], dtt[b][:], reads=[dtt[b]])
    with k.scope():
        cwp = k.sb([128, 3072], F32, "cwp")
        k.op("dve", lambda e: e.memset(cwp[:], 0.0), [], [cwp])
        k.dma("sp", cwp[0:4, :], c.inp["m2_conv_w"][0], writes=[cwp])
        k.dma("sp", cwp[4:5, :], c.inp["m2_conv_b"][0:1, :], writes=[cwp])
        cw = k.sb([128, 24, 8], F32, "cw")
        for cch in range(24):
            p = psf(c)
            k.op("pe", lambda e, p=p, cch=cch: e.transpose(out=p[:, 0:128], in_=cwp[:, cch * 128:(cch + 1) * 128], identity=c.identf[:]), [cwp, c.identf], [p])
            k.op("act", lambda e, p=p, cch=cch: e.copy(out=cw[:, cch, :], in_=p[:, 0:8]), [p], [cw])
        xin = [k.sb([128, T_], F32, "xin") for _ in range(2)]
        acc = [k.sb([128, T_], F32, "acc") for _ in range(2)]
        accb = [k.sb([128, T_], BF16, "accb") for _ in range(2)]
        stg = [k.sb([128, 4, 128], F32, "stg") for _ in range(2)]
        stgb = [k.sb([128, 8, 128], BF16, "stgb") for _ in range(2)]
        u = 0
        for s in range(c.NSEQ):
            for cch in range(24):
                ub = u % 2
                u += 1
                xi, ac = xin[ub], acc[ub]
                k.dma("sp", xi[:], XBC[s, cch], writes=[xi])
                k.op("dve", lambda e, xi=xi, ac=ac, cch=cch: e.tensor_scalar(out=ac[:], in0=xi[:], scalar1=cw[:, cch, 3:4], scalar2=None, op0=ALU.mult), [xi, cw], [ac])
                for kk in range(3):
                    shf = 3 - kk
                    k.op("dve", lambda e, xi=xi, ac=ac, cch=cch, kk=kk, shf=shf: e.scalar_tensor_tensor(out=ac[:, shf:T_], in0=xi[:, 0:T_ - shf], scalar=cw[:, cch, kk:kk + 1],
                                                                                                  in1=ac[:, shf:T_], op0=ALU.mult, op1=ALU.add), [xi, cw, ac], [ac])
                k.op("act", lambda e, ac=ac, cch=cch: e.activation(out=ac[:], in_=ac[:], func=AF.Silu, bias=cw[:, cch, 4:5], scale=1.0), [ac, cw], [ac])
                if cch < 16:
                    for n4 in range(T_ // 512):
                        p = psf(c)
                        for q4 in range(4):
                            t0 = n4 * 512 + q4 * 128
                            k.op("pe", lambda e, p=p, q4=q4, t0=t0, ac=ac: e.transpose(out=p[:, q4 * 128:(q4 + 1) * 128], in_=ac[:, t0:t0 + 128], identity=c.identf[:]),
                                 [ac, c.identf], [p])
                        sg_ = stg[n4 % 2]
                        k.op("act", lambda e, p=p, sg_=sg_: e.copy(out=sg_[:].rearrange("p a b -> p (a b)"), in_=p[:, 0:512]), [p], [sg_])
                        k.dma("sp", XS[s * T_ + n4 * 512:s * T_ + (n4 + 1) * 512, cch * 128:(cch + 1) * 128].rearrange("(n p) c -> p n c", p=128), sg_[:], reads=[sg_])
                else:
                    abf = accb[ub]
                    k.op("dve", lambda e, ac=ac, abf=abf: e.tensor_copy(out=abf[:], in_=ac[:]), [ac], [abf])
                    if cch < 20:
                        g = cch - 16
                        k.dma("sp", BTd[s, g], abf[:], reads=[abf])
                        for n8 in range(0, T_ // 128, 8):
                            nb_ = min(8, T_ // 128 - n8)
                            sb8 = stgb[(n8 // 8) % 2]
                            transpose_into(c, lambda sb8=sb8, nb_=nb_: sb8[:, 0:nb_, :].rearrange("p a b -> p (a b)"),
                                           lambda ci, n8=n8, abf=abf: abf[:, (n8 + ci) * 128:(n8 + ci + 1) * 128], nb_, abf, sb8)
                            k.dma("sp", BTOK[s * T_ + n8 * 128:s * T_ + (n8 + nb_) * 128, g * 128:(g + 1) * 128].rearrange("(n p) c -> p n c", p=128),
                                  sb8[:, 0:nb_, :], reads=[sb8])
                    else:
                        g = cch - 20
                        k.dma("sp", CTd[s, g], abf[:], reads=[abf])
    with k.scope():
        wo = k.sb([128, 16, 1024], BF16, "w_out")
        load_w(c, wo, c.inp["m2_w_out"][0], 2048, 1024)
        tl = Tail(c, li, 0)
        triU = k.sb([128, 128], F32, "triU")
        Lst = k.sb([128, 128], F32, "Lst")
        ones = k.sb([128, 128], F32, "ones")
        k.dma("sp", triU[:], c.inp["c_triU"][:, :], writes=[triU])
        k.dma("sp", Lst[:], c.inp["c_Lst"][:, :], writes=[Lst])
        k.op("dve", lambda e: e.memset(ones[:], 1.0), [], [ones])
        dtb = k.sb([128, 32], F32, "dtb")
        aneg = k.sb([128, 32], F32, "aneg")
        load_bc(c, dtb, c.inp["m2_dt_bias"][0:1, :])
        load_bc(c, aneg, c.inp["m2_a_log"][0:1, :])
        k.op("act", lambda e: e.activation(out=aneg[:], in_=aneg[:], func=AF.Exp), [aneg], [aneg])
        k.op("dve", lambda e: e.tensor_scalar(out=aneg[:], in0=aneg[:], scalar1=-1.0, scalar2=None, op0=ALU.mult), [aneg], [aneg])
        dsk = k.sb([128, 32], F32, "dsk")
        load_bc(c, dsk, c.inp["m2_d"][0:1, :])
        ng = k.sb([128, 2048], F32, "ng")
        load_bc(c, ng, c.inp["m2_norm_g"][0:1, :])
        hst = k.sb([128, 32, 64], F32, "hst")
        hstb = k.sb([128, 32, 64], BF16, "hstb")
        hf = [k.sb([128, D], F32, "hf") for _ in range(2)]
        xs = [k.sb([128, 32, 64], F32, "xs") for _ in range(2)]
        zz = [k.sb([128, 2048], F32, "zz") for _ in range(2)]
        dtr = [k.sb([128, 32], F32, "dtr") for _ in range(2)]
        btk = [k.sb([128, 512], BF16, "btk") for _ in range(2)]
        btc = [k.sb([128, 4, 128], BF16, "btc") for _ in range(2)]
        ctc = [k.sb([128, 4, 128], BF16, "ctc") for _ in range(2)]
        dt = k.sb([128, 32], F32, "dt")
        dta = k.sb([128, 32], F32, "dta")
        acs = k.sb([128, 32], F32, "acs")
        ea = k.sb([128, 32], F32, "ea")
        dec = k.sb([128, 32], F32, "dec")
        etot = k.sb([128, 32], F32, "etot")
        Rm = k.sb([128, 32, 128], F32, "Rm")
        LT = [k.sb([128, 8, 128], F32, "LT") for _ in range(2)]
        MT = k.sb([128, 32, 128], BF16, "MT")
        cbm = [k.sb([128, 128], F32, "cbm") for _ in range(2)]
        xdt = k.sb([128, 32, 64], BF16, "xdt")
        xdd = k.sb([128, 32, 64], BF16, "xdd")
        yy = k.sb([128, 32, 64], F32, "yy")
        ytmp = k.sb([128, 8, 64], F32, "ytmp")
        ssq = k.sb([128, 4], F32, "ssq")
        jk = k.sb([128, 512], F32, "jk")
        yb = k.sb([128, 2048], BF16, "yb")
        yT = k.sb([128, 16, 128], BF16, "yT")

        def load(ti):
            b = ti % 2
            s, tt = ti // tps, ti % tps
            k.dma("sp", hf[b][:], h_in[ti * 128:(ti + 1) * 128, :], writes=[hf[b]])
            k.dma("sp", xs[b][:].rearrange("p a b -> p (a b)"), XS[ti * 128:(ti + 1) * 128, :], writes=[xs[b]])
            k.dma("sp", zz[b][:], Zd[ti * 128:(ti + 1) * 128, :], writes=[zz[b]])
            k.dma("sp", dtr[b][:], DTd[ti * 128:(ti + 1) * 128, :], writes=[dtr[b]])
            k.dma("sp", btk[b][:], BTOK[ti * 128:(ti + 1) * 128, :], writes=[btk[b]])
            k.dma("sp", btc[b][:], BTd[s, :, :, tt * 128:(tt + 1) * 128].rearrange("g p t -> p g t"), writes=[btc[b]])
            k.dma("sp", ctc[b][:], CTd[s, :, :, tt * 128:(tt + 1) * 128].rearrange("g p t -> p g t"), writes=[ctc[b]])

        load(0)
        for ti in range(c.NTILES):
            if ti + 1 < c.NTILES:
                load(ti + 1)
            b = ti % 2
            s, tt = ti // tps, ti % tps
            if tt == 0:
                k.op("dve", lambda e: e.memset(hst[:], 0.0), [], [hst])
                k.op("pool", lambda e: e.memset(hstb[:], 0.0), [], [hstb])
            k.op("dve", lambda e, b=b: e.tensor_tensor(out=dt[:], in0=dtr[b][:], in1=dtb[:], op=ALU.add), [dtr[b], dtb], [dt])
            k.op("act", lambda e: e.activation(out=dt[:], in_=dt[:], func=AF.Exp), [dt], [dt])
            k.op("act", lambda e: e.activation(out=dt[:], in_=dt[:], func=AF.Ln, bias=1.0, scale=1.0), [dt], [dt])
            k.op("dve", lambda e: e.tensor_tensor(out=dta[:], in0=dt[:], in1=aneg[:], op=ALU.mult), [dt, aneg], [dta])
            p = psf(c)
            k.op("pe", lambda e, p=p: e.matmul(out=p[:, 0:32], lhsT=triU[:], rhs=dta[:], start=True, stop=True), [triU, dta], [p])
            k.op("pe", lambda e, p=p: e.matmul(out=p[:, 512:544], lhsT=ones[:], rhs=dta[:], start=True, stop=True), [ones, dta], [p])
            k.op("dve", lambda e, p=p: e.tensor_copy(out=acs[:], in_=p[:, 0:32]), [p], [acs])
            k.op("act", lambda e: e.activation(out=ea[:], in_=acs[:], func=AF.Exp), [acs], [ea])
            k.op("dve", lambda e, p=p: e.tensor_tensor(out=dec[:], in0=p[:, 512:544], in1=acs[:], op=ALU.subtract), [p, acs], [dec])
            k.op("act", lambda e: e.activation(out=dec[:], in_=dec[:], func=AF.Exp), [dec], [dec])
            k.op("act", lambda e, p=p: e.activation(out=etot[:], in_=p[:, 512:544], func=AF.Exp), [p], [etot])
            k.op("dve", lambda e: e.tensor_tensor(out=Rm[:], in0=triU[:].unsqueeze(1).broadcast_to([128, 32, 128]), in1=dta[:].unsqueeze(2).broadcast_to([128, 32, 128]),
                                                  op=ALU.mult), [triU, dta], [Rm])
            k.op("dve", lambda e, b=b: e.tensor_tensor(out=xdt[:], in0=xs[b][:], in1=dt[:].unsqueeze(2).broadcast_to([128, 32, 64]), op=ALU.mult), [xs[b], dt], [xdt])
            k.op("pool", lambda e: e.tensor_tensor(out=xdd[:], in0=xdt[:], in1=dec[:].unsqueeze(2).broadcast_to([128, 32, 64]), op=ALU.mult), [xdt, dec], [xdd])
            for g in range(4):
                gb = g % 2
                p = psf(c)
                k.op("pe", lambda e, p=p, g=g, b=b: e.matmul(out=p[:, 0:128], lhsT=btc[b][:, g, :], rhs=ctc[b][:, g, :], start=True, stop=True), [btc[b], ctc[b]], [p])
                k.op("dve", lambda e, p=p, gb=gb: e.tensor_tensor(out=cbm[gb][:], in0=p[:, 0:128], in1=triU[:], op=ALU.mult), [p, triU], [cbm[gb]])
                p = psf(c)
                for hh in range(2):
                    k.op("pe", lambda e, p=p, g=g, hh=hh: e.matmul(out=p[:, hh * 512:(hh + 1) * 512], lhsT=Lst[:],
                                                                  rhs=Rm[:, g * 8 + hh * 4:g * 8 + hh * 4 + 4, :].rearrange("p a b -> p (a b)"), start=True, stop=True),
                         [Lst, Rm], [p])
                k.op("act", lambda e, p=p, gb=gb: e.activation(out=LT[gb][:].rearrange("p a b -> p (a b)"), in_=p[:], func=AF.Exp), [p], [LT[gb]])
                k.op("dve", lambda e, g=g, gb=gb: e.tensor_tensor(out=MT[:, g * 8:(g + 1) * 8, :], in0=LT[gb][:], in1=cbm[gb][:].unsqueeze(1).broadcast_to([128, 8, 128]),
                                                                  op=ALU.mult), [LT[gb], cbm[gb]], [MT])
                p = psf(c)
                k.op("pe", lambda e, p=p, g=g, b=b: e.matmul(out=p[:, 0:512], lhsT=ctc[b][:, g, :], rhs=hstb[:, g * 8:(g + 1) * 8, :].rearrange("p a b -> p (a b)"),
                                                            start=True, stop=True), [ctc[b], hstb], [p])
                for r in range(8):
                    hh_ = g * 8 + r
                    k.op("pe", lambda e, p=p, r=r, hh_=hh_: e.matmul(out=p[:, 512 + r * 64:512 + (r + 1) * 64], lhsT=MT[:, hh_, :], rhs=xdt[:, hh_, :], start=True, stop=True),
                         [MT, xdt], [p])
                k.op("dve", lambda e, p=p, g=g: e.tensor_tensor(out=ytmp[:], in0=p[:, 0:512].rearrange("p (a b) -> p a b", a=8),
                                                                in1=ea[:, g * 8:(g + 1) * 8].unsqueeze(2).broadcast_to([128, 8, 64]), op=ALU.mult), [p, ea], [ytmp])
                k.op("dve", lambda e, p=p, g=g: e.tensor_tensor(out=yy[:, g * 8:(g + 1) * 8, :], in0=ytmp[:], in1=p[:, 512:1024].rearrange("p (a b) -> p a b", a=8), op=ALU.add),
                     [ytmp, p], [yy])
            k.op("dve", lambda e: e.tensor_tensor(out=hst[:], in0=hst[:], in1=etot[:].unsqueeze(2).broadcast_to([128, 32, 64]), op=ALU.mult), [hst, etot], [hst])
            for g2 in range(2):
                p = psf(c)
                for q2 in range(2):
                    g = g2 * 2 + q2
                    k.op("pe", lambda e, p=p, q2=q2, g=g, b=b: e.matmul(out=p[:, q2 * 512:(q2 + 1) * 512], lhsT=btk[b][:, g * 128:(g + 1) * 128],
                                                                       rhs=xdd[:, g * 8:(g + 1) * 8, :].rearrange("p a b -> p (a b)"), start=True, stop=True), [btk[b], xdd], [p])
                k.op("dve", lambda e, p=p, g2=g2: e.tensor_tensor(out=hst[:, g2 * 16:(g2 + 1) * 16, :].rearrange("p a b -> p (a b)"),
                                                                  in0=hst[:, g2 * 16:(g2 + 1) * 16, :].rearrange("p a b -> p (a b)"), in1=p[:], op=ALU.add), [hst, p], [hst])
            k.op("act", lambda e: e.copy(out=hstb[:], in_=hst[:]), [hst], [hstb])
            k.op("pool", lambda e, b=b: e.tensor_tensor(out=xs[b][:], in0=xs[b][:], in1=dsk[:].unsqueeze(2).broadcast_to([128, 32, 64]), op=ALU.mult), [xs[b], dsk], [xs[b]])
            k.op("pool", lambda e, b=b: e.tensor_tensor(out=yy[:], in0=yy[:], in1=xs[b][:], op=ALU.add), [yy, xs[b]], [yy])
            k.op("act", lambda e, b=b: e.activation(out=zz[b][:], in_=zz[b][:], func=AF.Silu), [zz[b]], [zz[b]])
            yyf = lambda: yy[:].rearrange("p a b -> p (a b)")
            k.op("dve", lambda e, b=b: e.tensor_tensor(out=yyf(), in0=yyf(), in1=zz[b][:], op=ALU.mult), [yy, zz[b]], [yy])
            for g in range(4):
                k.op("act", lambda e, g=g: e.activation(out=jk[:], in_=yyf()[:, g * 512:(g + 1) * 512], func=AF.Square, accum_out=ssq[:, g:g + 1]), [yy], [jk, ssq])
            k.op("act", lambda e: e.activation(out=ssq[:], in_=ssq[:], func=AF.Sqrt, scale=1.0 / 512.0, bias=LN_EPS), [ssq], [ssq])
            k.op("dve", lambda e: e.reciprocal(out=ssq[:], in_=ssq[:]), [ssq], [ssq])
            k.op("dve", lambda e: e.tensor_tensor(out=yy[:].rearrange("p (g a) b -> p g (a b)", g=4), in0=yy[:].rearrange("p (g a) b -> p g (a b)", g=4),
                                                  in1=ssq[:].unsqueeze(2).broadcast_to([128, 4, 512]), op=ALU.mult), [yy, ssq], [yy])
            k.op("pool", lambda e: e.tensor_tensor(out=yb[:], in0=yyf(), in1=ng[:], op=ALU.mult), [yy, ng], [yb])
            for h2 in range(2):
                transpose_into(c, lambda h2=h2: yT[:, h2 * 8:(h2 + 1) * 8, :].rearrange("p a b -> p (a b)"), lambda ci, h2=h2: yb[:, (h2 * 8 + ci) * 128:(h2 * 8 + ci + 1) * 128],
                               8, yb, yT, evac=("act" if h2 == 0 else "dve"))
            p = psf(c)
            for nb in range(2):
                for kc in range(16):
                    k.op("pe", lambda e, p=p, nb=nb, kc=kc: e.matmul(out=p[:, nb * 512:(nb + 1) * 512], lhsT=yT[:, kc, :], rhs=wo[:, kc, nb * 512:(nb + 1) * 512],
                                                                    start=(kc == 0), stop=(kc == 15)), [yT, wo], [p])
            tl.run(hf[b], lambda hf_, p=p: p[:, hf_ * 512:(hf_ + 1) * 512], p, h_out, ti)


def phase_ml(c, li, h_in, h_out):
    k = c.k
    T_ = c.T
    NCH = T_ // 128
    tps = NCH
    XM = k.dram("ml_XM", [c.NSEQ, 16, 128, T_], F32)
    OG = k.dram("ml_OG", [c.NT, 2048], F32)
    XCT = k.dram("ml_XCT", [c.NSEQ, 16, 128, T_], BF16)
    XMT = k.dram("ml_XMT", [c.NSEQ, 16, 128, T_], BF16)
    XC = k.dram("ml_XC", [c.NT, 2048], F32)
    QTd = k.dram("ml_QT", [c.NSEQ, 16, 128, T_], BF16)
    KTd = k.dram("ml_KT", [c.NSEQ, 16, 128, T_], BF16)
    KTOK = k.dram("ml_KTOK", [c.NT, 2048], BF16)
    VTOK = k.dram("ml_VTOK", [c.NT, 2048], BF16)
    GI = k.dram("ml_GI", [c.NSEQ, 4, T_], F32)
    GF = k.dram("ml_GF", [c.NSEQ, 4, T_], F32)
    GO = k.dram("ml_GO", [c.NT, 2048], BF16)
    with k.scope():
        w = k.sb([128, 8, 4096], BF16, "w_in")
        load_w(c, w, c.inp["ml_w_in"][0], 1024, 4096)
        hf = [k.sb([128, D], F32, "hf") for _ in range(2)]
        hb = k.sb([128, D], BF16, "hb")
        hT = k.sb([128, 8, 128], BF16, "hT")
        zt = [k.sb([128, 1024], F32, "zt") for _ in range(2)]
        xt = [k.sb([128, 8, 128], F32, "xt") for _ in range(2)]

        def load(ti):
            k.dma("sp", hf[ti % 2][:], h_in[ti * 128:(ti + 1) * 128, :], writes=[hf[ti % 2]])

        load(0)
        ev = 0
        for ti in range(c.NTILES):
            if ti + 1 < c.NTILES:
                load(ti + 1)
            b = ti % 2
            s, tt = ti // tps, ti % tps
            k.op("dve", lambda e, b=b: e.tensor_copy(out=hb[:], in_=hf[b][:]), [hf[b]], [hb])
            transpose_into(c, lambda: hT[:].rearrange("p a b -> p (a b)"), lambda ci: hb[:, ci * 128:(ci + 1) * 128], 8, hb, hT)
            for half in range(2):
                p = psf(c)
                for fc in range(8):
                    col = (half * 8 + fc) * 128
                    for kc in range(8):
                        k.op("pe", lambda e, p=p, fc=fc, col=col, kc=kc: e.matmul(out=p[:, fc * 128:(fc + 1) * 128], lhsT=w[:, kc, col:col + 128], rhs=hT[:, kc, :],
                                                                                 start=(kc == 0), stop=(kc == 7)), [hT, w], [p])
                xb_ = xt[ev % 2]
                k.op("dve", lambda e, p=p, xb_=xb_: e.tensor_copy(out=xb_[:].rearrange("p a b -> p (a b)"), in_=p[:]), [p], [xb_])
                k.dma("sp", XM[s, half * 8:(half + 1) * 8, :, tt * 128:(tt + 1) * 128].rearrange("n p t -> p n t"), xb_[:], reads=[xb_])
                p = psf(c)
                for nb in range(2):
                    n0 = 2048 + half * 1024 + nb * 512
                    for kc in range(8):
                        k.op("pe", lambda e, p=p, nb=nb, n0=n0, kc=kc: e.matmul(out=p[:, nb * 512:(nb + 1) * 512], lhsT=hT[:, kc, :], rhs=w[:, kc, n0:n0 + 512],
                                                                               start=(kc == 0), stop=(kc == 7)), [hT, w], [p])
                zb = zt[ev % 2]
                ev += 1
                k.op("act", lambda e, p=p, zb=zb: e.copy(out=zb[:], in_=p[:]), [p], [zb])
                k.dma("sp", OG[ti * 128:(ti + 1) * 128, half * 1024:(half + 1) * 1024], zb[:], reads=[zb])
    if 'mlA' in DBG:
        return
    with k.scope():
        cwp = k.sb([128, 2048], F32, "cwp")
        k.op("dve", lambda e: e.memset(cwp[:], 0.0), [], [cwp])
        k.dma("sp", cwp[0:4, :], c.inp["ml_conv_w"][0], writes=[cwp])
        k.dma("sp", cwp[4:5, :], c.inp["ml_conv_b"][0:1, :], writes=[cwp])
        cw = k.sb([128, 16, 8], F32, "cw")
        for cch in range(16):
            p = psf(c)
            k.op("pe", lambda e, p=p, cch=cch: e.transpose(out=p[:, 0:128], in_=cwp[:, cch * 128:(cch + 1) * 128], identity=c.identf[:]), [cwp, c.identf], [p])
            k.op("act", lambda e, p=p, cch=cch: e.copy(out=cw[:, cch, :], in_=p[:, 0:8]), [p], [cw])
        xin = [k.sb([128, T_], F32, "xin") for _ in range(2)]
        acc = [k.sb([128, T_], F32, "acc") for _ in range(2)]
        accb = [k.sb([128, T_], BF16, "accb") for _ in range(2)]
        xmb = [k.sb([128, T_], BF16, "xmb") for _ in range(2)]
        stg = [k.sb([128, 4, 128], F32, "stg") for _ in range(2)]
        u = 0
        for s in range(c.NSEQ):
            for cch in range(16):
                ub = u % 2
                u += 1
                xi, ac, abf, xb2 = xin[ub], acc[ub], accb[ub], xmb[ub]
                k.dma("sp", xi[:], XM[s, cch], writes=[xi])
                k.op("pool", lambda e, xi=xi, xb2=xb2: e.tensor_copy(out=xb2[:], in_=xi[:]), [xi], [xb2])
                k.dma("sp", XMT[s, cch], xb2[:], reads=[xb2])
                k.op("dve", lambda e, xi=xi, ac=ac, cch=cch: e.tensor_scalar(out=ac[:], in0=xi[:], scalar1=cw[:, cch, 3:4], scalar2=None, op0=ALU.mult), [xi, cw], [ac])
                for kk in range(3):
                    shf = 3 - kk
                    k.op("dve", lambda e, xi=xi, ac=ac, cch=cch, kk=kk, shf=shf: e.scalar_tensor_tensor(out=ac[:, shf:T_], in0=xi[:, 0:T_ - shf], scalar=cw[:, cch, kk:kk + 1],
                                                                                                  in1=ac[:, shf:T_], op0=ALU.mult, op1=ALU.add), [xi, cw, ac], [ac])
                k.op("act", lambda e, ac=ac, cch=cch: e.activation(out=ac[:], in_=ac[:], func=AF.Silu, bias=cw[:, cch, 4:5], scale=1.0), [ac, cw], [ac])
                k.op("pool", lambda e, ac=ac, abf=abf: e.tensor_copy(out=abf[:], in_=ac[:]), [ac], [abf])
                k.dma("sp", XCT[s, cch], abf[:], reads=[abf])
                for n4 in range(T_ // 512):
                    p = psf(c)
                    for q4 in range(4):
                        t0 = n4 * 512 + q4 * 128
                        k.op("pe", lambda e, p=p, q4=q4, t0=t0, ac=ac: e.transpose(out=p[:, q4 * 128:(q4 + 1) * 128], in_=ac[:, t0:t0 + 128], identity=c.identf[:]),
                             [ac, c.identf], [p])
                    sg_ = stg[n4 % 2]
                    k.op("act", lambda e, p=p, sg_=sg_: e.copy(out=sg_[:].rearrange("p a b -> p (a b)"), in_=p[:, 0:512]), [p], [sg_])
                    k.dma("sp", XC[s * T_ + n4 * 512:s * T_ + (n4 + 1) * 512, cch * 128:(cch + 1) * 128].rearrange("(n p) c -> p n c", p=128), sg_[:], reads=[sg_])
    if 'mlB' in DBG:
        return
    with k.scope():
        wq = k.sb([128, 16, 512], BF16, "wq")
        wk = k.sb([128, 16, 512], BF16, "wk")
        wv = k.sb([128, 16, 512], BF16, "wv")
        load_w(c, wq, c.inp["ml_w_q"][0].rearrange("h d e -> (h d) e"), 2048, 512)
        load_w(c, wk, c.inp["ml_w_k"][0].rearrange("h d e -> (h d) e"), 2048, 512)
        load_w(c, wv, c.inp["ml_w_v"][0].rearrange("h d e -> (h d) e"), 2048, 512)
        wgs = k.sb([128, 48, 8], F32, "wgs")
        for gs in range(3):
            k.dma("sp", wgs[:, gs * 16:(gs + 1) * 16, :], c.inp["ml_w_gates"][0, gs].rearrange("(c p) n -> p c n", p=128), writes=[wgs])
        wgp = k.sb([128, 48, 128], BF16, "wgp")
        k.op("dve", lambda e: e.memset(wgp[:], 0.0), [], [wgp])
        k.op("dve", lambda e: e.tensor_copy(out=wgp[:, :, 0:4], in_=wgs[:, :, 0:4]), [wgs], [wgp])
        k.op("dve", lambda e: e.tensor_copy(out=wgp[:, :, 32:36], in_=wgs[:, :, 4:8]), [wgs], [wgp])
        bcol = k.sb([128, 1], F32, "bcol")
        nbcol = k.sb([128, 1], F32, "nbcol")
        k.op("dve", lambda e: e.memset(bcol[:], 0.0), [], [bcol])
        k.dma("sp", bcol[0:4, :], c.inp["ml_b_gates"][0:1, 0:4].rearrange("o n -> n o"), writes=[bcol])
        k.dma("sp", bcol[32:36, :], c.inp["ml_b_gates"][0:1, 4:8].rearrange("o n -> n o"), writes=[bcol])
        k.op("dve", lambda e: e.tensor_scalar(out=nbcol[:], in0=bcol[:], scalar1=-1.0, scalar2=None, op0=ALU.mult), [bcol], [nbcol])
        xcT = [k.sb([128, 16, 128], BF16, "xcT") for _ in range(2)]
        xmT = [k.sb([128, 16, 128], BF16, "xmT") for _ in range(2)]
        qT = [k.sb([128, 16, 128], BF16, "qT") for _ in range(2)]
        kT = [k.sb([128, 16, 128], BF16, "kT") for _ in range(2)]
        kt = [k.sb([128, 2048], BF16, "kt") for _ in range(2)]
        vt = [k.sb([128, 2048], BF16, "vt") for _ in range(2)]
        vT = k.sb([128, 16, 128], BF16, "vT")
        gt = [k.sb([128, 128], F32, "gt") for _ in range(2)]
        lf = [k.sb([128, 128], F32, "lf") for _ in range(2)]
        KS = 512.0 ** -0.5

        def load(ti):
            s, tt = ti // tps, ti % tps
            k.dma("sp", xcT[ti % 2][:], XCT[s, :, :, tt * 128:(tt + 1) * 128].rearrange("n p t -> p n t"), writes=[xcT[ti % 2]])
            k.dma("sp", xmT[ti % 2][:], XMT[s, :, :, tt * 128:(tt + 1) * 128].rearrange("n p t -> p n t"), writes=[xmT[ti % 2]])

        load(0)
        for ti in range(c.NTILES):
            if ti + 1 < c.NTILES:
                load(ti + 1)
            b = ti % 2
            s, tt = ti // tps, ti % tps
            for which in range(2):
                wsrc = wq if which == 0 else wk
                dstT = qT[b] if which == 0 else kT[b]
                for half in range(2):
                    p = psf(c)
                    for oc in range(8):
                        o16 = half * 8 + oc
                        hh, ec = o16 // 4, o16 % 4
                        for dc in range(4):
                            k.op("pe", lambda e, p=p, oc=oc, hh=hh, ec=ec, dc=dc, wsrc=wsrc, b=b: e.matmul(
                                out=p[:, oc * 128:(oc + 1) * 128], lhsT=wsrc[:, hh * 4 + dc, ec * 128:(ec + 1) * 128], rhs=xcT[b][:, hh * 4 + dc, :],
                                start=(dc == 0), stop=(dc == 3)), [wsrc, xcT[b]], [p])
                    if which == 0:
                        k.op("act", lambda e, p=p, dstT=dstT, half=half: e.copy(out=dstT[:, half * 8:(half + 1) * 8, :].rearrange("p a b -> p (a b)"), in_=p[:]), [p], [dstT])
                    else:
                        k.op("act", lambda e, p=p, dstT=dstT, half=half: e.activation(out=dstT[:, half * 8:(half + 1) * 8, :].rearrange("p a b -> p (a b)"), in_=p[:],
                                                                                      func=AF.Copy, scale=KS), [p], [dstT])
                dd = QTd if which == 0 else KTd
                k.dma("sp", dd[s, :, :, tt * 128:(tt + 1) * 128].rearrange("n p t -> p n t"), dstT[:], reads=[dstT])
            for which in range(2):
                wsrc = wk if which == 0 else wv
                src = xcT[b] if which == 0 else xmT[b]
                dst = kt[b] if which == 0 else vt[b]
                for half in range(2):
                    p = psf(c)
                    for q2 in range(2):
                        hh = half * 2 + q2
                        for dc in range(4):
                            k.op("pe", lambda e, p=p, q2=q2, hh=hh, dc=dc, wsrc=wsrc, src=src: e.matmul(
                                out=p[:, q2 * 512:(q2 + 1) * 512], lhsT=src[:, hh * 4 + dc, :], rhs=wsrc[:, hh * 4 + dc, :], start=(dc == 0), stop=(dc == 3)), [wsrc, src], [p])
                    if which == 0:
                        k.op("dve", lambda e, p=p, dst=dst, half=half: e.tensor_scalar(out=dst[:, half * 1024:(half + 1) * 1024], in0=p[:], scalar1=KS, scalar2=None, op0=ALU.mult),
                             [p], [dst])
                    else:
                        k.op("dve", lambda e, p=p, dst=dst, half=half: e.tensor_copy(out=dst[:, half * 1024:(half + 1) * 1024], in_=p[:]), [p], [dst])
                dd = KTOK if which == 0 else VTOK
                k.dma("sp", dd[ti * 128:(ti + 1) * 128, :], dst[:], reads=[dst])
            for h2 in range(2):
                transpose_into(c, lambda h2=h2: vT[:, h2 * 8:(h2 + 1) * 8, :].rearrange("p a b -> p (a b)"), lambda ci, h2=h2, b=b: vt[b][:, (h2 * 8 + ci) * 128:(h2 * 8 + ci + 1) * 128],
                               8, vt[b], vT, evac=("act" if h2 == 0 else "dve"))
            p = psf(c)
            n_mm = 0
            for gs, srcT in enumerate((qT[b], kT[b], vT)):
                for fc in range(16):
                    k.op("pe", lambda e, p=p, gs=gs, fc=fc, srcT=srcT, n_mm=n_mm: e.matmul(out=p[:, 0:128], lhsT=wgp[:, gs * 16 + fc, :], rhs=srcT[:, fc, :],
                                                                                       start=(n_mm == 0), stop=(n_mm == 47)), [wgp, srcT], [p])
                    n_mm += 1
            k.op("act", lambda e, p=p, b=b: e.activation(out=gt[b][:], in_=p[:, 0:128], func=AF.Identity, bias=bcol[:, 0:1], scale=1.0), [p, bcol], [gt[b]])
            k.op("act", lambda e, p=p, b=b: e.activation(out=lf[b][:], in_=p[:, 0:128], func=AF.Exp, bias=nbcol[:, 0:1], scale=-1.0), [p, nbcol], [lf[b]])
            k.op("act", lambda e, b=b: e.activation(out=lf[b][:], in_=lf[b][:], func=AF.Ln, bias=1.0, scale=1.0), [lf[b]], [lf[b]])
            k.op("dve", lambda e, b=b: e.tensor_scalar(out=lf[b][:], in0=lf[b][:], scalar1=-1.0, scalar2=None, op0=ALU.mult), [lf[b]], [lf[b]])
            k.dma("sp", GI[s, :, tt * 128:(tt + 1) * 128], gt[b][0:4, :], reads=[gt[b]])
            k.dma("sp", GF[s, :, tt * 128:(tt + 1) * 128], lf[b][32:36, :], reads=[lf[b]])
    if 'mlC' in DBG:
        return
    with k.scope():
        sel = k.sb([128, 4, 128], F32, "sel")
        k.dma("sp", sel[:], c.inp["c_sel"][:, :, :], writes=[sel])
        negm = k.sb([128, 128], F32, "negm")
        k.dma("sp", negm[:], c.inp["c_Lst"][:, :], writes=[negm])
        k.op("dve", lambda e: e.tensor_scalar(out=negm[:], in0=negm[:], scalar1=NEG, scalar2=None, op0=ALU.mult), [negm], [negm])
        ones2 = k.sb([128, 2], BF16, "ones2")
        k.op("dve", lambda e: e.memset(ones2[:], 1.0), [], [ones2])
        ones512 = k.sb([128, 512], F32, "ones512")
        k.op("dve", lambda e: e.memset(ones512[:], 1.0), [], [ones512])
        ngt = k.sb([128, 2048], F32, "ngt")
        skt = k.sb([128, 2048], F32, "skt")
        load_bc(c, ngt, c.inp["ml_norm_g"][0:1, :])
        load_bc(c, skt, c.inp["ml_skip"][0:1, :])
        WROW = k.sb([128, T_], F32, "WROW")
        NCM = k.sb([128, T_], F32, "NCM")
        MROW = k.sb([128, T_], F32, "MROW")
        PM = k.sb([128, 4, NCH + 1], F32, "PM")
        NM = k.sb([128, 4, NCH + 1], F32, "NM")
        Cst = k.sb([128, 4, 4, 512], F32, "Cst")
        Cb = k.sb([128, 4, 4, 512], BF16, "Cb")
        nst = k.sb([128, 4, 4, 2], F32, "nst")
        nstb = k.sb([128, 4, 4, 2], BF16, "nstb")
        qTt = [k.sb([128, 16, 128], BF16, "qTt") for _ in range(2)]
        kTt = [k.sb([128, 16, 128], BF16, "kTt") for _ in range(2)]
        ktt = [k.sb([128, 2048], BF16, "ktt") for _ in range(2)]
        vtt = [k.sb([128, 2048], BF16, "vtt") for _ in range(2)]
        ogt = k.sb([128, 2048], F32, "ogt")
        xct = k.sb([128, 2048], F32, "xct")
        cols = k.sb([128, 3, 4], F32, "cols")
        ET = k.sb([128, 128], F32, "ET")
        ST = k.sb([128, 128], BF16, "ST")
        sc = k.sb([128, 1], F32, "sc")
        wkc = k.sb([128, 1], F32, "wkc")
        cd = k.sb([128, 1], F32, "cd")
        em = k.sb([128, 1], F32, "em")
        den = k.sb([128, 2], F32, "den")
        tmp512 = k.sb([128, 512], F32, "tmp512")
        num = k.sb([128, 512], F32, "num")
        kw = k.sb([128, 512], BF16, "kw")
        st6 = k.sb([128, 6], F32, "st6")
        mv2 = k.sb([128, 2], F32, "mv2")
        rs1 = k.sb([128, 1], F32, "rs1")
        outb = [k.sb([128, 2048], BF16, "outb") for _ in range(2)]

        def load(ti):
            s, tt = ti // tps, ti % tps
            b = ti % 2
            k.dma("sp", qTt[b][:], QTd[s, :, :, tt * 128:(tt + 1) * 128].rearrange("n p t -> p n t"), writes=[qTt[b]])
            k.dma("sp", kTt[b][:], KTd[s, :, :, tt * 128:(tt + 1) * 128].rearrange("n p t -> p n t"), writes=[kTt[b]])
            k.dma("sp", ktt[b][:], KTOK[ti * 128:(ti + 1) * 128, :], writes=[ktt[b]])
            k.dma("sp", vtt[b][:], VTOK[ti * 128:(ti + 1) * 128, :], writes=[vtt[b]])

        for s in range(c.NSEQ):
            with k.scope():
                F2 = k.sb([128, T_], F32, "F2")
                if s == 0:
                    k.op("dve", lambda e: e.memset(WROW[:], 0.0), [], [WROW])
                    k.op("pool", lambda e: e.memset(NCM[:], 0.0), [], [NCM])
                    k.op("pool", lambda e: e.memset(MROW[:], 0.0), [], [MROW])
                k.dma("sp", MROW[0:4, :], GF[s], writes=[MROW])
                k.dma("sp", WROW[0:4, :], GI[s], writes=[WROW])
                for blk in range(T_ // 512):
                    sl = slice(blk * 512, (blk + 1) * 512)
                    init = 0.0 if blk == 0 else F2[0:4, blk * 512 - 1:blk * 512]
                    k.op("dve", lambda e, sl=sl, init=init: e.tensor_tensor_scan(out=F2[0:4, sl], data0=ones512[0:4, :], data1=MROW[0:4, sl], initial=init,
                                                                                op0=ALU.mult, op1=ALU.add), [ones512, MROW, F2], [F2])
                k.op("dve", lambda e: e.tensor_tensor(out=WROW[0:4, :], in0=WROW[0:4, :], in1=F2[0:4, :], op=ALU.subtract), [WROW, F2], [WROW])
                for blk in range(T_ // 512):
                    sl = slice(blk * 512, (blk + 1) * 512)
                    init = 0.0 if blk == 0 else NCM[0:4, blk * 512 - 1:blk * 512]
                    k.op("dve", lambda e, sl=sl, init=init: e.tensor_tensor_scan(out=NCM[0:4, sl], data0=ones512[0:4, :], data1=WROW[0:4, sl], initial=init,
                                                                                op0=ALU.mult, op1=ALU.max), [ones512, WROW, NCM], [NCM])
                k.op("dve", lambda e: e.tensor_tensor(out=MROW[0:4, :], in0=F2[0:4, :], in1=NCM[0:4, :], op=ALU.add), [F2, NCM], [MROW])
                k.op("dve", lambda e: e.tensor_scalar(out=NCM[0:4, :], in0=NCM[0:4, :], scalar1=-1.0, scalar2=None, op0=ALU.mult), [NCM], [NCM])
                k.op("dve", lambda e: e.memset(NM[:], 0.0), [], [NM])
                for hh in range(4):
                    p = psf(c)
                    k.op("pe", lambda e, p=p, hh=hh: e.matmul(out=p[:, 0:NCH], lhsT=sel[:, hh, :], rhs=NCM[:, 127::128], start=True, stop=True), [sel, NCM], [p])
                    k.op("dve", lambda e, p=p, hh=hh: e.tensor_copy(out=NM[:, hh, 1:NCH + 1], in_=p[:, 0:NCH]), [p], [NM])
                k.op("dve", lambda e: e.tensor_scalar(out=PM[:], in0=NM[:], scalar1=-1.0, scalar2=None, op0=ALU.mult), [NM], [PM])
            if 'mlD' in DBG:
                return
            k.op("dve", lambda e: e.memset(Cst[:], 0.0), [], [Cst])
            k.op("pool", lambda e: e.memset(Cb[:], 0.0), [], [Cb])
            k.op("dve", lambda e: e.memset(nst[:], 0.0), [], [nst])
            k.op("dve", lambda e: e.memset(nstb[:], 0.0), [], [nstb])
            load(s * tps)
            for n in range(NCH):
                ti = s * tps + n
                if n + 1 < NCH:
                    load(ti + 1)
                b = ti % 2
                t0 = n * 128
                k.dma("sp", ogt[:], OG[ti * 128:(ti + 1) * 128, :], writes=[ogt])
                k.dma("sp", xct[:], XC[ti * 128:(ti + 1) * 128, :], writes=[xct])
                k.op("act", lambda e: e.activation(out=ogt[:], in_=ogt[:], func=AF.Sigmoid), [ogt], [ogt])
                p = psf(c)
                for a_, src in enumerate((WROW, NCM, MROW)):
                    k.op("pe", lambda e, p=p, a_=a_, src=src, t0=t0: e.transpose(out=p[:, a_ * 128:(a_ + 1) * 128], in_=src[:, t0:t0 + 128], identity=c.identf[:]),
                         [src, c.identf], [p])
                k.op("act", lambda e, p=p: e.copy(out=cols[:], in_=p[:, 0:384].rearrange("p (a b) -> p a b", a=3)[:, :, 0:4]), [p], [cols])
                for hh in range(4):
                    hs = slice(hh * 512, (hh + 1) * 512)
                    p = psf(c)
                    for dc in range(4):
                        k.op("pe", lambda e, p=p, hh=hh, dc=dc, b=b: e.matmul(out=p[:, 0:128], lhsT=kTt[b][:, hh * 4 + dc, :], rhs=qTt[b][:, hh * 4 + dc, :],
                                                                             start=(dc == 0), stop=(dc == 3)), [kTt[b], qTt[b]], [p])
                    k.op("pe", lambda e, p=p, hh=hh, t0=t0: e.matmul(out=p[:, 512:640], lhsT=WROW[:, t0:t0 + 128], rhs=sel[:, hh, :], start=True, stop=False), [WROW, sel], [p])
                    k.op("pe", lambda e, p=p, hh=hh, t0=t0: e.matmul(out=p[:, 512:640], lhsT=sel[:, hh, :], rhs=NCM[:, t0:t0 + 128], start=False, stop=False), [NCM, sel], [p])
                    k.op("pe", lambda e, p=p: e.matmul(out=p[:, 512:640], lhsT=c.identf[:], rhs=negm[:], start=False, stop=True), [c.identf, negm], [p])
                    k.op("act", lambda e, p=p: e.activation(out=ET[:], in_=p[:, 512:640], func=AF.Exp), [p], [ET])
                    k.op("dve", lambda e, p=p: e.tensor_tensor(out=ST[:], in0=ET[:], in1=p[:, 0:128], op=ALU.mult), [ET, p], [ST])
                    p1 = psf(c)
                    k.op("pe", lambda e, p1=p1, hs=hs, b=b: e.matmul(out=p1[:, 0:512], lhsT=ST[:], rhs=vtt[b][:, hs], start=True, stop=True), [ST, vtt[b]], [p1])
                    for dc in range(4):
                        k.op("pe", lambda e, p1=p1, hh=hh, dc=dc, b=b: e.matmul(out=p1[:, 512:1024], lhsT=qTt[b][:, hh * 4 + dc, :], rhs=Cb[:, hh, dc, :],
                                                                               start=(dc == 0), stop=(dc == 3)), [qTt[b], Cb], [p1])
                    p2 = psf(c)
                    k.op("pe", lambda e, p2=p2: e.matmul(out=p2[:, 0:2], lhsT=ST[:], rhs=ones2[:], start=True, stop=True), [ST, ones2], [p2])
                    for dc in range(4):
                        k.op("pe", lambda e, p2=p2, hh=hh, dc=dc, b=b: e.matmul(out=p2[:, 512:514], lhsT=qTt[b][:, hh * 4 + dc, :], rhs=nstb[:, hh, dc, :],
                                                                               start=(dc == 0), stop=(dc == 3)), [qTt[b], nstb], [p2])
                    k.op("act", lambda e, hh=hh, n=n: e.activation(out=sc[:], in_=cols[:, 1, hh:hh + 1], func=AF.Exp, bias=PM[:, hh, n:n + 1], scale=1.0), [cols, PM], [sc])
                    k.op("act", lambda e, p1=p1: e.activation(out=tmp512[:], in_=p1[:, 512:1024], func=AF.Copy, scale=sc[:, 0:1]), [p1, sc], [tmp512])
                    k.op("dve", lambda e, p1=p1: e.tensor_tensor(out=num[:], in0=tmp512[:], in1=p1[:, 0:512], op=ALU.add), [tmp512, p1], [num])
                    k.op("dve", lambda e, p2=p2: e.tensor_scalar(out=den[:, 0:1], in0=p2[:, 512:513], scalar1=sc[:, 0:1], scalar2=None, op0=ALU.mult), [p2, sc], [den])
                    k.op("dve", lambda e, p2=p2: e.tensor_tensor(out=den[:, 0:1], in0=den[:, 0:1], in1=p2[:, 0:1], op=ALU.add), [den, p2], [den])
                    k.op("act", lambda e: e.activation(out=den[:, 0:1], in_=den[:, 0:1], func=AF.Abs), [den], [den])
                    k.op("act", lambda e, hh=hh: e.activation(out=em[:], in_=cols[:, 2, hh:hh + 1], func=AF.Exp, scale=-1.0), [cols], [em])
                    k.op("dve", lambda e: e.tensor_tensor(out=den[:, 0:1], in0=den[:, 0:1], in1=em[:], op=ALU.max), [den, em], [den])
                    k.op("dve", lambda e: e.reciprocal(out=den[:, 0:1], in_=den[:, 0:1]), [den], [den])
                    k.op("dve", lambda e: e.tensor_scalar(out=num[:], in0=num[:], scalar1=den[:, 0:1], scalar2=None, op0=ALU.mult), [num, den], [num])
                    k.op("dve", lambda e: e.bn_stats(out=st6[:], in_=num[:]), [num], [st6])
                    k.op("dve", lambda e: e.bn_aggr(out=mv2[:], in_=st6[:]), [st6], [mv2])
                    k.op("act", lambda e: e.activation(out=rs1[:], in_=mv2[:, 1:2], func=AF.Sqrt, bias=LN_EPS, scale=1.0), [mv2], [rs1])
                    k.op("dve", lambda e: e.reciprocal(out=rs1[:], in_=rs1[:]), [rs1], [rs1])
                    k.op("dve", lambda e: e.tensor_scalar(out=num[:], in0=num[:], scalar1=mv2[:, 0:1], scalar2=rs1[:, 0:1], op0=ALU.subtract, op1=ALU.mult), [num, mv2, rs1], [num])
                    k.op("pool", lambda e, hs=hs: e.tensor_tensor(out=num[:], in0=num[:], in1=ngt[:, hs], op=ALU.mult), [num, ngt], [num])
                    k.op("pool", lambda e, hs=hs: e.tensor_tensor(out=tmp512[:], in0=xct[:, hs], in1=skt[:, hs], op=ALU.mult), [xct, skt], [tmp512])
                    k.op("pool", lambda e: e.tensor_tensor(out=num[:], in0=num[:], in1=tmp512[:], op=ALU.add), [num, tmp512], [num])
                    k.op("dve", lambda e, hs=hs, b=b: e.tensor_tensor(out=outb[b][:, hs], in0=num[:], in1=ogt[:, hs], op=ALU.mult), [num, ogt], [outb[b]])
                    k.op("act", lambda e, hh=hh, n=n: e.activation(out=wkc[:], in_=cols[:, 0, hh:hh + 1], func=AF.Exp, bias=NM[:, hh, n + 1:n + 2], scale=1.0), [cols, NM], [wkc])
                    k.op("act", lambda e, hh=hh, n=n: e.activation(out=cd[:], in_=PM[:, hh, n:n + 1], func=AF.Exp, bias=NM[:, hh, n + 1:n + 2], scale=1.0), [PM, NM], [cd])
                    k.op("dve", lambda e, hs=hs, b=b: e.tensor_scalar(out=kw[:], in0=ktt[b][:, hs], scalar1=wkc[:, 0:1], scalar2=None, op0=ALU.mult), [ktt[b], wkc], [kw])
                    for d2 in range(2):
                        p3 = psf(c)
                        for q2 in range(2):
                            dkc = d2 * 2 + q2
                            k.op("pe", lambda e, p3=p3, q2=q2, dkc=dkc, hs=hs, b=b: e.matmul(out=p3[:, q2 * 512:(q2 + 1) * 512], lhsT=kw[:, dkc * 128:(dkc + 1) * 128],
                                                                                          rhs=vtt[b][:, hs], start=True, stop=True), [kw, vtt[b]], [p3])
                        k.op("dve", lambda e, p3=p3, d2=d2, hh=hh: e.scalar_tensor_tensor(out=Cst[:, hh, d2 * 2:(d2 + 1) * 2, :].rearrange("p a b -> p (a b)"),
                                                                                        in0=Cst[:, hh, d2 * 2:(d2 + 1) * 2, :].rearrange("p a b -> p (a b)"), scalar=cd[:, 0:1],
                                                                                        in1=p3[:], op0=ALU.mult, op1=ALU.add), [Cst, cd, p3], [Cst])
                    k.op("act", lambda e, hh=hh: e.copy(out=Cb[:, hh, :, :], in_=Cst[:, hh, :, :]), [Cst], [Cb])
                    p4 = psf(c)
                    for dkc in range(4):
                        k.op("pe", lambda e, p4=p4, dkc=dkc: e.matmul(out=p4[:, dkc * 2:dkc * 2 + 2], lhsT=kw[:, dkc * 128:(dkc + 1) * 128], rhs=ones2[:], start=True, stop=True),
                             [kw, ones2], [p4])
                    k.op("dve", lambda e, p4=p4, hh=hh: e.scalar_tensor_tensor(out=nst[:, hh, :, :].rearrange("p a b -> p (a b)"), in0=nst[:, hh, :, :].rearrange("p a b -> p (a b)"),
                                                                               scalar=cd[:, 0:1], in1=p4[:, 0:8], op0=ALU.mult, op1=ALU.add), [nst, cd, p4], [nst])
                    k.op("dve", lambda e, hh=hh: e.tensor_copy(out=nstb[:, hh, :, :], in_=nst[:, hh, :, :]), [nst], [nstb])
                k.dma("sp", GO[ti * 128:(ti + 1) * 128, :], outb[b][:], reads=[outb[b]])
    if 'mlE' in DBG:
        return
    with k.scope():
        wd = k.sb([128, 16, 1024], BF16, "w_down")
        load_w(c, wd, c.inp["ml_w_down"][0], 2048, 1024)
        tl = Tail(c, li, 0)
        hf = [k.sb([128, D], F32, "hf") for _ in range(2)]
        ab = [k.sb([128, 2048], BF16, "ab") for _ in range(2)]
        aT = [k.sb([128, 16, 128], BF16, "aT") for _ in range(2)]

        def load2(ti):
            k.dma("sp", hf[ti % 2][:], h_in[ti * 128:(ti + 1) * 128, :], writes=[hf[ti % 2]])
            k.dma("sp", ab[ti % 2][:], GO[ti * 128:(ti + 1) * 128, :], writes=[ab[ti % 2]])

        load2(0)
        for ti in range(c.NTILES):
            if ti + 1 < c.NTILES:
                load2(ti + 1)
            b = ti % 2
            for h2 in range(2):
                transpose_into(c, lambda h2=h2, b=b: aT[b][:, h2 * 8:(h2 + 1) * 8, :].rearrange("p a b -> p (a b)"),
                               lambda ci, h2=h2, b=b: ab[b][:, (h2 * 8 + ci) * 128:(h2 * 8 + ci + 1) * 128], 8, ab[b], aT[b], evac=("act" if h2 == 0 else "dve"))
            p = psf(c)
            for nb in range(2):
                for kc in range(16):
                    k.op("pe", lambda e, p=p, nb=nb, kc=kc, b=b: e.matmul(out=p[:, nb * 512:(nb + 1) * 512], lhsT=aT[b][:, kc, :], rhs=wd[:, kc, nb * 512:(nb + 1) * 512],
                                                                         start=(kc == 0), stop=(kc == 15)), [aT[b], wd], [p])
            tl.run(hf[b], lambda hf_, p=p: p[:, hf_ * 512:(hf_ + 1) * 512], p, h_out, ti)


def build(T_, NSEQ, plan, needed):
    c = setup(T_, NSEQ, needed)
    k = c.k
    cur = c.inp["x"]
    bufs = [c.hA, c.hB]
    c.peer_layers = [li for kind, li in plan if kind == "peer"]
    bi = 0
    for pi, (kind, li) in enumerate(plan):
        dst = c.out if pi == len(plan) - 1 else bufs[bi]
        if kind == "xa":
            phase_xa(c, li, cur, dst)
        elif kind == "peer":
            phase_peer(c, li, cur, dst)
        elif kind == "s5":
            phase_s5(c, li, cur, dst)
        elif kind == "da":
            phase_da(c, li, cur, dst)
        elif kind == "m2":
            phase_m2(c, li, cur, dst)
        elif kind == "ml":
            phase_ml(c, li, cur, dst)
        cur = dst
        bi ^= 1
    return c, k.finish()


N_CORES = 8
SEQ_FULL = 4096
PLAN = []
for _i, _mx in enumerate(["s5", "da", "m2", "ml"]):
    PLAN += [(_mx, _i), ("xa", _i), ("peer", _i)]


def kernel(**inputs):
    nseq = 16 // N_CORES
    needed = set(n for n, _ in INPUT_SPECS)
    c, nc = build(SEQ_FULL, nseq, PLAN, needed)
    consts = host_consts(SEQ_FULL)
    x = np.ascontiguousarray(np.asarray(inputs["x"], dtype=np.float32))
    mem = np.ascontiguousarray(np.asarray(inputs["mem"], dtype=np.float32))
    shared = {}
    for name in c.inp:
        if name in ("x", "mem"):
            continue
        if name.startswith("c_"):
            shared[name] = consts[name]
        else:
            shared[name] = np.ascontiguousarray(np.asarray(inputs[name], dtype=np.float32))
    in_maps = []
    for ci in range(N_CORES):
        m = dict(shared)
        m["x"] = x[ci * nseq:(ci + 1) * nseq].reshape(nseq * SEQ_FULL, D)
        m["mem"] = mem[ci * nseq:(ci + 1) * nseq].reshape(nseq * 256, D)
        in_maps.append(m)
    res = run_bass_kernel_spmd(nc, in_maps, core_ids=list(range(N_CORES)))
    outs = [np.asarray(r["out"]).reshape(nseq, SEQ_FULL, D) for r in res.results]
    return np.concatenate(outs, axis=0).astype(np.float32)
```
